# Optimizing a Trainium2 kernel written in Bass

```python
import jax, jax.numpy as jnp
from jax import lax
import numpy as np

D_MODEL = 1024
BATCH = 16
SEQ = 2048
DEPTH = 2

POOL_GROUPS = 4
POOL_GROUP_DIM = 64
POOL_WINDOWS = (2, 4, 8, 16)
D_POOL = POOL_GROUPS * POOL_GROUP_DIM
D_CONV = 256
CONV_WIDTH = 31
N_HEADS = 8
N_KV_HEADS = 2
HEAD_DIM = 64
D_ATTN = N_HEADS * HEAD_DIM
D_KV = N_KV_HEADS * HEAD_DIM
IDX_HEADS = 8
IDX_DIM = 64
TOPK_MAX = 256
Q_BLOCK = 128
ROPE_THETA = 10000.0
N_BRANCH = 3
IN_SIZES = (D_POOL, 2 * D_CONV, D_ATTN, D_KV, D_KV, IDX_HEADS * IDX_DIM, IDX_DIM, IDX_HEADS, N_BRANCH * D_MODEL)
D_IN = sum(IN_SIZES)
D_FF = 2816
FFN_CONV_WIDTH = 3
LN_EPS = 1e-5
DEEPNORM_ALPHA = (2 * DEPTH) ** 0.25
DEEPNORM_BETA = (8 * DEPTH) ** -0.25

kernel_name = "hybrid_pool_conformer_dsa_gated_deepnorm"


def layer_norm(x, g, b):
    xf = x.astype(jnp.float32)
    mu = jnp.mean(xf, axis=-1, keepdims=True)
    var = jnp.mean(jnp.square(xf - mu), axis=-1, keepdims=True)
    y = (xf - mu) * lax.rsqrt(var + LN_EPS) * g.astype(jnp.float32) + b.astype(jnp.float32)
    return y.astype(x.dtype)


def rope(x, positions):
    half = x.shape[-1] // 2
    inv_freq = ROPE_THETA ** (-jnp.arange(half, dtype=jnp.float32) / half)
    ang = positions.astype(jnp.float32)[..., None] * inv_freq
    cos = jnp.cos(ang)[:, :, None, :]
    sin = jnp.sin(ang)[:, :, None, :]
    xf = x.astype(jnp.float32)
    x1, x2 = xf[..., :half], xf[..., half:]
    return jnp.concatenate([x1 * cos - x2 * sin, x2 * cos + x1 * sin], axis=-1).astype(x.dtype)


def causal_dwconv(x, w, b):
    k, c = w.shape
    y = lax.conv_general_dilated(
        x, w[:, None, :].astype(x.dtype), window_strides=(1,), padding=[(k - 1, 0)],
        dimension_numbers=('NWC', 'WIO', 'NWC'), feature_group_count=c)
    return y + b


def split_cols(z, sizes):
    out, off = [], 0
    for s in sizes:
        out.append(z[..., off:off + s])
        off += s
    return out


def pool_mixer(u, pool_w, pool_scale):
    B, L, _ = u.shape
    uf = u.astype(jnp.float32).reshape(B, L, POOL_GROUPS, POOL_GROUP_DIM)
    cs0 = jnp.pad(jnp.cumsum(uf, axis=1), ((0, 0), (1, 0), (0, 0), (0, 0)))
    t = jnp.arange(L)
    pooled = []
    for g, win in enumerate(POOL_WINDOWS):
        start = jnp.maximum(t + 1 - win, 0)
        s = cs0[:, 1:, g] - cs0[:, start, g]
        cnt = (t + 1 - start).astype(jnp.float32)[None, :, None]
        pooled.append(s / cnt)
    mixed = jnp.stack(pooled, axis=2) - uf
    y = jnp.einsum('blgc,gcd->blgd', mixed, pool_w.astype(jnp.float32))
    return (y.reshape(B, L, D_POOL) * pool_scale.astype(jnp.float32)).astype(u.dtype)


def conformer_conv(u, dw_w, dw_b, ln_g, ln_b, w_pw):
    a, gate = jnp.split(u, 2, axis=-1)
    h = a * jax.nn.sigmoid(gate)
    h = causal_dwconv(h, dw_w, dw_b)
    h = jax.nn.silu(layer_norm(h, ln_g, ln_b))
    return h @ w_pw


def dsa_attention(q, k, v, q_idx, k_idx, w_idx):
    B, L = q.shape[0], q.shape[1]
    topk = min(TOPK_MAX, L // 4)
    n_blk = L // Q_BLOCK
    group = N_HEADS // N_KV_HEADS
    key_pos = jnp.arange(L)
    k_idx_f = k_idx.astype(jnp.float32)

    def block(i):
        q0 = i * Q_BLOCK
        qpos = q0 + jnp.arange(Q_BLOCK)
        qb = lax.dynamic_slice_in_dim(q, q0, Q_BLOCK, axis=1)
        qib = lax.dynamic_slice_in_dim(q_idx, q0, Q_BLOCK, axis=1).astype(jnp.float32)
        wib = lax.dynamic_slice_in_dim(w_idx, q0, Q_BLOCK, axis=1).astype(jnp.float32)
        logits = jnp.einsum('bqhd,bsd->bqsh', qib, k_idx_f)
        score = jnp.einsum('bqsh,bqh->bqs', jax.nn.relu(logits), wib)
        causal = key_pos[None, :] <= qpos[:, None]
        score = jnp.where(causal[None], score, -jnp.inf)
        _, idx = lax.top_k(score, topk)
        valid = idx <= qpos[None, :, None]
        k_sel = jax.vmap(lambda kk, ii: kk[ii])(k, idx)
        v_sel = jax.vmap(lambda vv, ii: vv[ii])(v, idx)
        qg = qb.reshape(B, Q_BLOCK, N_KV_HEADS, group, HEAD_DIM).astype(jnp.float32)
        s = jnp.einsum('bqngd,bqknd->bngqk', qg, k_sel.astype(jnp.float32)) * (HEAD_DIM ** -0.5)
        s = jnp.where(valid[:, None, None], s, -jnp.inf)
        p = jax.nn.softmax(s, axis=-1)
        o = jnp.einsum('bngqk,bqknd->bqngd', p, v_sel.astype(jnp.float32))
        return o.reshape(B, Q_BLOCK, D_ATTN).astype(q.dtype)

    out = lax.map(block, jnp.arange(n_blk))
    return out.transpose(1, 0, 2, 3).reshape(B, L, D_ATTN)


def token_mix(h, positions, w_in, b_in, pool_w, pool_scale, w_pool_out, conv_dw_w, conv_dw_b,
              conv_ln_g, conv_ln_b, w_conv_out, w_attn_out, w_o):
    B, L, _ = h.shape
    z = h @ w_in + b_in
    u_pool, u_conv, q, k, v, qi, ki, wi, gates = split_cols(z, IN_SIZES)
    q = rope(q.reshape(B, L, N_HEADS, HEAD_DIM), positions)
    k = rope(k.reshape(B, L, N_KV_HEADS, HEAD_DIM), positions)
    v = v.reshape(B, L, N_KV_HEADS, HEAD_DIM)
    qi = rope(qi.reshape(B, L, IDX_HEADS, IDX_DIM), positions) * (IDX_DIM ** -0.5)
    ki = rope(ki[:, :, None, :], positions)[:, :, 0, :]
    wi = wi * (IDX_HEADS ** -0.5)
    y_pool = pool_mixer(u_pool, pool_w, pool_scale) @ w_pool_out
    y_conv = conformer_conv(u_conv, conv_dw_w, conv_dw_b, conv_ln_g, conv_ln_b, w_conv_out)
    y_attn = dsa_attention(q, k, v, qi, ki, wi) @ w_attn_out
    g = jax.nn.sigmoid(gates.astype(jnp.float32)).reshape(B, L, N_BRANCH, D_MODEL).astype(h.dtype)
    merged = g[:, :, 0] * y_pool + g[:, :, 1] * y_conv + g[:, :, 2] * y_attn
    return merged @ w_o


def conv_ffn(h, w_up, ffn_dw_w, ffn_dw_b, w_down):
    u = causal_dwconv(h @ w_up, ffn_dw_w, ffn_dw_b)
    gate, val = jnp.split(u, 2, axis=-1)
    return (jax.nn.silu(gate) * val) @ w_down


def setup_inputs(seed: int = 0) -> dict:
    key = jax.random.key(seed)
    ks = jax.random.split(key, 24)
    f32 = jnp.float32

    def nrm(k, shape, scale):
        return jax.random.normal(k, shape, f32) * scale

    return {
        "x": nrm(ks[0], (BATCH, SEQ, D_MODEL), 1.0),
        "positions": jnp.broadcast_to(jnp.arange(SEQ, dtype=jnp.int32), (BATCH, SEQ)),
        "ln_in_g": 1.0 + nrm(ks[1], (D_MODEL,), 0.02),
        "ln_in_b": nrm(ks[2], (D_MODEL,), 0.02),
        "w_in": nrm(ks[3], (DEPTH, D_MODEL, D_IN), D_MODEL ** -0.5),
        "b_in": nrm(ks[4], (DEPTH, D_IN), 0.02),
        "pool_w": nrm(ks[5], (DEPTH, POOL_GROUPS, POOL_GROUP_DIM, POOL_GROUP_DIM), POOL_GROUP_DIM ** -0.5),
        "pool_scale": 1.0 + nrm(ks[6], (DEPTH, D_POOL), 0.1),
        "w_pool_out": nrm(ks[7], (DEPTH, D_POOL, D_MODEL), D_POOL ** -0.5),
        "conv_dw_w": nrm(ks[8], (DEPTH, CONV_WIDTH, D_CONV), CONV_WIDTH ** -0.5),
        "conv_dw_b": nrm(ks[9], (DEPTH, D_CONV), 0.02),
        "conv_ln_g": 1.0 + nrm(ks[10], (DEPTH, D_CONV), 0.02),
        "conv_ln_b": nrm(ks[11], (DEPTH, D_CONV), 0.02),
        "w_conv_out": nrm(ks[12], (DEPTH, D_CONV, D_MODEL), D_CONV ** -0.5),
        "w_attn_out": nrm(ks[13], (DEPTH, D_ATTN, D_MODEL), D_ATTN ** -0.5),
        "w_o": nrm(ks[14], (DEPTH, D_MODEL, D_MODEL), DEEPNORM_BETA * D_MODEL ** -0.5),
        "ln1_g": 1.0 + nrm(ks[15], (DEPTH, D_MODEL), 0.02),
        "ln1_b": nrm(ks[16], (DEPTH, D_MODEL), 0.02),
        "w_up": nrm(ks[17], (DEPTH, D_MODEL, 2 * D_FF), D_MODEL ** -0.5),
        "ffn_dw_w": nrm(ks[18], (DEPTH, FFN_CONV_WIDTH, 2 * D_FF), FFN_CONV_WIDTH ** -0.5),
        "ffn_dw_b": nrm(ks[19], (DEPTH, 2 * D_FF), 0.02),
        "w_down": nrm(ks[20], (DEPTH, D_FF, D_MODEL), DEEPNORM_BETA * D_FF ** -0.5),
        "ln2_g": 1.0 + nrm(ks[21], (DEPTH, D_MODEL), 0.02),
        "ln2_b": nrm(ks[22], (DEPTH, D_MODEL), 0.02),
    }


def reference(x, positions, ln_in_g, ln_in_b, w_in, b_in, pool_w, pool_scale, w_pool_out,
              conv_dw_w, conv_dw_b, conv_ln_g, conv_ln_b, w_conv_out, w_attn_out, w_o,
              ln1_g, ln1_b, w_up, ffn_dw_w, ffn_dw_b, w_down, ln2_g, ln2_b):
    h = layer_norm(x, ln_in_g, ln_in_b)
    for l in range(DEPTH):
        mix = token_mix(h, positions, w_in[l], b_in[l], pool_w[l], pool_scale[l], w_pool_out[l],
                        conv_dw_w[l], conv_dw_b[l], conv_ln_g[l], conv_ln_b[l], w_conv_out[l],
                        w_attn_out[l], w_o[l])
        h = layer_norm(DEEPNORM_ALPHA * h + mix, ln1_g[l], ln1_b[l])
        ffn = conv_ffn(h, w_up[l], ffn_dw_w[l], ffn_dw_b[l], w_down[l])
        h = layer_norm(DEEPNORM_ALPHA * h + ffn, ln2_g[l], ln2_b[l])
    return h
```

```python
import numpy as np
from contextlib import ExitStack
import concourse.bass as bass
import concourse.mybir as mybir
from concourse.bass_utils import run_bass_kernel_spmd

F32 = mybir.dt.float32
BF16 = mybir.dt.bfloat16
I32 = mybir.dt.int32
ALU = mybir.AluOpType
AF = mybir.ActivationFunctionType
AX = mybir.AxisListType

L = 2048
D = 1024
NT = 16
NG = 4
DFF = 2816
NJ = 22
DEPTH = 2
ALPHA = float((2 * DEPTH) ** 0.25)
EPS = 1e-5
TOPK = 256
NIT = 16
NEG = -1.0e30
SEM_LIMIT = 30000

CH_POOL = [0, 1]
CH_CA = [2, 3]
CH_CG = [4, 5]
CH_Q = [6, 7, 8, 9]
CH_QS = [10, 11, 12, 13]
CH_K = 14
CH_KS = 15
CH_V = 16
CH_QI = [17, 18, 19, 20]
CH_QIS = [21, 22, 23, 24]
CH_KI = 25
CH_KIS = 26
CH_GATE0 = 27
NCH = 51
PC_BA = 0
PC_PSC = 51
PC_CDB = 53
PC_CLG = 55
PC_CLB = 57
PC_CDW = 59
PC_FWG = 121
PC_FWV = 187
PC_FBG = 253
PC_FBV = 275
NPAR = 297


class _Op:
    __slots__ = ("idx", "eng", "fn", "deps_raw", "deps_other", "is_dma", "slot", "signal", "stream", "val", "vc")

    def __init__(self, idx, eng, fn, is_dma, slot):
        self.idx = idx
        self.eng = eng
        self.fn = fn
        self.is_dma = is_dma
        self.slot = slot
        self.deps_raw = set()
        self.deps_other = set()
        self.signal = False
        self.stream = None
        self.val = 0
        self.vc = None


class Sched:
    def __init__(self, nc):
        self.nc = nc
        self.ops = []
        self.last_writer = {}
        self.readers = {}
        self.last_dma_on_slot = {}
        self.last_on_eng = {}
        self.dma_since_barrier = []
        self.barrier_deps = set()

    def barrier(self):
        self.barrier_deps = set(self.last_on_eng.values()) | set(self.dma_since_barrier)
        self.dma_since_barrier = []
        self.last_writer = {}
        self.readers = {}

    def add(self, eng, fn, reads=(), writes=(), dma=False, slot=None):
        idx = len(self.ops)
        if dma and slot is None:
            slot = ("auto", tuple(writes)[0])
        op = _Op(idx, eng, fn, dma, slot)
        for r in reads:
            w = self.last_writer.get(r)
            if w is not None:
                op.deps_raw.add(w)
        for t in writes:
            w = self.last_writer.get(t)
            if w is not None:
                op.deps_other.add(w)
            for rd in self.readers.get(t, ()):
                op.deps_other.add(rd)
        if dma:
            p = self.last_dma_on_slot.get(slot)
            if p is not None:
                op.deps_raw.add(p)
            self.last_dma_on_slot[slot] = idx
            self.dma_since_barrier.append(idx)
        op.deps_other |= self.barrier_deps
        for r in reads:
            self.readers.setdefault(r, []).append(idx)
        for t in writes:
            self.last_writer[t] = idx
            self.readers[t] = []
        op.deps_other -= op.deps_raw
        self.last_on_eng[eng] = idx
        self.ops.append(op)
        return idx

    def _needed(self, op, d):
        dop = self.ops[d]
        if dop.is_dma or op.is_dma:
            return True
        if dop.eng != op.eng:
            return True
        if op.eng == "pe":
            return False
        return d in op.deps_raw

    def emit(self):
        nc = self.nc
        ops = self.ops
        for op in ops:
            for d in (op.deps_raw | op.deps_other):
                if self._needed(op, d):
                    ops[d].signal = True
        for op in ops:
            if op.is_dma:
                op.signal = True
        sems = {}
        counts = {}
        sem_objs = []

        def new_sem():
            cm = nc.semaphore("s%d" % len(sem_objs))
            h = cm.__enter__()
            sem_objs.append(cm)
            return h

        for op in ops:
            if not op.signal:
                continue
            key = ("slot", op.slot) if op.is_dma else ("eng", op.eng)
            inc = 16 if op.is_dma else 1
            if key not in sems:
                sems[key] = [new_sem()]
                counts[key] = 0
            if counts[key] + inc > SEM_LIMIT:
                sems[key].append(new_sem())
                counts[key] = 0
            counts[key] += inc
            op.stream = (key, len(sems[key]) - 1)
            op.val = counts[key]
        self.n_sems = len(sem_objs)
        eng_clock = {}
        waits = [None] * len(ops)
        for op in ops:
            clk = eng_clock.setdefault(op.eng, {})
            best = {}
            for d in sorted(op.deps_raw | op.deps_other):
                if not self._needed(op, d):
                    continue
                dop = ops[d]
                if clk.get(dop.stream, 0) >= dop.val:
                    continue
                if best.get(dop.stream, 0) < dop.val:
                    best[dop.stream] = dop.val
                for k, v in dop.vc.items():
                    if clk.get(k, 0) < v:
                        clk[k] = v
            waits[op.idx] = best
            if op.signal:
                vc = dict(clk)
                vc[op.stream] = op.val
                op.vc = vc
        per_eng = {}
        for op in ops:
            per_eng.setdefault(op.eng, []).append(op)
        final_waits = {}
        for op in ops:
            if op.is_dma:
                if final_waits.get(op.stream, 0) < op.val:
                    final_waits[op.stream] = op.val

        def semh(stream):
            key, ep = stream
            return sems[key][ep]

        engmap = {"pe": "tensor", "act": "scalar", "dve": "vector", "pool": "gpsimd", "sp": "sync"}
        self.n_waits = 0
        with nc.Block() as block:
            for ename in ("sp", "pe", "act", "dve", "pool"):
                lst = per_eng.get(ename, [])

                def body(e, lst=lst, ename=ename):
                    for op in lst:
                        for s, v in waits[op.idx].items():
                            e.wait_ge(semh(s), v)
                            self.n_waits += 1
                        ins = op.fn(e)
                        if op.signal:
                            ins.then_inc(semh(op.stream), 16 if op.is_dma else 1)
                    if ename == "sp":
                        for s, v in final_waits.items():
                            e.wait_ge(semh(s), v)
                getattr(block, engmap[ename])(body)
        for cm in reversed(sem_objs):
            cm.__exit__(None, None, None)
        return self


def _chunk_cols():
    cols = []
    ar = np.arange

    def hc(base, h):
        return base + 64 * h + ar(64)

    def sw(c):
        return np.concatenate([c[32:], c[:32]])

    cols += [ar(128), 128 + ar(128)]
    cols += [256 + 128 * i + ar(128) for i in range(4)]
    for j in range(4):
        cols.append(np.concatenate([hc(768, j), hc(768, 4 + j)]))
    for j in range(4):
        cols.append(np.concatenate([sw(hc(768, j)), sw(hc(768, 4 + j))]))
    cols.append(np.concatenate([hc(1280, 0), hc(1280, 1)]))
    cols.append(np.concatenate([sw(hc(1280, 0)), sw(hc(1280, 1))]))
    cols.append(1408 + ar(128))
    for j in range(4):
        cols.append(np.concatenate([hc(1536, 2 * j), hc(1536, 2 * j + 1)]))
    for j in range(4):
        cols.append(np.concatenate([sw(hc(1536, 2 * j)), sw(hc(1536, 2 * j + 1))]))
    ki = 2048 + ar(64)
    cols.append(np.concatenate([ki, ki]))
    cols.append(np.concatenate([sw(ki), sw(ki)]))
    for i in range(3):
        for c in range(8):
            cols.append(2120 + 1024 * i + 128 * c + ar(128))
    assert len(cols) == NCH
    return np.stack(cols)


def _kp(w):
    K = w.shape[0] // 128
    return np.ascontiguousarray(w.reshape(K, 128, w.shape[1]).transpose(1, 0, 2))


def prep_weights(inp):
    cols = _chunk_cols()
    out = {}
    f = np.float32
    rep = lambda v: np.ascontiguousarray(np.broadcast_to(np.asarray(v, f)[None, :], (128, v.shape[0])))
    for l in range(DEPTH):
        w_in = np.asarray(inp["w_in"][l], f)
        b_in = np.asarray(inp["b_in"][l], f)
        wg = w_in[:, cols.reshape(-1)].reshape(8, 128, NCH, 128)
        out["wA%d" % l] = np.ascontiguousarray(wg.transpose(2, 1, 0, 3))
        out["wwi%d" % l] = _kp(np.ascontiguousarray(w_in[:, 2112:2120]))
        par = np.zeros((128, NPAR), f)
        par[:, PC_BA:PC_BA + NCH] = b_in[cols].T
        par[:, PC_PSC:PC_PSC + 2] = np.asarray(inp["pool_scale"][l], f).reshape(2, 128).T
        par[:, PC_CDB:PC_CDB + 2] = np.asarray(inp["conv_dw_b"][l], f).reshape(2, 128).T
        par[:, PC_CLG:PC_CLG + 2] = np.asarray(inp["conv_ln_g"][l], f).reshape(2, 128).T
        par[:, PC_CLB:PC_CLB + 2] = np.asarray(inp["conv_ln_b"][l], f).reshape(2, 128).T
        cdw = np.asarray(inp["conv_dw_w"][l], f)
        par[:, PC_CDW:PC_CDW + 62] = cdw.reshape(31, 2, 128).transpose(2, 1, 0).reshape(128, 62)
        fw = np.asarray(inp["ffn_dw_w"][l], f)
        par[:, PC_FWG:PC_FWG + 66] = fw[:, :DFF].reshape(3, NJ, 128).transpose(2, 1, 0).reshape(128, 66)
        par[:, PC_FWV:PC_FWV + 66] = fw[:, DFF:].reshape(3, NJ, 128).transpose(2, 1, 0).reshape(128, 66)
        fb = np.asarray(inp["ffn_dw_b"][l], f)
        par[:, PC_FBG:PC_FBG + NJ] = fb[:DFF].reshape(NJ, 128).T
        par[:, PC_FBV:PC_FBV + NJ] = fb[DFF:].reshape(NJ, 128).T
        out["par%d" % l] = par
        out["bwi%d" % l] = rep(np.tile(b_in[2112:2120], 16))
        pw = np.asarray(inp["pool_w"][l], f)
        bd = np.zeros((128, 2, 128), f)
        for c in range(2):
            bd[0:64, c, 0:64] = pw[2 * c]
            bd[64:128, c, 64:128] = pw[2 * c + 1]
        out["pwbd%d" % l] = bd
        wao = np.asarray(inp["w_attn_out"][l], f)
        rows = np.concatenate([np.concatenate([64 * j + np.arange(64), 64 * (4 + j) + np.arange(64)]) for j in range(4)])
        ow = np.concatenate([_kp(np.asarray(inp["w_pool_out"][l], f)), _kp(np.asarray(inp["w_conv_out"][l], f)),
                             _kp(np.ascontiguousarray(wao[rows]))], axis=1)
        out["outw%d" % l] = np.ascontiguousarray(ow)
        out["wo%d" % l] = _kp(np.asarray(inp["w_o"][l], f))
        wu = np.asarray(inp["w_up"][l], f)
        wug = wu[:, :DFF].reshape(8, 128, NJ, 128)
        wuv = wu[:, DFF:].reshape(8, 128, NJ, 128)
        out["wU%d" % l] = np.ascontiguousarray(np.concatenate([wug, wuv], axis=3).transpose(2, 1, 0, 3))
        out["wd%d" % l] = _kp(np.asarray(inp["w_down"][l], f))
        out["lnr%d" % l] = np.ascontiguousarray(np.concatenate(
            [rep(inp["ln1_g"][l]), rep(inp["ln1_b"][l]), rep(inp["ln2_g"][l]), rep(inp["ln2_b"][l])], axis=1))
    out["lnin"] = np.ascontiguousarray(np.concatenate([rep(inp["ln_in_g"]), rep(inp["ln_in_b"])], axis=1))
    half = 32
    invf = (10000.0 ** (-np.arange(half, dtype=np.float32) / half)).astype(f)
    cst = np.zeros((128, 2), f)
    cst[:, 0] = np.tile(invf, 4)
    cst[:, 1] = np.tile(np.concatenate([-np.ones(32, f), np.ones(32, f)]), 2)
    out["cst"] = cst
    return out


WSHAPES = {
    "wA": [NCH, 128, 8, 128], "wwi": [128, 8, 8], "par": [128, NPAR], "bwi": [128, 128], "pwbd": [128, 2, 128],
    "outw": [128, 8, 1024], "wo": [128, 8, 1024], "wU": [NJ, 128, 8, 256], "wd": [128, NJ, 1024], "lnr": [128, 4096],
}


def build(nseq=2, layers=(0, 1), dbg=None, first=True, last=True):
    dbg = dbg or set()
    nc = bass.Bass("TRN2", target_bir_lowering=False)
    dt = {}
    xin = nc.dram_tensor("x", [nseq, L, D], F32, kind="ExternalInput").ap()
    posin = nc.dram_tensor("pos", [nseq, 128, L], I32, kind="ExternalInput").ap()
    W = {}
    for l in layers:
        for k, shp in WSHAPES.items():
            W[(k, l)] = nc.dram_tensor("%s%d" % (k, l), shp, F32, kind="ExternalInput").ap()
    lnin = nc.dram_tensor("lnin", [128, 2048], F32, kind="ExternalInput").ap()
    cstin = nc.dram_tensor("cst", [128, 2], F32, kind="ExternalInput").ap()
    yout = nc.dram_tensor("y", [nseq, L, D], F32, kind="ExternalOutput").ap()
    hres = nc.dram_tensor("hres", [L, D], F32, kind="Internal").ap()
    ropd = nc.dram_tensor("ropd", [128, 2, L], F32, kind="Internal").ap()
    dbg_out = {}

    def dbg_tensor(name, shape, dtype):
        dbg_out[name] = nc.dram_tensor("dbg_" + name, shape, dtype, kind="ExternalOutput").ap()
        return dbg_out[name]

    S = Sched(nc)
    es_top = ExitStack()

    uid = [0]

    def sb(es, name, shape, dtype):
        uid[0] += 1
        return es.enter_context(nc.sbuf_tensor("sb%d_%s" % (uid[0], name), shape, dtype))

    def PE(out, lhsT, rhs, st, sp, r, w):
        S.add("pe", lambda e: e.matmul(out, lhsT=lhsT, rhs=rhs, start=st, stop=sp), reads=r, writes=w)

    def TR(out, in_, ident, r, w):
        S.add("pe", lambda e: e.transpose(out=out, in_=in_, identity=ident), reads=r, writes=w)

    def ACT(out, in_, func, r, w, bias=0.0, scale=1.0, accum=None):
        if accum is None:
            S.add("act", lambda e: e.activation(out=out, in_=in_, func=func, bias=bias, scale=scale), reads=r, writes=w)
        else:
            S.add("act", lambda e: e.activation(out=out, in_=in_, func=func, bias=bias, scale=scale, accum_out=accum),
                  reads=r, writes=w)

    def TT(eng, out, a, b, op, r, w):
        S.add(eng, lambda e: e.tensor_tensor(out=out, in0=a, in1=b, op=op), reads=r, writes=w)

    def TS(eng, out, a, s1, s2, op0, op1, r, w, accum=None):
        if accum is None:
            if op1 is None:
                S.add(eng, lambda e: e.tensor_scalar(out=out, in0=a, scalar1=s1, scalar2=None, op0=op0), reads=r, writes=w)
            else:
                S.add(eng, lambda e: e.tensor_scalar(out=out, in0=a, scalar1=s1, scalar2=s2, op0=op0, op1=op1),
                      reads=r, writes=w)
        else:
            S.add(eng, lambda e: e.tensor_scalar(out=out, in0=a, scalar1=s1, scalar2=s2, op0=op0, op1=op1,
                                                 accum_out=accum), reads=r, writes=w)

    def STT(out, a, s, b, op0, op1, r, w):
        S.add("dve", lambda e: e.scalar_tensor_tensor(out=out, in0=a, scalar=s, in1=b, op0=op0, op1=op1),
              reads=r, writes=w)

    def CP(eng, out, in_, r, w):
        S.add(eng, lambda e: e.tensor_copy(out=out, in_=in_), reads=r, writes=w)

    def MS(eng, out, v, w):
        S.add(eng, lambda e: e.memset(out, v), writes=w)

    def DMA(q, out, in_, r, w, slot=None):
        if q == "pool":
            S.add(q, lambda e: e.dma_start(out=out, in_=in_, max_dma_last_dim=4096), reads=r, writes=w, dma=True, slot=slot)
        else:
            S.add(q, lambda e: e.dma_start(out=out, in_=in_), reads=r, writes=w, dma=True, slot=slot)

    es = es_top
    psum = es.enter_context(nc.psum_tensor("psum", [128, 8, 512], F32))
    identf = sb(es, "identf", [128, 128], F32)
    identb = sb(es, "identb", [128, 128], BF16)
    onesf = sb(es, "onesf", [128, 128], F32)
    onesb = sb(es, "onesb", [128, 64], BF16)
    cmask = sb(es, "cmask", [128, 128], F32)
    cst = sb(es, "cst", [128, 2], F32)
    rct = sb(es, "rct", [128, 4, 16], F32)
    zero2 = sb(es, "zero2", [128, 2], F32)
    hT = sb(es, "hT", [128, 8, L], BF16)
    arena = sb(es, "arena", [128, NJ * 1024], BF16)
    NWA = 4
    wAb = [sb(es, "wA%d" % i, [128, 8, 128], BF16) for i in range(NWA)]
    parb = {l: sb(es, "par%d" % l, [128, NPAR], F32) for l in layers}
    LN = {}

    def ln_alloc(esx):
        LN["row"] = sb(esx, "lnrow", [128, 2, 1024], F32)
        LN["t"] = [sb(esx, "lnt%d" % i, [128, 1024], F32) for i in range(4)]
        LN["st"] = [sb(esx, "lnst%d" % i, [128, 2, 6], F32) for i in range(4)]
        LN["mv"] = [sb(esx, "lnmv%d" % i, [128, 8], F32) for i in range(4)]

    def PS(b, n0=0, n1=512):
        return psum[:, b, n0:n1]

    def PT_(b):
        return ("ps", b)

    bank_ctr = [0]

    def nb():
        b = bank_ctr[0] % 8
        bank_ctr[0] += 1
        return b

    pair_ctr = [0]

    def npair():
        b = (pair_ctr[0] % 4) * 2
        pair_ctr[0] += 1
        return b

    MS("pool", identf[:], 0.0, ["identf"])
    S.add("pool", lambda e: e.affine_select(out=identf[:], in_=identf[:], pattern=[[-1, 128]], compare_op=ALU.not_equal,
                                            fill=1.0, base=0, channel_multiplier=1), reads=["identf"], writes=["identf"])
    CP("dve", identb[:], identf[:], ["identf"], ["identb"])
    MS("dve", onesf[:], 1.0, ["onesf"])
    MS("dve", onesb[:], 1.0, ["onesb"])
    MS("pool", cmask[:], 0.0, ["cmask"])
    S.add("pool", lambda e: e.affine_select(out=cmask[:], in_=cmask[:], pattern=[[-1, 128]], compare_op=ALU.is_ge,
                                            fill=NEG, base=0, channel_multiplier=1), reads=["cmask"], writes=["cmask"])
    MS("dve", zero2[:], 0.0, ["zero2"])
    DMA("sp", cst[:], cstin, [], ["cst"])
    for l in layers:
        DMA("sp", parb[l][:], W[("par", l)], [], [("par", l)])
    with ExitStack() as es0:
        ti = sb(es0, "ti", [128, 16], I32)
        tf = sb(es0, "tf", [128, 16], F32)
        S.add("pool", lambda e: e.iota(ti[:], pattern=[[1, 16]], base=1, channel_multiplier=0), writes=["ti"])
        CP("dve", tf[:], ti[:], ["ti"], ["tf"])
        for wi_ in range(4):
            TS("dve", rct[:, wi_, :], tf[:], float(2 ** (wi_ + 1)), None, ALU.min, None, ["tf"], [("rct", wi_)])
            S.add("dve", lambda e, wi_=wi_: e.reciprocal(out=rct[:, wi_, :], in_=rct[:, wi_, :]),
                  reads=[("rct", wi_)], writes=[("rct", wi_)])
        S.barrier()

    wa_ctr = [0]

    def load_chunk(l, c):
        slot = wa_ctr[0] % NWA
        wa_ctr[0] += 1
        DMA("pool", wAb[slot][:], W[("wA", l)][c], [], [("wA", slot)], slot=("wA", slot))
        return slot

    class Stream:
        def __init__(self, l, chunks, ahead=2):
            self.l = l
            self.chunks = list(chunks)
            self.slots = {}
            self.next = 0
            self.ahead = ahead

        def get(self, i):
            while self.next < len(self.chunks) and self.next <= i + self.ahead:
                self.slots[self.next] = load_chunk(self.l, self.chunks[self.next])
                self.next += 1
            return self.slots[i]

    def proj_chunk(slot, g, bank, ncols=128):
        for k in range(8):
            PE(PS(bank)[0:ncols, :], wAb[slot][:, k, 0:ncols], hT[:, k, g * 512:(g + 1) * 512], k == 0, k == 7,
               [("wA", slot), ("hT", g)], [PT_(bank)])

    def ln_rows_load(src_ap):
        DMA("sp", LN["row"][:], src_ap.rearrange("p (a n) -> p a n", a=2), [], ["lnrow"])

    def ln_load(s, i, x_src=None):
        t = LN["t"][i % 4]
        tk = ("lnt", i % 4)
        if x_src is not None:
            DMA("sp", t[:], x_src, [], [tk], slot=("lnt_in", i % 4))
        else:
            DMA("sp", t[:], hres[i * 128:(i + 1) * 128, :], [("hres", i)], [tk], slot=("lnt_in", i % 4))

    def ln_a(s, i, mixbank):
        t = LN["t"][i % 4]
        tk = ("lnt", i % 4)
        st = LN["st"][i % 4]
        mv = LN["mv"][i % 4]
        mk = ("lnmv", i % 4)
        if mixbank is not None:
            STT(t[:].rearrange("p (a n) -> p a n", a=2), t[:].rearrange("p (a n) -> p a n", a=2), ALPHA,
                psum[:, mixbank:mixbank + 2, :], ALU.mult, ALU.add, [tk], [tk, PT_(mixbank), PT_(mixbank + 1)])
        for a in range(2):
            S.add("dve", lambda e, a=a: e.bn_stats(out=st[:, a, :], in_=t[:, a * 512:(a + 1) * 512]), reads=[tk],
                  writes=[("lnst", i % 4, a)])
        S.add("dve", lambda e: e.bn_aggr(out=mv[:, 0:2], in_=st[:].rearrange("p a s -> p (a s)")),
              reads=[("lnst", i % 4, 0), ("lnst", i % 4, 1)], writes=[mk])
        ACT(mv[:, 2:3], mv[:, 1:2], AF.Sqrt, [mk], [mk], bias=EPS)
        S.add("dve", lambda e: e.reciprocal(out=mv[:, 3:4], in_=mv[:, 2:3]), reads=[mk], writes=[mk])
        TS("dve", mv[:, 4:5], mv[:, 0:1], mv[:, 3:4], -1.0, ALU.mult, ALU.mult, [mk], [mk])

    def ln_b(s, i, final):
        lnrow = LN["row"]
        t = LN["t"][i % 4]
        tk = ("lnt", i % 4)
        mv = LN["mv"][i % 4]
        mk = ("lnmv", i % 4)
        ACT(t[:], t[:], AF.Identity, [tk, mk], [tk], bias=mv[:, 4:5], scale=mv[:, 3:4])
        TT("dve", t[:], t[:], lnrow[:, 0, :], ALU.mult, [tk, "lnrow"], [tk])
        TT("pool", t[:], t[:], lnrow[:, 1, :], ALU.add, [tk, "lnrow"], [tk])
        if final:
            DMA("sp", yout[s, i * 128:(i + 1) * 128, :], t[:], [tk], [("yout", i)], slot=("lnt_out", i % 4))
        else:
            DMA("sp", hres[i * 128:(i + 1) * 128, :], t[:], [tk], [("hres", i)], slot=("lnt_out", i % 4))
            g = i // 4
            for hb in range(2):
                b = nb()
                for kk in range(4):
                    k = hb * 4 + kk
                    TR(PS(b, kk * 128, (kk + 1) * 128), t[:, k * 128:(k + 1) * 128], identf[:], [tk, "identf"], [PT_(b)])
                dst = hT[:, hb * 4:(hb + 1) * 4, i * 128:(i + 1) * 128]
                src = PS(b).rearrange("p (a n) -> p a n", a=4)
                if hb == 0:
                    ACT(dst, src, AF.Identity, [], [PT_(b), ("hT", g)])
                else:
                    CP("dve", dst, src, [], [PT_(b), ("hT", g)])

    for s in range(nseq):
        with ExitStack() as e0:
            posi = sb(e0, "posi", [128, L], I32)
            ang = sb(e0, "ang", [128, L], F32)
            t1 = sb(e0, "t1", [128, L], F32)
            t2i = sb(e0, "t2i", [128, L], I32)
            t3 = sb(e0, "t3", [128, L], F32)
            rop = sb(e0, "rop", [128, 2, L], F32)
            DMA("sp", posi[:], posin[s], [], ["posi"])
            CP("dve", ang[:], posi[:], ["posi"], ["ang"])
            TS("dve", ang[:], ang[:], cst[:, 0:1], None, ALU.mult, None, ["ang", "cst"], ["ang"])
            C1 = 6.28125
            C2 = float(2.0 * np.pi - 6.28125)
            for which in range(2):
                if which == 0:
                    TS("dve", t1[:], ang[:], float(np.pi / 2), None, ALU.add, None, ["ang"], ["t1"])
                    src = t1
                    sk = "t1"
                else:
                    src = ang
                    sk = "ang"
                TS("dve", t3[:], src[:], float(1.0 / (2 * np.pi)), None, ALU.mult, None, [sk], ["t3"])
                CP("dve", t2i[:], t3[:], ["t3"], ["t2i"])
                CP("dve", t3[:], t2i[:], ["t2i"], ["t3"])
                STT(t1[:], t3[:], -C1, src[:], ALU.mult, ALU.add, ["t3", sk], ["t1"])
                STT(t1[:], t3[:], -C2, t1[:], ALU.mult, ALU.add, ["t3", "t1"], ["t1"])
                TS("dve", t3[:], t1[:], float(np.pi), float(-2 * np.pi), ALU.is_gt, ALU.mult, ["t1"], ["t3"])
                TT("dve", t1[:], t1[:], t3[:], ALU.add, ["t1", "t3"], ["t1"])
                TS("dve", t3[:], t1[:], float(-np.pi), float(2 * np.pi), ALU.is_lt, ALU.mult, ["t1"], ["t3"])
                TT("dve", t1[:], t1[:], t3[:], ALU.add, ["t1", "t3"], ["t1"])
                TS("dve", t1[:], t1[:], 3.1415925, -3.1415925, ALU.min, ALU.max, ["t1"], ["t1"])
                if which == 0:
                    ACT(rop[:, 0, :], t1[:], AF.Sin, ["t1"], [("rop", 0)])
                else:
                    ACT(rop[:, 1, :], t1[:], AF.Sin, ["t1", "cst"], [("rop", 1)], scale=cst[:, 1:2])
            DMA("sp", ropd, rop[:], [("rop", 0), ("rop", 1)], ["ropd"])
            if s == 0 and "rop" in dbg:
                DMA("sp", dbg_tensor("rop", [128, 2, L], F32), rop[:], [("rop", 0), ("rop", 1)], ["dbg_rop"])
            S.barrier()
        e0b = ExitStack()
        ln_alloc(e0b)
        if first:
            ln_rows_load(lnin)
            ln_load(s, 0, xin[s, 0:128, :])
            for i in range(NT):
                if i + 1 < NT:
                    ln_load(s, i + 1, xin[s, (i + 1) * 128:(i + 2) * 128, :])
                if i >= 2:
                    ln_b(s, i - 2, False)
                ln_a(s, i, None)
            ln_b(s, NT - 2, False)
            ln_b(s, NT - 1, False)
        else:
            for i in range(NT):
                t = LN["t"][i % 2]
                tk = ("lnt", i % 2)
                DMA("sp", t[:], xin[s, i * 128:(i + 1) * 128, :], [], [tk], slot=("lnt_in", i % 2))
                DMA("sp", hres[i * 128:(i + 1) * 128, :], t[:], [tk], [("hres", i)], slot=("lnt_out", i % 2))
                for hb in range(2):
                    b = nb()
                    for kk in range(4):
                        k = hb * 4 + kk
                        TR(PS(b, kk * 128, (kk + 1) * 128), t[:, k * 128:(k + 1) * 128], identf[:], [tk, "identf"], [PT_(b)])
                    CP("dve", hT[:, hb * 4:(hb + 1) * 4, i * 128:(i + 1) * 128], PS(b).rearrange("p (a n) -> p a n", a=4),
                       [], [PT_(b), ("hT", i // 4)])
        if s == 0 and "hT0" in dbg:
            DMA("sp", dbg_tensor("hT0", [128, 8, L], BF16), hT[:], [("hT", g) for g in range(4)], ["dbg_hT0"])
        S.barrier()
        e0b.close()

        for l in layers:
            par = parb[l]
            pk = ("par", l)
            last_layer = (l == layers[-1])
            qT = arena[:, 0:4 * L].rearrange("p (c t) -> p c t", c=4)
            qiT = arena[:, 4 * L:8 * L].rearrange("p (c t) -> p c t", c=4)
            merged = arena[:, 0:8 * L].rearrange("p (c t) -> p c t", c=8)
            actb = arena[:, 0:NJ * 1024].rearrange("p (c t) -> p c t", c=NJ)
            with ExitStack() as eA:
                OT = sb(eA, "OT", [128, 4, L], BF16)
                with ExitStack() as eB:
                    kT = sb(eB, "kT", [128, L], BF16)
                    vtm = sb(eB, "vtm", [128, NT, 128], BF16)
                    kiT = sb(eB, "kiT", [128, L], BF16)
                    witm = sb(eB, "witm", [128, NT * 8], F32)
                    with ExitStack() as e2:
                        rop = sb(e2, "rop2", [128, 2, L], F32)
                        vT = sb(e2, "vT", [128, L], F32)
                        ta = [sb(e2, "ta%d" % i, [128, 512], F32) for i in range(2)]
                        tb = [sb(e2, "tb%d" % i, [128, 512], F32) for i in range(2)]
                        wwib = sb(e2, "wwib", [128, 8, 8], BF16)
                        bwib = sb(e2, "bwib", [128, 128], F32)
                        DMA("sp", rop[:], ropd, ["ropd"], ["rop"])
                        DMA("pool", wwib[:], W[("wwi", l)], [], ["wwib"])
                        DMA("sp", bwib[:], W[("bwi", l)], [], ["bwib"])
                        pairs = []
                        for j in range(4):
                            pairs.append((CH_Q[j], CH_QS[j], ("q", j)))
                        pairs.append((CH_K, CH_KS, ("k", 0)))
                        for j in range(4):
                            pairs.append((CH_QI[j], CH_QIS[j], ("qi", j)))
                        pairs.append((CH_KI, CH_KIS, ("ki", 0)))
                        chunks = []
                        for a_, b_, _ in pairs:
                            chunks += [a_, b_]
                        chunks.append(CH_V)
                        stm = Stream(l, chunks)
                        cnt2 = 0
                        for pi, (ca, cs, (kind, j)) in enumerate(pairs):
                            sa = stm.get(2 * pi)
                            ss = stm.get(2 * pi + 1)
                            for g in range(NG):
                                ba = nb()
                                bs_ = nb()
                                proj_chunk(sa, g, ba)
                                proj_chunk(ss, g, bs_)
                                A_ = ta[cnt2 % 2]
                                B_ = tb[cnt2 % 2]
                                ak = ("ta", cnt2 % 2)
                                bk = ("tb", cnt2 % 2)
                                cnt2 += 1
                                gs = slice(g * 512, (g + 1) * 512)
                                STT(A_[:], PS(ba), par[:, PC_BA + ca:PC_BA + ca + 1], rop[:, 0, gs], ALU.add, ALU.mult,
                                    [pk, "rop"], [ak, PT_(ba)])
                                ACT(B_[:], PS(bs_), AF.Identity, [pk], [bk, PT_(bs_)], bias=par[:, PC_BA + cs:PC_BA + cs + 1])
                                TT("pool", B_[:], B_[:], rop[:, 1, gs], ALU.mult, [bk, "rop"], [bk])
                                if kind == "q":
                                    dst = qT[:, j, gs]
                                elif kind == "k":
                                    dst = kT[:, gs]
                                elif kind == "qi":
                                    dst = qiT[:, j, gs]
                                else:
                                    dst = kiT[:, gs]
                                TT("dve", dst, A_[:], B_[:], ALU.add, [ak, bk], [(kind, j, g)])
                        sv = stm.get(len(chunks) - 1)
                        for g in range(NG):
                            b = nb()
                            proj_chunk(sv, g, b)
                            ACT(vT[:, g * 512:(g + 1) * 512], PS(b), AF.Identity, [pk], [("vT", g), PT_(b)],
                                bias=par[:, PC_BA + CH_V:PC_BA + CH_V + 1])
                        for g in range(NG):
                            b = nb()
                            for kk in range(4):
                                i = g * 4 + kk
                                TR(PS(b, kk * 128, (kk + 1) * 128), vT[:, i * 128:(i + 1) * 128], identf[:],
                                   [("vT", g), "identf"], [PT_(b)])
                            CP("dve", vtm[:, g * 4:(g + 1) * 4, :], PS(b).rearrange("p (a n) -> p a n", a=4), [],
                               [PT_(b), ("vtm", g)])
                        b = nb()
                        for i in range(NT):
                            for k in range(8):
                                PE(PS(b, i * 8, (i + 1) * 8), hT[:, k, i * 128:(i + 1) * 128], wwib[:, k, :], k == 0, k == 7,
                                   ["wwib", ("hT", i // 4)], [PT_(b)])
                        TT("dve", witm[:], PS(b, 0, 128), bwib[:], ALU.add, ["bwib"], [PT_(b), "witm"])
                        if s == 0 and l == layers[0]:
                            if "qT" in dbg:
                                DMA("sp", dbg_tensor("qT", [128, 4, L], BF16), qT, [("q", j, g) for j in range(4) for g in range(4)], ["dbg_qT"])
                            if "kT" in dbg:
                                DMA("sp", dbg_tensor("kT", [128, L], BF16), kT[:], [("k", 0, g) for g in range(4)], ["dbg_kT"])
                            if "kiT" in dbg:
                                DMA("sp", dbg_tensor("kiT", [128, L], BF16), kiT[:], [("ki", 0, g) for g in range(4)], ["dbg_kiT"])
                            if "qiT" in dbg:
                                DMA("sp", dbg_tensor("qiT", [128, 4, L], BF16), qiT, [("qi", j, g) for j in range(4) for g in range(4)], ["dbg_qiT"])
                            if "vtm" in dbg:
                                DMA("sp", dbg_tensor("vtm", [128, NT, 128], BF16), vtm[:], [("vtm", g) for g in range(4)], ["dbg_vtm"])
                            if "witm" in dbg:
                                DMA("sp", dbg_tensor("witm", [128, 128], F32), witm[:], ["witm"], ["dbg_witm"])
                        S.barrier()

                    with ExitStack() as e3:
                        SCW = 58 * 128
                        SC = sb(e3, "SC", [128, SCW], F32)
                        selT = sb(e3, "selT", [128, NT, 512], BF16)
                        selq = sb(e3, "selq", [128, L], F32)
                        junkD = sb(e3, "junkD", [128, L], BF16)
                        junkA = sb(e3, "junkA", [128, L], BF16)
                        rt = [sb(e3, "rt%d" % i, [128, 2, 512], BF16) for i in range(4)]
                        Dg = [sb(e3, "Dg%d" % i, [128, 8, 128], BF16) for i in range(2)]
                        PTb = [sb(e3, "PT%d" % i, [128, 2, 512], BF16) for i in range(4)]
                        rec = [sb(e3, "rec%d" % i, [128, 512], F32) for i in range(2)]
                        amax = sb(e3, "amax", [128, 4], F32)
                        lo = sb(e3, "lo", [128, 4], F32)
                        mid = sb(e3, "mid", [128, 4], F32)
                        nmid = sb(e3, "nmid", [128, 4], F32)
                        cnt = sb(e3, "cnt", [128, 4], F32)
                        thr = sb(e3, "thr", [128, 4], F32)
                        dd = sb(e3, "dd", [128, 4], F32)
                        tt_ = sb(e3, "tt_", [128, 4], F32)
                        Wt = sb(e3, "Wt", [128, NIT, 4], F32)
                        rt_ctr = [0]
                        pt_ctr = [0]
                        dp_ctr = [0]

                        if s == 0 and l == layers[0] and "tau" in dbg:
                            dtau = dbg_tensor("tau", [128, NT], F32)
                            dsel = dbg_tensor("selT", [4, 128, NT, 512], BF16)
                        else:
                            dtau = None

                        def blk_off(g, r):
                            return sum((4 * g + rr + 1) * 128 for rr in range(r))

                        sb_ctr = [0]

                        def dpair():
                            b = (dp_ctr[0] % 3) * 2
                            dp_ctr[0] += 1
                            return b

                        def idx_steps(g):
                            steps = []
                            for r in range(4):
                                i = 4 * g + r
                                off = blk_off(g, r)
                                wdt = (i + 1) * 128

                                def mk_dg(i=i):
                                    D_ = Dg[i % 2]
                                    for h in range(8):
                                        TS("dve", D_[:, h, :], identb[:], witm[:, i * 8 + h:i * 8 + h + 1], None, ALU.mult, None,
                                           ["identb", "witm"], [("Dg", i % 2)])
                                steps.append((mk_dg, None))
                                for kc in range((wdt + 511) // 512):
                                    ncols = min(512, wdt - kc * 512)
                                    sbank = [None]
                                    for jh in range(4):
                                        st = {}

                                        def stepA(i=i, kc=kc, ncols=ncols, jh=jh, st=st):
                                            bp = dpair()
                                            st["R"] = rt[rt_ctr[0] % 4]
                                            st["rk"] = ("rt", rt_ctr[0] % 4)
                                            rt_ctr[0] += 1
                                            for hh in range(2):
                                                base = hh * 64
                                                PE(PS(bp + hh, 0, ncols), qiT[base:base + 64, jh, i * 128:(i + 1) * 128],
                                                   kiT[base:base + 64, kc * 512:kc * 512 + ncols], True, True,
                                                   [("qi", jh, i // 4), ("ki", 0, kc)], [PT_(bp + hh)])
                                            ACT(st["R"][:, :, 0:ncols], psum[:, bp:bp + 2, 0:ncols], AF.Relu, [],
                                                [st["rk"], PT_(bp), PT_(bp + 1)])

                                        def stepB(i=i, off=off, kc=kc, ncols=ncols, jh=jh, r=r, sbank=sbank, st=st):
                                            if jh == 0:
                                                sbank[0] = 6 + (sb_ctr[0] % 2)
                                                sb_ctr[0] += 1
                                            bs_ = sbank[0]
                                            for hh in range(2):
                                                h = 2 * jh + hh
                                                PE(PS(bs_, 0, ncols), Dg[i % 2][:, h, :], st["R"][:, hh, 0:ncols], h == 0, h == 7,
                                                   [st["rk"], ("Dg", i % 2)], [PT_(bs_)])
                                            if jh == 3:
                                                CP("dve", SC[:, off + kc * 512:off + kc * 512 + ncols], PS(bs_, 0, ncols), [],
                                                   [PT_(bs_), ("SC", r, kc)])
                                        steps.append((stepA, stepB))

                                def fin(i=i, off=off, wdt=wdt, r=r):
                                    sks = [("SC", r, kc) for kc in range((wdt + 511) // 512)]
                                    S.add("dve", lambda e: e.tensor_reduce(out=amax[:, r:r + 1], in_=SC[:, off:off + wdt], axis=AX.X,
                                                                          op=ALU.max, apply_absolute_value=True),
                                          reads=sks, writes=[("amax", r)])
                                    dk = ("SC", r, i // 4)
                                    TT("pool", SC[:, off + i * 128:off + (i + 1) * 128], SC[:, off + i * 128:off + (i + 1) * 128],
                                       cmask[:], ALU.add, [dk, "cmask", ("amax", r)], [dk])
                                steps.append((None, fin))
                            return steps

                        def bis_steps(g):
                            steps = []
                            blocks = [r for r in range(4) if 4 * g + r >= 2]

                            def init():
                                aks = [("amax", r) for r in range(4)]
                                MS("dve", lo[:], -1.0e29, ["lo"])
                                MS("dve", thr[:], TOPK - 0.5, ["thr"])
                                MS("dve", cnt[:], 0.0, [("cnt", r) for r in range(4)])
                                for r in blocks:
                                    wdt = (4 * g + r + 1) * 128
                                    TS("dve", lo[:, r:r + 1], amax[:, r:r + 1], -1.0, None, ALU.mult, None, aks, ["lo"])
                                    if r % 2 == 1:
                                        MS("dve", thr[:, r:r + 1], float(2 * TOPK - 1 - wdt), ["thr"])
                                TS("dve", Wt[:, 0, :], amax[:], 1.0000005, 1.0e-30, ALU.mult, ALU.add, aks, ["Wt"])
                                for it in range(1, NIT):
                                    TS("dve", Wt[:, it, :], Wt[:, 0, :], float(2.0 ** (-it)), None, ALU.mult, None, ["Wt"], ["Wt"])
                                for r in range(4):
                                    if r not in blocks:
                                        MS("dve", Wt[:, :, r:r + 1], 0.0, ["Wt"])
                            steps.append(init)
                            if not blocks:
                                return steps
                            for it in range(NIT):
                                def step(it=it):
                                    TT("pool", mid[:], lo[:], Wt[:, it, :], ALU.add, ["lo", "Wt"], ["mid"])
                                    TS("pool", nmid[:], mid[:], -1.0, 0.0, ALU.mult, ALU.add, ["mid"], ["nmid"])
                                    for r in blocks:
                                        i = 4 * g + r
                                        off = blk_off(g, r)
                                        wdt = (i + 1) * 128
                                        sks = [("SC", r, kc) for kc in range((wdt + 511) // 512)]
                                        if r % 2 == 0:
                                            TS("dve", junkD[:, 0:wdt], SC[:, off:off + wdt], mid[:, r:r + 1], None, ALU.is_ge, ALU.add,
                                               sks + ["mid"], ["junkD", ("cnt", r)], accum=cnt[:, r:r + 1])
                                        else:
                                            ACT(junkA[:, 0:wdt], SC[:, off:off + wdt], AF.Sign, sks + ["nmid"], ["junkA", ("cnt", r)],
                                                bias=nmid[:, r:r + 1], accum=cnt[:, r:r + 1])
                                    cks = [("cnt", r) for r in range(4)]
                                    TT("pool", dd[:], cnt[:], thr[:], ALU.subtract, cks + ["thr"], ["dd"])
                                    TS("pool", dd[:], dd[:], 0.0, None, ALU.is_ge, None, ["dd"], ["dd"])
                                    TT("pool", tt_[:], dd[:], Wt[:, it, :], ALU.mult, ["dd", "Wt"], ["tt_"])
                                    TT("pool", lo[:], lo[:], tt_[:], ALU.add, ["lo", "tt_"], ["lo"])
                                steps.append(step)
                            return steps

                        def sel_steps(g):
                            steps = []
                            for r in range(4):
                                def step(r=r):
                                    i = 4 * g + r
                                    off = blk_off(g, r)
                                    wdt = (i + 1) * 128
                                    sks = [("SC", r, kc) for kc in range((wdt + 511) // 512)]
                                    TS("dve", selq[:, 0:wdt], SC[:, off:off + wdt], lo[:, r:r + 1], None, ALU.is_ge, None,
                                       sks + ["lo"], ["selq"])
                                    for m in range((i + 4) // 4):
                                        nk = min(4, i + 1 - 4 * m)
                                        b = nb()
                                        for kk in range(nk):
                                            kt = 4 * m + kk
                                            TR(PS(b, kk * 128, (kk + 1) * 128), selq[:, kt * 128:(kt + 1) * 128], identf[:],
                                               ["selq", "identf"], [PT_(b)])
                                        dst = selT[:, 4 * m:4 * m + nk, r * 128:(r + 1) * 128]
                                        src = PS(b, 0, nk * 128).rearrange("p (a n) -> p a n", a=nk)
                                        wk = [("selT", 4 * m + kk) for kk in range(nk)]
                                        if m % 2 == 0:
                                            ACT(dst, src, AF.Identity, [], [PT_(b)] + wk)
                                        else:
                                            CP("dve", dst, src, [], [PT_(b)] + wk)
                                steps.append(step)
                            if dtau is not None:
                                def dump():
                                    DMA("sp", dtau[:, 4 * g:4 * g + 4], lo[:], ["lo"], [("dtau", g)])
                                    DMA("sp", dsel[g], selT[:], [("selT", kt) for kt in range(NT)], [("dsel", g)])
                                steps.append(dump)
                            return steps

                        def attn_steps(g):
                            steps = []
                            nkt = 4 * g + 4
                            for j in range(4):
                                for kt in range(nkt):
                                    st = {}

                                    def stepA(j=j, kt=kt, st=st):
                                        q0 = max(0, kt - 4 * g) * 128
                                        N = 512 - q0
                                        bp = dpair()
                                        P_ = PTb[pt_ctr[0] % 4]
                                        ptk = ("PT", pt_ctr[0] % 4)
                                        pt_ctr[0] += 1
                                        st["P"] = P_
                                        st["ptk"] = ptk
                                        for hh in range(2):
                                            base = hh * 64
                                            PE(PS(bp + hh, 0, N), kT[base:base + 64, kt * 128:(kt + 1) * 128],
                                               qT[base:base + 64, j, g * 512 + q0:(g + 1) * 512], True, True,
                                               [("k", 0, kt // 4), ("q", j, g)], [PT_(bp + hh)])
                                        ACT(P_[:, :, 0:N], psum[:, bp:bp + 2, 0:N], AF.Exp, [], [ptk, PT_(bp), PT_(bp + 1)], scale=0.125)
                                        for hh in range(2):
                                            TT("dve", P_[:, hh, 0:N], P_[:, hh, 0:N], selT[:, kt, q0:512], ALU.mult,
                                               [ptk, ("selT", kt)], [ptk])

                                    def stepB(j=j, kt=kt, st=st):
                                        q0 = max(0, kt - 4 * g) * 128
                                        N = 512 - q0
                                        po = 6
                                        pd = 7
                                        P_ = st["P"]
                                        ptk = st["ptk"]
                                        for hh in range(2):
                                            base = hh * 64
                                            PE(psum[base:base + 64, po, q0:512], vtm[:, kt, base:base + 64], P_[:, hh, 0:N],
                                               kt == 0, kt == nkt - 1, [ptk, ("vtm", kt // 4)], [PT_(po)])
                                            PE(psum[base:base + 64, pd, q0:512], onesb[:, 0:64], P_[:, hh, 0:N],
                                               kt == 0, kt == nkt - 1, [ptk, "onesb"], [PT_(pd)])
                                        if kt == nkt - 1:
                                            R = rec[j % 2]
                                            rk = ("rec", j % 2)
                                            S.add("dve", lambda e: e.reciprocal(out=R[:], in_=PS(pd)), reads=[], writes=[rk, PT_(pd)])
                                            TT("dve", OT[:, j, g * 512:(g + 1) * 512], PS(po), R[:], ALU.mult, [rk], [PT_(po), ("OT", j, g)])
                                    steps.append((stepA, stepB))
                            return steps

                        def pipe(pairs, lag):
                            out = []
                            n = len(pairs)
                            for t_ in range(n + lag):
                                if t_ < n and pairs[t_][0] is not None:
                                    out.append(pairs[t_][0])
                                if t_ - lag >= 0 and pairs[t_ - lag][1] is not None:
                                    out.append(pairs[t_ - lag][1])
                            return out

                        def run(steps):
                            for st_ in steps:
                                st_()

                        def interleave(a, b):
                            na, nb_ = len(a), len(b)
                            ia = ib = 0
                            while ia < na or ib < nb_:
                                if ib >= nb_ or (ia < na and ia * nb_ <= ib * na):
                                    a[ia]()
                                    ia += 1
                                else:
                                    b[ib]()
                                    ib += 1

                        run(pipe(idx_steps(0), 1))
                        run(bis_steps(0))
                        run(sel_steps(0))
                        for g in range(1, NG):
                            run(pipe(idx_steps(g), 1))
                            interleave(bis_steps(g), pipe(attn_steps(g - 1), 2))
                            run(sel_steps(g))
                        run(pipe(attn_steps(NG - 1), 2))
                        if s == 0 and l == layers[0] and "OT" in dbg:
                            DMA("sp", dbg_tensor("OT", [128, 4, L], BF16), OT[:], [("OT", j, g) for j in range(4) for g in range(4)], ["dbg_OT"])
                        S.barrier()
                S.barrier()
                with ExitStack() as eC:
                    ypool = sb(eC, "ypool", [128, 2, L], BF16)
                    yconv = sb(eC, "yconv", [128, 2, L], BF16)
                    with ExitStack() as e1:
                        up = sb(e1, "up", [128, 2, 16 + L], F32)
                        sA = sb(e1, "sA", [128, 16 + L], F32)
                        sB = sb(e1, "sB", [128, 16 + L], F32)
                        mixed = sb(e1, "mixed", [128, 2, L], BF16)
                        t16 = sb(e1, "t16", [128, 16], F32)
                        pwb = sb(e1, "pwb", [128, 2, 128], BF16)
                        DMA("pool", pwb[:], W[("pwbd", l)], [], ["pwb"])
                        MS("pool", up[:, :, 0:16], 0.0, [("up", 0, -1), ("up", 1, -1)])
                        MS("pool", sA[:, 0:16], 0.0, ["sA"])
                        MS("pool", sB[:, 0:16], 0.0, ["sB"])
                        stm = Stream(l, CH_POOL)
                        for c in range(2):
                            sl = stm.get(c)
                            for g in range(NG):
                                b = nb()
                                proj_chunk(sl, g, b)
                                ACT(up[:, c, 16 + g * 512:16 + (g + 1) * 512], PS(b), AF.Identity, [pk], [PT_(b), ("up", c, g)],
                                    bias=par[:, PC_BA + c:PC_BA + c + 1])
                        for c in range(2):
                            uk = [("up", c, g) for g in range(-1, 4)]
                            U = up[:, c, :]
                            TT("dve", sA[:, 16:], U[:, 16:], U[:, 15:15 + L], ALU.add, uk, ["sA"])
                            TT("dve", sB[:, 16:], sA[:, 16:], sA[:, 14:14 + L], ALU.add, ["sA"], ["sB"])
                            if c == 1:
                                TT("dve", sA[:, 16:], sB[:, 16:], sB[:, 12:12 + L], ALU.add, ["sB"], ["sA"])
                                TT("dve", sB[:, 16:], sA[:, 16:], sA[:, 8:8 + L], ALU.add, ["sA"], ["sB"])
                            for half, (sbuf_, sk) in enumerate(((sA, "sA"), (sB, "sB"))):
                                widx = 2 * c + half
                                win = float(2 ** (widx + 1))
                                pr = slice(half * 64, half * 64 + 64)
                                STT(mixed[pr, c, 16:], sbuf_[pr, 32:], 1.0 / win, U[pr, 32:], ALU.mult, ALU.subtract, [sk] + uk,
                                    [("mixed", c, half)])
                                TT("dve", t16[pr, :], sbuf_[pr, 16:32], rct[pr, widx, :], ALU.mult, [sk, ("rct", widx)], ["t16"])
                                TT("dve", mixed[pr, c, 0:16], t16[pr, :], U[pr, 16:32], ALU.subtract, ["t16"] + uk, [("mixed", c, half)])
                        for c in range(2):
                            for g in range(NG):
                                b = nb()
                                PE(PS(b), pwb[:, c, :], mixed[:, c, g * 512:(g + 1) * 512], True, True,
                                   ["pwb", ("mixed", c, 0), ("mixed", c, 1)], [PT_(b)])
                                ACT(ypool[:, c, g * 512:(g + 1) * 512], PS(b), AF.Identity, [pk], [PT_(b), ("ypool", c, g)],
                                    scale=par[:, PC_PSC + c:PC_PSC + c + 1])
                        S.barrier()
                    with ExitStack() as e1:
                        glu = sb(e1, "glu", [128, 2, 30 + L], BF16)
                        dg = sb(e1, "dg", [128, 2, 31, 128], BF16)
                        xcv = sb(e1, "xcv", [128, 2, L], F32)
                        xsq = [sb(e1, "xsq%d" % i, [128, 512], F32) for i in range(2)]
                        sgt = [sb(e1, "sgt%d" % i, [128, 512], F32) for i in range(2)]
                        mean_t = sb(e1, "mean_t", [128, 512], F32)
                        var_t = sb(e1, "var_t", [128, 512], F32)
                        dtmp = [sb(e1, "dtmp%d" % i, [128, 512], F32) for i in range(2)]
                        MS("pool", glu[:, :, 0:30], 0.0, [("glu", 0, -1), ("glu", 1, -1)])
                        for c in range(2):
                            for jj in range(31):
                                TS("dve", dg[:, c, jj, :], identb[:], par[:, PC_CDW + c * 31 + jj:PC_CDW + c * 31 + jj + 1], None,
                                   ALU.mult, None, ["identb", pk], [("dg", c)])
                        stm = Stream(l, [CH_CA[0], CH_CG[0], CH_CA[1], CH_CG[1]])
                        cc = 0
                        for c in range(2):
                            sa_ = stm.get(2 * c)
                            sg_ = stm.get(2 * c + 1)
                            for g in range(NG):
                                ba = nb()
                                bg = nb()
                                proj_chunk(sa_, g, ba)
                                proj_chunk(sg_, g, bg)
                                T_ = sgt[cc % 2]
                                tk_ = ("sgt", cc % 2)
                                cc += 1
                                ACT(T_[:], PS(bg), AF.Sigmoid, [pk], [tk_, PT_(bg)], bias=par[:, PC_BA + CH_CG[c]:PC_BA + CH_CG[c] + 1])
                                STT(glu[:, c, 30 + g * 512:30 + (g + 1) * 512], PS(ba), par[:, PC_BA + CH_CA[c]:PC_BA + CH_CA[c] + 1],
                                    T_[:], ALU.add, ALU.mult, [pk, tk_], [PT_(ba), ("glu", c, g)])
                        for g in range(NG):
                            bm = nb()
                            bq = nb()
                            for c in range(2):
                                b = nb()
                                gk = [("glu", c, gg) for gg in range(-1, 4)]
                                for jj in range(31):
                                    PE(PS(b), dg[:, c, jj, :], glu[:, c, g * 512 + jj:g * 512 + jj + 512], jj == 0, jj == 30,
                                       [("dg", c)] + gk, [PT_(b)])
                                X2 = xsq[c]
                                ACT(xcv[:, c, g * 512:(g + 1) * 512], PS(b), AF.Identity, [pk], [PT_(b), ("xcv", c, g)],
                                    bias=par[:, PC_CDB + c:PC_CDB + c + 1])
                                ACT(X2[:], PS(b), AF.Square, [pk], [PT_(b), ("xsq", c)], bias=par[:, PC_CDB + c:PC_CDB + c + 1])
                            for c in range(2):
                                PE(PS(bm), onesf[:], xcv[:, c, g * 512:(g + 1) * 512], c == 0, c == 1, ["onesf", ("xcv", c, g)], [PT_(bm)])
                            for c in range(2):
                                PE(PS(bq), onesf[:], xsq[c][:], c == 0, c == 1, ["onesf", ("xsq", c)], [PT_(bq)])
                            TS("dve", mean_t[:], PS(bm), 1.0 / 256, None, ALU.mult, None, [], ["mean_t", PT_(bm)])
                            TT("dve", var_t[:], mean_t[:], mean_t[:], ALU.mult, ["mean_t"], ["var_t"])
                            STT(var_t[:], PS(bq), 1.0 / 256, var_t[:], ALU.mult, ALU.subtract, ["var_t"], ["var_t", PT_(bq)])
                            ACT(var_t[:], var_t[:], AF.Sqrt, ["var_t"], ["var_t"], bias=EPS)
                            S.add("dve", lambda e: e.reciprocal(out=var_t[:], in_=var_t[:]), reads=["var_t"], writes=["var_t"])
                            for c in range(2):
                                Dm = dtmp[c]
                                dk_ = ("dtmp", c)
                                TT("dve", Dm[:], xcv[:, c, g * 512:(g + 1) * 512], mean_t[:], ALU.subtract, [("xcv", c, g), "mean_t"], [dk_])
                                TT("pool", Dm[:], Dm[:], var_t[:], ALU.mult, [dk_, "var_t"], [dk_])
                                ACT(yconv[:, c, g * 512:(g + 1) * 512], Dm[:], AF.Silu, [dk_, pk], [("yconv", c, g)],
                                    bias=par[:, PC_CLB + c:PC_CLB + c + 1], scale=par[:, PC_CLG + c:PC_CLG + c + 1])
                        S.barrier()
                    if s == 0 and l == layers[0]:
                        if "ypool" in dbg:
                            DMA("sp", dbg_tensor("ypool", [128, 2, L], BF16), ypool[:], [("ypool", c, g) for c in range(2) for g in range(4)], ["dbg_ypool"])
                        if "yconv" in dbg:
                            DMA("sp", dbg_tensor("yconv", [128, 2, L], BF16), yconv[:], [("yconv", c, g) for c in range(2) for g in range(4)], ["dbg_yconv"])
                    with ExitStack() as e4:
                        outw = sb(e4, "outw", [128, 8, 1024], BF16)
                        sg3 = [sb(e4, "sg3_%d" % i, [128, 512], F32) for i in range(3)]
                        mm_ = [sb(e4, "mm_%d" % i, [128, 512], F32) for i in range(3)]
                        DMA("pool", outw[:], W[("outw", l)], [], ["outw"])
                        chunks = []
                        for c in range(8):
                            chunks += [CH_GATE0 + i * 8 + c for i in range(3)]
                        stm = Stream(l, chunks, ahead=1)
                        ysrc = [(ypool, "ypool", 0, 2), (yconv, "yconv", 2, 2), (OT, "OT", 4, 4)]
                        for c in range(8):
                            sl3 = [stm.get(3 * c + i) for i in range(3)]
                            for g in range(NG):
                                gs = slice(g * 512, (g + 1) * 512)
                                for i in range(3):
                                    bg = nb()
                                    proj_chunk(sl3[i], g, bg)
                                    ch = CH_GATE0 + i * 8 + c
                                    ACT(sg3[i][:], PS(bg), AF.Sigmoid, [pk], [("sg3", i), PT_(bg)], bias=par[:, PC_BA + ch:PC_BA + ch + 1])
                                for i, (ysb, yn, k0, nk) in enumerate(ysrc):
                                    by = nb()
                                    for k in range(nk):
                                        PE(PS(by), outw[:, k0 + k, c * 128:(c + 1) * 128], ysb[:, k, gs], k == 0, k == nk - 1,
                                           ["outw", (yn, k, g)], [PT_(by)])
                                    TT("dve", mm_[i][:], PS(by), sg3[i][:], ALU.mult, [("sg3", i)], [("mm_", i), PT_(by)])
                                TT("pool", mm_[0][:], mm_[0][:], mm_[1][:], ALU.add, [("mm_", 0), ("mm_", 1)], [("mm_", 0)])
                                TT("pool", merged[:, c, gs], mm_[0][:], mm_[2][:], ALU.add, [("mm_", 0), ("mm_", 2)], [("merged", c, g)])
                        S.barrier()
                S.barrier()
            if s == 0 and l == layers[0] and "merged" in dbg:
                DMA("sp", dbg_tensor("merged", [128, 8, L], BF16), merged, [("merged", c, g) for c in range(8) for g in range(4)], ["dbg_merged"])
            with ExitStack() as e5:
                wo = sb(e5, "wo", [128, 8, 1024], BF16)
                ln_alloc(e5)
                DMA("pool", wo[:], W[("wo", l)], [], ["wo"])
                ln_rows_load(W[("lnr", l)][:, 0:2048])
                ln_load(s, 0)
                for i in range(NT):
                    if i + 1 < NT:
                        ln_load(s, i + 1)
                    if i >= 2:
                        ln_b(s, i - 2, False)
                    bp = npair()
                    for n in range(2):
                        for k in range(8):
                            PE(PS(bp + n), merged[:, k, i * 128:(i + 1) * 128], wo[:, k, n * 512:(n + 1) * 512], k == 0, k == 7,
                               ["wo", ("merged", k, i // 4)], [PT_(bp + n)])
                    ln_a(s, i, bp)
                ln_b(s, NT - 2, False)
                ln_b(s, NT - 1, False)
                S.barrier()
            if s == 0 and l == layers[0] and "hT1" in dbg:
                DMA("sp", dbg_tensor("hT1", [128, 8, L], BF16), hT[:], [("hT", g) for g in range(4)], ["dbg_hT1"])
            with ExitStack() as e6:
                wd = sb(e6, "wd", [128, NJ, 1024], BF16)
                ln_alloc(e6)
                NWU = 3
                wUb = [sb(e6, "wU%d" % i, [128, 8, 256], BF16) for i in range(NWU)]
                xg = [sb(e6, "xg%d" % i, [128, 2 + 1024], F32) for i in range(2)]
                xv = [sb(e6, "xv%d" % i, [128, 2 + 1024], F32) for i in range(2)]
                ug = [sb(e6, "ug%d" % i, [128, 1024], F32) for i in range(2)]
                uv = [sb(e6, "uv%d" % i, [128, 1024], F32) for i in range(2)]
                halo = sb(e6, "halo", [128, 2 * NJ, 2], F32)
                for kq in range(2):
                    DMA("pool", wd[:, kq * 11:(kq + 1) * 11, :], W[("wd", l)][:, kq * 11:(kq + 1) * 11, :], [], [("wd", kq)])
                ln_rows_load(W[("lnr", l)][:, 2048:4096])
                wu_ctr = [0]
                wu_slots = {}

                def wu_get(idx):
                    while wu_ctr[0] <= min(idx + 1, 2 * NJ - 1):
                        n_ = wu_ctr[0]
                        sl = n_ % NWU
                        DMA("pool", wUb[sl][:], W[("wU", l)][n_ % NJ], [], [("wU", sl)], slot=("wU", sl))
                        wu_slots[n_] = sl
                        wu_ctr[0] += 1
                    return wu_slots[idx]

                for hf in range(2):
                    t0 = hf * 1024
                    for j in range(NJ):
                        sl = wu_get(hf * NJ + j)
                        p_ = j % 2
                        XG, XV, UG, UV = xg[p_], xv[p_], ug[p_], uv[p_]
                        for (X, xk, coff, hidx) in ((XG, ("xg", p_), 0, j), (XV, ("xv", p_), 128, NJ + j)):
                            if hf == 0:
                                CP("pool", X[:, 0:2], zero2[:], ["zero2"], [xk])
                            else:
                                CP("pool", X[:, 0:2], halo[:, hidx, :], [("halo", hidx)], [xk])
                            for gg in range(2):
                                g = hf * 2 + gg
                                b = nb()
                                for k in range(8):
                                    PE(PS(b), wUb[sl][:, k, coff:coff + 128], hT[:, k, g * 512:(g + 1) * 512], k == 0, k == 7,
                                       [("wU", sl), ("hT", g)], [PT_(b)])
                                ACT(X[:, 2 + gg * 512:2 + (gg + 1) * 512], PS(b), AF.Identity, [], [PT_(b), xk])
                            if hf == 0:
                                CP("pool", halo[:, hidx, :], X[:, 1024:1026], [xk], [("halo", hidx)])
                        for (X, xk, U, uk, wc, bc) in ((XG, ("xg", p_), UG, ("ug", p_), PC_FWG + 3 * j, PC_FBG + j),
                                                      (XV, ("xv", p_), UV, ("uv", p_), PC_FWV + 3 * j, PC_FBV + j)):
                            TS("dve", U[:], X[:, 2:1026], par[:, wc + 2:wc + 3], par[:, bc:bc + 1], ALU.mult, ALU.add, [xk, pk], [uk])
                            STT(U[:], X[:, 1:1025], par[:, wc + 1:wc + 2], U[:], ALU.mult, ALU.add, [xk, pk, uk], [uk])
                            STT(U[:], X[:, 0:1024], par[:, wc:wc + 1], U[:], ALU.mult, ALU.add, [xk, pk, uk], [uk])
                        ACT(UG[:], UG[:], AF.Silu, [("ug", p_)], [("ug", p_)])
                        TT("pool", actb[:, j, 0:1024], UG[:], UV[:], ALU.mult, [("ug", p_), ("uv", p_)], [("act", j)])
                    ln_load(s, hf * 8)
                    for ii in range(8):
                        i = hf * 8 + ii
                        if ii + 1 < 8:
                            ln_load(s, i + 1)
                        if ii >= 2:
                            ln_b(s, i - 2, last_layer)
                        bp = npair()
                        for n in range(2):
                            for k in range(NJ):
                                PE(PS(bp + n), actb[:, k, ii * 128:(ii + 1) * 128], wd[:, k, n * 512:(n + 1) * 512], k == 0, k == NJ - 1,
                                   [("wd", k // 11), ("act", k)], [PT_(bp + n)])
                        ln_a(s, i, bp)
                    ln_b(s, hf * 8 + 6, last_layer)
                    ln_b(s, hf * 8 + 7, last_layer)
                S.barrier()
        if not last:
            pass
    S.emit()
    es_top.close()
    return nc, S, dbg_out


_CACHE = {}


def _in_maps(x, positions, wts, layers, ncores, nseq):
    maps = []
    for c in range(ncores):
        m = {"x": np.ascontiguousarray(x[c * nseq:(c + 1) * nseq]),
             "pos": np.ascontiguousarray(np.broadcast_to(positions[c * nseq:(c + 1) * nseq, None, :], (nseq, 128, L))).astype(np.int32),
             "lnin": wts["lnin"], "cst": wts["cst"]}
        for l in layers:
            for k in WSHAPES:
                m["%s%d" % (k, l)] = wts["%s%d" % (k, l)]
        maps.append(m)
    return maps


def kernel(**inputs):
    x = np.asarray(inputs["x"], np.float32)
    positions = np.asarray(inputs["positions"], np.int32)
    wts = prep_weights(inputs)
    ncores = 8
    nseq = x.shape[0] // ncores
    key = ("fused", nseq)
    if key not in _CACHE:
        _CACHE[key] = build(nseq=nseq, layers=(0, 1))
    nc, S, _ = _CACHE[key]
    maps = _in_maps(x, positions, wts, (0, 1), ncores, nseq)
    res = run_bass_kernel_spmd(nc, maps, core_ids=list(range(ncores)))
    out = np.concatenate([np.asarray(r["y"]) for r in res.results], axis=0)
    return out.astype(np.float32)
```

```python
import numpy as np
from contextlib import ExitStack
import concourse.bass as bass
import concourse.mybir as mybir
from concourse.bass_utils import run_bass_kernel_spmd

F32 = mybir.dt.float32
BF16 = mybir.dt.bfloat16
I32 = mybir.dt.int32
ALU = mybir.AluOpType
AF = mybir.ActivationFunctionType
AX = mybir.AxisListType

L = 2048
D = 1024
NT = 16
NG = 4
DFF = 2816
NJ = 22
DEPTH = 2
ALPHA = float((2 * DEPTH) ** 0.25)
EPS = 1e-5
TOPK = 256
NIT = 16
NEG = -1.0e30
SEM_LIMIT = 30000

CH_POOL = [0, 1]
CH_CA = [2, 3]
CH_CG = [4, 5]
CH_Q = [6, 7, 8, 9]
CH_QS = [10, 11, 12, 13]
CH_K = 14
CH_KS = 15
CH_V = 16
CH_QI = [17, 18, 19, 20]
CH_QIS = [21, 22, 23, 24]
CH_KI = 25
CH_KIS = 26
CH_GATE0 = 27
NCH = 51
PC_BA = 0
PC_PSC = 51
PC_CDB = 53
PC_CLG = 55
PC_CLB = 57
PC_CDW = 59
PC_FWG = 121
PC_FWV = 187
PC_FBG = 253
PC_FBV = 275
NPAR = 297


class _Op:
    __slots__ = ("idx", "eng", "fn", "deps_raw", "deps_other", "is_dma", "slot", "signal", "stream", "val", "vc")

    def __init__(self, idx, eng, fn, is_dma, slot):
        self.idx = idx
        self.eng = eng
        self.fn = fn
        self.is_dma = is_dma
        self.slot = slot
        self.deps_raw = set()
        self.deps_other = set()
        self.signal = False
        self.stream = None
        self.val = 0
        self.vc = None


class Sched:
    def __init__(self, nc):
        self.nc = nc
        self.ops = []
        self.last_writer = {}
        self.readers = {}
        self.last_dma_on_slot = {}
        self.last_on_eng = {}
        self.dma_since_barrier = []
        self.barrier_deps = set()

    def barrier(self):
        self.barrier_deps = set(self.last_on_eng.values()) | set(self.dma_since_barrier)
        self.dma_since_barrier = []
        self.last_writer = {}
        self.readers = {}

    def add(self, eng, fn, reads=(), writes=(), dma=False, slot=None):
        idx = len(self.ops)
        if dma and slot is None:
            slot = ("auto", tuple(writes)[0])
        op = _Op(idx, eng, fn, dma, slot)
        for r in reads:
            w = self.last_writer.get(r)
            if w is not None:
                op.deps_raw.add(w)
        for t in writes:
            w = self.last_writer.get(t)
            if w is not None:
                op.deps_other.add(w)
            for rd in self.readers.get(t, ()):
                op.deps_other.add(rd)
        if dma:
            p = self.last_dma_on_slot.get(slot)
            if p is not None:
                op.deps_raw.add(p)
            self.last_dma_on_slot[slot] = idx
            self.dma_since_barrier.append(idx)
        op.deps_other |= self.barrier_deps
        for r in reads:
            self.readers.setdefault(r, []).append(idx)
        for t in writes:
            self.last_writer[t] = idx
            self.readers[t] = []
        op.deps_other -= op.deps_raw
        self.last_on_eng[eng] = idx
        self.ops.append(op)
        return idx

    def _needed(self, op, d):
        dop = self.ops[d]
        if dop.is_dma or op.is_dma:
            return True
        if dop.eng != op.eng:
            return True
        if op.eng == "pe":
            return False
        return d in op.deps_raw

    def emit(self):
        nc = self.nc
        ops = self.ops
        for op in ops:
            for d in (op.deps_raw | op.deps_other):
                if self._needed(op, d):
                    ops[d].signal = True
        for op in ops:
            if op.is_dma:
                op.signal = True
        sems = {}
        counts = {}
        sem_objs = []

        def new_sem():
            cm = nc.semaphore("s%d" % len(sem_objs))
            h = cm.__enter__()
            sem_objs.append(cm)
            return h

        for op in ops:
            if not op.signal:
                continue
            key = ("slot", op.slot) if op.is_dma else ("eng", op.eng)
            inc = 16 if op.is_dma else 1
            if key not in sems:
                sems[key] = [new_sem()]
                counts[key] = 0
            if counts[key] + inc > SEM_LIMIT:
                sems[key].append(new_sem())
                counts[key] = 0
            counts[key] += inc
            op.stream = (key, len(sems[key]) - 1)
            op.val = counts[key]
        self.n_sems = len(sem_objs)
        eng_clock = {}
        waits = [None] * len(ops)
        for op in ops:
            clk = eng_clock.setdefault(op.eng, {})
            best = {}
            for d in sorted(op.deps_raw | op.deps_other):
                if not self._needed(op, d):
                    continue
                dop = ops[d]
                if clk.get(dop.stream, 0) >= dop.val:
                    continue
                if best.get(dop.stream, 0) < dop.val:
                    best[dop.stream] = dop.val
                for k, v in dop.vc.items():
                    if clk.get(k, 0) < v:
                        clk[k] = v
            waits[op.idx] = best
            if op.signal:
                vc = dict(clk)
                vc[op.stream] = op.val
                op.vc = vc
        per_eng = {}
        for op in ops:
            per_eng.setdefault(op.eng, []).append(op)
        final_waits = {}
        for op in ops:
            if op.is_dma:
                if final_waits.get(op.stream, 0) < op.val:
                    final_waits[op.stream] = op.val

        def semh(stream):
            key, ep = stream
            return sems[key][ep]

        engmap = {"pe": "tensor", "act": "scalar", "dve": "vector", "pool": "gpsimd", "sp": "sync"}
        self.n_waits = 0
        with nc.Block() as block:
            for ename in ("sp", "pe", "act", "dve", "pool"):
                lst = per_eng.get(ename, [])

                def body(e, lst=lst, ename=ename):
                    for op in lst:
                        for s, v in waits[op.idx].items():
                            e.wait_ge(semh(s), v)
                            self.n_waits += 1
                        ins = op.fn(e)
                        if op.signal:
                            ins.then_inc(semh(op.stream), 16 if op.is_dma else 1)
                    if ename == "sp":
                        for s, v in final_waits.items():
                            e.wait_ge(semh(s), v)
                getattr(block, engmap[ename])(body)
        for cm in reversed(sem_objs):
            cm.__exit__(None, None, None)
        return self


def _chunk_cols():
    cols = []
    ar = np.arange

    def hc(base, h):
        return base + 64 * h + ar(64)

    def sw(c):
        return np.concatenate([c[32:], c[:32]])

    cols += [ar(128), 128 + ar(128)]
    cols += [256 + 128 * i + ar(128) for i in range(4)]
    for j in range(4):
        cols.append(np.concatenate([hc(768, j), hc(768, 4 + j)]))
    for j in range(4):
        cols.append(np.concatenate([sw(hc(768, j)), sw(hc(768, 4 + j))]))
    cols.append(np.concatenate([hc(1280, 0), hc(1280, 1)]))
    cols.append(np.concatenate([sw(hc(1280, 0)), sw(hc(1280, 1))]))
    cols.append(1408 + ar(128))
    for j in range(4):
        cols.append(np.concatenate([hc(1536, 2 * j), hc(1536, 2 * j + 1)]))
    for j in range(4):
        cols.append(np.concatenate([sw(hc(1536, 2 * j)), sw(hc(1536, 2 * j + 1))]))
    ki = 2048 + ar(64)
    cols.append(np.concatenate([ki, ki]))
    cols.append(np.concatenate([sw(ki), sw(ki)]))
    for i in range(3):
        for c in range(8):
            cols.append(2120 + 1024 * i + 128 * c + ar(128))
    assert len(cols) == NCH
    return np.stack(cols)


def _kp(w):
    K = w.shape[0] // 128
    return np.ascontiguousarray(w.reshape(K, 128, w.shape[1]).transpose(1, 0, 2))


def prep_weights(inp):
    cols = _chunk_cols()
    out = {}
    f = np.float32
    rep = lambda v: np.ascontiguousarray(np.broadcast_to(np.asarray(v, f)[None, :], (128, v.shape[0])))
    for l in range(DEPTH):
        w_in = np.asarray(inp["w_in"][l], f)
        b_in = np.asarray(inp["b_in"][l], f)
        wg = w_in[:, cols.reshape(-1)].reshape(8, 128, NCH, 128)
        out["wA%d" % l] = np.ascontiguousarray(wg.transpose(2, 1, 0, 3))
        out["wwi%d" % l] = _kp(np.ascontiguousarray(w_in[:, 2112:2120]))
        par = np.zeros((128, NPAR), f)
        par[:, PC_BA:PC_BA + NCH] = b_in[cols].T
        par[:, PC_PSC:PC_PSC + 2] = np.asarray(inp["pool_scale"][l], f).reshape(2, 128).T
        par[:, PC_CDB:PC_CDB + 2] = np.asarray(inp["conv_dw_b"][l], f).reshape(2, 128).T
        par[:, PC_CLG:PC_CLG + 2] = np.asarray(inp["conv_ln_g"][l], f).reshape(2, 128).T
        par[:, PC_CLB:PC_CLB + 2] = np.asarray(inp["conv_ln_b"][l], f).reshape(2, 128).T
        cdw = np.asarray(inp["conv_dw_w"][l], f)
        par[:, PC_CDW:PC_CDW + 62] = cdw.reshape(31, 2, 128).transpose(2, 1, 0).reshape(128, 62)
        fw = np.asarray(inp["ffn_dw_w"][l], f)
        par[:, PC_FWG:PC_FWG + 66] = fw[:, :DFF].reshape(3, NJ, 128).transpose(2, 1, 0).reshape(128, 66)
        par[:, PC_FWV:PC_FWV + 66] = fw[:, DFF:].reshape(3, NJ, 128).transpose(2, 1, 0).reshape(128, 66)
        fb = np.asarray(inp["ffn_dw_b"][l], f)
        par[:, PC_FBG:PC_FBG + NJ] = fb[:DFF].reshape(NJ, 128).T
        par[:, PC_FBV:PC_FBV + NJ] = fb[DFF:].reshape(NJ, 128).T
        out["par%d" % l] = par
        out["bwi%d" % l] = rep(np.tile(b_in[2112:2120], 16))
        pw = np.asarray(inp["pool_w"][l], f)
        bd = np.zeros((128, 2, 128), f)
        for c in range(2):
            bd[0:64, c, 0:64] = pw[2 * c]
            bd[64:128, c, 64:128] = pw[2 * c + 1]
        out["pwbd%d" % l] = bd
        wao = np.asarray(inp["w_attn_out"][l], f)
        rows = np.concatenate([np.concatenate([64 * j + np.arange(64), 64 * (4 + j) + np.arange(64)]) for j in range(4)])
        ow = np.concatenate([_kp(np.asarray(inp["w_pool_out"][l], f)), _kp(np.asarray(inp["w_conv_out"][l], f)),
                             _kp(np.ascontiguousarray(wao[rows]))], axis=1)
        out["outw%d" % l] = np.ascontiguousarray(ow)
        out["wo%d" % l] = _kp(np.asarray(inp["w_o"][l], f))
        wu = np.asarray(inp["w_up"][l], f)
        wug = wu[:, :DFF].reshape(8, 128, NJ, 128)
        wuv = wu[:, DFF:].reshape(8, 128, NJ, 128)
        out["wU%d" % l] = np.ascontiguousarray(np.concatenate([wug, wuv], axis=3).transpose(2, 1, 0, 3))
        out["wd%d" % l] = _kp(np.asarray(inp["w_down"][l], f))
        out["lnr%d" % l] = np.ascontiguousarray(np.concatenate(
            [rep(inp["ln1_g"][l]), rep(inp["ln1_b"][l]), rep(inp["ln2_g"][l]), rep(inp["ln2_b"][l])], axis=1))
    out["lnin"] = np.ascontiguousarray(np.concatenate([rep(inp["ln_in_g"]), rep(inp["ln_in_b"])], axis=1))
    half = 32
    invf = (10000.0 ** (-np.arange(half, dtype=np.float32) / half)).astype(f)
    cst = np.zeros((128, 2), f)
    cst[:, 0] = np.tile(invf, 4)
    cst[:, 1] = np.tile(np.concatenate([-np.ones(32, f), np.ones(32, f)]), 2)
    out["cst"] = cst
    return out


WSHAPES = {
    "wA": [NCH, 128, 8, 128], "wwi": [128, 8, 8], "par": [128, NPAR], "bwi": [128, 128], "pwbd": [128, 2, 128],
    "outw": [128, 8, 1024], "wo": [128, 8, 1024], "wU": [NJ, 128, 8, 256], "wd": [128, NJ, 1024], "lnr": [128, 4096],
}


def build(nseq=2, layers=(0, 1), dbg=None, first=True, last=True):
    dbg = dbg or set()
    nc = bass.Bass("TRN2", target_bir_lowering=False)
    dt = {}
    xin = nc.dram_tensor("x", [nseq, L, D], F32, kind="ExternalInput").ap()
    posin = nc.dram_tensor("pos", [nseq, 128, L], I32, kind="ExternalInput").ap()
    W = {}
    for l in layers:
        for k, shp in WSHAPES.items():
            W[(k, l)] = nc.dram_tensor("%s%d" % (k, l), shp, F32, kind="ExternalInput").ap()
    lnin = nc.dram_tensor("lnin", [128, 2048], F32, kind="ExternalInput").ap()
    cstin = nc.dram_tensor("cst", [128, 2], F32, kind="ExternalInput").ap()
    yout = nc.dram_tensor("y", [nseq, L, D], F32, kind="ExternalOutput").ap()
    hres = nc.dram_tensor("hres", [L, D], F32, kind="Internal").ap()
    ropd = nc.dram_tensor("ropd", [128, 2, L], F32, kind="Internal").ap()
    dbg_out = {}

    def dbg_tensor(name, shape, dtype):
        dbg_out[name] = nc.dram_tensor("dbg_" + name, shape, dtype, kind="ExternalOutput").ap()
        return dbg_out[name]

    S = Sched(nc)
    es_top = ExitStack()

    uid = [0]

    def sb(es, name, shape, dtype):
        uid[0] += 1
        return es.enter_context(nc.sbuf_tensor("sb%d_%s" % (uid[0], name), shape, dtype))

    def PE(out, lhsT, rhs, st, sp, r, w):
        S.add("pe", lambda e: e.matmul(out, lhsT=lhsT, rhs=rhs, start=st, stop=sp), reads=r, writes=w)

    def TR(out, in_, ident, r, w):
        S.add("pe", lambda e: e.transpose(out=out, in_=in_, identity=ident), reads=r, writes=w)

    def ACT(out, in_, func, r, w, bias=0.0, scale=1.0, accum=None):
        if accum is None:
            S.add("act", lambda e: e.activation(out=out, in_=in_, func=func, bias=bias, scale=scale), reads=r, writes=w)
        else:
            S.add("act", lambda e: e.activation(out=out, in_=in_, func=func, bias=bias, scale=scale, accum_out=accum),
                  reads=r, writes=w)

    def TT(eng, out, a, b, op, r, w):
        S.add(eng, lambda e: e.tensor_tensor(out=out, in0=a, in1=b, op=op), reads=r, writes=w)

    def TS(eng, out, a, s1, s2, op0, op1, r, w, accum=None):
        if accum is None:
            if op1 is None:
                S.add(eng, lambda e: e.tensor_scalar(out=out, in0=a, scalar1=s1, scalar2=None, op0=op0), reads=r, writes=w)
            else:
                S.add(eng, lambda e: e.tensor_scalar(out=out, in0=a, scalar1=s1, scalar2=s2, op0=op0, op1=op1),
                      reads=r, writes=w)
        else:
            S.add(eng, lambda e: e.tensor_scalar(out=out, in0=a, scalar1=s1, scalar2=s2, op0=op0, op1=op1,
                                                 accum_out=accum), reads=r, writes=w)

    def STT(out, a, s, b, op0, op1, r, w):
        S.add("dve", lambda e: e.scalar_tensor_tensor(out=out, in0=a, scalar=s, in1=b, op0=op0, op1=op1),
              reads=r, writes=w)

    def CP(eng, out, in_, r, w):
        S.add(eng, lambda e: e.tensor_copy(out=out, in_=in_), reads=r, writes=w)

    def MS(eng, out, v, w):
        S.add(eng, lambda e: e.memset(out, v), writes=w)

    def DMA(q, out, in_, r, w, slot=None):
        if q == "pool":
            S.add(q, lambda e: e.dma_start(out=out, in_=in_, max_dma_last_dim=4096), reads=r, writes=w, dma=True, slot=slot)
        else:
            S.add(q, lambda e: e.dma_start(out=out, in_=in_), reads=r, writes=w, dma=True, slot=slot)

    es = es_top
    psum = es.enter_context(nc.psum_tensor("psum", [128, 8, 512], F32))
    identf = sb(es, "identf", [128, 128], F32)
    identb = sb(es, "identb", [128, 128], BF16)
    onesf = sb(es, "onesf", [128, 128], F32)
    onesb = sb(es, "onesb", [128, 64], BF16)
    cmask = sb(es, "cmask", [128, 128], F32)
    cst = sb(es, "cst", [128, 2], F32)
    rct = sb(es, "rct", [128, 4, 16], F32)
    zero2 = sb(es, "zero2", [128, 2], F32)
    hT = sb(es, "hT", [128, 8, L], BF16)
    arena = sb(es, "arena", [128, NJ * 1024], BF16)
    NWA = 4
    wAb = [sb(es, "wA%d" % i, [128, 8, 128], BF16) for i in range(NWA)]
    parb = {l: sb(es, "par%d" % l, [128, NPAR], F32) for l in layers}
    LN = {}

    def ln_alloc(esx):
        LN["row"] = sb(esx, "lnrow", [128, 2, 1024], F32)
        LN["t"] = [sb(esx, "lnt%d" % i, [128, 1024], F32) for i in range(4)]
        LN["st"] = [sb(esx, "lnst%d" % i, [128, 2, 6], F32) for i in range(4)]
        LN["mv"] = [sb(esx, "lnmv%d" % i, [128, 8], F32) for i in range(4)]

    def PS(b, n0=0, n1=512):
        return psum[:, b, n0:n1]

    def PT_(b):
        return ("ps", b)

    bank_ctr = [0]

    def nb():
        b = bank_ctr[0] % 8
        bank_ctr[0] += 1
        return b

    pair_ctr = [0]

    def npair():
        b = (pair_ctr[0] % 4) * 2
        pair_ctr[0] += 1
        return b

    MS("pool", identf[:], 0.0, ["identf"])
    S.add("pool", lambda e: e.affine_select(out=identf[:], in_=identf[:], pattern=[[-1, 128]], compare_op=ALU.not_equal,
                                            fill=1.0, base=0, channel_multiplier=1), reads=["identf"], writes=["identf"])
    CP("dve", identb[:], identf[:], ["identf"], ["identb"])
    MS("dve", onesf[:], 1.0, ["onesf"])
    MS("dve", onesb[:], 1.0, ["onesb"])
    MS("pool", cmask[:], 0.0, ["cmask"])
    S.add("pool", lambda e: e.affine_select(out=cmask[:], in_=cmask[:], pattern=[[-1, 128]], compare_op=ALU.is_ge,
                                            fill=NEG, base=0, channel_multiplier=1), reads=["cmask"], writes=["cmask"])
    MS("dve", zero2[:], 0.0, ["zero2"])
    DMA("sp", cst[:], cstin, [], ["cst"])
    for l in layers:
        DMA("sp", parb[l][:], W[("par", l)], [], [("par", l)])
    with ExitStack() as es0:
        ti = sb(es0, "ti", [128, 16], I32)
        tf = sb(es0, "tf", [128, 16], F32)
        S.add("pool", lambda e: e.iota(ti[:], pattern=[[1, 16]], base=1, channel_multiplier=0), writes=["ti"])
        CP("dve", tf[:], ti[:], ["ti"], ["tf"])
        for wi_ in range(4):
            TS("dve", rct[:, wi_, :], tf[:], float(2 ** (wi_ + 1)), None, ALU.min, None, ["tf"], [("rct", wi_)])
            S.add("dve", lambda e, wi_=wi_: e.reciprocal(out=rct[:, wi_, :], in_=rct[:, wi_, :]),
                  reads=[("rct", wi_)], writes=[("rct", wi_)])
        S.barrier()

    wa_ctr = [0]

    def load_chunk(l, c):
        slot = wa_ctr[0] % NWA
        wa_ctr[0] += 1
        DMA("pool", wAb[slot][:], W[("wA", l)][c], [], [("wA", slot)], slot=("wA", slot))
        return slot

    class Stream:
        def __init__(self, l, chunks, ahead=2):
            self.l = l
            self.chunks = list(chunks)
            self.slots = {}
            self.next = 0
            self.ahead = ahead

        def get(self, i):
            while self.next < len(self.chunks) and self.next <= i + self.ahead:
                self.slots[self.next] = load_chunk(self.l, self.chunks[self.next])
                self.next += 1
            return self.slots[i]

    def proj_chunk(slot, g, bank, ncols=128):
        for k in range(8):
            PE(PS(bank)[0:ncols, :], wAb[slot][:, k, 0:ncols], hT[:, k, g * 512:(g + 1) * 512], k == 0, k == 7,
               [("wA", slot), ("hT", g)], [PT_(bank)])

    def ln_rows_load(src_ap):
        DMA("sp", LN["row"][:], src_ap.rearrange("p (a n) -> p a n", a=2), [], ["lnrow"])

    def ln_load(s, i, x_src=None):
        t = LN["t"][i % 4]
        tk = ("lnt", i % 4)
        if x_src is not None:
            DMA("sp", t[:], x_src, [], [tk], slot=("lnt_in", i % 4))
        else:
            DMA("sp", t[:], hres[i * 128:(i + 1) * 128, :], [("hres", i)], [tk], slot=("lnt_in", i % 4))

    def ln_a(s, i, mixbank):
        t = LN["t"][i % 4]
        tk = ("lnt", i % 4)
        st = LN["st"][i % 4]
        mv = LN["mv"][i % 4]
        mk = ("lnmv", i % 4)
        if mixbank is not None:
            STT(t[:].rearrange("p (a n) -> p a n", a=2), t[:].rearrange("p (a n) -> p a n", a=2), ALPHA,
                psum[:, mixbank:mixbank + 2, :], ALU.mult, ALU.add, [tk], [tk, PT_(mixbank), PT_(mixbank + 1)])
        for a in range(2):
            S.add("dve", lambda e, a=a: e.bn_stats(out=st[:, a, :], in_=t[:, a * 512:(a + 1) * 512]), reads=[tk],
                  writes=[("lnst", i % 4, a)])
        S.add("dve", lambda e: e.bn_aggr(out=mv[:, 0:2], in_=st[:].rearrange("p a s -> p (a s)")),
              reads=[("lnst", i % 4, 0), ("lnst", i % 4, 1)], writes=[mk])
        ACT(mv[:, 2:3], mv[:, 1:2], AF.Sqrt, [mk], [mk], bias=EPS)
        S.add("dve", lambda e: e.reciprocal(out=mv[:, 3:4], in_=mv[:, 2:3]), reads=[mk], writes=[mk])
        TS("dve", mv[:, 4:5], mv[:, 0:1], mv[:, 3:4], -1.0, ALU.mult, ALU.mult, [mk], [mk])

    def ln_b(s, i, final):
        lnrow = LN["row"]
        t = LN["t"][i % 4]
        tk = ("lnt", i % 4)
        mv = LN["mv"][i % 4]
        mk = ("lnmv", i % 4)
        ACT(t[:], t[:], AF.Identity, [tk, mk], [tk], bias=mv[:, 4:5], scale=mv[:, 3:4])
        TT("dve", t[:], t[:], lnrow[:, 0, :], ALU.mult, [tk, "lnrow"], [tk])
        TT("pool", t[:], t[:], lnrow[:, 1, :], ALU.add, [tk, "lnrow"], [tk])
        if final:
            DMA("sp", yout[s, i * 128:(i + 1) * 128, :], t[:], [tk], [("yout", i)], slot=("lnt_out", i % 4))
        else:
            DMA("sp", hres[i * 128:(i + 1) * 128, :], t[:], [tk], [("hres", i)], slot=("lnt_out", i % 4))
            g = i // 4
            for hb in range(2):
                b = nb()
                for kk in range(4):
                    k = hb * 4 + kk
                    TR(PS(b, kk * 128, (kk + 1) * 128), t[:, k * 128:(k + 1) * 128], identf[:], [tk, "identf"], [PT_(b)])
                dst = hT[:, hb * 4:(hb + 1) * 4, i * 128:(i + 1) * 128]
                src = PS(b).rearrange("p (a n) -> p a n", a=4)
                if hb == 0:
                    ACT(dst, src, AF.Identity, [], [PT_(b), ("hT", g)])
                else:
                    CP("dve", dst, src, [], [PT_(b), ("hT", g)])

    for s in range(nseq):
        with ExitStack() as e0:
            posi = sb(e0, "posi", [128, L], I32)
            ang = sb(e0, "ang", [128, L], F32)
            t1 = sb(e0, "t1", [128, L], F32)
            t2i = sb(e0, "t2i", [128, L], I32)
            t3 = sb(e0, "t3", [128, L], F32)
            rop = sb(e0, "rop", [128, 2, L], F32)
            DMA("sp", posi[:], posin[s], [], ["posi"])
            CP("dve", ang[:], posi[:], ["posi"], ["ang"])
            TS("dve", ang[:], ang[:], cst[:, 0:1], None, ALU.mult, None, ["ang", "cst"], ["ang"])
            C1 = 6.28125
            C2 = float(2.0 * np.pi - 6.28125)
            for which in range(2):
                if which == 0:
                    TS("dve", t1[:], ang[:], float(np.pi / 2), None, ALU.add, None, ["ang"], ["t1"])
                    src = t1
                    sk = "t1"
                else:
                    src = ang
                    sk = "ang"
                TS("dve", t3[:], src[:], float(1.0 / (2 * np.pi)), None, ALU.mult, None, [sk], ["t3"])
                CP("dve", t2i[:], t3[:], ["t3"], ["t2i"])
                CP("dve", t3[:], t2i[:], ["t2i"], ["t3"])
                STT(t1[:], t3[:], -C1, src[:], ALU.mult, ALU.add, ["t3", sk], ["t1"])
                STT(t1[:], t3[:], -C2, t1[:], ALU.mult, ALU.add, ["t3", "t1"], ["t1"])
                TS("dve", t3[:], t1[:], float(np.pi), float(-2 * np.pi), ALU.is_gt, ALU.mult, ["t1"], ["t3"])
                TT("dve", t1[:], t1[:], t3[:], ALU.add, ["t1", "t3"], ["t1"])
                TS("dve", t3[:], t1[:], float(-np.pi), float(2 * np.pi), ALU.is_lt, ALU.mult, ["t1"], ["t3"])
                TT("dve", t1[:], t1[:], t3[:], ALU.add, ["t1", "t3"], ["t1"])
                TS("dve", t1[:], t1[:], 3.1415925, -3.1415925, ALU.min, ALU.max, ["t1"], ["t1"])
                if which == 0:
                    ACT(rop[:, 0, :], t1[:], AF.Sin, ["t1"], [("rop", 0)])
                else:
                    ACT(rop[:, 1, :], t1[:], AF.Sin, ["t1", "cst"], [("rop", 1)], scale=cst[:, 1:2])
            DMA("sp", ropd, rop[:], [("rop", 0), ("rop", 1)], ["ropd"])
            if s == 0 and "rop" in dbg:
                DMA("sp", dbg_tensor("rop", [128, 2, L], F32), rop[:], [("rop", 0), ("rop", 1)], ["dbg_rop"])
            S.barrier()
        e0b = ExitStack()
        ln_alloc(e0b)
        if first:
            ln_rows_load(lnin)
            ln_load(s, 0, xin[s, 0:128, :])
            for i in range(NT):
                if i + 1 < NT:
                    ln_load(s, i + 1, xin[s, (i + 1) * 128:(i + 2) * 128, :])
                if i >= 2:
                    ln_b(s, i - 2, False)
                ln_a(s, i, None)
            ln_b(s, NT - 2, False)
            ln_b(s, NT - 1, False)
        else:
            for i in range(NT):
                t = LN["t"][i % 2]
                tk = ("lnt", i % 2)
                DMA("sp", t[:], xin[s, i * 128:(i + 1) * 128, :], [], [tk], slot=("lnt_in", i % 2))
                DMA("sp", hres[i * 128:(i + 1) * 128, :], t[:], [tk], [("hres", i)], slot=("lnt_out", i % 2))
                for hb in range(2):
                    b = nb()
                    for kk in range(4):
                        k = hb * 4 + kk
                        TR(PS(b, kk * 128, (kk + 1) * 128), t[:, k * 128:(k + 1) * 128], identf[:], [tk, "identf"], [PT_(b)])
                    CP("dve", hT[:, hb * 4:(hb + 1) * 4, i * 128:(i + 1) * 128], PS(b).rearrange("p (a n) -> p a n", a=4),
                       [], [PT_(b), ("hT", i // 4)])
        if s == 0 and "hT0" in dbg:
            DMA("sp", dbg_tensor("hT0", [128, 8, L], BF16), hT[:], [("hT", g) for g in range(4)], ["dbg_hT0"])
        S.barrier()
        e0b.close()

        for l in layers:
            par = parb[l]
            pk = ("par", l)
            last_layer = (l == layers[-1])
            qT = arena[:, 0:4 * L].rearrange("p (c t) -> p c t", c=4)
            qiT = arena[:, 4 * L:8 * L].rearrange("p (c t) -> p c t", c=4)
            merged = arena[:, 0:8 * L].rearrange("p (c t) -> p c t", c=8)
            actb = arena[:, 0:NJ * 1024].rearrange("p (c t) -> p c t", c=NJ)
            with ExitStack() as eA:
                OT = sb(eA, "OT", [128, 4, L], BF16)
                with ExitStack() as eB:
                    kT = sb(eB, "kT", [128, L], BF16)
                    vtm = sb(eB, "vtm", [128, NT, 128], BF16)
                    kiT = sb(eB, "kiT", [128, L], BF16)
                    witm = sb(eB, "witm", [128, NT * 8], F32)
                    with ExitStack() as e2:
                        rop = sb(e2, "rop2", [128, 2, L], F32)
                        vT = sb(e2, "vT", [128, L], F32)
                        ta = [sb(e2, "ta%d" % i, [128, 512], F32) for i in range(2)]
                        tb = [sb(e2, "tb%d" % i, [128, 512], F32) for i in range(2)]
                        wwib = sb(e2, "wwib", [128, 8, 8], BF16)
                        bwib = sb(e2, "bwib", [128, 128], F32)
                        DMA("sp", rop[:], ropd, ["ropd"], ["rop"])
                        DMA("pool", wwib[:], W[("wwi", l)], [], ["wwib"])
                        DMA("sp", bwib[:], W[("bwi", l)], [], ["bwib"])
                        pairs = []
                        for j in range(4):
                            pairs.append((CH_Q[j], CH_QS[j], ("q", j)))
                        pairs.append((CH_K, CH_KS, ("k", 0)))
                        for j in range(4):
                            pairs.append((CH_QI[j], CH_QIS[j], ("qi", j)))
                        pairs.append((CH_KI, CH_KIS, ("ki", 0)))
                        chunks = []
                        for a_, b_, _ in pairs:
                            chunks += [a_, b_]
                        chunks.append(CH_V)
                        stm = Stream(l, chunks)
                        cnt2 = 0
                        for pi, (ca, cs, (kind, j)) in enumerate(pairs):
                            sa = stm.get(2 * pi)
                            ss = stm.get(2 * pi + 1)
                            for g in range(NG):
                                ba = nb()
                                bs_ = nb()
                                proj_chunk(sa, g, ba)
                                proj_chunk(ss, g, bs_)
                                A_ = ta[cnt2 % 2]
                                B_ = tb[cnt2 % 2]
                                ak = ("ta", cnt2 % 2)
                                bk = ("tb", cnt2 % 2)
                                cnt2 += 1
                                gs = slice(g * 512, (g + 1) * 512)
                                STT(A_[:], PS(ba), par[:, PC_BA + ca:PC_BA + ca + 1], rop[:, 0, gs], ALU.add, ALU.mult,
                                    [pk, "rop"], [ak, PT_(ba)])
                                ACT(B_[:], PS(bs_), AF.Identity, [pk], [bk, PT_(bs_)], bias=par[:, PC_BA + cs:PC_BA + cs + 1])
                                TT("pool", B_[:], B_[:], rop[:, 1, gs], ALU.mult, [bk, "rop"], [bk])
                                if kind == "q":
                                    dst = qT[:, j, gs]
                                elif kind == "k":
                                    dst = kT[:, gs]
                                elif kind == "qi":
                                    dst = qiT[:, j, gs]
                                else:
                                    dst = kiT[:, gs]
                                TT("dve", dst, A_[:], B_[:], ALU.add, [ak, bk], [(kind, j, g)])
                        sv = stm.get(len(chunks) - 1)
                        for g in range(NG):
                            b = nb()
                            proj_chunk(sv, g, b)
                            ACT(vT[:, g * 512:(g + 1) * 512], PS(b), AF.Identity, [pk], [("vT", g), PT_(b)],
                                bias=par[:, PC_BA + CH_V:PC_BA + CH_V + 1])
                        for g in range(NG):
                            b = nb()
                            for kk in range(4):
                                i = g * 4 + kk
                                TR(PS(b, kk * 128, (kk + 1) * 128), vT[:, i * 128:(i + 1) * 128], identf[:],
                                   [("vT", g), "identf"], [PT_(b)])
                            CP("dve", vtm[:, g * 4:(g + 1) * 4, :], PS(b).rearrange("p (a n) -> p a n", a=4), [],
                               [PT_(b), ("vtm", g)])
                        b = nb()
                        for i in range(NT):
                            for k in range(8):
                                PE(PS(b, i * 8, (i + 1) * 8), hT[:, k, i * 128:(i + 1) * 128], wwib[:, k, :], k == 0, k == 7,
                                   ["wwib", ("hT", i // 4)], [PT_(b)])
                        TT("dve", witm[:], PS(b, 0, 128), bwib[:], ALU.add, ["bwib"], [PT_(b), "witm"])
                        if s == 0 and l == layers[0]:
                            if "qT" in dbg:
                                DMA("sp", dbg_tensor("qT", [128, 4, L], BF16), qT, [("q", j, g) for j in range(4) for g in range(4)], ["dbg_qT"])
                            if "kT" in dbg:
                                DMA("sp", dbg_tensor("kT", [128, L], BF16), kT[:], [("k", 0, g) for g in range(4)], ["dbg_kT"])
                            if "kiT" in dbg:
                                DMA("sp", dbg_tensor("kiT", [128, L], BF16), kiT[:], [("ki", 0, g) for g in range(4)], ["dbg_kiT"])
                            if "qiT" in dbg:
                                DMA("sp", dbg_tensor("qiT", [128, 4, L], BF16), qiT, [("qi", j, g) for j in range(4) for g in range(4)], ["dbg_qiT"])
                            if "vtm" in dbg:
                                DMA("sp", dbg_tensor("vtm", [128, NT, 128], BF16), vtm[:], [("vtm", g) for g in range(4)], ["dbg_vtm"])
                            if "witm" in dbg:
                                DMA("sp", dbg_tensor("witm", [128, 128], F32), witm[:], ["witm"], ["dbg_witm"])
                        S.barrier()

                    with ExitStack() as e3:
                        SCW = 58 * 128
                        SC = sb(e3, "SC", [128, SCW], F32)
                        selT = sb(e3, "selT", [128, NT, 512], BF16)
                        selq = sb(e3, "selq", [128, L], F32)
                        junkD = sb(e3, "junkD", [128, L], BF16)
                        junkA = sb(e3, "junkA", [128, L], BF16)
                        rt = [sb(e3, "rt%d" % i, [128, 2, 512], BF16) for i in range(4)]
                        Dg = [sb(e3, "Dg%d" % i, [128, 8, 128], BF16) for i in range(2)]
                        PTb = [sb(e3, "PT%d" % i, [128, 2, 512], BF16) for i in range(4)]
                        rec = [sb(e3, "rec%d" % i, [128, 512], F32) for i in range(2)]
                        amax = sb(e3, "amax", [128, 4], F32)
                        lo = sb(e3, "lo", [128, 4], F32)
                        mid = sb(e3, "mid", [128, 4], F32)
                        nmid = sb(e3, "nmid", [128, 4], F32)
                        cnt = sb(e3, "cnt", [128, 4], F32)
                        thr = sb(e3, "thr", [128, 4], F32)
                        dd = sb(e3, "dd", [128, 4], F32)
                        tt_ = sb(e3, "tt_", [128, 4], F32)
                        Wt = sb(e3, "Wt", [128, NIT, 4], F32)
                        rt_ctr = [0]
                        pt_ctr = [0]
                        dp_ctr = [0]

                        if s == 0 and l == layers[0] and "tau" in dbg:
                            dtau = dbg_tensor("tau", [128, NT], F32)
                            dsel = dbg_tensor("selT", [4, 128, NT, 512], BF16)
                        else:
                            dtau = None

                        def blk_off(g, r):
                            return sum((4 * g + rr + 1) * 128 for rr in range(r))

                        sb_ctr = [0]

                        def dpair():
                            b = (dp_ctr[0] % 3) * 2
                            dp_ctr[0] += 1
                            return b

                        def idx_steps(g):
                            steps = []
                            for r in range(4):
                                i = 4 * g + r
                                off = blk_off(g, r)
                                wdt = (i + 1) * 128

                                def mk_dg(i=i):
                                    D_ = Dg[i % 2]
                                    for h in range(8):
                                        TS("dve", D_[:, h, :], identb[:], witm[:, i * 8 + h:i * 8 + h + 1], None, ALU.mult, None,
                                           ["identb", "witm"], [("Dg", i % 2)])
                                steps.append((mk_dg, None))
                                for kc in range((wdt + 511) // 512):
                                    ncols = min(512, wdt - kc * 512)
                                    sbank = [None]
                                    for jh in range(4):
                                        st = {}

                                        def stepA(i=i, kc=kc, ncols=ncols, jh=jh, st=st):
                                            bp = dpair()
                                            st["R"] = rt[rt_ctr[0] % 4]
                                            st["rk"] = ("rt", rt_ctr[0] % 4)
                                            rt_ctr[0] += 1
                                            for hh in range(2):
                                                base = hh * 64
                                                PE(PS(bp + hh, 0, ncols), qiT[base:base + 64, jh, i * 128:(i + 1) * 128],
                                                   kiT[base:base + 64, kc * 512:kc * 512 + ncols], True, True,
                                                   [("qi", jh, i // 4), ("ki", 0, kc)], [PT_(bp + hh)])
                                            ACT(st["R"][:, :, 0:ncols], psum[:, bp:bp + 2, 0:ncols], AF.Relu, [],
                                                [st["rk"], PT_(bp), PT_(bp + 1)])

                                        def stepB(i=i, off=off, kc=kc, ncols=ncols, jh=jh, r=r, sbank=sbank, st=st):
                                            if jh == 0:
                                                sbank[0] = 6 + (sb_ctr[0] % 2)
                                                sb_ctr[0] += 1
                                            bs_ = sbank[0]
                                            for hh in range(2):
                                                h = 2 * jh + hh
                                                PE(PS(bs_, 0, ncols), Dg[i % 2][:, h, :], st["R"][:, hh, 0:ncols], h == 0, h == 7,
                                                   [st["rk"], ("Dg", i % 2)], [PT_(bs_)])
                                            if jh == 3:
                                                CP("dve", SC[:, off + kc * 512:off + kc * 512 + ncols], PS(bs_, 0, ncols), [],
                                                   [PT_(bs_), ("SC", r, kc)])
                                        steps.append((stepA, stepB))

                                def fin(i=i, off=off, wdt=wdt, r=r):
                                    sks = [("SC", r, kc) for kc in range((wdt + 511) // 512)]
                                    S.add("dve", lambda e: e.tensor_reduce(out=amax[:, r:r + 1], in_=SC[:, off:off + wdt], axis=AX.X,
                                                                          op=ALU.max, apply_absolute_value=True),
                                          reads=sks, writes=[("amax", r)])
                                    dk = ("SC", r, i // 4)
                                    TT("pool", SC[:, off + i * 128:off + (i + 1) * 128], SC[:, off + i * 128:off + (i + 1) * 128],
                                       cmask[:], ALU.add, [dk, "cmask", ("amax", r)], [dk])
                                steps.append((None, fin))
                            return steps

                        def bis_steps(g):
                            steps = []
                            blocks = [r for r in range(4) if 4 * g + r >= 2]

                            def init():
                                aks = [("amax", r) for r in range(4)]
                                MS("dve", lo[:], -1.0e29, ["lo"])
                                MS("dve", thr[:], TOPK - 0.5, ["thr"])
                                MS("dve", cnt[:], 0.0, [("cnt", r) for r in range(4)])
                                for r in blocks:
                                    wdt = (4 * g + r + 1) * 128
                                    TS("dve", lo[:, r:r + 1], amax[:, r:r + 1], -1.0, None, ALU.mult, None, aks, ["lo"])
                                    if r % 2 == 1:
                                        MS("dve", thr[:, r:r + 1], float(2 * TOPK - 1 - wdt), ["thr"])
                                TS("dve", Wt[:, 0, :], amax[:], 1.0000005, 1.0e-30, ALU.mult, ALU.add, aks, ["Wt"])
                                for it in range(1, NIT):
                                    TS("dve", Wt[:, it, :], Wt[:, 0, :], float(2.0 ** (-it)), None, ALU.mult, None, ["Wt"], ["Wt"])
                                for r in range(4):
                                    if r not in blocks:
                                        MS("dve", Wt[:, :, r:r + 1], 0.0, ["Wt"])
                            steps.append(init)
                            if not blocks:
                                return steps
                            for it in range(NIT):
                                def step(it=it):
                                    TT("pool", mid[:], lo[:], Wt[:, it, :], ALU.add, ["lo", "Wt"], ["mid"])
                                    TS("pool", nmid[:], mid[:], -1.0, 0.0, ALU.mult, ALU.add, ["mid"], ["nmid"])
                                    for r in blocks:
                                        i = 4 * g + r
                                        off = blk_off(g, r)
                                        wdt = (i + 1) * 128
                                        sks = [("SC", r, kc) for kc in range((wdt + 511) // 512)]
                                        if r % 2 == 0:
                                            TS("dve", junkD[:, 0:wdt], SC[:, off:off + wdt], mid[:, r:r + 1], None, ALU.is_ge, ALU.add,
                                               sks + ["mid"], ["junkD", ("cnt", r)], accum=cnt[:, r:r + 1])
                                        else:
                                            ACT(junkA[:, 0:wdt], SC[:, off:off + wdt], AF.Sign, sks + ["nmid"], ["junkA", ("cnt", r)],
                                                bias=nmid[:, r:r + 1], accum=cnt[:, r:r + 1])
                                    cks = [("cnt", r) for r in range(4)]
                                    TT("pool", dd[:], cnt[:], thr[:], ALU.subtract, cks + ["thr"], ["dd"])
                                    TS("pool", dd[:], dd[:], 0.0, None, ALU.is_ge, None, ["dd"], ["dd"])
                                    TT("pool", tt_[:], dd[:], Wt[:, it, :], ALU.mult, ["dd", "Wt"], ["tt_"])
                                    TT("pool", lo[:], lo[:], tt_[:], ALU.add, ["lo", "tt_"], ["lo"])
                                steps.append(step)
                            return steps

                        def sel_steps(g):
                            steps = []
                            for r in range(4):
                                def step(r=r):
                                    i = 4 * g + r
                                    off = blk_off(g, r)
                                    wdt = (i + 1) * 128
                                    sks = [("SC", r, kc) for kc in range((wdt + 511) // 512)]
                                    TS("dve", selq[:, 0:wdt], SC[:, off:off + wdt], lo[:, r:r + 1], None, ALU.is_ge, None,
                                       sks + ["lo"], ["selq"])
                                    for m in range((i + 4) // 4):
                                        nk = min(4, i + 1 - 4 * m)
                                        b = nb()
                                        for kk in range(nk):
                                            kt = 4 * m + kk
                                            TR(PS(b, kk * 128, (kk + 1) * 128), selq[:, kt * 128:(kt + 1) * 128], identf[:],
                                               ["selq", "identf"], [PT_(b)])
                                        dst = selT[:, 4 * m:4 * m + nk, r * 128:(r + 1) * 128]
                                        src = PS(b, 0, nk * 128).rearrange("p (a n) -> p a n", a=nk)
                                        wk = [("selT", 4 * m + kk) for kk in range(nk)]
                                        if m % 2 == 0:
                                            ACT(dst, src, AF.Identity, [], [PT_(b)] + wk)
                                        else:
                                            CP("dve", dst, src, [], [PT_(b)] + wk)
                                steps.append(step)
                            if dtau is not None:
                                def dump():
                                    DMA("sp", dtau[:, 4 * g:4 * g + 4], lo[:], ["lo"], [("dtau", g)])
                                    DMA("sp", dsel[g], selT[:], [("selT", kt) for kt in range(NT)], [("dsel", g)])
                                steps.append(dump)
                            return steps

                        def attn_steps(g):
                            steps = []
                            nkt = 4 * g + 4
                            for j in range(4):
                                for kt in range(nkt):
                                    st = {}

                                    def stepA(j=j, kt=kt, st=st):
                                        q0 = max(0, kt - 4 * g) * 128
                                        N = 512 - q0
                                        bp = dpair()
                                        P_ = PTb[pt_ctr[0] % 4]
                                        ptk = ("PT", pt_ctr[0] % 4)
                                        pt_ctr[0] += 1
                                        st["P"] = P_
                                        st["ptk"] = ptk
                                        for hh in range(2):
                                            base = hh * 64
                                            PE(PS(bp + hh, 0, N), kT[base:base + 64, kt * 128:(kt + 1) * 128],
                                               qT[base:base + 64, j, g * 512 + q0:(g + 1) * 512], True, True,
                                               [("k", 0, kt // 4), ("q", j, g)], [PT_(bp + hh)])
                                        ACT(P_[:, :, 0:N], psum[:, bp:bp + 2, 0:N], AF.Exp, [], [ptk, PT_(bp), PT_(bp + 1)], scale=0.125)
                                        for hh in range(2):
                                            TT("dve", P_[:, hh, 0:N], P_[:, hh, 0:N], selT[:, kt, q0:512], ALU.mult,
                                               [ptk, ("selT", kt)], [ptk])

                                    def stepB(j=j, kt=kt, st=st):
                                        q0 = max(0, kt - 4 * g) * 128
                                        N = 512 - q0
                                        po = 6
                                        pd = 7
                                        P_ = st["P"]
                                        ptk = st["ptk"]
                                        for hh in range(2):
                                            base = hh * 64
                                            PE(psum[base:base + 64, po, q0:512], vtm[:, kt, base:base + 64], P_[:, hh, 0:N],
                                               kt == 0, kt == nkt - 1, [ptk, ("vtm", kt // 4)], [PT_(po)])
                                            PE(psum[base:base + 64, pd, q0:512], onesb[:, 0:64], P_[:, hh, 0:N],
                                               kt == 0, kt == nkt - 1, [ptk, "onesb"], [PT_(pd)])
                                        if kt == nkt - 1:
                                            R = rec[j % 2]
                                            rk = ("rec", j % 2)
                                            S.add("dve", lambda e: e.reciprocal(out=R[:], in_=PS(pd)), reads=[], writes=[rk, PT_(pd)])
                                            TT("dve", OT[:, j, g * 512:(g + 1) * 512], PS(po), R[:], ALU.mult, [rk], [PT_(po), ("OT", j, g)])
                                    steps.append((stepA, stepB))
                            return steps

                        def pipe(pairs, lag):
                            out = []
                            n = len(pairs)
                            for t_ in range(n + lag):
                                if t_ < n and pairs[t_][0] is not None:
                                    out.append(pairs[t_][0])
                                if t_ - lag >= 0 and pairs[t_ - lag][1] is not None:
                                    out.append(pairs[t_ - lag][1])
                            return out

                        def run(steps):
                            for st_ in steps:
                                st_()

                        def interleave(a, b):
                            na, nb_ = len(a), len(b)
                            ia = ib = 0
                            while ia < na or ib < nb_:
                                if ib >= nb_ or (ia < na and ia * nb_ <= ib * na):
                                    a[ia]()
                                    ia += 1
                                else:
                                    b[ib]()
                                    ib += 1

                        run(pipe(idx_steps(0), 1))
                        run(bis_steps(0))
                        run(sel_steps(0))
                        for g in range(1, NG):
                            run(pipe(idx_steps(g), 1))
                            interleave(bis_steps(g), pipe(attn_steps(g - 1), 2))
                            run(sel_steps(g))
                        run(pipe(attn_steps(NG - 1), 2))
                        if s == 0 and l == layers[0] and "OT" in dbg:
                            DMA("sp", dbg_tensor("OT", [128, 4, L], BF16), OT[:], [("OT", j, g) for j in range(4) for g in range(4)], ["dbg_OT"])
                        S.barrier()
                S.barrier()
                with ExitStack() as eC:
                    ypool = sb(eC, "ypool", [128, 2, L], BF16)
                    yconv = sb(eC, "yconv", [128, 2, L], BF16)
                    outw = sb(eC, "outw", [128, 8, 1024], BF16)
                    DMA("pool", outw[:], W[("outw", l)], [], ["outw"])
                    with ExitStack() as e1:
                        up = sb(e1, "up", [128, 2, 16 + L], F32)
                        sA = sb(e1, "sA", [128, 16 + L], F32)
                        sB = sb(e1, "sB", [128, 16 + L], F32)
                        mixed = sb(e1, "mixed", [128, 2, L], BF16)
                        t16 = sb(e1, "t16", [128, 16], F32)
                        pwb = sb(e1, "pwb", [128, 2, 128], BF16)
                        DMA("pool", pwb[:], W[("pwbd", l)], [], ["pwb"])
                        MS("pool", up[:, :, 0:16], 0.0, [("up", 0, -1), ("up", 1, -1)])
                        MS("pool", sA[:, 0:16], 0.0, ["sA"])
                        MS("pool", sB[:, 0:16], 0.0, ["sB"])
                        stm = Stream(l, CH_POOL)
                        for c in range(2):
                            sl = stm.get(c)
                            for g in range(NG):
                                b = nb()
                                proj_chunk(sl, g, b)
                                ACT(up[:, c, 16 + g * 512:16 + (g + 1) * 512], PS(b), AF.Identity, [pk], [PT_(b), ("up", c, g)],
                                    bias=par[:, PC_BA + c:PC_BA + c + 1])
                        for c in range(2):
                            uk = [("up", c, g) for g in range(-1, 4)]
                            U = up[:, c, :]
                            TT("dve", sA[:, 16:], U[:, 16:], U[:, 15:15 + L], ALU.add, uk, ["sA"])
                            TT("dve", sB[:, 16:], sA[:, 16:], sA[:, 14:14 + L], ALU.add, ["sA"], ["sB"])
                            if c == 1:
                                TT("dve", sA[:, 16:], sB[:, 16:], sB[:, 12:12 + L], ALU.add, ["sB"], ["sA"])
                                TT("dve", sB[:, 16:], sA[:, 16:], sA[:, 8:8 + L], ALU.add, ["sA"], ["sB"])
                            for half, (sbuf_, sk) in enumerate(((sA, "sA"), (sB, "sB"))):
                                widx = 2 * c + half
                                win = float(2 ** (widx + 1))
                                pr = slice(half * 64, half * 64 + 64)
                                STT(mixed[pr, c, 16:], sbuf_[pr, 32:], 1.0 / win, U[pr, 32:], ALU.mult, ALU.subtract, [sk] + uk,
                                    [("mixed", c, half)])
                                TT("dve", t16[pr, :], sbuf_[pr, 16:32], rct[pr, widx, :], ALU.mult, [sk, ("rct", widx)], ["t16"])
                                TT("dve", mixed[pr, c, 0:16], t16[pr, :], U[pr, 16:32], ALU.subtract, ["t16"] + uk, [("mixed", c, half)])
                        for c in range(2):
                            for g in range(NG):
                                b = nb()
                                PE(PS(b), pwb[:, c, :], mixed[:, c, g * 512:(g + 1) * 512], True, True,
                                   ["pwb", ("mixed", c, 0), ("mixed", c, 1)], [PT_(b)])
                                ACT(ypool[:, c, g * 512:(g + 1) * 512], PS(b), AF.Identity, [pk], [PT_(b), ("ypool", c, g)],
                                    scale=par[:, PC_PSC + c:PC_PSC + c + 1])
                        S.barrier()
                    with ExitStack() as e1:
                        glu = sb(e1, "glu", [128, 2, 30 + L], BF16)
                        dg = sb(e1, "dg", [128, 2, 31, 128], BF16)
                        xcv = sb(e1, "xcv", [128, 2, L], F32)
                        xsq = [sb(e1, "xsq%d" % i, [128, 512], F32) for i in range(2)]
                        sgt = [sb(e1, "sgt%d" % i, [128, 512], F32) for i in range(2)]
                        mean_t = sb(e1, "mean_t", [128, 512], F32)
                        var_t = sb(e1, "var_t", [128, 512], F32)
                        dtmp = [sb(e1, "dtmp%d" % i, [128, 512], F32) for i in range(2)]
                        MS("pool", glu[:, :, 0:30], 0.0, [("glu", 0, -1), ("glu", 1, -1)])
                        for c in range(2):
                            for jj in range(31):
                                TS("dve", dg[:, c, jj, :], identb[:], par[:, PC_CDW + c * 31 + jj:PC_CDW + c * 31 + jj + 1], None,
                                   ALU.mult, None, ["identb", pk], [("dg", c)])
                        stm = Stream(l, [CH_CA[0], CH_CG[0], CH_CA[1], CH_CG[1]])
                        cc = 0
                        for c in range(2):
                            sa_ = stm.get(2 * c)
                            sg_ = stm.get(2 * c + 1)
                            for g in range(NG):
                                ba = nb()
                                bg = nb()
                                proj_chunk(sa_, g, ba)
                                proj_chunk(sg_, g, bg)
                                T_ = sgt[cc % 2]
                                tk_ = ("sgt", cc % 2)
                                cc += 1
                                ACT(T_[:], PS(bg), AF.Sigmoid, [pk], [tk_, PT_(bg)], bias=par[:, PC_BA + CH_CG[c]:PC_BA + CH_CG[c] + 1])
                                STT(glu[:, c, 30 + g * 512:30 + (g + 1) * 512], PS(ba), par[:, PC_BA + CH_CA[c]:PC_BA + CH_CA[c] + 1],
                                    T_[:], ALU.add, ALU.mult, [pk, tk_], [PT_(ba), ("glu", c, g)])
                        for g in range(NG):
                            bm = nb()
                            bq = nb()
                            for c in range(2):
                                b = nb()
                                gk = [("glu", c, gg) for gg in range(-1, 4)]
                                for jj in range(31):
                                    PE(PS(b), dg[:, c, jj, :], glu[:, c, g * 512 + jj:g * 512 + jj + 512], jj == 0, jj == 30,
                                       [("dg", c)] + gk, [PT_(b)])
                                X2 = xsq[c]
                                ACT(xcv[:, c, g * 512:(g + 1) * 512], PS(b), AF.Identity, [pk], [PT_(b), ("xcv", c, g)],
                                    bias=par[:, PC_CDB + c:PC_CDB + c + 1])
                                ACT(X2[:], PS(b), AF.Square, [pk], [PT_(b), ("xsq", c)], bias=par[:, PC_CDB + c:PC_CDB + c + 1])
                            for c in range(2):
                                PE(PS(bm), onesf[:], xcv[:, c, g * 512:(g + 1) * 512], c == 0, c == 1, ["onesf", ("xcv", c, g)], [PT_(bm)])
                            for c in range(2):
                                PE(PS(bq), onesf[:], xsq[c][:], c == 0, c == 1, ["onesf", ("xsq", c)], [PT_(bq)])
                            TS("dve", mean_t[:], PS(bm), 1.0 / 256, None, ALU.mult, None, [], ["mean_t", PT_(bm)])
                            TT("dve", var_t[:], mean_t[:], mean_t[:], ALU.mult, ["mean_t"], ["var_t"])
                            STT(var_t[:], PS(bq), 1.0 / 256, var_t[:], ALU.mult, ALU.subtract, ["var_t"], ["var_t", PT_(bq)])
                            ACT(var_t[:], var_t[:], AF.Sqrt, ["var_t"], ["var_t"], bias=EPS)
                            S.add("dve", lambda e: e.reciprocal(out=var_t[:], in_=var_t[:]), reads=["var_t"], writes=["var_t"])
                            for c in range(2):
                                Dm = dtmp[c]
                                dk_ = ("dtmp", c)
                                TT("dve", Dm[:], xcv[:, c, g * 512:(g + 1) * 512], mean_t[:], ALU.subtract, [("xcv", c, g), "mean_t"], [dk_])
                                TT("pool", Dm[:], Dm[:], var_t[:], ALU.mult, [dk_, "var_t"], [dk_])
                                ACT(yconv[:, c, g * 512:(g + 1) * 512], Dm[:], AF.Silu, [dk_, pk], [("yconv", c, g)],
                                    bias=par[:, PC_CLB + c:PC_CLB + c + 1], scale=par[:, PC_CLG + c:PC_CLG + c + 1])
                        S.barrier()
                    if s == 0 and l == layers[0]:
                        if "ypool" in dbg:
                            DMA("sp", dbg_tensor("ypool", [128, 2, L], BF16), ypool[:], [("ypool", c, g) for c in range(2) for g in range(4)], ["dbg_ypool"])
                        if "yconv" in dbg:
                            DMA("sp", dbg_tensor("yconv", [128, 2, L], BF16), yconv[:], [("yconv", c, g) for c in range(2) for g in range(4)], ["dbg_yconv"])
                    with ExitStack() as e4:
                        wo = sb(e4, "wo", [128, 8, 1024], BF16)
                        DMA("pool", wo[:], W[("wo", l)], [], ["wo"])
                        sg3 = [sb(e4, "sg3_%d" % i, [128, 512], F32) for i in range(3)]
                        mm_ = [sb(e4, "mm_%d" % i, [128, 512], F32) for i in range(3)]
                        chunks = []
                        for c in range(8):
                            chunks += [CH_GATE0 + i * 8 + c for i in range(3)]
                        stm = Stream(l, chunks, ahead=1)
                        ysrc = [(ypool, "ypool", 0, 2), (yconv, "yconv", 2, 2), (OT, "OT", 4, 4)]
                        for c in range(8):
                            sl3 = [stm.get(3 * c + i) for i in range(3)]
                            for g in range(NG):
                                gs = slice(g * 512, (g + 1) * 512)
                                for i in range(3):
                                    bg = nb()
                                    proj_chunk(sl3[i], g, bg)
                                    ch = CH_GATE0 + i * 8 + c
                                    ACT(sg3[i][:], PS(bg), AF.Sigmoid, [pk], [("sg3", i), PT_(bg)], bias=par[:, PC_BA + ch:PC_BA + ch + 1])
                                for i, (ysb, yn, k0, nk) in enumerate(ysrc):
                                    by = nb()
                                    for k in range(nk):
                                        PE(PS(by), outw[:, k0 + k, c * 128:(c + 1) * 128], ysb[:, k, gs], k == 0, k == nk - 1,
                                           ["outw", (yn, k, g)], [PT_(by)])
                                    TT("dve", mm_[i][:], PS(by), sg3[i][:], ALU.mult, [("sg3", i)], [("mm_", i), PT_(by)])
                                TT("pool", mm_[0][:], mm_[0][:], mm_[1][:], ALU.add, [("mm_", 0), ("mm_", 1)], [("mm_", 0)])
                                TT("pool", merged[:, c, gs], mm_[0][:], mm_[2][:], ALU.add, [("mm_", 0), ("mm_", 2)], [("merged", c, g)])
                        with ExitStack() as e5:
                            ln_alloc(e5)
                            ln_rows_load(W[("lnr", l)][:, 0:2048])
                            ln_load(s, 0)
                            for i in range(NT):
                                if i + 1 < NT:
                                    ln_load(s, i + 1)
                                if i >= 2:
                                    ln_b(s, i - 2, False)
                                bp = npair()
                                for n in range(2):
                                    for k in range(8):
                                        PE(PS(bp + n), merged[:, k, i * 128:(i + 1) * 128], wo[:, k, n * 512:(n + 1) * 512], k == 0, k == 7,
                                           ["wo", ("merged", k, i // 4)], [PT_(bp + n)])
                                ln_a(s, i, bp)
                            ln_b(s, NT - 2, False)
                            ln_b(s, NT - 1, False)
                            S.barrier()
                S.barrier()
            if s == 0 and l == layers[0] and "merged" in dbg:
                DMA("sp", dbg_tensor("merged", [128, 8, L], BF16), merged, [("merged", c, g) for c in range(8) for g in range(4)], ["dbg_merged"])
            if s == 0 and l == layers[0] and "hT1" in dbg:
                DMA("sp", dbg_tensor("hT1", [128, 8, L], BF16), hT[:], [("hT", g) for g in range(4)], ["dbg_hT1"])
            with ExitStack() as e6:
                wd = sb(e6, "wd", [128, NJ, 1024], BF16)
                ln_alloc(e6)
                NWU = 3
                wUb = [sb(e6, "wU%d" % i, [128, 8, 256], BF16) for i in range(NWU)]
                xg = [sb(e6, "xg%d" % i, [128, 2 + 1024], F32) for i in range(2)]
                xv = [sb(e6, "xv%d" % i, [128, 2 + 1024], F32) for i in range(2)]
                ug = [sb(e6, "ug%d" % i, [128, 1024], F32) for i in range(2)]
                uv = [sb(e6, "uv%d" % i, [128, 1024], F32) for i in range(2)]
                halo = sb(e6, "halo", [128, 2 * NJ, 2], F32)
                ln_rows_load(W[("lnr", l)][:, 2048:4096])
                wu_ctr = [0]
                wu_slots = {}

                def wu_get(idx):
                    while wu_ctr[0] <= min(idx + 1, 2 * NJ - 1):
                        n_ = wu_ctr[0]
                        sl = n_ % NWU
                        DMA("pool", wUb[sl][:], W[("wU", l)][n_ % NJ], [], [("wU", sl)], slot=("wU", sl))
                        wu_slots[n_] = sl
                        wu_ctr[0] += 1
                    return wu_slots[idx]

                wu_get(0)
                wd_parts = [(0, 6), (6, 11), (11, 17), (17, 22)]
                for kq, (k0_, k1_) in enumerate(wd_parts):
                    DMA("pool", wd[:, k0_:k1_, :], W[("wd", l)][:, k0_:k1_, :], [], [("wd", kq)])
                for hf in range(2):
                    t0 = hf * 1024
                    for j in range(NJ):
                        sl = wu_get(hf * NJ + j)
                        p_ = j % 2
                        XG, XV, UG, UV = xg[p_], xv[p_], ug[p_], uv[p_]
                        for (X, xk, coff, hidx) in ((XG, ("xg", p_), 0, j), (XV, ("xv", p_), 128, NJ + j)):
                            if hf == 0:
                                CP("pool", X[:, 0:2], zero2[:], ["zero2"], [xk])
                            else:
                                CP("pool", X[:, 0:2], halo[:, hidx, :], [("halo", hidx)], [xk])
                            for gg in range(2):
                                g = hf * 2 + gg
                                b = nb()
                                for k in range(8):
                                    PE(PS(b), wUb[sl][:, k, coff:coff + 128], hT[:, k, g * 512:(g + 1) * 512], k == 0, k == 7,
                                       [("wU", sl), ("hT", g)], [PT_(b)])
                                ACT(X[:, 2 + gg * 512:2 + (gg + 1) * 512], PS(b), AF.Identity, [], [PT_(b), xk])
                            if hf == 0:
                                CP("pool", halo[:, hidx, :], X[:, 1024:1026], [xk], [("halo", hidx)])
                        for (X, xk, U, uk, wc, bc) in ((XG, ("xg", p_), UG, ("ug", p_), PC_FWG + 3 * j, PC_FBG + j),
                                                      (XV, ("xv", p_), UV, ("uv", p_), PC_FWV + 3 * j, PC_FBV + j)):
                            TS("dve", U[:], X[:, 2:1026], par[:, wc + 2:wc + 3], par[:, bc:bc + 1], ALU.mult, ALU.add, [xk, pk], [uk])
                            STT(U[:], X[:, 1:1025], par[:, wc + 1:wc + 2], U[:], ALU.mult, ALU.add, [xk, pk, uk], [uk])
                            STT(U[:], X[:, 0:1024], par[:, wc:wc + 1], U[:], ALU.mult, ALU.add, [xk, pk, uk], [uk])
                        ACT(UG[:], UG[:], AF.Silu, [("ug", p_)], [("ug", p_)])
                        TT("pool", actb[:, j, 0:1024], UG[:], UV[:], ALU.mult, [("ug", p_), ("uv", p_)], [("act", j)])
                    ln_load(s, hf * 8)
                    for ii in range(8):
                        i = hf * 8 + ii
                        if ii + 1 < 8:
                            ln_load(s, i + 1)
                        if ii >= 2:
                            ln_b(s, i - 2, last_layer)
                        bp = npair()
                        for n in range(2):
                            for k in range(NJ):
                                PE(PS(bp + n), actb[:, k, ii * 128:(ii + 1) * 128], wd[:, k, n * 512:(n + 1) * 512], k == 0, k == NJ - 1,
                                   [("wd", 0 if k < 6 else (1 if k < 11 else (2 if k < 17 else 3))), ("act", k)], [PT_(bp + n)])
                        ln_a(s, i, bp)
                    ln_b(s, hf * 8 + 6, last_layer)
                    ln_b(s, hf * 8 + 7, last_layer)
                S.barrier()
        if not last:
            pass
    S.emit()
    es_top.close()
    return nc, S, dbg_out


_CACHE = {}


def _in_maps(x, positions, wts, layers, ncores, nseq):
    maps = []
    for c in range(ncores):
        m = {"x": np.ascontiguousarray(x[c * nseq:(c + 1) * nseq]),
             "pos": np.ascontiguousarray(np.broadcast_to(positions[c * nseq:(c + 1) * nseq, None, :], (nseq, 128, L))).astype(np.int32),
             "lnin": wts["lnin"], "cst": wts["cst"]}
        for l in layers:
            for k in WSHAPES:
                m["%s%d" % (k, l)] = wts["%s%d" % (k, l)]
        maps.append(m)
    return maps


def kernel(**inputs):
    x = np.asarray(inputs["x"], np.float32)
    positions = np.asarray(inputs["positions"], np.int32)
    wts = prep_weights(inputs)
    ncores = 8
    nseq = x.shape[0] // ncores
    key = ("fused", nseq)
    if key not in _CACHE:
        _CACHE[key] = build(nseq=nseq, layers=(0, 1))
    nc, S, _ = _CACHE[key]
    maps = _in_maps(x, positions, wts, (0, 1), ncores, nseq)
    res = run_bass_kernel_spmd(nc, maps, core_ids=list(range(ncores)))
    out = np.concatenate([np.asarray(r["y"]) for r in res.results], axis=0)
    return out.astype(np.float32)
```

```python
import numpy as np
from contextlib import ExitStack
import concourse.bass as bass
import concourse.mybir as mybir
from concourse.bass_utils import run_bass_kernel_spmd

F32 = mybir.dt.float32
BF16 = mybir.dt.bfloat16
I32 = mybir.dt.int32
ALU = mybir.AluOpType
AF = mybir.ActivationFunctionType
AX = mybir.AxisListType

L = 2048
D = 1024
NT = 16
NG = 4
DFF = 2816
NJ = 22
DEPTH = 2
ALPHA = float((2 * DEPTH) ** 0.25)
EPS = 1e-5
TOPK = 256
NIT = 16
NEG = -1.0e30
SEM_LIMIT = 30000

CH_POOL = [0, 1]
CH_CA = [2, 3]
CH_CG = [4, 5]
CH_Q = [6, 7, 8, 9]
CH_QS = [10, 11, 12, 13]
CH_K = 14
CH_KS = 15
CH_V = 16
CH_QI = [17, 18, 19, 20]
CH_QIS = [21, 22, 23, 24]
CH_KI = 25
CH_KIS = 26
CH_GATE0 = 27
NCH = 51
PC_BA = 0
PC_PSC = 51
PC_CDB = 53
PC_CLG = 55
PC_CLB = 57
PC_CDW = 59
PC_FWG = 121
PC_FWV = 187
PC_FBG = 253
PC_FBV = 275
NPAR = 297


class _Op:
    __slots__ = ("idx", "eng", "fn", "deps_raw", "deps_other", "is_dma", "slot", "signal", "stream", "val", "vc")

    def __init__(self, idx, eng, fn, is_dma, slot):
        self.idx = idx
        self.eng = eng
        self.fn = fn
        self.is_dma = is_dma
        self.slot = slot
        self.deps_raw = set()
        self.deps_other = set()
        self.signal = False
        self.stream = None
        self.val = 0
        self.vc = None


class Sched:
    def __init__(self, nc):
        self.nc = nc
        self.ops = []
        self.last_writer = {}
        self.readers = {}
        self.last_dma_on_slot = {}
        self.last_on_eng = {}
        self.dma_since_barrier = []
        self.barrier_deps = set()

    def barrier(self):
        self.barrier_deps = set(self.last_on_eng.values()) | set(self.dma_since_barrier)
        self.dma_since_barrier = []
        self.last_writer = {}
        self.readers = {}

    def add(self, eng, fn, reads=(), writes=(), dma=False, slot=None):
        idx = len(self.ops)
        if dma and slot is None:
            slot = ("auto", tuple(writes)[0])
        op = _Op(idx, eng, fn, dma, slot)
        for r in reads:
            w = self.last_writer.get(r)
            if w is not None:
                op.deps_raw.add(w)
        for t in writes:
            w = self.last_writer.get(t)
            if w is not None:
                op.deps_other.add(w)
            for rd in self.readers.get(t, ()):
                op.deps_other.add(rd)
        if dma:
            p = self.last_dma_on_slot.get(slot)
            if p is not None:
                op.deps_raw.add(p)
            self.last_dma_on_slot[slot] = idx
            self.dma_since_barrier.append(idx)
        op.deps_other |= self.barrier_deps
        for r in reads:
            self.readers.setdefault(r, []).append(idx)
        for t in writes:
            self.last_writer[t] = idx
            self.readers[t] = []
        op.deps_other -= op.deps_raw
        self.last_on_eng[eng] = idx
        self.ops.append(op)
        return idx

    def _needed(self, op, d):
        dop = self.ops[d]
        if dop.is_dma or op.is_dma:
            return True
        if dop.eng != op.eng:
            return True
        if op.eng == "pe":
            return False
        return d in op.deps_raw

    def emit(self):
        nc = self.nc
        ops = self.ops
        for op in ops:
            for d in (op.deps_raw | op.deps_other):
                if self._needed(op, d):
                    ops[d].signal = True
        for op in ops:
            if op.is_dma:
                op.signal = True
        sems = {}
        counts = {}
        sem_objs = []

        def new_sem():
            cm = nc.semaphore("s%d" % len(sem_objs))
            h = cm.__enter__()
            sem_objs.append(cm)
            return h

        for op in ops:
            if not op.signal:
                continue
            key = ("slot", op.slot) if op.is_dma else ("eng", op.eng)
            inc = 16 if op.is_dma else 1
            if key not in sems:
                sems[key] = [new_sem()]
                counts[key] = 0
            if counts[key] + inc > SEM_LIMIT:
                sems[key].append(new_sem())
                counts[key] = 0
            counts[key] += inc
            op.stream = (key, len(sems[key]) - 1)
            op.val = counts[key]
        self.n_sems = len(sem_objs)
        eng_clock = {}
        waits = [None] * len(ops)
        for op in ops:
            clk = eng_clock.setdefault(op.eng, {})
            best = {}
            for d in sorted(op.deps_raw | op.deps_other):
                if not self._needed(op, d):
                    continue
                dop = ops[d]
                if clk.get(dop.stream, 0) >= dop.val:
                    continue
                if best.get(dop.stream, 0) < dop.val:
                    best[dop.stream] = dop.val
                for k, v in dop.vc.items():
                    if clk.get(k, 0) < v:
                        clk[k] = v
            waits[op.idx] = best
            if op.signal:
                vc = dict(clk)
                vc[op.stream] = op.val
                op.vc = vc
        per_eng = {}
        for op in ops:
            per_eng.setdefault(op.eng, []).append(op)
        final_waits = {}
        for op in ops:
            if op.is_dma:
                if final_waits.get(op.stream, 0) < op.val:
                    final_waits[op.stream] = op.val

        def semh(stream):
            key, ep = stream
            return sems[key][ep]

        engmap = {"pe": "tensor", "act": "scalar", "dve": "vector", "pool": "gpsimd", "sp": "sync"}
        self.n_waits = 0
        with nc.Block() as block:
            for ename in ("sp", "pe", "act", "dve", "pool"):
                lst = per_eng.get(ename, [])

                def body(e, lst=lst, ename=ename):
                    for op in lst:
                        for s, v in waits[op.idx].items():
                            e.wait_ge(semh(s), v)
                            self.n_waits += 1
                        ins = op.fn(e)
                        if op.signal:
                            ins.then_inc(semh(op.stream), 16 if op.is_dma else 1)
                    if ename == "sp":
                        for s, v in final_waits.items():
                            e.wait_ge(semh(s), v)
                getattr(block, engmap[ename])(body)
        for cm in reversed(sem_objs):
            cm.__exit__(None, None, None)
        return self


def _chunk_cols():
    cols = []
    ar = np.arange

    def hc(base, h):
        return base + 64 * h + ar(64)

    def sw(c):
        return np.concatenate([c[32:], c[:32]])

    cols += [ar(128), 128 + ar(128)]
    cols += [256 + 128 * i + ar(128) for i in range(4)]
    for j in range(4):
        cols.append(np.concatenate([hc(768, j), hc(768, 4 + j)]))
    for j in range(4):
        cols.append(np.concatenate([sw(hc(768, j)), sw(hc(768, 4 + j))]))
    cols.append(np.concatenate([hc(1280, 0), hc(1280, 1)]))
    cols.append(np.concatenate([sw(hc(1280, 0)), sw(hc(1280, 1))]))
    cols.append(1408 + ar(128))
    for j in range(4):
        cols.append(np.concatenate([hc(1536, 2 * j), hc(1536, 2 * j + 1)]))
    for j in range(4):
        cols.append(np.concatenate([sw(hc(1536, 2 * j)), sw(hc(1536, 2 * j + 1))]))
    ki = 2048 + ar(64)
    cols.append(np.concatenate([ki, ki]))
    cols.append(np.concatenate([sw(ki), sw(ki)]))
    for i in range(3):
        for c in range(8):
            cols.append(2120 + 1024 * i + 128 * c + ar(128))
    assert len(cols) == NCH
    return np.stack(cols)


def _kp(w):
    K = w.shape[0] // 128
    return np.ascontiguousarray(w.reshape(K, 128, w.shape[1]).transpose(1, 0, 2))


def prep_weights(inp):
    cols = _chunk_cols()
    out = {}
    f = np.float32
    rep = lambda v: np.ascontiguousarray(np.broadcast_to(np.asarray(v, f)[None, :], (128, v.shape[0])))
    for l in range(DEPTH):
        w_in = np.asarray(inp["w_in"][l], f)
        b_in = np.asarray(inp["b_in"][l], f)
        wg = w_in[:, cols.reshape(-1)].reshape(8, 128, NCH, 128)
        out["wA%d" % l] = np.ascontiguousarray(wg.transpose(2, 1, 0, 3))
        out["wwi%d" % l] = _kp(np.ascontiguousarray(w_in[:, 2112:2120]))
        par = np.zeros((128, NPAR), f)
        par[:, PC_BA:PC_BA + NCH] = b_in[cols].T
        par[:, PC_PSC:PC_PSC + 2] = np.asarray(inp["pool_scale"][l], f).reshape(2, 128).T
        par[:, PC_CDB:PC_CDB + 2] = np.asarray(inp["conv_dw_b"][l], f).reshape(2, 128).T
        par[:, PC_CLG:PC_CLG + 2] = np.asarray(inp["conv_ln_g"][l], f).reshape(2, 128).T
        par[:, PC_CLB:PC_CLB + 2] = np.asarray(inp["conv_ln_b"][l], f).reshape(2, 128).T
        cdw = np.asarray(inp["conv_dw_w"][l], f)
        par[:, PC_CDW:PC_CDW + 62] = cdw.reshape(31, 2, 128).transpose(2, 1, 0).reshape(128, 62)
        fw = np.asarray(inp["ffn_dw_w"][l], f)
        par[:, PC_FWG:PC_FWG + 66] = fw[:, :DFF].reshape(3, NJ, 128).transpose(2, 1, 0).reshape(128, 66)
        par[:, PC_FWV:PC_FWV + 66] = fw[:, DFF:].reshape(3, NJ, 128).transpose(2, 1, 0).reshape(128, 66)
        fb = np.asarray(inp["ffn_dw_b"][l], f)
        par[:, PC_FBG:PC_FBG + NJ] = fb[:DFF].reshape(NJ, 128).T
        par[:, PC_FBV:PC_FBV + NJ] = fb[DFF:].reshape(NJ, 128).T
        out["par%d" % l] = par
        out["bwi%d" % l] = rep(np.tile(b_in[2112:2120], 16))
        pw = np.asarray(inp["pool_w"][l], f)
        bd = np.zeros((128, 2, 128), f)
        for c in range(2):
            bd[0:64, c, 0:64] = pw[2 * c]
            bd[64:128, c, 64:128] = pw[2 * c + 1]
        out["pwbd%d" % l] = bd
        wao = np.asarray(inp["w_attn_out"][l], f)
        rows = np.concatenate([np.concatenate([64 * j + np.arange(64), 64 * (4 + j) + np.arange(64)]) for j in range(4)])
        ow = np.concatenate([_kp(np.asarray(inp["w_pool_out"][l], f)), _kp(np.asarray(inp["w_conv_out"][l], f)),
                             _kp(np.ascontiguousarray(wao[rows]))], axis=1)
        out["outw%d" % l] = np.ascontiguousarray(ow)
        out["wo%d" % l] = _kp(np.asarray(inp["w_o"][l], f))
        wu = np.asarray(inp["w_up"][l], f)
        wug = wu[:, :DFF].reshape(8, 128, NJ, 128)
        wuv = wu[:, DFF:].reshape(8, 128, NJ, 128)
        out["wU%d" % l] = np.ascontiguousarray(np.concatenate([wug, wuv], axis=3).transpose(2, 1, 0, 3))
        out["wd%d" % l] = _kp(np.asarray(inp["w_down"][l], f))
        out["lnr%d" % l] = np.ascontiguousarray(np.concatenate(
            [rep(inp["ln1_g"][l]), rep(inp["ln1_b"][l]), rep(inp["ln2_g"][l]), rep(inp["ln2_b"][l])], axis=1))
    out["lnin"] = np.ascontiguousarray(np.concatenate([rep(inp["ln_in_g"]), rep(inp["ln_in_b"])], axis=1))
    half = 32
    invf = (10000.0 ** (-np.arange(half, dtype=np.float32) / half)).astype(f)
    cst = np.zeros((128, 2), f)
    cst[:, 0] = np.tile(invf, 4)
    cst[:, 1] = np.tile(np.concatenate([-np.ones(32, f), np.ones(32, f)]), 2)
    out["cst"] = cst
    return out


WSHAPES = {
    "wA": [NCH, 128, 8, 128], "wwi": [128, 8, 8], "par": [128, NPAR], "bwi": [128, 128], "pwbd": [128, 2, 128],
    "outw": [128, 8, 1024], "wo": [128, 8, 1024], "wU": [NJ, 128, 8, 256], "wd": [128, NJ, 1024], "lnr": [128, 4096],
}


def build(nseq=2, layers=(0, 1), dbg=None, first=True, last=True):
    dbg = dbg or set()
    nc = bass.Bass("TRN2", target_bir_lowering=False)
    dt = {}
    xin = nc.dram_tensor("x", [nseq, L, D], F32, kind="ExternalInput").ap()
    posin = nc.dram_tensor("pos", [nseq, 128, L], I32, kind="ExternalInput").ap()
    W = {}
    for l in layers:
        for k, shp in WSHAPES.items():
            W[(k, l)] = nc.dram_tensor("%s%d" % (k, l), shp, F32, kind="ExternalInput").ap()
    lnin = nc.dram_tensor("lnin", [128, 2048], F32, kind="ExternalInput").ap()
    cstin = nc.dram_tensor("cst", [128, 2], F32, kind="ExternalInput").ap()
    yout = nc.dram_tensor("y", [nseq, L, D], F32, kind="ExternalOutput").ap()
    hres = nc.dram_tensor("hres", [L, D], F32, kind="Internal").ap()
    ropd = nc.dram_tensor("ropd", [128, 2, L], F32, kind="Internal").ap()
    dbg_out = {}

    def dbg_tensor(name, shape, dtype):
        dbg_out[name] = nc.dram_tensor("dbg_" + name, shape, dtype, kind="ExternalOutput").ap()
        return dbg_out[name]

    S = Sched(nc)
    es_top = ExitStack()

    uid = [0]

    def sb(es, name, shape, dtype):
        uid[0] += 1
        return es.enter_context(nc.sbuf_tensor("sb%d_%s" % (uid[0], name), shape, dtype))

    def PE(out, lhsT, rhs, st, sp, r, w):
        S.add("pe", lambda e: e.matmul(out, lhsT=lhsT, rhs=rhs, start=st, stop=sp), reads=r, writes=w)

    def TR(out, in_, ident, r, w):
        S.add("pe", lambda e: e.transpose(out=out, in_=in_, identity=ident), reads=r, writes=w)

    def ACT(out, in_, func, r, w, bias=0.0, scale=1.0, accum=None):
        if accum is None:
            S.add("act", lambda e: e.activation(out=out, in_=in_, func=func, bias=bias, scale=scale), reads=r, writes=w)
        else:
            S.add("act", lambda e: e.activation(out=out, in_=in_, func=func, bias=bias, scale=scale, accum_out=accum),
                  reads=r, writes=w)

    def TT(eng, out, a, b, op, r, w):
        S.add(eng, lambda e: e.tensor_tensor(out=out, in0=a, in1=b, op=op), reads=r, writes=w)

    def TS(eng, out, a, s1, s2, op0, op1, r, w, accum=None):
        if accum is None:
            if op1 is None:
                S.add(eng, lambda e: e.tensor_scalar(out=out, in0=a, scalar1=s1, scalar2=None, op0=op0), reads=r, writes=w)
            else:
                S.add(eng, lambda e: e.tensor_scalar(out=out, in0=a, scalar1=s1, scalar2=s2, op0=op0, op1=op1),
                      reads=r, writes=w)
        else:
            S.add(eng, lambda e: e.tensor_scalar(out=out, in0=a, scalar1=s1, scalar2=s2, op0=op0, op1=op1,
                                                 accum_out=accum), reads=r, writes=w)

    def STT(out, a, s, b, op0, op1, r, w):
        S.add("dve", lambda e: e.scalar_tensor_tensor(out=out, in0=a, scalar=s, in1=b, op0=op0, op1=op1),
              reads=r, writes=w)

    def CP(eng, out, in_, r, w):
        S.add(eng, lambda e: e.tensor_copy(out=out, in_=in_), reads=r, writes=w)

    def MS(eng, out, v, w):
        S.add(eng, lambda e: e.memset(out, v), writes=w)

    def DMA(q, out, in_, r, w, slot=None):
        if q == "pool":
            S.add(q, lambda e: e.dma_start(out=out, in_=in_, max_dma_last_dim=4096), reads=r, writes=w, dma=True, slot=slot)
        else:
            S.add(q, lambda e: e.dma_start(out=out, in_=in_), reads=r, writes=w, dma=True, slot=slot)

    es = es_top
    psum = es.enter_context(nc.psum_tensor("psum", [128, 8, 512], F32))
    identf = sb(es, "identf", [128, 128], F32)
    identb = sb(es, "identb", [128, 128], BF16)
    onesf = sb(es, "onesf", [128, 128], F32)
    onesb = sb(es, "onesb", [128, 64], BF16)
    cmask = sb(es, "cmask", [128, 128], F32)
    cst = sb(es, "cst", [128, 2], F32)
    rct = sb(es, "rct", [128, 4, 16], F32)
    zero2 = sb(es, "zero2", [128, 2], F32)
    hT = sb(es, "hT", [128, 8, L], BF16)
    arena = sb(es, "arena", [128, NJ * 1024], BF16)
    NWA = 4
    wAb = [sb(es, "wA%d" % i, [128, 8, 128], BF16) for i in range(NWA)]
    parb = {l: sb(es, "par%d" % l, [128, NPAR], F32) for l in layers}
    LN = {}

    def ln_alloc(esx):
        LN["row"] = sb(esx, "lnrow", [128, 2, 1024], F32)
        LN["t"] = [sb(esx, "lnt%d" % i, [128, 1024], F32) for i in range(5)]
        LN["st"] = [sb(esx, "lnst%d" % i, [128, 2, 6], F32) for i in range(5)]
        LN["mv"] = [sb(esx, "lnmv%d" % i, [128, 8], F32) for i in range(5)]

    def PS(b, n0=0, n1=512):
        return psum[:, b, n0:n1]

    def PT_(b):
        return ("ps", b)

    bank_ctr = [0]

    def nb():
        b = bank_ctr[0] % 8
        bank_ctr[0] += 1
        return b

    pair_ctr = [0]

    def npair():
        b = (pair_ctr[0] % 4) * 2
        pair_ctr[0] += 1
        return b

    MS("pool", identf[:], 0.0, ["identf"])
    S.add("pool", lambda e: e.affine_select(out=identf[:], in_=identf[:], pattern=[[-1, 128]], compare_op=ALU.not_equal,
                                            fill=1.0, base=0, channel_multiplier=1), reads=["identf"], writes=["identf"])
    CP("dve", identb[:], identf[:], ["identf"], ["identb"])
    MS("dve", onesf[:], 1.0, ["onesf"])
    MS("dve", onesb[:], 1.0, ["onesb"])
    MS("pool", cmask[:], 0.0, ["cmask"])
    S.add("pool", lambda e: e.affine_select(out=cmask[:], in_=cmask[:], pattern=[[-1, 128]], compare_op=ALU.is_ge,
                                            fill=NEG, base=0, channel_multiplier=1), reads=["cmask"], writes=["cmask"])
    MS("dve", zero2[:], 0.0, ["zero2"])
    DMA("sp", cst[:], cstin, [], ["cst"])
    for l in layers:
        DMA("sp", parb[l][:], W[("par", l)], [], [("par", l)])
    with ExitStack() as es0:
        ti = sb(es0, "ti", [128, 16], I32)
        tf = sb(es0, "tf", [128, 16], F32)
        S.add("pool", lambda e: e.iota(ti[:], pattern=[[1, 16]], base=1, channel_multiplier=0), writes=["ti"])
        CP("dve", tf[:], ti[:], ["ti"], ["tf"])
        for wi_ in range(4):
            TS("dve", rct[:, wi_, :], tf[:], float(2 ** (wi_ + 1)), None, ALU.min, None, ["tf"], [("rct", wi_)])
            S.add("dve", lambda e, wi_=wi_: e.reciprocal(out=rct[:, wi_, :], in_=rct[:, wi_, :]),
                  reads=[("rct", wi_)], writes=[("rct", wi_)])
        S.barrier()

    wa_ctr = [0]

    def load_chunk(l, c):
        slot = wa_ctr[0] % NWA
        wa_ctr[0] += 1
        DMA("pool", wAb[slot][:], W[("wA", l)][c], [], [("wA", slot)], slot=("wA", slot))
        return slot

    class Stream:
        def __init__(self, l, chunks, ahead=2):
            self.l = l
            self.chunks = list(chunks)
            self.slots = {}
            self.next = 0
            self.ahead = ahead

        def get(self, i):
            while self.next < len(self.chunks) and self.next <= i + self.ahead:
                self.slots[self.next] = load_chunk(self.l, self.chunks[self.next])
                self.next += 1
            return self.slots[i]

    def proj_chunk(slot, g, bank, ncols=128):
        for k in range(8):
            PE(PS(bank)[0:ncols, :], wAb[slot][:, k, 0:ncols], hT[:, k, g * 512:(g + 1) * 512], k == 0, k == 7,
               [("wA", slot), ("hT", g)], [PT_(bank)])

    def ln_rows_load(src_ap):
        DMA("sp", LN["row"][:], src_ap.rearrange("p (a n) -> p a n", a=2), [], ["lnrow"])

    def ln_load(s, i, x_src=None):
        t = LN["t"][i % 5]
        tk = ("lnt", i % 5)
        if x_src is not None:
            DMA("sp", t[:], x_src, [], [tk], slot=("lnt_in", i % 5))
        else:
            DMA("sp", t[:], hres[i * 128:(i + 1) * 128, :], [("hres", i)], [tk], slot=("lnt_in", i % 5))

    def ln_a(s, i, mixbank):
        t = LN["t"][i % 5]
        tk = ("lnt", i % 5)
        st = LN["st"][i % 5]
        mv = LN["mv"][i % 5]
        mk = ("lnmv", i % 5)
        if mixbank is not None:
            STT(t[:].rearrange("p (a n) -> p a n", a=2), t[:].rearrange("p (a n) -> p a n", a=2), ALPHA,
                psum[:, mixbank:mixbank + 2, :], ALU.mult, ALU.add, [tk], [tk, PT_(mixbank), PT_(mixbank + 1)])
        for a in range(2):
            S.add("dve", lambda e, a=a: e.bn_stats(out=st[:, a, :], in_=t[:, a * 512:(a + 1) * 512]), reads=[tk],
                  writes=[("lnst", i % 5, a)])
        S.add("dve", lambda e: e.bn_aggr(out=mv[:, 0:2], in_=st[:].rearrange("p a s -> p (a s)")),
              reads=[("lnst", i % 5, 0), ("lnst", i % 5, 1)], writes=[mk])
        ACT(mv[:, 2:3], mv[:, 1:2], AF.Sqrt, [mk], [mk], bias=EPS)
        S.add("dve", lambda e: e.reciprocal(out=mv[:, 3:4], in_=mv[:, 2:3]), reads=[mk], writes=[mk])
        TS("dve", mv[:, 4:5], mv[:, 0:1], mv[:, 3:4], -1.0, ALU.mult, ALU.mult, [mk], [mk])

    def ln_bc(s, i, final):
        lnrow = LN["row"]
        t = LN["t"][i % 5]
        tk = ("lnt", i % 5)
        mv = LN["mv"][i % 5]
        mk = ("lnmv", i % 5)
        ACT(t[:], t[:], AF.Identity, [tk, mk], [tk], bias=mv[:, 4:5], scale=mv[:, 3:4])
        TT("dve", t[:], t[:], lnrow[:, 0, :], ALU.mult, [tk, "lnrow"], [tk])
        TT("pool", t[:], t[:], lnrow[:, 1, :], ALU.add, [tk, "lnrow"], [tk])
        if final:
            DMA("sp", yout[s, i * 128:(i + 1) * 128, :], t[:], [tk], [("yout", i)], slot=("lnt_out", i % 5))
        else:
            DMA("sp", hres[i * 128:(i + 1) * 128, :], t[:], [tk], [("hres", i)], slot=("lnt_out", i % 5))

    def ln_btr(s, i, final):
        t = LN["t"][i % 5]
        tk = ("lnt", i % 5)
        if not final:
            g = i // 4
            for hb in range(2):
                b = nb()
                for kk in range(4):
                    k = hb * 4 + kk
                    TR(PS(b, kk * 128, (kk + 1) * 128), t[:, k * 128:(k + 1) * 128], identf[:], [tk, "identf"], [PT_(b)])
                dst = hT[:, hb * 4:(hb + 1) * 4, i * 128:(i + 1) * 128]
                src = PS(b).rearrange("p (a n) -> p a n", a=4)
                if hb == 0:
                    ACT(dst, src, AF.Identity, [], [PT_(b), ("hT", g)])
                else:
                    CP("dve", dst, src, [], [PT_(b), ("hT", g)])

    def ln_pipeline(s, tiles, mm_fn, final, x_src_fn=None):
        n = len(tiles)
        ln_load(s, tiles[0], None if x_src_fn is None else x_src_fn(tiles[0]))
        for idx in range(n + 3):
            if idx + 1 < n:
                ln_load(s, tiles[idx + 1], None if x_src_fn is None else x_src_fn(tiles[idx + 1]))
            if 0 <= idx - 1 < n:
                ln_bc(s, tiles[idx - 1], final)
            if 0 <= idx - 3 < n:
                ln_btr(s, tiles[idx - 3], final)
            if idx < n:
                bp = mm_fn(tiles[idx]) if mm_fn is not None else None
                ln_a(s, tiles[idx], bp)

    for s in range(nseq):
        with ExitStack() as e0:
            posi = sb(e0, "posi", [128, L], I32)
            ang = sb(e0, "ang", [128, L], F32)
            t1 = sb(e0, "t1", [128, L], F32)
            t2i = sb(e0, "t2i", [128, L], I32)
            t3 = sb(e0, "t3", [128, L], F32)
            rop = sb(e0, "rop", [128, 2, L], F32)
            DMA("sp", posi[:], posin[s], [], ["posi"])
            CP("dve", ang[:], posi[:], ["posi"], ["ang"])
            TS("dve", ang[:], ang[:], cst[:, 0:1], None, ALU.mult, None, ["ang", "cst"], ["ang"])
            C1 = 6.28125
            C2 = float(2.0 * np.pi - 6.28125)
            for which in range(2):
                if which == 0:
                    TS("dve", t1[:], ang[:], float(np.pi / 2), None, ALU.add, None, ["ang"], ["t1"])
                    src = t1
                    sk = "t1"
                else:
                    src = ang
                    sk = "ang"
                TS("dve", t3[:], src[:], float(1.0 / (2 * np.pi)), None, ALU.mult, None, [sk], ["t3"])
                CP("dve", t2i[:], t3[:], ["t3"], ["t2i"])
                CP("dve", t3[:], t2i[:], ["t2i"], ["t3"])
                STT(t1[:], t3[:], -C1, src[:], ALU.mult, ALU.add, ["t3", sk], ["t1"])
                STT(t1[:], t3[:], -C2, t1[:], ALU.mult, ALU.add, ["t3", "t1"], ["t1"])
                TS("dve", t3[:], t1[:], float(np.pi), float(-2 * np.pi), ALU.is_gt, ALU.mult, ["t1"], ["t3"])
                TT("dve", t1[:], t1[:], t3[:], ALU.add, ["t1", "t3"], ["t1"])
                TS("dve", t3[:], t1[:], float(-np.pi), float(2 * np.pi), ALU.is_lt, ALU.mult, ["t1"], ["t3"])
                TT("dve", t1[:], t1[:], t3[:], ALU.add, ["t1", "t3"], ["t1"])
                TS("dve", t1[:], t1[:], 3.1415925, -3.1415925, ALU.min, ALU.max, ["t1"], ["t1"])
                if which == 0:
                    ACT(rop[:, 0, :], t1[:], AF.Sin, ["t1"], [("rop", 0)])
                else:
                    ACT(rop[:, 1, :], t1[:], AF.Sin, ["t1", "cst"], [("rop", 1)], scale=cst[:, 1:2])
            DMA("sp", ropd, rop[:], [("rop", 0), ("rop", 1)], ["ropd"])
            if s == 0 and "rop" in dbg:
                DMA("sp", dbg_tensor("rop", [128, 2, L], F32), rop[:], [("rop", 0), ("rop", 1)], ["dbg_rop"])
            S.barrier()
        e0b = ExitStack()
        ln_alloc(e0b)
        if first:
            ln_rows_load(lnin)
            ln_pipeline(s, list(range(NT)), None, False, x_src_fn=lambda i_: xin[s, i_ * 128:(i_ + 1) * 128, :])
        else:
            for i in range(NT):
                t = LN["t"][i % 2]
                tk = ("lnt", i % 2)
                DMA("sp", t[:], xin[s, i * 128:(i + 1) * 128, :], [], [tk], slot=("lnt_in", i % 2))
                DMA("sp", hres[i * 128:(i + 1) * 128, :], t[:], [tk], [("hres", i)], slot=("lnt_out", i % 2))
                for hb in range(2):
                    b = nb()
                    for kk in range(4):
                        k = hb * 4 + kk
                        TR(PS(b, kk * 128, (kk + 1) * 128), t[:, k * 128:(k + 1) * 128], identf[:], [tk, "identf"], [PT_(b)])
                    CP("dve", hT[:, hb * 4:(hb + 1) * 4, i * 128:(i + 1) * 128], PS(b).rearrange("p (a n) -> p a n", a=4),
                       [], [PT_(b), ("hT", i // 4)])
        if s == 0 and "hT0" in dbg:
            DMA("sp", dbg_tensor("hT0", [128, 8, L], BF16), hT[:], [("hT", g) for g in range(4)], ["dbg_hT0"])
        S.barrier()
        e0b.close()

        for l in layers:
            par = parb[l]
            pk = ("par", l)
            last_layer = (l == layers[-1])
            qT = arena[:, 0:4 * L].rearrange("p (c t) -> p c t", c=4)
            qiT = arena[:, 4 * L:8 * L].rearrange("p (c t) -> p c t", c=4)
            merged = arena[:, 0:8 * L].rearrange("p (c t) -> p c t", c=8)
            actb = arena[:, 0:NJ * 1024].rearrange("p (c t) -> p c t", c=NJ)
            with ExitStack() as eA:
                OT = sb(eA, "OT", [128, 4, L], BF16)
                with ExitStack() as eB:
                    kT = sb(eB, "kT", [128, L], BF16)
                    vtm = sb(eB, "vtm", [128, NT, 128], BF16)
                    kiT = sb(eB, "kiT", [128, L], BF16)
                    witm = sb(eB, "witm", [128, NT * 8], F32)
                    with ExitStack() as e2:
                        rop = sb(e2, "rop2", [128, 2, L], F32)
                        vT = sb(e2, "vT", [128, L], F32)
                        ta = [sb(e2, "ta%d" % i, [128, 512], F32) for i in range(2)]
                        tb = [sb(e2, "tb%d" % i, [128, 512], F32) for i in range(2)]
                        wwib = sb(e2, "wwib", [128, 8, 8], BF16)
                        bwib = sb(e2, "bwib", [128, 128], F32)
                        DMA("sp", rop[:], ropd, ["ropd"], ["rop"])
                        DMA("pool", wwib[:], W[("wwi", l)], [], ["wwib"])
                        DMA("sp", bwib[:], W[("bwi", l)], [], ["bwib"])
                        pairs = []
                        for j in range(4):
                            pairs.append((CH_Q[j], CH_QS[j], ("q", j)))
                        pairs.append((CH_K, CH_KS, ("k", 0)))
                        for j in range(4):
                            pairs.append((CH_QI[j], CH_QIS[j], ("qi", j)))
                        pairs.append((CH_KI, CH_KIS, ("ki", 0)))
                        chunks = []
                        for a_, b_, _ in pairs:
                            chunks += [a_, b_]
                        chunks.append(CH_V)
                        stm = Stream(l, chunks)
                        cnt2 = 0
                        for pi, (ca, cs, (kind, j)) in enumerate(pairs):
                            sa = stm.get(2 * pi)
                            ss = stm.get(2 * pi + 1)
                            for g in range(NG):
                                ba = nb()
                                bs_ = nb()
                                proj_chunk(sa, g, ba)
                                proj_chunk(ss, g, bs_)
                                A_ = ta[cnt2 % 2]
                                B_ = tb[cnt2 % 2]
                                ak = ("ta", cnt2 % 2)
                                bk = ("tb", cnt2 % 2)
                                cnt2 += 1
                                gs = slice(g * 512, (g + 1) * 512)
                                STT(A_[:], PS(ba), par[:, PC_BA + ca:PC_BA + ca + 1], rop[:, 0, gs], ALU.add, ALU.mult,
                                    [pk, "rop"], [ak, PT_(ba)])
                                ACT(B_[:], PS(bs_), AF.Identity, [pk], [bk, PT_(bs_)], bias=par[:, PC_BA + cs:PC_BA + cs + 1])
                                TT("pool", B_[:], B_[:], rop[:, 1, gs], ALU.mult, [bk, "rop"], [bk])
                                if kind == "q":
                                    dst = qT[:, j, gs]
                                elif kind == "k":
                                    dst = kT[:, gs]
                                elif kind == "qi":
                                    dst = qiT[:, j, gs]
                                else:
                                    dst = kiT[:, gs]
                                TT("dve", dst, A_[:], B_[:], ALU.add, [ak, bk], [(kind, j, g)])
                        sv = stm.get(len(chunks) - 1)
                        for g in range(NG):
                            b = nb()
                            proj_chunk(sv, g, b)
                            ACT(vT[:, g * 512:(g + 1) * 512], PS(b), AF.Identity, [pk], [("vT", g), PT_(b)],
                                bias=par[:, PC_BA + CH_V:PC_BA + CH_V + 1])
                        for g in range(NG):
                            b = nb()
                            for kk in range(4):
                                i = g * 4 + kk
                                TR(PS(b, kk * 128, (kk + 1) * 128), vT[:, i * 128:(i + 1) * 128], identf[:],
                                   [("vT", g), "identf"], [PT_(b)])
                            CP("dve", vtm[:, g * 4:(g + 1) * 4, :], PS(b).rearrange("p (a n) -> p a n", a=4), [],
                               [PT_(b), ("vtm", g)])
                        b = nb()
                        for i in range(NT):
                            for k in range(8):
                                PE(PS(b, i * 8, (i + 1) * 8), hT[:, k, i * 128:(i + 1) * 128], wwib[:, k, :], k == 0, k == 7,
                                   ["wwib", ("hT", i // 4)], [PT_(b)])
                        TT("dve", witm[:], PS(b, 0, 128), bwib[:], ALU.add, ["bwib"], [PT_(b), "witm"])
                        if s == 0 and l == layers[0]:
                            if "qT" in dbg:
                                DMA("sp", dbg_tensor("qT", [128, 4, L], BF16), qT, [("q", j, g) for j in range(4) for g in range(4)], ["dbg_qT"])
                            if "kT" in dbg:
                                DMA("sp", dbg_tensor("kT", [128, L], BF16), kT[:], [("k", 0, g) for g in range(4)], ["dbg_kT"])
                            if "kiT" in dbg:
                                DMA("sp", dbg_tensor("kiT", [128, L], BF16), kiT[:], [("ki", 0, g) for g in range(4)], ["dbg_kiT"])
                            if "qiT" in dbg:
                                DMA("sp", dbg_tensor("qiT", [128, 4, L], BF16), qiT, [("qi", j, g) for j in range(4) for g in range(4)], ["dbg_qiT"])
                            if "vtm" in dbg:
                                DMA("sp", dbg_tensor("vtm", [128, NT, 128], BF16), vtm[:], [("vtm", g) for g in range(4)], ["dbg_vtm"])
                            if "witm" in dbg:
                                DMA("sp", dbg_tensor("witm", [128, 128], F32), witm[:], ["witm"], ["dbg_witm"])
                        S.barrier()

                    with ExitStack() as e3:
                        SCW = 58 * 128
                        SC = sb(e3, "SC", [128, SCW], F32)
                        selT = sb(e3, "selT", [128, NT, 512], BF16)
                        selq = sb(e3, "selq", [128, L], F32)
                        junkD = sb(e3, "junkD", [128, L], BF16)
                        junkA = sb(e3, "junkA", [128, L], BF16)
                        rt = [sb(e3, "rt%d" % i, [128, 2, 512], BF16) for i in range(4)]
                        Dg = [sb(e3, "Dg%d" % i, [128, 8, 128], BF16) for i in range(2)]
                        PTb = [sb(e3, "PT%d" % i, [128, 2, 512], BF16) for i in range(4)]
                        rec = [sb(e3, "rec%d" % i, [128, 512], F32) for i in range(2)]
                        amax = sb(e3, "amax", [128, 4], F32)
                        lo = sb(e3, "lo", [128, 4], F32)
                        mid = sb(e3, "mid", [128, 4], F32)
                        nmid = sb(e3, "nmid", [128, 4], F32)
                        cnt = sb(e3, "cnt", [128, 4], F32)
                        thr = sb(e3, "thr", [128, 4], F32)
                        dd = sb(e3, "dd", [128, 4], F32)
                        tt_ = sb(e3, "tt_", [128, 4], F32)
                        Wt = sb(e3, "Wt", [128, NIT, 4], F32)
                        rt_ctr = [0]
                        pt_ctr = [0]
                        dp_ctr = [0]

                        if s == 0 and l == layers[0] and "tau" in dbg:
                            dtau = dbg_tensor("tau", [128, NT], F32)
                            dsel = dbg_tensor("selT", [4, 128, NT, 512], BF16)
                        else:
                            dtau = None

                        def blk_off(g, r):
                            return sum((4 * g + rr + 1) * 128 for rr in range(r))

                        sb_ctr = [0]

                        def dpair():
                            b = (dp_ctr[0] % 3) * 2
                            dp_ctr[0] += 1
                            return b

                        def idx_steps(g):
                            steps = []
                            for r in range(4):
                                i = 4 * g + r
                                off = blk_off(g, r)
                                wdt = (i + 1) * 128

                                def mk_dg(i=i):
                                    D_ = Dg[i % 2]
                                    for h in range(8):
                                        TS("dve", D_[:, h, :], identb[:], witm[:, i * 8 + h:i * 8 + h + 1], None, ALU.mult, None,
                                           ["identb", "witm"], [("Dg", i % 2)])
                                steps.append((mk_dg, None))
                                for kc in range((wdt + 511) // 512):
                                    ncols = min(512, wdt - kc * 512)
                                    sbank = [None]
                                    for jh in range(4):
                                        st = {}

                                        def stepA(i=i, kc=kc, ncols=ncols, jh=jh, st=st):
                                            bp = dpair()
                                            st["R"] = rt[rt_ctr[0] % 4]
                                            st["rk"] = ("rt", rt_ctr[0] % 4)
                                            rt_ctr[0] += 1
                                            for hh in range(2):
                                                base = hh * 64
                                                PE(PS(bp + hh, 0, ncols), qiT[base:base + 64, jh, i * 128:(i + 1) * 128],
                                                   kiT[base:base + 64, kc * 512:kc * 512 + ncols], True, True,
                                                   [("qi", jh, i // 4), ("ki", 0, kc)], [PT_(bp + hh)])
                                            ACT(st["R"][:, :, 0:ncols], psum[:, bp:bp + 2, 0:ncols], AF.Relu, [],
                                                [st["rk"], PT_(bp), PT_(bp + 1)])

                                        def stepB(i=i, off=off, kc=kc, ncols=ncols, jh=jh, r=r, sbank=sbank, st=st):
                                            if jh == 0:
                                                sbank[0] = 6 + (sb_ctr[0] % 2)
                                                sb_ctr[0] += 1
                                            bs_ = sbank[0]
                                            for hh in range(2):
                                                h = 2 * jh + hh
                                                PE(PS(bs_, 0, ncols), Dg[i % 2][:, h, :], st["R"][:, hh, 0:ncols], h == 0, h == 7,
                                                   [st["rk"], ("Dg", i % 2)], [PT_(bs_)])
                                            if jh == 3:
                                                CP("dve", SC[:, off + kc * 512:off + kc * 512 + ncols], PS(bs_, 0, ncols), [],
                                                   [PT_(bs_), ("SC", r, kc)])
                                        steps.append((stepA, stepB))

                                def fin(i=i, off=off, wdt=wdt, r=r):
                                    sks = [("SC", r, kc) for kc in range((wdt + 511) // 512)]
                                    S.add("dve", lambda e: e.tensor_reduce(out=amax[:, r:r + 1], in_=SC[:, off:off + wdt], axis=AX.X,
                                                                          op=ALU.max, apply_absolute_value=True),
                                          reads=sks, writes=[("amax", r)])
                                    dk = ("SC", r, i // 4)
                                    TT("pool", SC[:, off + i * 128:off + (i + 1) * 128], SC[:, off + i * 128:off + (i + 1) * 128],
                                       cmask[:], ALU.add, [dk, "cmask", ("amax", r)], [dk])
                                steps.append((None, fin))
                            return steps

                        def bis_steps(g):
                            steps = []
                            blocks = [r for r in range(4) if 4 * g + r >= 2]

                            def init():
                                aks = [("amax", r) for r in range(4)]
                                MS("dve", lo[:], -1.0e29, ["lo"])
                                MS("dve", thr[:], TOPK - 0.5, ["thr"])
                                MS("dve", cnt[:], 0.0, [("cnt", r) for r in range(4)])
                                for r in blocks:
                                    wdt = (4 * g + r + 1) * 128
                                    TS("dve", lo[:, r:r + 1], amax[:, r:r + 1], -1.0, None, ALU.mult, None, aks, ["lo"])
                                    if r % 2 == 1:
                                        MS("dve", thr[:, r:r + 1], float(2 * TOPK - 1 - wdt), ["thr"])
                                TS("dve", Wt[:, 0, :], amax[:], 1.0000005, 1.0e-30, ALU.mult, ALU.add, aks, ["Wt"])
                                for it in range(1, NIT):
                                    TS("dve", Wt[:, it, :], Wt[:, 0, :], float(2.0 ** (-it)), None, ALU.mult, None, ["Wt"], ["Wt"])
                                for r in range(4):
                                    if r not in blocks:
                                        MS("dve", Wt[:, :, r:r + 1], 0.0, ["Wt"])
                            steps.append(init)
                            if not blocks:
                                return steps
                            for it in range(NIT):
                                def step(it=it):
                                    TT("pool", mid[:], lo[:], Wt[:, it, :], ALU.add, ["lo", "Wt"], ["mid"])
                                    TS("pool", nmid[:], mid[:], -1.0, 0.0, ALU.mult, ALU.add, ["mid"], ["nmid"])
                                    for r in blocks:
                                        i = 4 * g + r
                                        off = blk_off(g, r)
                                        wdt = (i + 1) * 128
                                        sks = [("SC", r, kc) for kc in range((wdt + 511) // 512)]
                                        if r % 2 == 0:
                                            TS("dve", junkD[:, 0:wdt], SC[:, off:off + wdt], mid[:, r:r + 1], None, ALU.is_ge, ALU.add,
                                               sks + ["mid"], ["junkD", ("cnt", r)], accum=cnt[:, r:r + 1])
                                        else:
                                            ACT(junkA[:, 0:wdt], SC[:, off:off + wdt], AF.Sign, sks + ["nmid"], ["junkA", ("cnt", r)],
                                                bias=nmid[:, r:r + 1], accum=cnt[:, r:r + 1])
                                    cks = [("cnt", r) for r in range(4)]
                                    TT("pool", dd[:], cnt[:], thr[:], ALU.subtract, cks + ["thr"], ["dd"])
                                    TS("pool", dd[:], dd[:], 0.0, None, ALU.is_ge, None, ["dd"], ["dd"])
                                    TT("pool", tt_[:], dd[:], Wt[:, it, :], ALU.mult, ["dd", "Wt"], ["tt_"])
                                    TT("pool", lo[:], lo[:], tt_[:], ALU.add, ["lo", "tt_"], ["lo"])
                                steps.append(step)
                            return steps

                        def sel_steps(g):
                            steps = []
                            for r in range(4):
                                def step(r=r):
                                    i = 4 * g + r
                                    off = blk_off(g, r)
                                    wdt = (i + 1) * 128
                                    sks = [("SC", r, kc) for kc in range((wdt + 511) // 512)]
                                    TS("dve", selq[:, 0:wdt], SC[:, off:off + wdt], lo[:, r:r + 1], None, ALU.is_ge, None,
                                       sks + ["lo"], ["selq"])
                                    for m in range((i + 4) // 4):
                                        nk = min(4, i + 1 - 4 * m)
                                        b = nb()
                                        for kk in range(nk):
                                            kt = 4 * m + kk
                                            TR(PS(b, kk * 128, (kk + 1) * 128), selq[:, kt * 128:(kt + 1) * 128], identf[:],
                                               ["selq", "identf"], [PT_(b)])
                                        dst = selT[:, 4 * m:4 * m + nk, r * 128:(r + 1) * 128]
                                        src = PS(b, 0, nk * 128).rearrange("p (a n) -> p a n", a=nk)
                                        wk = [("selT", 4 * m + kk) for kk in range(nk)]
                                        if m % 2 == 0:
                                            ACT(dst, src, AF.Identity, [], [PT_(b)] + wk)
                                        else:
                                            CP("dve", dst, src, [], [PT_(b)] + wk)
                                steps.append(step)
                            if dtau is not None:
                                def dump():
                                    DMA("sp", dtau[:, 4 * g:4 * g + 4], lo[:], ["lo"], [("dtau", g)])
                                    DMA("sp", dsel[g], selT[:], [("selT", kt) for kt in range(NT)], [("dsel", g)])
                                steps.append(dump)
                            return steps

                        def attn_steps(g):
                            steps = []
                            nkt = 4 * g + 4
                            for j in range(4):
                                for kt in range(nkt):
                                    st = {}

                                    def stepA(j=j, kt=kt, st=st):
                                        q0 = max(0, kt - 4 * g) * 128
                                        N = 512 - q0
                                        bp = dpair()
                                        P_ = PTb[pt_ctr[0] % 4]
                                        ptk = ("PT", pt_ctr[0] % 4)
                                        pt_ctr[0] += 1
                                        st["P"] = P_
                                        st["ptk"] = ptk
                                        for hh in range(2):
                                            base = hh * 64
                                            PE(PS(bp + hh, 0, N), kT[base:base + 64, kt * 128:(kt + 1) * 128],
                                               qT[base:base + 64, j, g * 512 + q0:(g + 1) * 512], True, True,
                                               [("k", 0, kt // 4), ("q", j, g)], [PT_(bp + hh)])
                                        ACT(P_[:, :, 0:N], psum[:, bp:bp + 2, 0:N], AF.Exp, [], [ptk, PT_(bp), PT_(bp + 1)], scale=0.125)
                                        for hh in range(2):
                                            TT("dve", P_[:, hh, 0:N], P_[:, hh, 0:N], selT[:, kt, q0:512], ALU.mult,
                                               [ptk, ("selT", kt)], [ptk])

                                    def stepB(j=j, kt=kt, st=st):
                                        q0 = max(0, kt - 4 * g) * 128
                                        N = 512 - q0
                                        po = 6
                                        pd = 7
                                        P_ = st["P"]
                                        ptk = st["ptk"]
                                        for hh in range(2):
                                            base = hh * 64
                                            PE(psum[base:base + 64, po, q0:512], vtm[:, kt, base:base + 64], P_[:, hh, 0:N],
                                               kt == 0, kt == nkt - 1, [ptk, ("vtm", kt // 4)], [PT_(po)])
                                            PE(psum[base:base + 64, pd, q0:512], onesb[:, 0:64], P_[:, hh, 0:N],
                                               kt == 0, kt == nkt - 1, [ptk, "onesb"], [PT_(pd)])
                                        if kt == nkt - 1:
                                            R = rec[j % 2]
                                            rk = ("rec", j % 2)
                                            S.add("dve", lambda e: e.reciprocal(out=R[:], in_=PS(pd)), reads=[], writes=[rk, PT_(pd)])
                                            TT("dve", OT[:, j, g * 512:(g + 1) * 512], PS(po), R[:], ALU.mult, [rk], [PT_(po), ("OT", j, g)])
                                    steps.append((stepA, stepB))
                            return steps

                        def pipe(pairs, lag):
                            out = []
                            n = len(pairs)
                            for t_ in range(n + lag):
                                if t_ < n and pairs[t_][0] is not None:
                                    out.append(pairs[t_][0])
                                if t_ - lag >= 0 and pairs[t_ - lag][1] is not None:
                                    out.append(pairs[t_ - lag][1])
                            return out

                        def run(steps):
                            for st_ in steps:
                                st_()

                        def interleave(a, b):
                            na, nb_ = len(a), len(b)
                            ia = ib = 0
                            while ia < na or ib < nb_:
                                if ib >= nb_ or (ia < na and ia * nb_ <= ib * na):
                                    a[ia]()
                                    ia += 1
                                else:
                                    b[ib]()
                                    ib += 1

                        run(pipe(idx_steps(0), 1))
                        run(bis_steps(0))
                        run(sel_steps(0))
                        for g in range(1, NG):
                            run(pipe(idx_steps(g), 1))
                            interleave(bis_steps(g), pipe(attn_steps(g - 1), 2))
                            run(sel_steps(g))
                        run(pipe(attn_steps(NG - 1), 2))
                        if s == 0 and l == layers[0] and "OT" in dbg:
                            DMA("sp", dbg_tensor("OT", [128, 4, L], BF16), OT[:], [("OT", j, g) for j in range(4) for g in range(4)], ["dbg_OT"])
                        S.barrier()
                S.barrier()
                with ExitStack() as eC:
                    ypool = sb(eC, "ypool", [128, 2, L], BF16)
                    yconv = sb(eC, "yconv", [128, 2, L], BF16)
                    outw = sb(eC, "outw", [128, 8, 1024], BF16)
                    DMA("pool", outw[:], W[("outw", l)], [], ["outw"])
                    with ExitStack() as e1:
                        up = sb(e1, "up", [128, 2, 16 + L], F32)
                        sA = sb(e1, "sA", [128, 16 + L], F32)
                        sB = sb(e1, "sB", [128, 16 + L], F32)
                        mixed = sb(e1, "mixed", [128, 2, L], BF16)
                        t16 = sb(e1, "t16", [128, 16], F32)
                        pwb = sb(e1, "pwb", [128, 2, 128], BF16)
                        DMA("pool", pwb[:], W[("pwbd", l)], [], ["pwb"])
                        MS("pool", up[:, :, 0:16], 0.0, [("up", 0, -1), ("up", 1, -1)])
                        MS("pool", sA[:, 0:16], 0.0, ["sA"])
                        MS("pool", sB[:, 0:16], 0.0, ["sB"])
                        stm = Stream(l, CH_POOL)
                        for c in range(2):
                            sl = stm.get(c)
                            for g in range(NG):
                                b = nb()
                                proj_chunk(sl, g, b)
                                ACT(up[:, c, 16 + g * 512:16 + (g + 1) * 512], PS(b), AF.Identity, [pk], [PT_(b), ("up", c, g)],
                                    bias=par[:, PC_BA + c:PC_BA + c + 1])
                        for c in range(2):
                            uk = [("up", c, g) for g in range(-1, 4)]
                            U = up[:, c, :]
                            TT("dve", sA[:, 16:], U[:, 16:], U[:, 15:15 + L], ALU.add, uk, ["sA"])
                            TT("dve", sB[:, 16:], sA[:, 16:], sA[:, 14:14 + L], ALU.add, ["sA"], ["sB"])
                            if c == 1:
                                TT("dve", sA[:, 16:], sB[:, 16:], sB[:, 12:12 + L], ALU.add, ["sB"], ["sA"])
                                TT("dve", sB[:, 16:], sA[:, 16:], sA[:, 8:8 + L], ALU.add, ["sA"], ["sB"])
                            for half, (sbuf_, sk) in enumerate(((sA, "sA"), (sB, "sB"))):
                                widx = 2 * c + half
                                win = float(2 ** (widx + 1))
                                pr = slice(half * 64, half * 64 + 64)
                                STT(mixed[pr, c, 16:], sbuf_[pr, 32:], 1.0 / win, U[pr, 32:], ALU.mult, ALU.subtract, [sk] + uk,
                                    [("mixed", c, half)])
                                TT("dve", t16[pr, :], sbuf_[pr, 16:32], rct[pr, widx, :], ALU.mult, [sk, ("rct", widx)], ["t16"])
                                TT("dve", mixed[pr, c, 0:16], t16[pr, :], U[pr, 16:32], ALU.subtract, ["t16"] + uk, [("mixed", c, half)])
                        for c in range(2):
                            for g in range(NG):
                                b = nb()
                                PE(PS(b), pwb[:, c, :], mixed[:, c, g * 512:(g + 1) * 512], True, True,
                                   ["pwb", ("mixed", c, 0), ("mixed", c, 1)], [PT_(b)])
                                ACT(ypool[:, c, g * 512:(g + 1) * 512], PS(b), AF.Identity, [pk], [PT_(b), ("ypool", c, g)],
                                    scale=par[:, PC_PSC + c:PC_PSC + c + 1])
                        S.barrier()
                    with ExitStack() as e1:
                        glu = sb(e1, "glu", [128, 2, 30 + L], BF16)
                        dg = sb(e1, "dg", [128, 2, 31, 128], BF16)
                        xcv = sb(e1, "xcv", [128, 2, L], F32)
                        xsq = [sb(e1, "xsq%d" % i, [128, 512], F32) for i in range(2)]
                        sgt = [sb(e1, "sgt%d" % i, [128, 512], F32) for i in range(2)]
                        mean_t = sb(e1, "mean_t", [128, 512], F32)
                        var_t = sb(e1, "var_t", [128, 512], F32)
                        dtmp = [sb(e1, "dtmp%d" % i, [128, 512], F32) for i in range(2)]
                        MS("pool", glu[:, :, 0:30], 0.0, [("glu", 0, -1), ("glu", 1, -1)])
                        for c in range(2):
                            for jj in range(31):
                                TS("dve", dg[:, c, jj, :], identb[:], par[:, PC_CDW + c * 31 + jj:PC_CDW + c * 31 + jj + 1], None,
                                   ALU.mult, None, ["identb", pk], [("dg", c)])
                        stm = Stream(l, [CH_CA[0], CH_CG[0], CH_CA[1], CH_CG[1]])
                        cc = 0
                        for c in range(2):
                            sa_ = stm.get(2 * c)
                            sg_ = stm.get(2 * c + 1)
                            for g in range(NG):
                                ba = nb()
                                bg = nb()
                                proj_chunk(sa_, g, ba)
                                proj_chunk(sg_, g, bg)
                                T_ = sgt[cc % 2]
                                tk_ = ("sgt", cc % 2)
                                cc += 1
                                ACT(T_[:], PS(bg), AF.Sigmoid, [pk], [tk_, PT_(bg)], bias=par[:, PC_BA + CH_CG[c]:PC_BA + CH_CG[c] + 1])
                                STT(glu[:, c, 30 + g * 512:30 + (g + 1) * 512], PS(ba), par[:, PC_BA + CH_CA[c]:PC_BA + CH_CA[c] + 1],
                                    T_[:], ALU.add, ALU.mult, [pk, tk_], [PT_(ba), ("glu", c, g)])
                        for g in range(NG):
                            bm = nb()
                            bq = nb()
                            for c in range(2):
                                b = nb()
                                gk = [("glu", c, gg) for gg in range(-1, 4)]
                                for jj in range(31):
                                    PE(PS(b), dg[:, c, jj, :], glu[:, c, g * 512 + jj:g * 512 + jj + 512], jj == 0, jj == 30,
                                       [("dg", c)] + gk, [PT_(b)])
                                X2 = xsq[c]
                                ACT(xcv[:, c, g * 512:(g + 1) * 512], PS(b), AF.Identity, [pk], [PT_(b), ("xcv", c, g)],
                                    bias=par[:, PC_CDB + c:PC_CDB + c + 1])
                                ACT(X2[:], PS(b), AF.Square, [pk], [PT_(b), ("xsq", c)], bias=par[:, PC_CDB + c:PC_CDB + c + 1])
                            for c in range(2):
                                PE(PS(bm), onesf[:], xcv[:, c, g * 512:(g + 1) * 512], c == 0, c == 1, ["onesf", ("xcv", c, g)], [PT_(bm)])
                            for c in range(2):
                                PE(PS(bq), onesf[:], xsq[c][:], c == 0, c == 1, ["onesf", ("xsq", c)], [PT_(bq)])
                            TS("dve", mean_t[:], PS(bm), 1.0 / 256, None, ALU.mult, None, [], ["mean_t", PT_(bm)])
                            TT("dve", var_t[:], mean_t[:], mean_t[:], ALU.mult, ["mean_t"], ["var_t"])
                            STT(var_t[:], PS(bq), 1.0 / 256, var_t[:], ALU.mult, ALU.subtract, ["var_t"], ["var_t", PT_(bq)])
                            ACT(var_t[:], var_t[:], AF.Sqrt, ["var_t"], ["var_t"], bias=EPS)
                            S.add("dve", lambda e: e.reciprocal(out=var_t[:], in_=var_t[:]), reads=["var_t"], writes=["var_t"])
                            for c in range(2):
                                Dm = dtmp[c]
                                dk_ = ("dtmp", c)
                                TT("dve", Dm[:], xcv[:, c, g * 512:(g + 1) * 512], mean_t[:], ALU.subtract, [("xcv", c, g), "mean_t"], [dk_])
                                TT("pool", Dm[:], Dm[:], var_t[:], ALU.mult, [dk_, "var_t"], [dk_])
                                ACT(yconv[:, c, g * 512:(g + 1) * 512], Dm[:], AF.Silu, [dk_, pk], [("yconv", c, g)],
                                    bias=par[:, PC_CLB + c:PC_CLB + c + 1], scale=par[:, PC_CLG + c:PC_CLG + c + 1])
                        S.barrier()
                    if s == 0 and l == layers[0]:
                        if "ypool" in dbg:
                            DMA("sp", dbg_tensor("ypool", [128, 2, L], BF16), ypool[:], [("ypool", c, g) for c in range(2) for g in range(4)], ["dbg_ypool"])
                        if "yconv" in dbg:
                            DMA("sp", dbg_tensor("yconv", [128, 2, L], BF16), yconv[:], [("yconv", c, g) for c in range(2) for g in range(4)], ["dbg_yconv"])
                    with ExitStack() as e4:
                        wo = sb(e4, "wo", [128, 8, 1024], BF16)
                        DMA("pool", wo[:], W[("wo", l)], [], ["wo"])
                        sg3 = [sb(e4, "sg3_%d" % i, [128, 512], F32) for i in range(3)]
                        mm_ = [sb(e4, "mm_%d" % i, [128, 512], F32) for i in range(3)]
                        chunks = []
                        for c in range(8):
                            chunks += [CH_GATE0 + i * 8 + c for i in range(3)]
                        stm = Stream(l, chunks, ahead=1)
                        ysrc = [(ypool, "ypool", 0, 2), (yconv, "yconv", 2, 2), (OT, "OT", 4, 4)]
                        for c in range(8):
                            sl3 = [stm.get(3 * c + i) for i in range(3)]
                            for g in range(NG):
                                gs = slice(g * 512, (g + 1) * 512)
                                for i in range(3):
                                    bg = nb()
                                    proj_chunk(sl3[i], g, bg)
                                    ch = CH_GATE0 + i * 8 + c
                                    ACT(sg3[i][:], PS(bg), AF.Sigmoid, [pk], [("sg3", i), PT_(bg)], bias=par[:, PC_BA + ch:PC_BA + ch + 1])
                                for i, (ysb, yn, k0, nk) in enumerate(ysrc):
                                    by = nb()
                                    for k in range(nk):
                                        PE(PS(by), outw[:, k0 + k, c * 128:(c + 1) * 128], ysb[:, k, gs], k == 0, k == nk - 1,
                                           ["outw", (yn, k, g)], [PT_(by)])
                                    TT("dve", mm_[i][:], PS(by), sg3[i][:], ALU.mult, [("sg3", i)], [("mm_", i), PT_(by)])
                                TT("pool", mm_[0][:], mm_[0][:], mm_[1][:], ALU.add, [("mm_", 0), ("mm_", 1)], [("mm_", 0)])
                                TT("pool", merged[:, c, gs], mm_[0][:], mm_[2][:], ALU.add, [("mm_", 0), ("mm_", 2)], [("merged", c, g)])
                        with ExitStack() as e5:
                            ln_alloc(e5)
                            ln_rows_load(W[("lnr", l)][:, 0:2048])
                            def mm5(i):
                                bp = npair()
                                for n in range(2):
                                    for k in range(8):
                                        PE(PS(bp + n), merged[:, k, i * 128:(i + 1) * 128], wo[:, k, n * 512:(n + 1) * 512], k == 0, k == 7,
                                           ["wo", ("merged", k, i // 4)], [PT_(bp + n)])
                                return bp
                            ln_pipeline(s, list(range(NT)), mm5, False)
                            S.barrier()
                S.barrier()
            if s == 0 and l == layers[0] and "merged" in dbg:
                DMA("sp", dbg_tensor("merged", [128, 8, L], BF16), merged, [("merged", c, g) for c in range(8) for g in range(4)], ["dbg_merged"])
            if s == 0 and l == layers[0] and "hT1" in dbg:
                DMA("sp", dbg_tensor("hT1", [128, 8, L], BF16), hT[:], [("hT", g) for g in range(4)], ["dbg_hT1"])
            with ExitStack() as e6:
                wd = sb(e6, "wd", [128, NJ, 1024], BF16)
                ln_alloc(e6)
                NWU = 3
                wUb = [sb(e6, "wU%d" % i, [128, 8, 256], BF16) for i in range(NWU)]
                xg = [sb(e6, "xg%d" % i, [128, 2 + 1024], F32) for i in range(2)]
                xv = [sb(e6, "xv%d" % i, [128, 2 + 1024], F32) for i in range(2)]
                ug = [sb(e6, "ug%d" % i, [128, 1024], F32) for i in range(2)]
                uv = [sb(e6, "uv%d" % i, [128, 1024], F32) for i in range(2)]
                halo = sb(e6, "halo", [128, 2 * NJ, 2], F32)
                ln_rows_load(W[("lnr", l)][:, 2048:4096])
                wu_ctr = [0]
                wu_slots = {}

                def wu_get(idx):
                    while wu_ctr[0] <= min(idx + 1, 2 * NJ - 1):
                        n_ = wu_ctr[0]
                        sl = n_ % NWU
                        DMA("pool", wUb[sl][:], W[("wU", l)][n_ % NJ], [], [("wU", sl)], slot=("wU", sl))
                        wu_slots[n_] = sl
                        wu_ctr[0] += 1
                    return wu_slots[idx]

                wu_get(0)
                wd_parts = [(0, 6), (6, 11), (11, 17), (17, 22)]
                for kq, (k0_, k1_) in enumerate(wd_parts):
                    DMA("pool", wd[:, k0_:k1_, :], W[("wd", l)][:, k0_:k1_, :], [], [("wd", kq)])
                for hf in range(2):
                    t0 = hf * 1024
                    for j in range(NJ):
                        sl = wu_get(hf * NJ + j)
                        p_ = j % 2
                        XG, XV, UG, UV = xg[p_], xv[p_], ug[p_], uv[p_]
                        for (X, xk, coff, hidx) in ((XG, ("xg", p_), 0, j), (XV, ("xv", p_), 128, NJ + j)):
                            if hf == 0:
                                CP("pool", X[:, 0:2], zero2[:], ["zero2"], [xk])
                            else:
                                CP("pool", X[:, 0:2], halo[:, hidx, :], [("halo", hidx)], [xk])
                            for gg in range(2):
                                g = hf * 2 + gg
                                b = nb()
                                for k in range(8):
                                    PE(PS(b), wUb[sl][:, k, coff:coff + 128], hT[:, k, g * 512:(g + 1) * 512], k == 0, k == 7,
                                       [("wU", sl), ("hT", g)], [PT_(b)])
                                ACT(X[:, 2 + gg * 512:2 + (gg + 1) * 512], PS(b), AF.Identity, [], [PT_(b), xk])
                            if hf == 0:
                                CP("pool", halo[:, hidx, :], X[:, 1024:1026], [xk], [("halo", hidx)])
                        for (X, xk, U, uk, wc, bc) in ((XG, ("xg", p_), UG, ("ug", p_), PC_FWG + 3 * j, PC_FBG + j),
                                                      (XV, ("xv", p_), UV, ("uv", p_), PC_FWV + 3 * j, PC_FBV + j)):
                            TS("dve", U[:], X[:, 2:1026], par[:, wc + 2:wc + 3], par[:, bc:bc + 1], ALU.mult, ALU.add, [xk, pk], [uk])
                            STT(U[:], X[:, 1:1025], par[:, wc + 1:wc + 2], U[:], ALU.mult, ALU.add, [xk, pk, uk], [uk])
                            STT(U[:], X[:, 0:1024], par[:, wc:wc + 1], U[:], ALU.mult, ALU.add, [xk, pk, uk], [uk])
                        ACT(UG[:], UG[:], AF.Silu, [("ug", p_)], [("ug", p_)])
                        TT("pool", actb[:, j, 0:1024], UG[:], UV[:], ALU.mult, [("ug", p_), ("uv", p_)], [("act", j)])
                    def mm6(i, hf=hf):
                        ii = i - hf * 8
                        bp = npair()
                        for n in range(2):
                            for k in range(NJ):
                                PE(PS(bp + n), actb[:, k, ii * 128:(ii + 1) * 128], wd[:, k, n * 512:(n + 1) * 512], k == 0, k == NJ - 1,
                                   [("wd", 0 if k < 6 else (1 if k < 11 else (2 if k < 17 else 3))), ("act", k)], [PT_(bp + n)])
                        return bp
                    ln_pipeline(s, [hf * 8 + ii for ii in range(8)], mm6, last_layer)
                S.barrier()
        if not last:
            pass
    S.emit()
    es_top.close()
    return nc, S, dbg_out


_CACHE = {}


def _in_maps(x, positions, wts, layers, ncores, nseq):
    maps = []
    for c in range(ncores):
        m = {"x": np.ascontiguousarray(x[c * nseq:(c + 1) * nseq]),
             "pos": np.ascontiguousarray(np.broadcast_to(positions[c * nseq:(c + 1) * nseq, None, :], (nseq, 128, L))).astype(np.int32),
             "lnin": wts["lnin"], "cst": wts["cst"]}
        for l in layers:
            for k in WSHAPES:
                m["%s%d" % (k, l)] = wts["%s%d" % (k, l)]
        maps.append(m)
    return maps


def kernel(**inputs):
    x = np.asarray(inputs["x"], np.float32)
    positions = np.asarray(inputs["positions"], np.int32)
    wts = prep_weights(inputs)
    ncores = 8
    nseq = x.shape[0] // ncores
    key = ("fused", nseq)
    if key not in _CACHE:
        _CACHE[key] = build(nseq=nseq, layers=(0, 1))
    nc, S, _ = _CACHE[key]
    maps = _in_maps(x, positions, wts, (0, 1), ncores, nseq)
    res = run_bass_kernel_spmd(nc, maps, core_ids=list(range(ncores)))
    out = np.concatenate([np.asarray(r["y"]) for r in res.results], axis=0)
    return out.astype(np.float32)
```

```python
import numpy as np
from contextlib import ExitStack
import concourse.bass as bass
import concourse.mybir as mybir
from concourse.bass_utils import run_bass_kernel_spmd

F32 = mybir.dt.float32
BF16 = mybir.dt.bfloat16
I32 = mybir.dt.int32
ALU = mybir.AluOpType
AF = mybir.ActivationFunctionType
AX = mybir.AxisListType

L = 2048
D = 1024
NT = 16
NG = 4
DFF = 2816
NJ = 22
DEPTH = 2
ALPHA = float((2 * DEPTH) ** 0.25)
EPS = 1e-5
TOPK = 256
NIT = 16
NEG = -1.0e30
SEM_LIMIT = 30000

CH_POOL = [0, 1]
CH_CA = [2, 3]
CH_CG = [4, 5]
CH_Q = [6, 7, 8, 9]
CH_QS = [10, 11, 12, 13]
CH_K = 14
CH_KS = 15
CH_V = 16
CH_QI = [17, 18, 19, 20]
CH_QIS = [21, 22, 23, 24]
CH_KI = 25
CH_KIS = 26
CH_GATE0 = 27
NCH = 51
PC_BA = 0
PC_PSC = 51
PC_CDB = 53
PC_CLG = 55
PC_CLB = 57
PC_CDW = 59
PC_FWG = 121
PC_FWV = 187
PC_FBG = 253
PC_FBV = 275
NPAR = 297


class _Op:
    __slots__ = ("idx", "eng", "fn", "deps_raw", "deps_other", "is_dma", "slot", "signal", "stream", "val", "vc")

    def __init__(self, idx, eng, fn, is_dma, slot):
        self.idx = idx
        self.eng = eng
        self.fn = fn
        self.is_dma = is_dma
        self.slot = slot
        self.deps_raw = set()
        self.deps_other = set()
        self.signal = False
        self.stream = None
        self.val = 0
        self.vc = None


class Sched:
    def __init__(self, nc):
        self.nc = nc
        self.ops = []
        self.last_writer = {}
        self.readers = {}
        self.last_dma_on_slot = {}
        self.last_on_eng = {}
        self.dma_since_barrier = []
        self.barrier_deps = set()

    def barrier(self):
        self.barrier_deps = set(self.last_on_eng.values()) | set(self.dma_since_barrier)
        self.dma_since_barrier = []
        self.last_writer = {}
        self.readers = {}

    def add(self, eng, fn, reads=(), writes=(), dma=False, slot=None):
        idx = len(self.ops)
        if dma and slot is None:
            slot = ("auto", tuple(writes)[0])
        op = _Op(idx, eng, fn, dma, slot)
        for r in reads:
            w = self.last_writer.get(r)
            if w is not None:
                op.deps_raw.add(w)
        for t in writes:
            w = self.last_writer.get(t)
            if w is not None:
                op.deps_other.add(w)
            for rd in self.readers.get(t, ()):
                op.deps_other.add(rd)
        if dma:
            p = self.last_dma_on_slot.get(slot)
            if p is not None:
                op.deps_raw.add(p)
            self.last_dma_on_slot[slot] = idx
            self.dma_since_barrier.append(idx)
        op.deps_other |= self.barrier_deps
        for r in reads:
            self.readers.setdefault(r, []).append(idx)
        for t in writes:
            self.last_writer[t] = idx
            self.readers[t] = []
        op.deps_other -= op.deps_raw
        self.last_on_eng[eng] = idx
        self.ops.append(op)
        return idx

    def _needed(self, op, d):
        dop = self.ops[d]
        if dop.is_dma or op.is_dma:
            return True
        if dop.eng != op.eng:
            return True
        if op.eng == "pe":
            return False
        return d in op.deps_raw

    def emit(self):
        nc = self.nc
        ops = self.ops
        for op in ops:
            for d in (op.deps_raw | op.deps_other):
                if self._needed(op, d):
                    ops[d].signal = True
        for op in ops:
            if op.is_dma:
                op.signal = True
        sems = {}
        counts = {}
        sem_objs = []

        def new_sem():
            cm = nc.semaphore("s%d" % len(sem_objs))
            h = cm.__enter__()
            sem_objs.append(cm)
            return h

        for op in ops:
            if not op.signal:
                continue
            key = ("slot", op.slot) if op.is_dma else ("eng", op.eng)
            inc = 16 if op.is_dma else 1
            if key not in sems:
                sems[key] = [new_sem()]
                counts[key] = 0
            if counts[key] + inc > SEM_LIMIT:
                sems[key].append(new_sem())
                counts[key] = 0
            counts[key] += inc
            op.stream = (key, len(sems[key]) - 1)
            op.val = counts[key]
        self.n_sems = len(sem_objs)
        eng_clock = {}
        waits = [None] * len(ops)
        for op in ops:
            clk = eng_clock.setdefault(op.eng, {})
            best = {}
            for d in sorted(op.deps_raw | op.deps_other):
                if not self._needed(op, d):
                    continue
                dop = ops[d]
                if clk.get(dop.stream, 0) >= dop.val:
                    continue
                if best.get(dop.stream, 0) < dop.val:
                    best[dop.stream] = dop.val
                for k, v in dop.vc.items():
                    if clk.get(k, 0) < v:
                        clk[k] = v
            waits[op.idx] = best
            if op.signal:
                vc = dict(clk)
                vc[op.stream] = op.val
                op.vc = vc
        per_eng = {}
        for op in ops:
            per_eng.setdefault(op.eng, []).append(op)
        final_waits = {}
        for op in ops:
            if op.is_dma:
                if final_waits.get(op.stream, 0) < op.val:
                    final_waits[op.stream] = op.val

        def semh(stream):
            key, ep = stream
            return sems[key][ep]

        engmap = {"pe": "tensor", "act": "scalar", "dve": "vector", "pool": "gpsimd", "sp": "sync"}
        self.n_waits = 0
        with nc.Block() as block:
            for ename in ("sp", "pe", "act", "dve", "pool"):
                lst = per_eng.get(ename, [])

                def body(e, lst=lst, ename=ename):
                    for op in lst:
                        for s, v in waits[op.idx].items():
                            e.wait_ge(semh(s), v)
                            self.n_waits += 1
                        ins = op.fn(e)
                        if op.signal:
                            ins.then_inc(semh(op.stream), 16 if op.is_dma else 1)
                    if ename == "sp":
                        for s, v in final_waits.items():
                            e.wait_ge(semh(s), v)
                getattr(block, engmap[ename])(body)
        for cm in reversed(sem_objs):
            cm.__exit__(None, None, None)
        return self


def _chunk_cols():
    cols = []
    ar = np.arange

    def hc(base, h):
        return base + 64 * h + ar(64)

    def sw(c):
        return np.concatenate([c[32:], c[:32]])

    cols += [ar(128), 128 + ar(128)]
    cols += [256 + 128 * i + ar(128) for i in range(4)]
    for j in range(4):
        cols.append(np.concatenate([hc(768, j), hc(768, 4 + j)]))
    for j in range(4):
        cols.append(np.concatenate([sw(hc(768, j)), sw(hc(768, 4 + j))]))
    cols.append(np.concatenate([hc(1280, 0), hc(1280, 1)]))
    cols.append(np.concatenate([sw(hc(1280, 0)), sw(hc(1280, 1))]))
    cols.append(1408 + ar(128))
    for j in range(4):
        cols.append(np.concatenate([hc(1536, 2 * j), hc(1536, 2 * j + 1)]))
    for j in range(4):
        cols.append(np.concatenate([sw(hc(1536, 2 * j)), sw(hc(1536, 2 * j + 1))]))
    ki = 2048 + ar(64)
    cols.append(np.concatenate([ki, ki]))
    cols.append(np.concatenate([sw(ki), sw(ki)]))
    for i in range(3):
        for c in range(8):
            cols.append(2120 + 1024 * i + 128 * c + ar(128))
    assert len(cols) == NCH
    return np.stack(cols)


def _kp(w):
    K = w.shape[0] // 128
    return np.ascontiguousarray(w.reshape(K, 128, w.shape[1]).transpose(1, 0, 2))


def prep_weights(inp):
    cols = _chunk_cols()
    out = {}
    f = np.float32
    rep = lambda v: np.ascontiguousarray(np.broadcast_to(np.asarray(v, f)[None, :], (128, v.shape[0])))
    for l in range(DEPTH):
        w_in = np.asarray(inp["w_in"][l], f)
        b_in = np.asarray(inp["b_in"][l], f)
        wg = w_in[:, cols.reshape(-1)].reshape(8, 128, NCH, 128)
        out["wA%d" % l] = np.ascontiguousarray(wg.transpose(2, 1, 0, 3))
        out["wwi%d" % l] = _kp(np.ascontiguousarray(w_in[:, 2112:2120]))
        par = np.zeros((128, NPAR), f)
        par[:, PC_BA:PC_BA + NCH] = b_in[cols].T
        par[:, PC_PSC:PC_PSC + 2] = np.asarray(inp["pool_scale"][l], f).reshape(2, 128).T
        par[:, PC_CDB:PC_CDB + 2] = np.asarray(inp["conv_dw_b"][l], f).reshape(2, 128).T
        par[:, PC_CLG:PC_CLG + 2] = np.asarray(inp["conv_ln_g"][l], f).reshape(2, 128).T
        par[:, PC_CLB:PC_CLB + 2] = np.asarray(inp["conv_ln_b"][l], f).reshape(2, 128).T
        cdw = np.asarray(inp["conv_dw_w"][l], f)
        par[:, PC_CDW:PC_CDW + 62] = cdw.reshape(31, 2, 128).transpose(2, 1, 0).reshape(128, 62)
        fw = np.asarray(inp["ffn_dw_w"][l], f)
        par[:, PC_FWG:PC_FWG + 66] = fw[:, :DFF].reshape(3, NJ, 128).transpose(2, 1, 0).reshape(128, 66)
        par[:, PC_FWV:PC_FWV + 66] = fw[:, DFF:].reshape(3, NJ, 128).transpose(2, 1, 0).reshape(128, 66)
        fb = np.asarray(inp["ffn_dw_b"][l], f)
        par[:, PC_FBG:PC_FBG + NJ] = fb[:DFF].reshape(NJ, 128).T
        par[:, PC_FBV:PC_FBV + NJ] = fb[DFF:].reshape(NJ, 128).T
        out["par%d" % l] = par
        out["bwi%d" % l] = rep(np.tile(b_in[2112:2120], 16))
        pw = np.asarray(inp["pool_w"][l], f)
        bd = np.zeros((128, 2, 128), f)
        for c in range(2):
            bd[0:64, c, 0:64] = pw[2 * c]
            bd[64:128, c, 64:128] = pw[2 * c + 1]
        out["pwbd%d" % l] = bd
        wao = np.asarray(inp["w_attn_out"][l], f)
        rows = np.concatenate([np.concatenate([64 * j + np.arange(64), 64 * (4 + j) + np.arange(64)]) for j in range(4)])
        ow = np.concatenate([_kp(np.asarray(inp["w_pool_out"][l], f)), _kp(np.asarray(inp["w_conv_out"][l], f)),
                             _kp(np.ascontiguousarray(wao[rows]))], axis=1)
        out["outw%d" % l] = np.ascontiguousarray(ow)
        out["wo%d" % l] = _kp(np.asarray(inp["w_o"][l], f))
        wu = np.asarray(inp["w_up"][l], f)
        wug = wu[:, :DFF].reshape(8, 128, NJ, 128)
        wuv = wu[:, DFF:].reshape(8, 128, NJ, 128)
        out["wU%d" % l] = np.ascontiguousarray(np.concatenate([wug, wuv], axis=3).transpose(2, 1, 0, 3))
        out["wd%d" % l] = _kp(np.asarray(inp["w_down"][l], f))
        out["lnr%d" % l] = np.ascontiguousarray(np.concatenate(
            [rep(inp["ln1_g"][l]), rep(inp["ln1_b"][l]), rep(inp["ln2_g"][l]), rep(inp["ln2_b"][l])], axis=1))
    out["lnin"] = np.ascontiguousarray(np.concatenate([rep(inp["ln_in_g"]), rep(inp["ln_in_b"])], axis=1))
    half = 32
    invf = (10000.0 ** (-np.arange(half, dtype=np.float32) / half)).astype(f)
    cst = np.zeros((128, 2), f)
    cst[:, 0] = np.tile(invf, 4)
    cst[:, 1] = np.tile(np.concatenate([-np.ones(32, f), np.ones(32, f)]), 2)
    out["cst"] = cst
    return out


WSHAPES = {
    "wA": [NCH, 128, 8, 128], "wwi": [128, 8, 8], "par": [128, NPAR], "bwi": [128, 128], "pwbd": [128, 2, 128],
    "outw": [128, 8, 1024], "wo": [128, 8, 1024], "wU": [NJ, 128, 8, 256], "wd": [128, NJ, 1024], "lnr": [128, 4096],
}


def build(nseq=2, layers=(0, 1), dbg=None, first=True, last=True):
    dbg = dbg or set()
    nc = bass.Bass("TRN2", target_bir_lowering=False)
    dt = {}
    xin = nc.dram_tensor("x", [nseq, L, D], F32, kind="ExternalInput").ap()
    posin = nc.dram_tensor("pos", [nseq, 128, L], I32, kind="ExternalInput").ap()
    W = {}
    for l in layers:
        for k, shp in WSHAPES.items():
            W[(k, l)] = nc.dram_tensor("%s%d" % (k, l), shp, F32, kind="ExternalInput").ap()
    lnin = nc.dram_tensor("lnin", [128, 2048], F32, kind="ExternalInput").ap()
    cstin = nc.dram_tensor("cst", [128, 2], F32, kind="ExternalInput").ap()
    yout = nc.dram_tensor("y", [nseq, L, D], F32, kind="ExternalOutput").ap()
    hres = nc.dram_tensor("hres", [L, D], F32, kind="Internal").ap()
    ropd = nc.dram_tensor("ropd", [128, 2, L], F32, kind="Internal").ap()
    dbg_out = {}

    def dbg_tensor(name, shape, dtype):
        dbg_out[name] = nc.dram_tensor("dbg_" + name, shape, dtype, kind="ExternalOutput").ap()
        return dbg_out[name]

    S = Sched(nc)
    es_top = ExitStack()

    uid = [0]

    def sb(es, name, shape, dtype):
        uid[0] += 1
        return es.enter_context(nc.sbuf_tensor("sb%d_%s" % (uid[0], name), shape, dtype))

    def PE(out, lhsT, rhs, st, sp, r, w):
        S.add("pe", lambda e: e.matmul(out, lhsT=lhsT, rhs=rhs, start=st, stop=sp), reads=r, writes=w)

    def TR(out, in_, ident, r, w):
        S.add("pe", lambda e: e.transpose(out=out, in_=in_, identity=ident), reads=r, writes=w)

    def ACT(out, in_, func, r, w, bias=0.0, scale=1.0, accum=None):
        if accum is None:
            S.add("act", lambda e: e.activation(out=out, in_=in_, func=func, bias=bias, scale=scale), reads=r, writes=w)
        else:
            S.add("act", lambda e: e.activation(out=out, in_=in_, func=func, bias=bias, scale=scale, accum_out=accum),
                  reads=r, writes=w)

    def TT(eng, out, a, b, op, r, w):
        S.add(eng, lambda e: e.tensor_tensor(out=out, in0=a, in1=b, op=op), reads=r, writes=w)

    def TS(eng, out, a, s1, s2, op0, op1, r, w, accum=None):
        if accum is None:
            if op1 is None:
                S.add(eng, lambda e: e.tensor_scalar(out=out, in0=a, scalar1=s1, scalar2=None, op0=op0), reads=r, writes=w)
            else:
                S.add(eng, lambda e: e.tensor_scalar(out=out, in0=a, scalar1=s1, scalar2=s2, op0=op0, op1=op1),
                      reads=r, writes=w)
        else:
            S.add(eng, lambda e: e.tensor_scalar(out=out, in0=a, scalar1=s1, scalar2=s2, op0=op0, op1=op1,
                                                 accum_out=accum), reads=r, writes=w)

    def STT(out, a, s, b, op0, op1, r, w):
        S.add("dve", lambda e: e.scalar_tensor_tensor(out=out, in0=a, scalar=s, in1=b, op0=op0, op1=op1),
              reads=r, writes=w)

    def CP(eng, out, in_, r, w):
        S.add(eng, lambda e: e.tensor_copy(out=out, in_=in_), reads=r, writes=w)

    def MS(eng, out, v, w):
        S.add(eng, lambda e: e.memset(out, v), writes=w)

    def DMA(q, out, in_, r, w, slot=None):
        if q == "pool":
            S.add(q, lambda e: e.dma_start(out=out, in_=in_, max_dma_last_dim=4096), reads=r, writes=w, dma=True, slot=slot)
        else:
            S.add(q, lambda e: e.dma_start(out=out, in_=in_), reads=r, writes=w, dma=True, slot=slot)

    es = es_top
    psum = es.enter_context(nc.psum_tensor("psum", [128, 8, 512], F32))
    identf = sb(es, "identf", [128, 128], F32)
    identb = sb(es, "identb", [128, 128], BF16)
    onesf = sb(es, "onesf", [128, 128], F32)
    onesb = sb(es, "onesb", [128, 64], BF16)
    cmask = sb(es, "cmask", [128, 128], F32)
    cst = sb(es, "cst", [128, 2], F32)
    rct = sb(es, "rct", [128, 4, 16], F32)
    zero2 = sb(es, "zero2", [128, 2], F32)
    hT = sb(es, "hT", [128, 8, L], BF16)
    arena = sb(es, "arena", [128, NJ * 1024], BF16)
    NWA = 4
    wAb = [sb(es, "wA%d" % i, [128, 8, 128], BF16) for i in range(NWA)]
    parb = {l: sb(es, "par%d" % l, [128, NPAR], F32) for l in layers}
    LN = {}

    def ln_alloc(esx):
        LN["row"] = sb(esx, "lnrow", [128, 2, 1024], F32)
        LN["t"] = [sb(esx, "lnt%d" % i, [128, 1024], F32) for i in range(5)]
        LN["st"] = [sb(esx, "lnst%d" % i, [128, 2, 6], F32) for i in range(5)]
        LN["mv"] = [sb(esx, "lnmv%d" % i, [128, 8], F32) for i in range(5)]

    def PS(b, n0=0, n1=512):
        return psum[:, b, n0:n1]

    def PT_(b):
        return ("ps", b)

    bank_ctr = [0]

    def nb():
        b = bank_ctr[0] % 8
        bank_ctr[0] += 1
        return b

    pair_ctr = [0]

    def npair():
        b = (pair_ctr[0] % 4) * 2
        pair_ctr[0] += 1
        return b

    MS("pool", identf[:], 0.0, ["identf"])
    S.add("pool", lambda e: e.affine_select(out=identf[:], in_=identf[:], pattern=[[-1, 128]], compare_op=ALU.not_equal,
                                            fill=1.0, base=0, channel_multiplier=1), reads=["identf"], writes=["identf"])
    CP("dve", identb[:], identf[:], ["identf"], ["identb"])
    MS("dve", onesf[:], 1.0, ["onesf"])
    MS("dve", onesb[:], 1.0, ["onesb"])
    MS("pool", cmask[:], 0.0, ["cmask"])
    S.add("pool", lambda e: e.affine_select(out=cmask[:], in_=cmask[:], pattern=[[-1, 128]], compare_op=ALU.is_ge,
                                            fill=NEG, base=0, channel_multiplier=1), reads=["cmask"], writes=["cmask"])
    MS("dve", zero2[:], 0.0, ["zero2"])
    DMA("sp", cst[:], cstin, [], ["cst"])
    for l in layers:
        DMA("sp", parb[l][:], W[("par", l)], [], [("par", l)])
    with ExitStack() as es0:
        ti = sb(es0, "ti", [128, 16], I32)
        tf = sb(es0, "tf", [128, 16], F32)
        S.add("pool", lambda e: e.iota(ti[:], pattern=[[1, 16]], base=1, channel_multiplier=0), writes=["ti"])
        CP("dve", tf[:], ti[:], ["ti"], ["tf"])
        for wi_ in range(4):
            TS("dve", rct[:, wi_, :], tf[:], float(2 ** (wi_ + 1)), None, ALU.min, None, ["tf"], [("rct", wi_)])
            S.add("dve", lambda e, wi_=wi_: e.reciprocal(out=rct[:, wi_, :], in_=rct[:, wi_, :]),
                  reads=[("rct", wi_)], writes=[("rct", wi_)])
        S.barrier()

    wa_ctr = [0]

    def load_chunk(l, c):
        slot = wa_ctr[0] % NWA
        wa_ctr[0] += 1
        DMA("pool", wAb[slot][:], W[("wA", l)][c], [], [("wA", slot)], slot=("wA", slot))
        return slot

    class Stream:
        def __init__(self, l, chunks, ahead=2):
            self.l = l
            self.chunks = list(chunks)
            self.slots = {}
            self.next = 0
            self.ahead = ahead

        def get(self, i):
            while self.next < len(self.chunks) and self.next <= i + self.ahead:
                self.slots[self.next] = load_chunk(self.l, self.chunks[self.next])
                self.next += 1
            return self.slots[i]

    def proj_chunk(slot, g, bank, ncols=128):
        for k in range(8):
            PE(PS(bank)[0:ncols, :], wAb[slot][:, k, 0:ncols], hT[:, k, g * 512:(g + 1) * 512], k == 0, k == 7,
               [("wA", slot), ("hT", g)], [PT_(bank)])

    def ln_rows_load(src_ap):
        DMA("sp", LN["row"][:], src_ap.rearrange("p (a n) -> p a n", a=2), [], ["lnrow"])

    def ln_load(s, i, x_src=None):
        t = LN["t"][i % 5]
        tk = ("lnt", i % 5)
        if x_src is not None:
            DMA("sp", t[:], x_src, [], [tk], slot=("lnt_in", i % 5))
        else:
            DMA("sp", t[:], hres[i * 128:(i + 1) * 128, :], [("hres", i)], [tk], slot=("lnt_in", i % 5))

    def ln_a(s, i, mixbank):
        t = LN["t"][i % 5]
        tk = ("lnt", i % 5)
        st = LN["st"][i % 5]
        mv = LN["mv"][i % 5]
        mk = ("lnmv", i % 5)
        if mixbank is not None:
            STT(t[:].rearrange("p (a n) -> p a n", a=2), t[:].rearrange("p (a n) -> p a n", a=2), ALPHA,
                psum[:, mixbank:mixbank + 2, :], ALU.mult, ALU.add, [tk], [tk, PT_(mixbank), PT_(mixbank + 1)])
        for a in range(2):
            S.add("dve", lambda e, a=a: e.bn_stats(out=st[:, a, :], in_=t[:, a * 512:(a + 1) * 512]), reads=[tk],
                  writes=[("lnst", i % 5, a)])
        S.add("dve", lambda e: e.bn_aggr(out=mv[:, 0:2], in_=st[:].rearrange("p a s -> p (a s)")),
              reads=[("lnst", i % 5, 0), ("lnst", i % 5, 1)], writes=[mk])
        ACT(mv[:, 2:3], mv[:, 1:2], AF.Sqrt, [mk], [mk], bias=EPS)
        S.add("dve", lambda e: e.reciprocal(out=mv[:, 3:4], in_=mv[:, 2:3]), reads=[mk], writes=[mk])
        TS("dve", mv[:, 4:5], mv[:, 0:1], mv[:, 3:4], -1.0, ALU.mult, ALU.mult, [mk], [mk])

    def ln_bc(s, i, final):
        lnrow = LN["row"]
        t = LN["t"][i % 5]
        tk = ("lnt", i % 5)
        mv = LN["mv"][i % 5]
        mk = ("lnmv", i % 5)
        ACT(t[:], t[:], AF.Identity, [tk, mk], [tk], bias=mv[:, 4:5], scale=mv[:, 3:4])
        TT("dve", t[:], t[:], lnrow[:, 0, :], ALU.mult, [tk, "lnrow"], [tk])
        TT("pool", t[:], t[:], lnrow[:, 1, :], ALU.add, [tk, "lnrow"], [tk])
        if final:
            DMA("sp", yout[s, i * 128:(i + 1) * 128, :], t[:], [tk], [("yout", i)], slot=("lnt_out", i % 5))
        else:
            DMA("sp", hres[i * 128:(i + 1) * 128, :], t[:], [tk], [("hres", i)], slot=("lnt_out", i % 5))

    def ln_btr(s, i, final):
        t = LN["t"][i % 5]
        tk = ("lnt", i % 5)
        if not final:
            g = i // 4
            for hb in range(2):
                b = nb()
                for kk in range(4):
                    k = hb * 4 + kk
                    TR(PS(b, kk * 128, (kk + 1) * 128), t[:, k * 128:(k + 1) * 128], identf[:], [tk, "identf"], [PT_(b)])
                dst = hT[:, hb * 4:(hb + 1) * 4, i * 128:(i + 1) * 128]
                src = PS(b).rearrange("p (a n) -> p a n", a=4)
                if hb == 0:
                    ACT(dst, src, AF.Identity, [], [PT_(b), ("hT", g)])
                else:
                    CP("dve", dst, src, [], [PT_(b), ("hT", g)])

    def ln_pipeline(s, tiles, mm_fn, final, x_src_fn=None):
        n = len(tiles)
        ln_load(s, tiles[0], None if x_src_fn is None else x_src_fn(tiles[0]))
        for idx in range(n + 3):
            if idx + 1 < n:
                ln_load(s, tiles[idx + 1], None if x_src_fn is None else x_src_fn(tiles[idx + 1]))
            if 0 <= idx - 1 < n:
                ln_bc(s, tiles[idx - 1], final)
            if 0 <= idx - 3 < n:
                ln_btr(s, tiles[idx - 3], final)
            if idx < n:
                bp = mm_fn(tiles[idx]) if mm_fn is not None else None
                ln_a(s, tiles[idx], bp)

    for s in range(nseq):
        with ExitStack() as e0:
            posi = sb(e0, "posi", [128, L], I32)
            ang = sb(e0, "ang", [128, L], F32)
            t1 = sb(e0, "t1", [128, L], F32)
            t2i = sb(e0, "t2i", [128, L], I32)
            t3 = sb(e0, "t3", [128, L], F32)
            rop = sb(e0, "rop", [128, 2, L], F32)
            DMA("sp", posi[:], posin[s], [], ["posi"])
            CP("dve", ang[:], posi[:], ["posi"], ["ang"])
            TS("dve", ang[:], ang[:], cst[:, 0:1], None, ALU.mult, None, ["ang", "cst"], ["ang"])
            C1 = 6.28125
            C2 = float(2.0 * np.pi - 6.28125)
            for which in range(2):
                if which == 0:
                    TS("dve", t1[:], ang[:], float(np.pi / 2), None, ALU.add, None, ["ang"], ["t1"])
                    src = t1
                    sk = "t1"
                else:
                    src = ang
                    sk = "ang"
                TS("dve", t3[:], src[:], float(1.0 / (2 * np.pi)), None, ALU.mult, None, [sk], ["t3"])
                CP("dve", t2i[:], t3[:], ["t3"], ["t2i"])
                CP("dve", t3[:], t2i[:], ["t2i"], ["t3"])
                STT(t1[:], t3[:], -C1, src[:], ALU.mult, ALU.add, ["t3", sk], ["t1"])
                STT(t1[:], t3[:], -C2, t1[:], ALU.mult, ALU.add, ["t3", "t1"], ["t1"])
                TS("dve", t3[:], t1[:], float(np.pi), float(-2 * np.pi), ALU.is_gt, ALU.mult, ["t1"], ["t3"])
                TT("dve", t1[:], t1[:], t3[:], ALU.add, ["t1", "t3"], ["t1"])
                TS("dve", t3[:], t1[:], float(-np.pi), float(2 * np.pi), ALU.is_lt, ALU.mult, ["t1"], ["t3"])
                TT("dve", t1[:], t1[:], t3[:], ALU.add, ["t1", "t3"], ["t1"])
                TS("dve", t1[:], t1[:], 3.1415925, -3.1415925, ALU.min, ALU.max, ["t1"], ["t1"])
                if which == 0:
                    ACT(rop[:, 0, :], t1[:], AF.Sin, ["t1"], [("rop", 0)])
                else:
                    ACT(rop[:, 1, :], t1[:], AF.Sin, ["t1", "cst"], [("rop", 1)], scale=cst[:, 1:2])
            DMA("sp", ropd, rop[:], [("rop", 0), ("rop", 1)], ["ropd"])
            if s == 0 and "rop" in dbg:
                DMA("sp", dbg_tensor("rop", [128, 2, L], F32), rop[:], [("rop", 0), ("rop", 1)], ["dbg_rop"])
            S.barrier()
        e0b = ExitStack()
        ln_alloc(e0b)
        if first:
            ln_rows_load(lnin)
            ln_pipeline(s, list(range(NT)), None, False, x_src_fn=lambda i_: xin[s, i_ * 128:(i_ + 1) * 128, :])
        else:
            for i in range(NT):
                t = LN["t"][i % 2]
                tk = ("lnt", i % 2)
                DMA("sp", t[:], xin[s, i * 128:(i + 1) * 128, :], [], [tk], slot=("lnt_in", i % 2))
                DMA("sp", hres[i * 128:(i + 1) * 128, :], t[:], [tk], [("hres", i)], slot=("lnt_out", i % 2))
                for hb in range(2):
                    b = nb()
                    for kk in range(4):
                        k = hb * 4 + kk
                        TR(PS(b, kk * 128, (kk + 1) * 128), t[:, k * 128:(k + 1) * 128], identf[:], [tk, "identf"], [PT_(b)])
                    CP("dve", hT[:, hb * 4:(hb + 1) * 4, i * 128:(i + 1) * 128], PS(b).rearrange("p (a n) -> p a n", a=4),
                       [], [PT_(b), ("hT", i // 4)])
        if s == 0 and "hT0" in dbg:
            DMA("sp", dbg_tensor("hT0", [128, 8, L], BF16), hT[:], [("hT", g) for g in range(4)], ["dbg_hT0"])
        S.barrier()
        e0b.close()

        for l in layers:
            par = parb[l]
            pk = ("par", l)
            last_layer = (l == layers[-1])
            qT = arena[:, 0:4 * L].rearrange("p (c t) -> p c t", c=4)
            qiT = arena[:, 4 * L:8 * L].rearrange("p (c t) -> p c t", c=4)
            merged = arena[:, 0:8 * L].rearrange("p (c t) -> p c t", c=8)
            actb = arena[:, 0:NJ * 1024].rearrange("p (c t) -> p c t", c=NJ)
            with ExitStack() as eA:
                OT = sb(eA, "OT", [128, 4, L], BF16)
                with ExitStack() as eB:
                    kT = sb(eB, "kT", [128, L], BF16)
                    vtm = sb(eB, "vtm", [128, NT, 128], BF16)
                    kiT = sb(eB, "kiT", [128, L], BF16)
                    witm = sb(eB, "witm", [128, NT * 8], F32)
                    with ExitStack() as e2:
                        rop = sb(e2, "rop2", [128, 2, L], F32)
                        vT = sb(e2, "vT", [128, L], F32)
                        ta = [sb(e2, "ta%d" % i, [128, 512], F32) for i in range(2)]
                        tb = [sb(e2, "tb%d" % i, [128, 512], F32) for i in range(2)]
                        wwib = sb(e2, "wwib", [128, 8, 8], BF16)
                        bwib = sb(e2, "bwib", [128, 128], F32)
                        DMA("sp", rop[:], ropd, ["ropd"], ["rop"])
                        DMA("pool", wwib[:], W[("wwi", l)], [], ["wwib"])
                        DMA("sp", bwib[:], W[("bwi", l)], [], ["bwib"])
                        pairs = []
                        for j in range(4):
                            pairs.append((CH_Q[j], CH_QS[j], ("q", j)))
                        pairs.append((CH_K, CH_KS, ("k", 0)))
                        for j in range(4):
                            pairs.append((CH_QI[j], CH_QIS[j], ("qi", j)))
                        pairs.append((CH_KI, CH_KIS, ("ki", 0)))
                        chunks = []
                        for a_, b_, _ in pairs:
                            chunks += [a_, b_]
                        chunks.append(CH_V)
                        stm = Stream(l, chunks)
                        cnt2 = 0
                        for pi, (ca, cs, (kind, j)) in enumerate(pairs):
                            sa = stm.get(2 * pi)
                            ss = stm.get(2 * pi + 1)
                            for g in range(NG):
                                ba = nb()
                                bs_ = nb()
                                proj_chunk(sa, g, ba)
                                proj_chunk(ss, g, bs_)
                                A_ = ta[cnt2 % 2]
                                B_ = tb[cnt2 % 2]
                                ak = ("ta", cnt2 % 2)
                                bk = ("tb", cnt2 % 2)
                                cnt2 += 1
                                gs = slice(g * 512, (g + 1) * 512)
                                STT(A_[:], PS(ba), par[:, PC_BA + ca:PC_BA + ca + 1], rop[:, 0, gs], ALU.add, ALU.mult,
                                    [pk, "rop"], [ak, PT_(ba)])
                                ACT(B_[:], PS(bs_), AF.Identity, [pk], [bk, PT_(bs_)], bias=par[:, PC_BA + cs:PC_BA + cs + 1])
                                TT("pool", B_[:], B_[:], rop[:, 1, gs], ALU.mult, [bk, "rop"], [bk])
                                if kind == "q":
                                    dst = qT[:, j, gs]
                                elif kind == "k":
                                    dst = kT[:, gs]
                                elif kind == "qi":
                                    dst = qiT[:, j, gs]
                                else:
                                    dst = kiT[:, gs]
                                TT("dve", dst, A_[:], B_[:], ALU.add, [ak, bk], [(kind, j, g)])
                        sv = stm.get(len(chunks) - 1)
                        for g in range(NG):
                            b = nb()
                            proj_chunk(sv, g, b)
                            ACT(vT[:, g * 512:(g + 1) * 512], PS(b), AF.Identity, [pk], [("vT", g), PT_(b)],
                                bias=par[:, PC_BA + CH_V:PC_BA + CH_V + 1])
                        for g in range(NG):
                            b = nb()
                            for kk in range(4):
                                i = g * 4 + kk
                                TR(PS(b, kk * 128, (kk + 1) * 128), vT[:, i * 128:(i + 1) * 128], identf[:],
                                   [("vT", g), "identf"], [PT_(b)])
                            CP("dve", vtm[:, g * 4:(g + 1) * 4, :], PS(b).rearrange("p (a n) -> p a n", a=4), [],
                               [PT_(b), ("vtm", g)])
                        b = nb()
                        for i in range(NT):
                            for k in range(8):
                                PE(PS(b, i * 8, (i + 1) * 8), hT[:, k, i * 128:(i + 1) * 128], wwib[:, k, :], k == 0, k == 7,
                                   ["wwib", ("hT", i // 4)], [PT_(b)])
                        TT("dve", witm[:], PS(b, 0, 128), bwib[:], ALU.add, ["bwib"], [PT_(b), "witm"])
                        if s == 0 and l == layers[0]:
                            if "qT" in dbg:
                                DMA("sp", dbg_tensor("qT", [128, 4, L], BF16), qT, [("q", j, g) for j in range(4) for g in range(4)], ["dbg_qT"])
                            if "kT" in dbg:
                                DMA("sp", dbg_tensor("kT", [128, L], BF16), kT[:], [("k", 0, g) for g in range(4)], ["dbg_kT"])
                            if "kiT" in dbg:
                                DMA("sp", dbg_tensor("kiT", [128, L], BF16), kiT[:], [("ki", 0, g) for g in range(4)], ["dbg_kiT"])
                            if "qiT" in dbg:
                                DMA("sp", dbg_tensor("qiT", [128, 4, L], BF16), qiT, [("qi", j, g) for j in range(4) for g in range(4)], ["dbg_qiT"])
                            if "vtm" in dbg:
                                DMA("sp", dbg_tensor("vtm", [128, NT, 128], BF16), vtm[:], [("vtm", g) for g in range(4)], ["dbg_vtm"])
                            if "witm" in dbg:
                                DMA("sp", dbg_tensor("witm", [128, 128], F32), witm[:], ["witm"], ["dbg_witm"])
                        S.barrier()

                    with ExitStack() as e3:
                        SCW = 58 * 128
                        SC = sb(e3, "SC", [128, SCW], F32)
                        selT = sb(e3, "selT", [128, NT, 512], BF16)
                        selq = sb(e3, "selq", [128, L], F32)
                        junkD = sb(e3, "junkD", [128, L], BF16)
                        junkA = sb(e3, "junkA", [128, L], BF16)
                        rt = [sb(e3, "rt%d" % i, [128, 2, 512], BF16) for i in range(4)]
                        Dg = [sb(e3, "Dg%d" % i, [128, 8, 128], BF16) for i in range(2)]
                        PTb = [sb(e3, "PT%d" % i, [128, 2, 512], BF16) for i in range(4)]
                        rec = [sb(e3, "rec%d" % i, [128, 512], F32) for i in range(2)]
                        amax = sb(e3, "amax", [128, 4], F32)
                        lo = sb(e3, "lo", [128, 4], F32)
                        mid = sb(e3, "mid", [128, 4], F32)
                        nmid = sb(e3, "nmid", [128, 4], F32)
                        cnt = sb(e3, "cnt", [128, 4], F32)
                        thr = sb(e3, "thr", [128, 4], F32)
                        dd = sb(e3, "dd", [128, 4], F32)
                        tt_ = sb(e3, "tt_", [128, 4], F32)
                        Wt = sb(e3, "Wt", [128, NIT, 4], F32)
                        rt_ctr = [0]
                        pt_ctr = [0]
                        dp_ctr = [0]

                        if s == 0 and l == layers[0] and "tau" in dbg:
                            dtau = dbg_tensor("tau", [128, NT], F32)
                            dsel = dbg_tensor("selT", [4, 128, NT, 512], BF16)
                        else:
                            dtau = None

                        def blk_off(g, r):
                            return sum((4 * g + rr + 1) * 128 for rr in range(r))

                        sb_ctr = [0]

                        def dpair():
                            b = (dp_ctr[0] % 3) * 2
                            dp_ctr[0] += 1
                            return b

                        def idx_steps(g):
                            steps = []
                            for r in range(4):
                                i = 4 * g + r
                                off = blk_off(g, r)
                                wdt = (i + 1) * 128

                                def mk_dg(i=i):
                                    D_ = Dg[i % 2]
                                    for h in range(8):
                                        TS("dve", D_[:, h, :], identb[:], witm[:, i * 8 + h:i * 8 + h + 1], None, ALU.mult, None,
                                           ["identb", "witm"], [("Dg", i % 2)])
                                steps.append((mk_dg, None))
                                for kc in range((wdt + 511) // 512):
                                    ncols = min(512, wdt - kc * 512)
                                    sbank = [None]
                                    for jh in range(4):
                                        st = {}

                                        def stepA(i=i, kc=kc, ncols=ncols, jh=jh, st=st):
                                            bp = dpair()
                                            st["R"] = rt[rt_ctr[0] % 4]
                                            st["rk"] = ("rt", rt_ctr[0] % 4)
                                            rt_ctr[0] += 1
                                            for hh in range(2):
                                                base = hh * 64
                                                PE(PS(bp + hh, 0, ncols), qiT[base:base + 64, jh, i * 128:(i + 1) * 128],
                                                   kiT[base:base + 64, kc * 512:kc * 512 + ncols], True, True,
                                                   [("qi", jh, i // 4), ("ki", 0, kc)], [PT_(bp + hh)])
                                            ACT(st["R"][:, :, 0:ncols], psum[:, bp:bp + 2, 0:ncols], AF.Relu, [],
                                                [st["rk"], PT_(bp), PT_(bp + 1)])

                                        def stepB(i=i, off=off, kc=kc, ncols=ncols, jh=jh, r=r, sbank=sbank, st=st):
                                            if jh == 0:
                                                sbank[0] = 6 + (sb_ctr[0] % 2)
                                                sb_ctr[0] += 1
                                            bs_ = sbank[0]
                                            for hh in range(2):
                                                h = 2 * jh + hh
                                                PE(PS(bs_, 0, ncols), Dg[i % 2][:, h, :], st["R"][:, hh, 0:ncols], h == 0, h == 7,
                                                   [st["rk"], ("Dg", i % 2)], [PT_(bs_)])
                                            if jh == 3:
                                                CP("dve", SC[:, off + kc * 512:off + kc * 512 + ncols], PS(bs_, 0, ncols), [],
                                                   [PT_(bs_), ("SC", r, kc)])
                                        steps.append((stepA, stepB))

                                def fin(i=i, off=off, wdt=wdt, r=r):
                                    sks = [("SC", r, kc) for kc in range((wdt + 511) // 512)]
                                    S.add("dve", lambda e: e.tensor_reduce(out=amax[:, r:r + 1], in_=SC[:, off:off + wdt], axis=AX.X,
                                                                          op=ALU.max, apply_absolute_value=True),
                                          reads=sks, writes=[("amax", r)])
                                    dk = ("SC", r, i // 4)
                                    TT("pool", SC[:, off + i * 128:off + (i + 1) * 128], SC[:, off + i * 128:off + (i + 1) * 128],
                                       cmask[:], ALU.add, [dk, "cmask", ("amax", r)], [dk])
                                steps.append((None, fin))
                            return steps

                        def bis_steps(g):
                            steps = []
                            blocks = [r for r in range(4) if 4 * g + r >= 2]

                            def init():
                                aks = [("amax", r) for r in range(4)]
                                MS("dve", lo[:], -1.0e29, ["lo"])
                                MS("dve", thr[:], TOPK - 0.5, ["thr"])
                                MS("dve", cnt[:], 0.0, [("cnt", r) for r in range(4)])
                                for r in blocks:
                                    wdt = (4 * g + r + 1) * 128
                                    TS("dve", lo[:, r:r + 1], amax[:, r:r + 1], -1.0, None, ALU.mult, None, aks, ["lo"])
                                    if r % 2 == 1:
                                        MS("dve", thr[:, r:r + 1], float(2 * TOPK - 1 - wdt), ["thr"])
                                TS("dve", Wt[:, 0, :], amax[:], 1.0000005, 1.0e-30, ALU.mult, ALU.add, aks, ["Wt"])
                                for it in range(1, NIT):
                                    TS("dve", Wt[:, it, :], Wt[:, 0, :], float(2.0 ** (-it)), None, ALU.mult, None, ["Wt"], ["Wt"])
                                for r in range(4):
                                    if r not in blocks:
                                        MS("dve", Wt[:, :, r:r + 1], 0.0, ["Wt"])
                            steps.append(init)
                            if not blocks:
                                return steps
                            for it in range(NIT):
                                def step(it=it):
                                    TT("pool", mid[:], lo[:], Wt[:, it, :], ALU.add, ["lo", "Wt"], ["mid"])
                                    TS("pool", nmid[:], mid[:], -1.0, 0.0, ALU.mult, ALU.add, ["mid"], ["nmid"])
                                    for r in blocks:
                                        i = 4 * g + r
                                        off = blk_off(g, r)
                                        wdt = (i + 1) * 128
                                        sks = [("SC", r, kc) for kc in range((wdt + 511) // 512)]
                                        if r % 2 == 0:
                                            TS("dve", junkD[:, 0:wdt], SC[:, off:off + wdt], mid[:, r:r + 1], None, ALU.is_ge, ALU.add,
                                               sks + ["mid"], ["junkD", ("cnt", r)], accum=cnt[:, r:r + 1])
                                        else:
                                            ACT(junkA[:, 0:wdt], SC[:, off:off + wdt], AF.Sign, sks + ["nmid"], ["junkA", ("cnt", r)],
                                                bias=nmid[:, r:r + 1], accum=cnt[:, r:r + 1])
                                    cks = [("cnt", r) for r in range(4)]
                                    TT("pool", dd[:], cnt[:], thr[:], ALU.subtract, cks + ["thr"], ["dd"])
                                    TS("pool", dd[:], dd[:], 0.0, None, ALU.is_ge, None, ["dd"], ["dd"])
                                    TT("pool", tt_[:], dd[:], Wt[:, it, :], ALU.mult, ["dd", "Wt"], ["tt_"])
                                    TT("pool", lo[:], lo[:], tt_[:], ALU.add, ["lo", "tt_"], ["lo"])
                                steps.append(step)
                            return steps

                        def sel_steps(g):
                            steps = []
                            for r in range(4):
                                def step(r=r):
                                    i = 4 * g + r
                                    off = blk_off(g, r)
                                    wdt = (i + 1) * 128
                                    sks = [("SC", r, kc) for kc in range((wdt + 511) // 512)]
                                    TS("dve", selq[:, 0:wdt], SC[:, off:off + wdt], lo[:, r:r + 1], None, ALU.is_ge, None,
                                       sks + ["lo"], ["selq"])
                                    for m in range((i + 4) // 4):
                                        nk = min(4, i + 1 - 4 * m)
                                        b = nb()
                                        for kk in range(nk):
                                            kt = 4 * m + kk
                                            TR(PS(b, kk * 128, (kk + 1) * 128), selq[:, kt * 128:(kt + 1) * 128], identf[:],
                                               ["selq", "identf"], [PT_(b)])
                                        dst = selT[:, 4 * m:4 * m + nk, r * 128:(r + 1) * 128]
                                        src = PS(b, 0, nk * 128).rearrange("p (a n) -> p a n", a=nk)
                                        wk = [("selT", 4 * m + kk) for kk in range(nk)]
                                        if m % 2 == 0:
                                            ACT(dst, src, AF.Identity, [], [PT_(b)] + wk)
                                        else:
                                            CP("dve", dst, src, [], [PT_(b)] + wk)
                                steps.append(step)
                            if dtau is not None:
                                def dump():
                                    DMA("sp", dtau[:, 4 * g:4 * g + 4], lo[:], ["lo"], [("dtau", g)])
                                    DMA("sp", dsel[g], selT[:], [("selT", kt) for kt in range(NT)], [("dsel", g)])
                                steps.append(dump)
                            return steps

                        def attn_steps(g):
                            steps = []
                            nkt = 4 * g + 4
                            for j in range(4):
                                for kt in range(nkt):
                                    st = {}

                                    def stepA(j=j, kt=kt, st=st):
                                        q0 = max(0, kt - 4 * g) * 128
                                        N = 512 - q0
                                        bp = dpair()
                                        P_ = PTb[pt_ctr[0] % 4]
                                        ptk = ("PT", pt_ctr[0] % 4)
                                        pt_ctr[0] += 1
                                        st["P"] = P_
                                        st["ptk"] = ptk
                                        for hh in range(2):
                                            base = hh * 64
                                            PE(PS(bp + hh, 0, N), kT[base:base + 64, kt * 128:(kt + 1) * 128],
                                               qT[base:base + 64, j, g * 512 + q0:(g + 1) * 512], True, True,
                                               [("k", 0, kt // 4), ("q", j, g)], [PT_(bp + hh)])
                                        ACT(P_[:, :, 0:N], psum[:, bp:bp + 2, 0:N], AF.Exp, [], [ptk, PT_(bp), PT_(bp + 1)], scale=0.125)
                                        for hh in range(2):
                                            TT("dve", P_[:, hh, 0:N], P_[:, hh, 0:N], selT[:, kt, q0:512], ALU.mult,
                                               [ptk, ("selT", kt)], [ptk])

                                    def stepB(j=j, kt=kt, st=st):
                                        q0 = max(0, kt - 4 * g) * 128
                                        N = 512 - q0
                                        po = 6
                                        pd = 7
                                        P_ = st["P"]
                                        ptk = st["ptk"]
                                        for hh in range(2):
                                            base = hh * 64
                                            PE(psum[base:base + 64, po, q0:512], vtm[:, kt, base:base + 64], P_[:, hh, 0:N],
                                               kt == 0, kt == nkt - 1, [ptk, ("vtm", kt // 4)], [PT_(po)])
                                            PE(psum[base:base + 64, pd, q0:512], onesb[:, 0:64], P_[:, hh, 0:N],
                                               kt == 0, kt == nkt - 1, [ptk, "onesb"], [PT_(pd)])
                                        if kt == nkt - 1:
                                            R = rec[j % 2]
                                            rk = ("rec", j % 2)
                                            S.add("dve", lambda e: e.reciprocal(out=R[:], in_=PS(pd)), reads=[], writes=[rk, PT_(pd)])
                                            TT("dve", OT[:, j, g * 512:(g + 1) * 512], PS(po), R[:], ALU.mult, [rk], [PT_(po), ("OT", j, g)])
                                    steps.append((stepA, stepB))
                            return steps

                        def pipe(pairs, lag):
                            out = []
                            n = len(pairs)
                            for t_ in range(n + lag):
                                if t_ < n and pairs[t_][0] is not None:
                                    out.append(pairs[t_][0])
                                if t_ - lag >= 0 and pairs[t_ - lag][1] is not None:
                                    out.append(pairs[t_ - lag][1])
                            return out

                        def run(steps):
                            for st_ in steps:
                                st_()

                        def interleave(a, b):
                            na, nb_ = len(a), len(b)
                            ia = ib = 0
                            while ia < na or ib < nb_:
                                if ib >= nb_ or (ia < na and ia * nb_ <= ib * na):
                                    a[ia]()
                                    ia += 1
                                else:
                                    b[ib]()
                                    ib += 1

                        run(pipe(idx_steps(0), 1))
                        run(bis_steps(0))
                        run(sel_steps(0))
                        for g in range(1, NG):
                            run(pipe(idx_steps(g), 1))
                            interleave(bis_steps(g), pipe(attn_steps(g - 1), 2))
                            run(sel_steps(g))
                        run(pipe(attn_steps(NG - 1), 2))
                        if s == 0 and l == layers[0] and "OT" in dbg:
                            DMA("sp", dbg_tensor("OT", [128, 4, L], BF16), OT[:], [("OT", j, g) for j in range(4) for g in range(4)], ["dbg_OT"])
                        S.barrier()
                S.barrier()
                with ExitStack() as eC:
                    ypool = sb(eC, "ypool", [128, 2, L], BF16)
                    yconv = sb(eC, "yconv", [128, 2, L], BF16)
                    outw = sb(eC, "outw", [128, 8, 1024], BF16)
                    DMA("pool", outw[:], W[("outw", l)], [], ["outw"])
                    with ExitStack() as e1:
                        up = sb(e1, "up", [128, 2, 16 + L], F32)
                        sA = sb(e1, "sA", [128, 16 + L], F32)
                        sB = sb(e1, "sB", [128, 16 + L], F32)
                        mixed = sb(e1, "mixed", [128, 2, L], BF16)
                        t16 = sb(e1, "t16", [128, 16], F32)
                        pwb = sb(e1, "pwb", [128, 2, 128], BF16)
                        DMA("pool", pwb[:], W[("pwbd", l)], [], ["pwb"])
                        MS("pool", up[:, :, 0:16], 0.0, [("up", 0, -1), ("up", 1, -1)])
                        MS("pool", sA[:, 0:16], 0.0, ["sA"])
                        MS("pool", sB[:, 0:16], 0.0, ["sB"])
                        stm = Stream(l, CH_POOL)
                        for c in range(2):
                            sl = stm.get(c)
                            for g in range(NG):
                                b = nb()
                                proj_chunk(sl, g, b)
                                ACT(up[:, c, 16 + g * 512:16 + (g + 1) * 512], PS(b), AF.Identity, [pk], [PT_(b), ("up", c, g)],
                                    bias=par[:, PC_BA + c:PC_BA + c + 1])
                        for c in range(2):
                            uk = [("up", c, g) for g in range(-1, 4)]
                            U = up[:, c, :]
                            TT("dve", sA[:, 16:], U[:, 16:], U[:, 15:15 + L], ALU.add, uk, ["sA"])
                            TT("dve", sB[:, 16:], sA[:, 16:], sA[:, 14:14 + L], ALU.add, ["sA"], ["sB"])
                            if c == 1:
                                TT("dve", sA[:, 16:], sB[:, 16:], sB[:, 12:12 + L], ALU.add, ["sB"], ["sA"])
                                TT("dve", sB[:, 16:], sA[:, 16:], sA[:, 8:8 + L], ALU.add, ["sA"], ["sB"])
                            for half, (sbuf_, sk) in enumerate(((sA, "sA"), (sB, "sB"))):
                                widx = 2 * c + half
                                win = float(2 ** (widx + 1))
                                pr = slice(half * 64, half * 64 + 64)
                                STT(mixed[pr, c, 16:], sbuf_[pr, 32:], 1.0 / win, U[pr, 32:], ALU.mult, ALU.subtract, [sk] + uk,
                                    [("mixed", c, half)])
                                TT("dve", t16[pr, :], sbuf_[pr, 16:32], rct[pr, widx, :], ALU.mult, [sk, ("rct", widx)], ["t16"])
                                TT("dve", mixed[pr, c, 0:16], t16[pr, :], U[pr, 16:32], ALU.subtract, ["t16"] + uk, [("mixed", c, half)])
                        for c in range(2):
                            for g in range(NG):
                                b = nb()
                                PE(PS(b), pwb[:, c, :], mixed[:, c, g * 512:(g + 1) * 512], True, True,
                                   ["pwb", ("mixed", c, 0), ("mixed", c, 1)], [PT_(b)])
                                ACT(ypool[:, c, g * 512:(g + 1) * 512], PS(b), AF.Identity, [pk], [PT_(b), ("ypool", c, g)],
                                    scale=par[:, PC_PSC + c:PC_PSC + c + 1])
                        S.barrier()
                    with ExitStack() as e1:
                        glu = sb(e1, "glu", [128, 2, 30 + L], BF16)
                        dg = sb(e1, "dg", [128, 2, 31, 128], BF16)
                        xcv = sb(e1, "xcv", [128, 2, L], F32)
                        xsq = [sb(e1, "xsq%d" % i, [128, 512], F32) for i in range(2)]
                        sgt = [sb(e1, "sgt%d" % i, [128, 512], F32) for i in range(2)]
                        mean_t = sb(e1, "mean_t", [128, 512], F32)
                        var_t = sb(e1, "var_t", [128, 512], F32)
                        dtmp = [sb(e1, "dtmp%d" % i, [128, 512], F32) for i in range(2)]
                        MS("pool", glu[:, :, 0:30], 0.0, [("glu", 0, -1), ("glu", 1, -1)])
                        for c in range(2):
                            for jj in range(31):
                                TS("dve", dg[:, c, jj, :], identb[:], par[:, PC_CDW + c * 31 + jj:PC_CDW + c * 31 + jj + 1], None,
                                   ALU.mult, None, ["identb", pk], [("dg", c)])
                        stm = Stream(l, [CH_CA[0], CH_CG[0], CH_CA[1], CH_CG[1]])
                        cc = 0
                        for c in range(2):
                            sa_ = stm.get(2 * c)
                            sg_ = stm.get(2 * c + 1)
                            for g in range(NG):
                                ba = nb()
                                bg = nb()
                                proj_chunk(sa_, g, ba)
                                proj_chunk(sg_, g, bg)
                                T_ = sgt[cc % 2]
                                tk_ = ("sgt", cc % 2)
                                cc += 1
                                ACT(T_[:], PS(bg), AF.Sigmoid, [pk], [tk_, PT_(bg)], bias=par[:, PC_BA + CH_CG[c]:PC_BA + CH_CG[c] + 1])
                                STT(glu[:, c, 30 + g * 512:30 + (g + 1) * 512], PS(ba), par[:, PC_BA + CH_CA[c]:PC_BA + CH_CA[c] + 1],
                                    T_[:], ALU.add, ALU.mult, [pk, tk_], [PT_(ba), ("glu", c, g)])
                        for g in range(NG):
                            bm = nb()
                            bq = nb()
                            for c in range(2):
                                b = nb()
                                gk = [("glu", c, gg) for gg in range(-1, 4)]
                                for jj in range(31):
                                    PE(PS(b), dg[:, c, jj, :], glu[:, c, g * 512 + jj:g * 512 + jj + 512], jj == 0, jj == 30,
                                       [("dg", c)] + gk, [PT_(b)])
                                X2 = xsq[c]
                                ACT(xcv[:, c, g * 512:(g + 1) * 512], PS(b), AF.Identity, [pk], [PT_(b), ("xcv", c, g)],
                                    bias=par[:, PC_CDB + c:PC_CDB + c + 1])
                                ACT(X2[:], PS(b), AF.Square, [pk], [PT_(b), ("xsq", c)], bias=par[:, PC_CDB + c:PC_CDB + c + 1])
                            for c in range(2):
                                PE(PS(bm), onesf[:], xcv[:, c, g * 512:(g + 1) * 512], c == 0, c == 1, ["onesf", ("xcv", c, g)], [PT_(bm)])
                            for c in range(2):
                                PE(PS(bq), onesf[:], xsq[c][:], c == 0, c == 1, ["onesf", ("xsq", c)], [PT_(bq)])
                            TS("dve", mean_t[:], PS(bm), 1.0 / 256, None, ALU.mult, None, [], ["mean_t", PT_(bm)])
                            TT("dve", var_t[:], mean_t[:], mean_t[:], ALU.mult, ["mean_t"], ["var_t"])
                            STT(var_t[:], PS(bq), 1.0 / 256, var_t[:], ALU.mult, ALU.subtract, ["var_t"], ["var_t", PT_(bq)])
                            ACT(var_t[:], var_t[:], AF.Sqrt, ["var_t"], ["var_t"], bias=EPS)
                            S.add("dve", lambda e: e.reciprocal(out=var_t[:], in_=var_t[:]), reads=["var_t"], writes=["var_t"])
                            for c in range(2):
                                Dm = dtmp[c]
                                dk_ = ("dtmp", c)
                                TT("dve", Dm[:], xcv[:, c, g * 512:(g + 1) * 512], mean_t[:], ALU.subtract, [("xcv", c, g), "mean_t"], [dk_])
                                TT("pool", Dm[:], Dm[:], var_t[:], ALU.mult, [dk_, "var_t"], [dk_])
                                ACT(yconv[:, c, g * 512:(g + 1) * 512], Dm[:], AF.Silu, [dk_, pk], [("yconv", c, g)],
                                    bias=par[:, PC_CLB + c:PC_CLB + c + 1], scale=par[:, PC_CLG + c:PC_CLG + c + 1])
                        S.barrier()
                    if s == 0 and l == layers[0]:
                        if "ypool" in dbg:
                            DMA("sp", dbg_tensor("ypool", [128, 2, L], BF16), ypool[:], [("ypool", c, g) for c in range(2) for g in range(4)], ["dbg_ypool"])
                        if "yconv" in dbg:
                            DMA("sp", dbg_tensor("yconv", [128, 2, L], BF16), yconv[:], [("yconv", c, g) for c in range(2) for g in range(4)], ["dbg_yconv"])
                    with ExitStack() as e4:
                        wo = sb(e4, "wo", [128, 8, 1024], BF16)
                        DMA("pool", wo[:], W[("wo", l)], [], ["wo"])
                        sg3 = [sb(e4, "sg3_%d" % i, [128, 512], F32) for i in range(3)]
                        mm_ = [sb(e4, "mm_%d" % i, [128, 512], F32) for i in range(3)]
                        chunks = []
                        for c in range(8):
                            chunks += [CH_GATE0 + i * 8 + c for i in range(3)]
                        stm = Stream(l, chunks, ahead=1)
                        ysrc = [(ypool, "ypool", 0, 2), (yconv, "yconv", 2, 2), (OT, "OT", 4, 4)]
                        for c in range(8):
                            sl3 = [stm.get(3 * c + i) for i in range(3)]
                            for g in range(NG):
                                gs = slice(g * 512, (g + 1) * 512)
                                for i in range(3):
                                    bg = nb()
                                    proj_chunk(sl3[i], g, bg)
                                    ch = CH_GATE0 + i * 8 + c
                                    ACT(sg3[i][:], PS(bg), AF.Sigmoid, [pk], [("sg3", i), PT_(bg)], bias=par[:, PC_BA + ch:PC_BA + ch + 1])
                                for i, (ysb, yn, k0, nk) in enumerate(ysrc):
                                    by = nb()
                                    for k in range(nk):
                                        PE(PS(by), outw[:, k0 + k, c * 128:(c + 1) * 128], ysb[:, k, gs], k == 0, k == nk - 1,
                                           ["outw", (yn, k, g)], [PT_(by)])
                                    TT("dve", mm_[i][:], PS(by), sg3[i][:], ALU.mult, [("sg3", i)], [("mm_", i), PT_(by)])
                                TT("pool", mm_[0][:], mm_[0][:], mm_[1][:], ALU.add, [("mm_", 0), ("mm_", 1)], [("mm_", 0)])
                                TT("pool", merged[:, c, gs], mm_[0][:], mm_[2][:], ALU.add, [("mm_", 0), ("mm_", 2)], [("merged", c, g)])
                        with ExitStack() as e5:
                            ln_alloc(e5)
                            ln_rows_load(W[("lnr", l)][:, 0:2048])
                            def mm5(i):
                                bp = npair()
                                for n in range(2):
                                    for k in range(8):
                                        PE(PS(bp + n), merged[:, k, i * 128:(i + 1) * 128], wo[:, k, n * 512:(n + 1) * 512], k == 0, k == 7,
                                           ["wo", ("merged", k, i // 4)], [PT_(bp + n)])
                                return bp
                            ln_pipeline(s, list(range(NT)), mm5, False)
                            S.barrier()
                S.barrier()
            if s == 0 and l == layers[0] and "merged" in dbg:
                DMA("sp", dbg_tensor("merged", [128, 8, L], BF16), merged, [("merged", c, g) for c in range(8) for g in range(4)], ["dbg_merged"])
            if s == 0 and l == layers[0] and "hT1" in dbg:
                DMA("sp", dbg_tensor("hT1", [128, 8, L], BF16), hT[:], [("hT", g) for g in range(4)], ["dbg_hT1"])
            with ExitStack() as e6:
                wd = sb(e6, "wd", [128, NJ, 1024], BF16)
                ln_alloc(e6)
                NWU = 3
                wUb = [sb(e6, "wU%d" % i, [128, 8, 256], BF16) for i in range(NWU)]
                xg = [sb(e6, "xg%d" % i, [128, 2 + 1024], F32) for i in range(2)]
                xv = [sb(e6, "xv%d" % i, [128, 2 + 1024], F32) for i in range(2)]
                ug = [sb(e6, "ug%d" % i, [128, 1024], F32) for i in range(2)]
                uv = [sb(e6, "uv%d" % i, [128, 1024], F32) for i in range(2)]
                halo = sb(e6, "halo", [128, 2 * NJ, 2], F32)
                ln_rows_load(W[("lnr", l)][:, 2048:4096])
                wu_ctr = [0]
                wu_slots = {}

                def wu_get(idx):
                    while wu_ctr[0] <= min(idx + 1, 2 * NJ - 1):
                        n_ = wu_ctr[0]
                        sl = n_ % NWU
                        DMA("pool", wUb[sl][:], W[("wU", l)][n_ % NJ], [], [("wU", sl)], slot=("wU", sl))
                        wu_slots[n_] = sl
                        wu_ctr[0] += 1
                    return wu_slots[idx]

                wu_get(0)
                wd_parts = [(0, 6), (6, 11), (11, 17), (17, 22)]
                for kq, (k0_, k1_) in enumerate(wd_parts):
                    DMA("pool", wd[:, k0_:k1_, :], W[("wd", l)][:, k0_:k1_, :], [], [("wd", kq)])
                for hf in range(2):
                    t0 = hf * 1024
                    for j in range(NJ):
                        sl = wu_get(hf * NJ + j)
                        p_ = j % 2
                        XG, XV, UG, UV = xg[p_], xv[p_], ug[p_], uv[p_]
                        for (X, xk, coff, hidx) in ((XG, ("xg", p_), 0, j), (XV, ("xv", p_), 128, NJ + j)):
                            if hf == 0:
                                ACT(X[:, 0:2], zero2[:], AF.Identity, ["zero2"], [("xh",) + xk])
                            else:
                                ACT(X[:, 0:2], halo[:, hidx, :], AF.Identity, [("halo", hidx)], [("xh",) + xk])
                            for gg in range(2):
                                g = hf * 2 + gg
                                b = nb()
                                for k in range(8):
                                    PE(PS(b), wUb[sl][:, k, coff:coff + 128], hT[:, k, g * 512:(g + 1) * 512], k == 0, k == 7,
                                       [("wU", sl), ("hT", g)], [PT_(b)])
                                ACT(X[:, 2 + gg * 512:2 + (gg + 1) * 512], PS(b), AF.Identity, [], [PT_(b), xk])
                            if hf == 0:
                                ACT(halo[:, hidx, :], X[:, 1024:1026], AF.Identity, [xk], [("halo", hidx)])
                        for (X, xk, U, uk, wc, bc) in ((XG, ("xg", p_), UG, ("ug", p_), PC_FWG + 3 * j, PC_FBG + j),
                                                      (XV, ("xv", p_), UV, ("uv", p_), PC_FWV + 3 * j, PC_FBV + j)):
                            TS("dve", U[:], X[:, 2:1026], par[:, wc + 2:wc + 3], par[:, bc:bc + 1], ALU.mult, ALU.add, [xk, pk], [uk])
                            STT(U[:], X[:, 1:1025], par[:, wc + 1:wc + 2], U[:], ALU.mult, ALU.add, [xk, ("xh",) + xk, pk, uk], [uk])
                            STT(U[:], X[:, 0:1024], par[:, wc:wc + 1], U[:], ALU.mult, ALU.add, [xk, ("xh",) + xk, pk, uk], [uk])
                        ACT(UG[:], UG[:], AF.Silu, [("ug", p_)], [("ug", p_)])
                        TT("pool", actb[:, j, 0:1024], UG[:], UV[:], ALU.mult, [("ug", p_), ("uv", p_)], [("act", j)])
                    def mm6(i, hf=hf):
                        ii = i - hf * 8
                        bp = npair()
                        for n in range(2):
                            for k in range(NJ):
                                PE(PS(bp + n), actb[:, k, ii * 128:(ii + 1) * 128], wd[:, k, n * 512:(n + 1) * 512], k == 0, k == NJ - 1,
                                   [("wd", 0 if k < 6 else (1 if k < 11 else (2 if k < 17 else 3))), ("act", k)], [PT_(bp + n)])
                        return bp
                    ln_pipeline(s, [hf * 8 + ii for ii in range(8)], mm6, last_layer)
                S.barrier()
        if not last:
            pass
    S.emit()
    es_top.close()
    return nc, S, dbg_out


_CACHE = {}


def _in_maps(x, positions, wts, layers, ncores, nseq):
    maps = []
    for c in range(ncores):
        m = {"x": np.ascontiguousarray(x[c * nseq:(c + 1) * nseq]),
             "pos": np.ascontiguousarray(np.broadcast_to(positions[c * nseq:(c + 1) * nseq, None, :], (nseq, 128, L))).astype(np.int32),
             "lnin": wts["lnin"], "cst": wts["cst"]}
        for l in layers:
            for k in WSHAPES:
                m["%s%d" % (k, l)] = wts["%s%d" % (k, l)]
        maps.append(m)
    return maps


def kernel(**inputs):
    x = np.asarray(inputs["x"], np.float32)
    positions = np.asarray(inputs["positions"], np.int32)
    wts = prep_weights(inputs)
    ncores = 8
    nseq = x.shape[0] // ncores
    key = ("fused", nseq)
    if key not in _CACHE:
        _CACHE[key] = build(nseq=nseq, layers=(0, 1))
    nc, S, _ = _CACHE[key]
    maps = _in_maps(x, positions, wts, (0, 1), ncores, nseq)
    res = run_bass_kernel_spmd(nc, maps, core_ids=list(range(ncores)))
    out = np.concatenate([np.asarray(r["y"]) for r in res.results], axis=0)
    return out.astype(np.float32)
```

```python
import numpy as np
from contextlib import ExitStack
import concourse.bass as bass
import concourse.mybir as mybir
from concourse.bass_utils import run_bass_kernel_spmd

F32 = mybir.dt.float32
BF16 = mybir.dt.bfloat16
I32 = mybir.dt.int32
ALU = mybir.AluOpType
AF = mybir.ActivationFunctionType
AX = mybir.AxisListType

L = 2048
D = 1024
NT = 16
NG = 4
DFF = 2816
NJ = 22
DEPTH = 2
ALPHA = float((2 * DEPTH) ** 0.25)
EPS = 1e-5
TOPK = 256
NIT = 16
NEG = -1.0e30
SEM_LIMIT = 30000

CH_POOL = [0, 1]
CH_CA = [2, 3]
CH_CG = [4, 5]
CH_Q = [6, 7, 8, 9]
CH_QS = [10, 11, 12, 13]
CH_K = 14
CH_KS = 15
CH_V = 16
CH_QI = [17, 18, 19, 20]
CH_QIS = [21, 22, 23, 24]
CH_KI = 25
CH_KIS = 26
CH_GATE0 = 27
NCH = 51
PC_BA = 0
PC_PSC = 51
PC_CDB = 53
PC_CLG = 55
PC_CLB = 57
PC_CDW = 59
PC_FWG = 121
PC_FWV = 187
PC_FBG = 253
PC_FBV = 275
NPAR = 297


class _Op:
    __slots__ = ("idx", "eng", "fn", "deps_raw", "deps_other", "is_dma", "slot", "signal", "stream", "val", "vc")

    def __init__(self, idx, eng, fn, is_dma, slot):
        self.idx = idx
        self.eng = eng
        self.fn = fn
        self.is_dma = is_dma
        self.slot = slot
        self.deps_raw = set()
        self.deps_other = set()
        self.signal = False
        self.stream = None
        self.val = 0
        self.vc = None


class Sched:
    def __init__(self, nc):
        self.nc = nc
        self.ops = []
        self.last_writer = {}
        self.readers = {}
        self.last_dma_on_slot = {}
        self.last_on_eng = {}
        self.dma_since_barrier = []
        self.barrier_deps = set()

    def barrier(self):
        self.barrier_deps = set(self.last_on_eng.values()) | set(self.dma_since_barrier)
        self.dma_since_barrier = []
        self.last_writer = {}
        self.readers = {}

    def add(self, eng, fn, reads=(), writes=(), dma=False, slot=None):
        idx = len(self.ops)
        if dma and slot is None:
            slot = ("auto", tuple(writes)[0])
        op = _Op(idx, eng, fn, dma, slot)
        for r in reads:
            w = self.last_writer.get(r)
            if w is not None:
                op.deps_raw.add(w)
        for t in writes:
            w = self.last_writer.get(t)
            if w is not None:
                op.deps_other.add(w)
            for rd in self.readers.get(t, ()):
                op.deps_other.add(rd)
        if dma:
            p = self.last_dma_on_slot.get(slot)
            if p is not None:
                op.deps_raw.add(p)
            self.last_dma_on_slot[slot] = idx
            self.dma_since_barrier.append(idx)
        op.deps_other |= self.barrier_deps
        for r in reads:
            self.readers.setdefault(r, []).append(idx)
        for t in writes:
            self.last_writer[t] = idx
            self.readers[t] = []
        op.deps_other -= op.deps_raw
        self.last_on_eng[eng] = idx
        self.ops.append(op)
        return idx

    def _needed(self, op, d):
        dop = self.ops[d]
        if dop.is_dma or op.is_dma:
            return True
        if dop.eng != op.eng:
            return True
        if op.eng == "pe":
            return False
        return d in op.deps_raw

    def emit(self):
        nc = self.nc
        ops = self.ops
        for op in ops:
            for d in (op.deps_raw | op.deps_other):
                if self._needed(op, d):
                    ops[d].signal = True
        for op in ops:
            if op.is_dma:
                op.signal = True
        sems = {}
        counts = {}
        sem_objs = []

        def new_sem():
            cm = nc.semaphore("s%d" % len(sem_objs))
            h = cm.__enter__()
            sem_objs.append(cm)
            return h

        for op in ops:
            if not op.signal:
                continue
            key = ("slot", op.slot) if op.is_dma else ("eng", op.eng)
            inc = 16 if op.is_dma else 1
            if key not in sems:
                sems[key] = [new_sem()]
                counts[key] = 0
            if counts[key] + inc > SEM_LIMIT:
                sems[key].append(new_sem())
                counts[key] = 0
            counts[key] += inc
            op.stream = (key, len(sems[key]) - 1)
            op.val = counts[key]
        self.n_sems = len(sem_objs)
        eng_clock = {}
        waits = [None] * len(ops)
        for op in ops:
            clk = eng_clock.setdefault(op.eng, {})
            best = {}
            for d in sorted(op.deps_raw | op.deps_other):
                if not self._needed(op, d):
                    continue
                dop = ops[d]
                if clk.get(dop.stream, 0) >= dop.val:
                    continue
                if best.get(dop.stream, 0) < dop.val:
                    best[dop.stream] = dop.val
                for k, v in dop.vc.items():
                    if clk.get(k, 0) < v:
                        clk[k] = v
            waits[op.idx] = best
            if op.signal:
                vc = dict(clk)
                vc[op.stream] = op.val
                op.vc = vc
        per_eng = {}
        for op in ops:
            per_eng.setdefault(op.eng, []).append(op)
        final_waits = {}
        for op in ops:
            if op.is_dma:
                if final_waits.get(op.stream, 0) < op.val:
                    final_waits[op.stream] = op.val

        def semh(stream):
            key, ep = stream
            return sems[key][ep]

        engmap = {"pe": "tensor", "act": "scalar", "dve": "vector", "pool": "gpsimd", "sp": "sync"}
        self.n_waits = 0
        with nc.Block() as block:
            for ename in ("sp", "pe", "act", "dve", "pool"):
                lst = per_eng.get(ename, [])

                def body(e, lst=lst, ename=ename):
                    for op in lst:
                        for s, v in waits[op.idx].items():
                            e.wait_ge(semh(s), v)
                            self.n_waits += 1
                        ins = op.fn(e)
                        if op.signal:
                            ins.then_inc(semh(op.stream), 16 if op.is_dma else 1)
                    if ename == "sp":
                        for s, v in final_waits.items():
                            e.wait_ge(semh(s), v)
                getattr(block, engmap[ename])(body)
        for cm in reversed(sem_objs):
            cm.__exit__(None, None, None)
        return self


def _chunk_cols():
    cols = []
    ar = np.arange

    def hc(base, h):
        return base + 64 * h + ar(64)

    def sw(c):
        return np.concatenate([c[32:], c[:32]])

    cols += [ar(128), 128 + ar(128)]
    cols += [256 + 128 * i + ar(128) for i in range(4)]
    for j in range(4):
        cols.append(np.concatenate([hc(768, j), hc(768, 4 + j)]))
    for j in range(4):
        cols.append(np.concatenate([sw(hc(768, j)), sw(hc(768, 4 + j))]))
    cols.append(np.concatenate([hc(1280, 0), hc(1280, 1)]))
    cols.append(np.concatenate([sw(hc(1280, 0)), sw(hc(1280, 1))]))
    cols.append(1408 + ar(128))
    for j in range(4):
        cols.append(np.concatenate([hc(1536, 2 * j), hc(1536, 2 * j + 1)]))
    for j in range(4):
        cols.append(np.concatenate([sw(hc(1536, 2 * j)), sw(hc(1536, 2 * j + 1))]))
    ki = 2048 + ar(64)
    cols.append(np.concatenate([ki, ki]))
    cols.append(np.concatenate([sw(ki), sw(ki)]))
    for i in range(3):
        for c in range(8):
            cols.append(2120 + 1024 * i + 128 * c + ar(128))
    assert len(cols) == NCH
    return np.stack(cols)


def _kp(w):
    K = w.shape[0] // 128
    return np.ascontiguousarray(w.reshape(K, 128, w.shape[1]).transpose(1, 0, 2))


def prep_weights(inp):
    cols = _chunk_cols()
    out = {}
    f = np.float32
    rep = lambda v: np.ascontiguousarray(np.broadcast_to(np.asarray(v, f)[None, :], (128, v.shape[0])))
    for l in range(DEPTH):
        w_in = np.asarray(inp["w_in"][l], f)
        b_in = np.asarray(inp["b_in"][l], f)
        wg = w_in[:, cols.reshape(-1)].reshape(8, 128, NCH, 128)
        out["wA%d" % l] = np.ascontiguousarray(wg.transpose(2, 1, 0, 3))
        out["wwi%d" % l] = _kp(np.ascontiguousarray(w_in[:, 2112:2120]))
        par = np.zeros((128, NPAR), f)
        par[:, PC_BA:PC_BA + NCH] = b_in[cols].T
        par[:, PC_PSC:PC_PSC + 2] = np.asarray(inp["pool_scale"][l], f).reshape(2, 128).T
        par[:, PC_CDB:PC_CDB + 2] = np.asarray(inp["conv_dw_b"][l], f).reshape(2, 128).T
        par[:, PC_CLG:PC_CLG + 2] = np.asarray(inp["conv_ln_g"][l], f).reshape(2, 128).T
        par[:, PC_CLB:PC_CLB + 2] = np.asarray(inp["conv_ln_b"][l], f).reshape(2, 128).T
        cdw = np.asarray(inp["conv_dw_w"][l], f)
        par[:, PC_CDW:PC_CDW + 62] = cdw.reshape(31, 2, 128).transpose(2, 1, 0).reshape(128, 62)
        fw = np.asarray(inp["ffn_dw_w"][l], f)
        par[:, PC_FWG:PC_FWG + 66] = fw[:, :DFF].reshape(3, NJ, 128).transpose(2, 1, 0).reshape(128, 66)
        par[:, PC_FWV:PC_FWV + 66] = fw[:, DFF:].reshape(3, NJ, 128).transpose(2, 1, 0).reshape(128, 66)
        fb = np.asarray(inp["ffn_dw_b"][l], f)
        par[:, PC_FBG:PC_FBG + NJ] = fb[:DFF].reshape(NJ, 128).T
        par[:, PC_FBV:PC_FBV + NJ] = fb[DFF:].reshape(NJ, 128).T
        out["par%d" % l] = par
        out["bwi%d" % l] = rep(np.tile(b_in[2112:2120], 16))
        pw = np.asarray(inp["pool_w"][l], f)
        bd = np.zeros((128, 2, 128), f)
        for c in range(2):
            bd[0:64, c, 0:64] = pw[2 * c]
            bd[64:128, c, 64:128] = pw[2 * c + 1]
        out["pwbd%d" % l] = bd
        wao = np.asarray(inp["w_attn_out"][l], f)
        rows = np.concatenate([np.concatenate([64 * j + np.arange(64), 64 * (4 + j) + np.arange(64)]) for j in range(4)])
        ow = np.concatenate([_kp(np.asarray(inp["w_pool_out"][l], f)), _kp(np.asarray(inp["w_conv_out"][l], f)),
                             _kp(np.ascontiguousarray(wao[rows]))], axis=1)
        out["outw%d" % l] = np.ascontiguousarray(ow)
        out["wo%d" % l] = _kp(np.asarray(inp["w_o"][l], f))
        wu = np.asarray(inp["w_up"][l], f)
        wug = wu[:, :DFF].reshape(8, 128, NJ, 128)
        wuv = wu[:, DFF:].reshape(8, 128, NJ, 128)
        out["wU%d" % l] = np.ascontiguousarray(np.concatenate([wug, wuv], axis=3).transpose(2, 1, 0, 3))
        out["wd%d" % l] = _kp(np.asarray(inp["w_down"][l], f))
        out["lnr%d" % l] = np.ascontiguousarray(np.concatenate(
            [rep(inp["ln1_g"][l]), rep(inp["ln1_b"][l]), rep(inp["ln2_g"][l]), rep(inp["ln2_b"][l])], axis=1))
    out["lnin"] = np.ascontiguousarray(np.concatenate([rep(inp["ln_in_g"]), rep(inp["ln_in_b"])], axis=1))
    half = 32
    invf = (10000.0 ** (-np.arange(half, dtype=np.float32) / half)).astype(f)
    cst = np.zeros((128, 2), f)
    cst[:, 0] = np.tile(invf, 4)
    cst[:, 1] = np.tile(np.concatenate([-np.ones(32, f), np.ones(32, f)]), 2)
    out["cst"] = cst
    return out


WSHAPES = {
    "wA": [NCH, 128, 8, 128], "wwi": [128, 8, 8], "par": [128, NPAR], "bwi": [128, 128], "pwbd": [128, 2, 128],
    "outw": [128, 8, 1024], "wo": [128, 8, 1024], "wU": [NJ, 128, 8, 256], "wd": [128, NJ, 1024], "lnr": [128, 4096],
}


def build(nseq=2, layers=(0, 1), dbg=None, first=True, last=True):
    dbg = dbg or set()
    nc = bass.Bass("TRN2", target_bir_lowering=False)
    dt = {}
    xin = nc.dram_tensor("x", [nseq, L, D], F32, kind="ExternalInput").ap()
    posin = nc.dram_tensor("pos", [nseq, 128, L], I32, kind="ExternalInput").ap()
    W = {}
    for l in layers:
        for k, shp in WSHAPES.items():
            W[(k, l)] = nc.dram_tensor("%s%d" % (k, l), shp, F32, kind="ExternalInput").ap()
    lnin = nc.dram_tensor("lnin", [128, 2048], F32, kind="ExternalInput").ap()
    cstin = nc.dram_tensor("cst", [128, 2], F32, kind="ExternalInput").ap()
    yout = nc.dram_tensor("y", [nseq, L, D], F32, kind="ExternalOutput").ap()
    hres = nc.dram_tensor("hres", [L, D], F32, kind="Internal").ap()
    ropd = nc.dram_tensor("ropd", [128, 2, L], F32, kind="Internal").ap()
    dbg_out = {}

    def dbg_tensor(name, shape, dtype):
        dbg_out[name] = nc.dram_tensor("dbg_" + name, shape, dtype, kind="ExternalOutput").ap()
        return dbg_out[name]

    S = Sched(nc)
    es_top = ExitStack()

    uid = [0]

    def sb(es, name, shape, dtype):
        uid[0] += 1
        return es.enter_context(nc.sbuf_tensor("sb%d_%s" % (uid[0], name), shape, dtype))

    def PE(out, lhsT, rhs, st, sp, r, w):
        S.add("pe", lambda e: e.matmul(out, lhsT=lhsT, rhs=rhs, start=st, stop=sp), reads=r, writes=w)

    def TR(out, in_, ident, r, w):
        S.add("pe", lambda e: e.transpose(out=out, in_=in_, identity=ident), reads=r, writes=w)

    def ACT(out, in_, func, r, w, bias=0.0, scale=1.0, accum=None):
        if accum is None:
            S.add("act", lambda e: e.activation(out=out, in_=in_, func=func, bias=bias, scale=scale), reads=r, writes=w)
        else:
            S.add("act", lambda e: e.activation(out=out, in_=in_, func=func, bias=bias, scale=scale, accum_out=accum),
                  reads=r, writes=w)

    def TT(eng, out, a, b, op, r, w):
        S.add(eng, lambda e: e.tensor_tensor(out=out, in0=a, in1=b, op=op), reads=r, writes=w)

    def TS(eng, out, a, s1, s2, op0, op1, r, w, accum=None):
        if accum is None:
            if op1 is None:
                S.add(eng, lambda e: e.tensor_scalar(out=out, in0=a, scalar1=s1, scalar2=None, op0=op0), reads=r, writes=w)
            else:
                S.add(eng, lambda e: e.tensor_scalar(out=out, in0=a, scalar1=s1, scalar2=s2, op0=op0, op1=op1),
                      reads=r, writes=w)
        else:
            S.add(eng, lambda e: e.tensor_scalar(out=out, in0=a, scalar1=s1, scalar2=s2, op0=op0, op1=op1,
                                                 accum_out=accum), reads=r, writes=w)

    def STT(out, a, s, b, op0, op1, r, w):
        S.add("dve", lambda e: e.scalar_tensor_tensor(out=out, in0=a, scalar=s, in1=b, op0=op0, op1=op1),
              reads=r, writes=w)

    def CP(eng, out, in_, r, w):
        S.add(eng, lambda e: e.tensor_copy(out=out, in_=in_), reads=r, writes=w)

    def MS(eng, out, v, w):
        S.add(eng, lambda e: e.memset(out, v), writes=w)

    def DMA(q, out, in_, r, w, slot=None):
        if q == "pool":
            S.add(q, lambda e: e.dma_start(out=out, in_=in_, max_dma_last_dim=4096), reads=r, writes=w, dma=True, slot=slot)
        else:
            S.add(q, lambda e: e.dma_start(out=out, in_=in_), reads=r, writes=w, dma=True, slot=slot)

    es = es_top
    psum = es.enter_context(nc.psum_tensor("psum", [128, 8, 512], F32))
    identf = sb(es, "identf", [128, 128], F32)
    identb = sb(es, "identb", [128, 128], BF16)
    onesf = sb(es, "onesf", [128, 128], F32)
    onesb = sb(es, "onesb", [128, 64], BF16)
    cmask = sb(es, "cmask", [128, 128], F32)
    cst = sb(es, "cst", [128, 2], F32)
    rct = sb(es, "rct", [128, 4, 16], F32)
    zero2 = sb(es, "zero2", [128, 2], F32)
    hT = sb(es, "hT", [128, 8, L], BF16)
    arena = sb(es, "arena", [128, NJ * 1024], BF16)
    NWA = 4
    wAb = [sb(es, "wA%d" % i, [128, 8, 128], BF16) for i in range(NWA)]
    parb = {l: sb(es, "par%d" % l, [128, NPAR], F32) for l in layers}
    LN = {}

    def ln_alloc(esx):
        LN["row"] = sb(esx, "lnrow", [128, 2, 1024], F32)
        LN["t"] = [sb(esx, "lnt%d" % i, [128, 1024], F32) for i in range(5)]
        LN["st"] = [sb(esx, "lnst%d" % i, [128, 2, 6], F32) for i in range(5)]
        LN["mv"] = [sb(esx, "lnmv%d" % i, [128, 8], F32) for i in range(5)]

    def PS(b, n0=0, n1=512):
        return psum[:, b, n0:n1]

    def PT_(b):
        return ("ps", b)

    bank_ctr = [0]

    def nb():
        b = bank_ctr[0] % 8
        bank_ctr[0] += 1
        return b

    pair_ctr = [0]

    def npair():
        b = (pair_ctr[0] % 4) * 2
        pair_ctr[0] += 1
        return b

    MS("pool", identf[:], 0.0, ["identf"])
    S.add("pool", lambda e: e.affine_select(out=identf[:], in_=identf[:], pattern=[[-1, 128]], compare_op=ALU.not_equal,
                                            fill=1.0, base=0, channel_multiplier=1), reads=["identf"], writes=["identf"])
    CP("dve", identb[:], identf[:], ["identf"], ["identb"])
    MS("dve", onesf[:], 1.0, ["onesf"])
    MS("dve", onesb[:], 1.0, ["onesb"])
    MS("pool", cmask[:], 0.0, ["cmask"])
    S.add("pool", lambda e: e.affine_select(out=cmask[:], in_=cmask[:], pattern=[[-1, 128]], compare_op=ALU.is_ge,
                                            fill=NEG, base=0, channel_multiplier=1), reads=["cmask"], writes=["cmask"])
    MS("dve", zero2[:], 0.0, ["zero2"])
    DMA("sp", cst[:], cstin, [], ["cst"])
    for l in layers:
        DMA("sp", parb[l][:], W[("par", l)], [], [("par", l)])
    with ExitStack() as es0:
        ti = sb(es0, "ti", [128, 16], I32)
        tf = sb(es0, "tf", [128, 16], F32)
        S.add("pool", lambda e: e.iota(ti[:], pattern=[[1, 16]], base=1, channel_multiplier=0), writes=["ti"])
        CP("dve", tf[:], ti[:], ["ti"], ["tf"])
        for wi_ in range(4):
            TS("dve", rct[:, wi_, :], tf[:], float(2 ** (wi_ + 1)), None, ALU.min, None, ["tf"], [("rct", wi_)])
            S.add("dve", lambda e, wi_=wi_: e.reciprocal(out=rct[:, wi_, :], in_=rct[:, wi_, :]),
                  reads=[("rct", wi_)], writes=[("rct", wi_)])
        S.barrier()

    wa_ctr = [0]

    def load_chunk(l, c):
        slot = wa_ctr[0] % NWA
        wa_ctr[0] += 1
        DMA("pool", wAb[slot][:], W[("wA", l)][c], [], [("wA", slot)], slot=("wA", slot))
        return slot

    class Stream:
        def __init__(self, l, chunks, ahead=2):
            self.l = l
            self.chunks = list(chunks)
            self.slots = {}
            self.next = 0
            self.ahead = ahead

        def get(self, i):
            while self.next < len(self.chunks) and self.next <= i + self.ahead:
                self.slots[self.next] = load_chunk(self.l, self.chunks[self.next])
                self.next += 1
            return self.slots[i]

    def proj_chunk(slot, g, bank, ncols=128):
        for k in range(8):
            PE(PS(bank)[0:ncols, :], wAb[slot][:, k, 0:ncols], hT[:, k, g * 512:(g + 1) * 512], k == 0, k == 7,
               [("wA", slot), ("hT", g)], [PT_(bank)])

    def ln_rows_load(src_ap):
        DMA("sp", LN["row"][:], src_ap.rearrange("p (a n) -> p a n", a=2), [], ["lnrow"])

    def ln_load(s, i, x_src=None):
        t = LN["t"][i % 5]
        tk = ("lnt", i % 5)
        if x_src is not None:
            DMA("sp", t[:], x_src, [], [tk], slot=("lnt_in", i % 5))
        else:
            DMA("sp", t[:], hres[i * 128:(i + 1) * 128, :], [("hres", i)], [tk], slot=("lnt_in", i % 5))

    def ln_a(s, i, mixbank):
        t = LN["t"][i % 5]
        tk = ("lnt", i % 5)
        st = LN["st"][i % 5]
        mv = LN["mv"][i % 5]
        mk = ("lnmv", i % 5)
        if mixbank is not None:
            STT(t[:].rearrange("p (a n) -> p a n", a=2), t[:].rearrange("p (a n) -> p a n", a=2), ALPHA,
                psum[:, mixbank:mixbank + 2, :], ALU.mult, ALU.add, [tk], [tk, PT_(mixbank), PT_(mixbank + 1)])
        for a in range(2):
            S.add("dve", lambda e, a=a: e.bn_stats(out=st[:, a, :], in_=t[:, a * 512:(a + 1) * 512]), reads=[tk],
                  writes=[("lnst", i % 5, a)])
        S.add("dve", lambda e: e.bn_aggr(out=mv[:, 0:2], in_=st[:].rearrange("p a s -> p (a s)")),
              reads=[("lnst", i % 5, 0), ("lnst", i % 5, 1)], writes=[mk])
        ACT(mv[:, 2:3], mv[:, 1:2], AF.Sqrt, [mk], [mk], bias=EPS)
        S.add("dve", lambda e: e.reciprocal(out=mv[:, 3:4], in_=mv[:, 2:3]), reads=[mk], writes=[mk])
        TS("dve", mv[:, 4:5], mv[:, 0:1], mv[:, 3:4], -1.0, ALU.mult, ALU.mult, [mk], [mk])

    def ln_bc(s, i, final):
        lnrow = LN["row"]
        t = LN["t"][i % 5]
        tk = ("lnt", i % 5)
        mv = LN["mv"][i % 5]
        mk = ("lnmv", i % 5)
        ACT(t[:], t[:], AF.Identity, [tk, mk], [tk], bias=mv[:, 4:5], scale=mv[:, 3:4])
        TT("dve", t[:], t[:], lnrow[:, 0, :], ALU.mult, [tk, "lnrow"], [tk])
        TT("pool", t[:], t[:], lnrow[:, 1, :], ALU.add, [tk, "lnrow"], [tk])
        if final:
            DMA("sp", yout[s, i * 128:(i + 1) * 128, :], t[:], [tk], [("yout", i)], slot=("lnt_out", i % 5))
        else:
            DMA("sp", hres[i * 128:(i + 1) * 128, :], t[:], [tk], [("hres", i)], slot=("lnt_out", i % 5))

    def ln_btr(s, i, final):
        t = LN["t"][i % 5]
        tk = ("lnt", i % 5)
        if not final:
            g = i // 4
            for hb in range(2):
                b = nb()
                for kk in range(4):
                    k = hb * 4 + kk
                    TR(PS(b, kk * 128, (kk + 1) * 128), t[:, k * 128:(k + 1) * 128], identf[:], [tk, "identf"], [PT_(b)])
                dst = hT[:, hb * 4:(hb + 1) * 4, i * 128:(i + 1) * 128]
                src = PS(b).rearrange("p (a n) -> p a n", a=4)
                if hb == 0:
                    ACT(dst, src, AF.Identity, [], [PT_(b), ("hT", g)])
                else:
                    CP("dve", dst, src, [], [PT_(b), ("hT", g)])

    def ln_pipeline(s, tiles, mm_fn, final, x_src_fn=None):
        n = len(tiles)
        ln_load(s, tiles[0], None if x_src_fn is None else x_src_fn(tiles[0]))
        for idx in range(n + 3):
            if idx + 1 < n:
                ln_load(s, tiles[idx + 1], None if x_src_fn is None else x_src_fn(tiles[idx + 1]))
            if 0 <= idx - 1 < n:
                ln_bc(s, tiles[idx - 1], final)
            if 0 <= idx - 3 < n:
                ln_btr(s, tiles[idx - 3], final)
            if idx < n:
                bp = mm_fn(tiles[idx]) if mm_fn is not None else None
                ln_a(s, tiles[idx], bp)

    for s in range(nseq):
        with ExitStack() as e0:
            posi = sb(e0, "posi", [128, L], I32)
            ang = sb(e0, "ang", [128, L], F32)
            t1 = sb(e0, "t1", [128, L], F32)
            t2i = sb(e0, "t2i", [128, L], I32)
            t3 = sb(e0, "t3", [128, L], F32)
            rop = sb(e0, "rop", [128, 2, L], F32)
            DMA("sp", posi[:], posin[s], [], ["posi"])
            CP("dve", ang[:], posi[:], ["posi"], ["ang"])
            TS("dve", ang[:], ang[:], cst[:, 0:1], None, ALU.mult, None, ["ang", "cst"], ["ang"])
            C1 = 6.28125
            C2 = float(2.0 * np.pi - 6.28125)
            for which in range(2):
                if which == 0:
                    TS("dve", t1[:], ang[:], float(np.pi / 2), None, ALU.add, None, ["ang"], ["t1"])
                    src = t1
                    sk = "t1"
                else:
                    src = ang
                    sk = "ang"
                TS("dve", t3[:], src[:], float(1.0 / (2 * np.pi)), None, ALU.mult, None, [sk], ["t3"])
                CP("dve", t2i[:], t3[:], ["t3"], ["t2i"])
                CP("dve", t3[:], t2i[:], ["t2i"], ["t3"])
                STT(t1[:], t3[:], -C1, src[:], ALU.mult, ALU.add, ["t3", sk], ["t1"])
                STT(t1[:], t3[:], -C2, t1[:], ALU.mult, ALU.add, ["t3", "t1"], ["t1"])
                TS("dve", t3[:], t1[:], float(np.pi), float(-2 * np.pi), ALU.is_gt, ALU.mult, ["t1"], ["t3"])
                TT("dve", t1[:], t1[:], t3[:], ALU.add, ["t1", "t3"], ["t1"])
                TS("dve", t3[:], t1[:], float(-np.pi), float(2 * np.pi), ALU.is_lt, ALU.mult, ["t1"], ["t3"])
                TT("dve", t1[:], t1[:], t3[:], ALU.add, ["t1", "t3"], ["t1"])
                TS("dve", t1[:], t1[:], 3.1415925, -3.1415925, ALU.min, ALU.max, ["t1"], ["t1"])
                if which == 0:
                    ACT(rop[:, 0, :], t1[:], AF.Sin, ["t1"], [("rop", 0)])
                else:
                    ACT(rop[:, 1, :], t1[:], AF.Sin, ["t1", "cst"], [("rop", 1)], scale=cst[:, 1:2])
            DMA("sp", ropd, rop[:], [("rop", 0), ("rop", 1)], ["ropd"])
            if s == 0 and "rop" in dbg:
                DMA("sp", dbg_tensor("rop", [128, 2, L], F32), rop[:], [("rop", 0), ("rop", 1)], ["dbg_rop"])
            S.barrier()
        e0b = ExitStack()
        ln_alloc(e0b)
        if first:
            ln_rows_load(lnin)
            ln_pipeline(s, list(range(NT)), None, False, x_src_fn=lambda i_: xin[s, i_ * 128:(i_ + 1) * 128, :])
        else:
            for i in range(NT):
                t = LN["t"][i % 2]
                tk = ("lnt", i % 2)
                DMA("sp", t[:], xin[s, i * 128:(i + 1) * 128, :], [], [tk], slot=("lnt_in", i % 2))
                DMA("sp", hres[i * 128:(i + 1) * 128, :], t[:], [tk], [("hres", i)], slot=("lnt_out", i % 2))
                for hb in range(2):
                    b = nb()
                    for kk in range(4):
                        k = hb * 4 + kk
                        TR(PS(b, kk * 128, (kk + 1) * 128), t[:, k * 128:(k + 1) * 128], identf[:], [tk, "identf"], [PT_(b)])
                    CP("dve", hT[:, hb * 4:(hb + 1) * 4, i * 128:(i + 1) * 128], PS(b).rearrange("p (a n) -> p a n", a=4),
                       [], [PT_(b), ("hT", i // 4)])
        if s == 0 and "hT0" in dbg:
            DMA("sp", dbg_tensor("hT0", [128, 8, L], BF16), hT[:], [("hT", g) for g in range(4)], ["dbg_hT0"])
        S.barrier()
        e0b.close()

        for l in layers:
            par = parb[l]
            pk = ("par", l)
            last_layer = (l == layers[-1])
            qT = arena[:, 0:4 * L].rearrange("p (c t) -> p c t", c=4)
            qiT = arena[:, 4 * L:8 * L].rearrange("p (c t) -> p c t", c=4)
            merged = arena[:, 0:8 * L].rearrange("p (c t) -> p c t", c=8)
            actb = arena[:, 0:NJ * 1024].rearrange("p (c t) -> p c t", c=NJ)
            with ExitStack() as eA:
                OT = sb(eA, "OT", [128, 4, L], BF16)
                with ExitStack() as eB:
                    kT = sb(eB, "kT", [128, L], BF16)
                    vtm = sb(eB, "vtm", [128, NT, 128], BF16)
                    kiT = sb(eB, "kiT", [128, L], BF16)
                    witm = sb(eB, "witm", [128, NT * 8], F32)
                    with ExitStack() as e2:
                        rop = sb(e2, "rop2", [128, 2, L], F32)
                        vT = sb(e2, "vT", [128, L], F32)
                        ta = [sb(e2, "ta%d" % i, [128, 512], F32) for i in range(2)]
                        tb = [sb(e2, "tb%d" % i, [128, 512], F32) for i in range(2)]
                        wwib = sb(e2, "wwib", [128, 8, 8], BF16)
                        bwib = sb(e2, "bwib", [128, 128], F32)
                        DMA("sp", rop[:], ropd, ["ropd"], ["rop"])
                        DMA("pool", wwib[:], W[("wwi", l)], [], ["wwib"])
                        DMA("sp", bwib[:], W[("bwi", l)], [], ["bwib"])
                        pairs = []
                        for j in range(4):
                            pairs.append((CH_Q[j], CH_QS[j], ("q", j)))
                        pairs.append((CH_K, CH_KS, ("k", 0)))
                        for j in range(4):
                            pairs.append((CH_QI[j], CH_QIS[j], ("qi", j)))
                        pairs.append((CH_KI, CH_KIS, ("ki", 0)))
                        chunks = []
                        for a_, b_, _ in pairs:
                            chunks += [a_, b_]
                        chunks.append(CH_V)
                        stm = Stream(l, chunks)
                        cnt2 = 0
                        for pi, (ca, cs, (kind, j)) in enumerate(pairs):
                            sa = stm.get(2 * pi)
                            ss = stm.get(2 * pi + 1)
                            for g in range(NG):
                                ba = nb()
                                bs_ = nb()
                                proj_chunk(sa, g, ba)
                                proj_chunk(ss, g, bs_)
                                A_ = ta[cnt2 % 2]
                                B_ = tb[cnt2 % 2]
                                ak = ("ta", cnt2 % 2)
                                bk = ("tb", cnt2 % 2)
                                cnt2 += 1
                                gs = slice(g * 512, (g + 1) * 512)
                                STT(A_[:], PS(ba), par[:, PC_BA + ca:PC_BA + ca + 1], rop[:, 0, gs], ALU.add, ALU.mult,
                                    [pk, "rop"], [ak, PT_(ba)])
                                ACT(B_[:], PS(bs_), AF.Identity, [pk], [bk, PT_(bs_)], bias=par[:, PC_BA + cs:PC_BA + cs + 1])
                                TT("pool", B_[:], B_[:], rop[:, 1, gs], ALU.mult, [bk, "rop"], [bk])
                                if kind == "q":
                                    dst = qT[:, j, gs]
                                elif kind == "k":
                                    dst = kT[:, gs]
                                elif kind == "qi":
                                    dst = qiT[:, j, gs]
                                else:
                                    dst = kiT[:, gs]
                                TT("dve", dst, A_[:], B_[:], ALU.add, [ak, bk], [(kind, j, g)])
                        sv = stm.get(len(chunks) - 1)
                        for g in range(NG):
                            b = nb()
                            proj_chunk(sv, g, b)
                            ACT(vT[:, g * 512:(g + 1) * 512], PS(b), AF.Identity, [pk], [("vT", g), PT_(b)],
                                bias=par[:, PC_BA + CH_V:PC_BA + CH_V + 1])
                        for g in range(NG):
                            b = nb()
                            for kk in range(4):
                                i = g * 4 + kk
                                TR(PS(b, kk * 128, (kk + 1) * 128), vT[:, i * 128:(i + 1) * 128], identf[:],
                                   [("vT", g), "identf"], [PT_(b)])
                            CP("dve", vtm[:, g * 4:(g + 1) * 4, :], PS(b).rearrange("p (a n) -> p a n", a=4), [],
                               [PT_(b), ("vtm", g)])
                        b = nb()
                        for i in range(NT):
                            for k in range(8):
                                PE(PS(b, i * 8, (i + 1) * 8), hT[:, k, i * 128:(i + 1) * 128], wwib[:, k, :], k == 0, k == 7,
                                   ["wwib", ("hT", i // 4)], [PT_(b)])
                        TT("dve", witm[:], PS(b, 0, 128), bwib[:], ALU.add, ["bwib"], [PT_(b), "witm"])
                        if s == 0 and l == layers[0]:
                            if "qT" in dbg:
                                DMA("sp", dbg_tensor("qT", [128, 4, L], BF16), qT, [("q", j, g) for j in range(4) for g in range(4)], ["dbg_qT"])
                            if "kT" in dbg:
                                DMA("sp", dbg_tensor("kT", [128, L], BF16), kT[:], [("k", 0, g) for g in range(4)], ["dbg_kT"])
                            if "kiT" in dbg:
                                DMA("sp", dbg_tensor("kiT", [128, L], BF16), kiT[:], [("ki", 0, g) for g in range(4)], ["dbg_kiT"])
                            if "qiT" in dbg:
                                DMA("sp", dbg_tensor("qiT", [128, 4, L], BF16), qiT, [("qi", j, g) for j in range(4) for g in range(4)], ["dbg_qiT"])
                            if "vtm" in dbg:
                                DMA("sp", dbg_tensor("vtm", [128, NT, 128], BF16), vtm[:], [("vtm", g) for g in range(4)], ["dbg_vtm"])
                            if "witm" in dbg:
                                DMA("sp", dbg_tensor("witm", [128, 128], F32), witm[:], ["witm"], ["dbg_witm"])
                        S.barrier()

                    with ExitStack() as e3:
                        SCW = 58 * 128
                        SC = sb(e3, "SC", [128, SCW], F32)
                        selT = sb(e3, "selT", [128, NT, 512], BF16)
                        selq = sb(e3, "selq", [128, L], F32)
                        junkD = sb(e3, "junkD", [128, L], BF16)
                        junkA = sb(e3, "junkA", [128, L], BF16)
                        rt = [sb(e3, "rt%d" % i, [128, 2, 512], BF16) for i in range(4)]
                        Dg = [sb(e3, "Dg%d" % i, [128, 8, 128], BF16) for i in range(2)]
                        PTb = [sb(e3, "PT%d" % i, [128, 2, 512], BF16) for i in range(4)]
                        rec = [sb(e3, "rec%d" % i, [128, 512], F32) for i in range(2)]
                        amax = sb(e3, "amax", [128, 4], F32)
                        lo = sb(e3, "lo", [128, 4], F32)
                        mid = sb(e3, "mid", [128, 4], F32)
                        nmid = sb(e3, "nmid", [128, 4], F32)
                        cnt = sb(e3, "cnt", [128, 4], F32)
                        thr = sb(e3, "thr", [128, 4], F32)
                        dd = sb(e3, "dd", [128, 4], F32)
                        tt_ = sb(e3, "tt_", [128, 4], F32)
                        Wt = sb(e3, "Wt", [128, NIT, 4], F32)
                        rt_ctr = [0]
                        pt_ctr = [0]
                        dp_ctr = [0]

                        if s == 0 and l == layers[0] and "tau" in dbg:
                            dtau = dbg_tensor("tau", [128, NT], F32)
                            dsel = dbg_tensor("selT", [4, 128, NT, 512], BF16)
                        else:
                            dtau = None

                        def blk_off(g, r):
                            return sum((4 * g + rr + 1) * 128 for rr in range(r))

                        sb_ctr = [0]

                        def dpair():
                            b = (dp_ctr[0] % 3) * 2
                            dp_ctr[0] += 1
                            return b

                        def idx_steps(g):
                            steps = []
                            for r in range(4):
                                i = 4 * g + r
                                off = blk_off(g, r)
                                wdt = (i + 1) * 128

                                def mk_dg(i=i):
                                    D_ = Dg[i % 2]
                                    for h in range(8):
                                        TS("dve", D_[:, h, :], identb[:], witm[:, i * 8 + h:i * 8 + h + 1], None, ALU.mult, None,
                                           ["identb", "witm"], [("Dg", i % 2)])
                                steps.append((mk_dg, None))
                                for kc in range((wdt + 511) // 512):
                                    ncols = min(512, wdt - kc * 512)
                                    sbank = [None]
                                    for jh in range(4):
                                        st = {}

                                        def stepA(i=i, kc=kc, ncols=ncols, jh=jh, st=st):
                                            bp = dpair()
                                            st["R"] = rt[rt_ctr[0] % 4]
                                            st["rk"] = ("rt", rt_ctr[0] % 4)
                                            rt_ctr[0] += 1
                                            for hh in range(2):
                                                base = hh * 64
                                                PE(PS(bp + hh, 0, ncols), qiT[base:base + 64, jh, i * 128:(i + 1) * 128],
                                                   kiT[base:base + 64, kc * 512:kc * 512 + ncols], True, True,
                                                   [("qi", jh, i // 4), ("ki", 0, kc)], [PT_(bp + hh)])
                                            ACT(st["R"][:, :, 0:ncols], psum[:, bp:bp + 2, 0:ncols], AF.Relu, [],
                                                [st["rk"], PT_(bp), PT_(bp + 1)])

                                        def stepB(i=i, off=off, kc=kc, ncols=ncols, jh=jh, r=r, sbank=sbank, st=st):
                                            if jh == 0:
                                                sbank[0] = 6 + (sb_ctr[0] % 2)
                                                sb_ctr[0] += 1
                                            bs_ = sbank[0]
                                            for hh in range(2):
                                                h = 2 * jh + hh
                                                PE(PS(bs_, 0, ncols), Dg[i % 2][:, h, :], st["R"][:, hh, 0:ncols], h == 0, h == 7,
                                                   [st["rk"], ("Dg", i % 2)], [PT_(bs_)])
                                            if jh == 3:
                                                CP("dve", SC[:, off + kc * 512:off + kc * 512 + ncols], PS(bs_, 0, ncols), [],
                                                   [PT_(bs_), ("SC", r, kc)])
                                        steps.append((stepA, stepB))

                                def fin(i=i, off=off, wdt=wdt, r=r):
                                    sks = [("SC", r, kc) for kc in range((wdt + 511) // 512)]
                                    S.add("dve", lambda e: e.tensor_reduce(out=amax[:, r:r + 1], in_=SC[:, off:off + wdt], axis=AX.X,
                                                                          op=ALU.max, apply_absolute_value=True),
                                          reads=sks, writes=[("amax", r)])
                                    dk = ("SC", r, i // 4)
                                    TT("pool", SC[:, off + i * 128:off + (i + 1) * 128], SC[:, off + i * 128:off + (i + 1) * 128],
                                       cmask[:], ALU.add, [dk, "cmask", ("amax", r)], [dk])
                                steps.append((None, fin))
                            return steps

                        def bis_steps(g):
                            steps = []
                            blocks = [r for r in range(4) if 4 * g + r >= 2]

                            def init():
                                aks = [("amax", r) for r in range(4)]
                                MS("dve", lo[:], -1.0e29, ["lo"])
                                MS("dve", thr[:], TOPK - 0.5, ["thr"])
                                MS("dve", cnt[:], 0.0, [("cnt", r) for r in range(4)])
                                for r in blocks:
                                    wdt = (4 * g + r + 1) * 128
                                    TS("dve", lo[:, r:r + 1], amax[:, r:r + 1], -1.0, None, ALU.mult, None, aks, ["lo"])
                                    if r % 2 == 1:
                                        MS("dve", thr[:, r:r + 1], float(2 * TOPK - 1 - wdt), ["thr"])
                                TS("dve", Wt[:, 0, :], amax[:], 1.0000005, 1.0e-30, ALU.mult, ALU.add, aks, ["Wt"])
                                for it in range(1, NIT):
                                    TS("dve", Wt[:, it, :], Wt[:, 0, :], float(2.0 ** (-it)), None, ALU.mult, None, ["Wt"], ["Wt"])
                                for r in range(4):
                                    if r not in blocks:
                                        MS("dve", Wt[:, :, r:r + 1], 0.0, ["Wt"])
                            steps.append(init)
                            if not blocks:
                                return steps
                            for it in range(NIT):
                                def step(it=it):
                                    TT("pool", mid[:], lo[:], Wt[:, it, :], ALU.add, ["lo", "Wt"], ["mid"])
                                    TS("pool", nmid[:], mid[:], -1.0, 0.0, ALU.mult, ALU.add, ["mid"], ["nmid"])
                                    for r in blocks:
                                        i = 4 * g + r
                                        off = blk_off(g, r)
                                        wdt = (i + 1) * 128
                                        sks = [("SC", r, kc) for kc in range((wdt + 511) // 512)]
                                        if r % 2 == 0:
                                            TS("dve", junkD[:, 0:wdt], SC[:, off:off + wdt], mid[:, r:r + 1], None, ALU.is_ge, ALU.add,
                                               sks + ["mid"], ["junkD", ("cnt", r)], accum=cnt[:, r:r + 1])
                                        else:
                                            ACT(junkA[:, 0:wdt], SC[:, off:off + wdt], AF.Sign, sks + ["nmid"], ["junkA", ("cnt", r)],
                                                bias=nmid[:, r:r + 1], accum=cnt[:, r:r + 1])
                                    cks = [("cnt", r) for r in range(4)]
                                    TT("pool", dd[:], cnt[:], thr[:], ALU.subtract, cks + ["thr"], ["dd"])
                                    TS("pool", dd[:], dd[:], 0.0, None, ALU.is_ge, None, ["dd"], ["dd"])
                                    TT("pool", tt_[:], dd[:], Wt[:, it, :], ALU.mult, ["dd", "Wt"], ["tt_"])
                                    TT("pool", lo[:], lo[:], tt_[:], ALU.add, ["lo", "tt_"], ["lo"])
                                steps.append(step)
                            return steps

                        def sel_steps(g):
                            steps = []
                            for r in range(4):
                                def step(r=r):
                                    i = 4 * g + r
                                    off = blk_off(g, r)
                                    wdt = (i + 1) * 128
                                    sks = [("SC", r, kc) for kc in range((wdt + 511) // 512)]
                                    TS("dve", selq[:, 0:wdt], SC[:, off:off + wdt], lo[:, r:r + 1], None, ALU.is_ge, None,
                                       sks + ["lo"], ["selq"])
                                    for m in range((i + 4) // 4):
                                        nk = min(4, i + 1 - 4 * m)
                                        b = nb()
                                        for kk in range(nk):
                                            kt = 4 * m + kk
                                            TR(PS(b, kk * 128, (kk + 1) * 128), selq[:, kt * 128:(kt + 1) * 128], identf[:],
                                               ["selq", "identf"], [PT_(b)])
                                        dst = selT[:, 4 * m:4 * m + nk, r * 128:(r + 1) * 128]
                                        src = PS(b, 0, nk * 128).rearrange("p (a n) -> p a n", a=nk)
                                        wk = [("selT", 4 * m + kk) for kk in range(nk)]
                                        if m % 2 == 0:
                                            ACT(dst, src, AF.Identity, [], [PT_(b)] + wk)
                                        else:
                                            CP("dve", dst, src, [], [PT_(b)] + wk)
                                steps.append(step)
                            if dtau is not None:
                                def dump():
                                    DMA("sp", dtau[:, 4 * g:4 * g + 4], lo[:], ["lo"], [("dtau", g)])
                                    DMA("sp", dsel[g], selT[:], [("selT", kt) for kt in range(NT)], [("dsel", g)])
                                steps.append(dump)
                            return steps

                        def attn_steps(g):
                            steps = []
                            nkt = 4 * g + 4
                            for j in range(4):
                                for kt in range(nkt):
                                    st = {}

                                    def stepA(j=j, kt=kt, st=st):
                                        q0 = max(0, kt - 4 * g) * 128
                                        N = 512 - q0
                                        bp = dpair()
                                        P_ = PTb[pt_ctr[0] % 4]
                                        ptk = ("PT", pt_ctr[0] % 4)
                                        pt_ctr[0] += 1
                                        st["P"] = P_
                                        st["ptk"] = ptk
                                        for hh in range(2):
                                            base = hh * 64
                                            PE(PS(bp + hh, 0, N), kT[base:base + 64, kt * 128:(kt + 1) * 128],
                                               qT[base:base + 64, j, g * 512 + q0:(g + 1) * 512], True, True,
                                               [("k", 0, kt // 4), ("q", j, g)], [PT_(bp + hh)])
                                        ACT(P_[:, :, 0:N], psum[:, bp:bp + 2, 0:N], AF.Exp, [], [ptk, PT_(bp), PT_(bp + 1)], scale=0.125)
                                        for hh in range(2):
                                            TT("dve", P_[:, hh, 0:N], P_[:, hh, 0:N], selT[:, kt, q0:512], ALU.mult,
                                               [ptk, ("selT", kt)], [ptk])

                                    def stepB(j=j, kt=kt, st=st):
                                        q0 = max(0, kt - 4 * g) * 128
                                        N = 512 - q0
                                        po = 6
                                        pd = 7
                                        P_ = st["P"]
                                        ptk = st["ptk"]
                                        for hh in range(2):
                                            base = hh * 64
                                            PE(psum[base:base + 64, po, q0:512], vtm[:, kt, base:base + 64], P_[:, hh, 0:N],
                                               kt == 0, kt == nkt - 1, [ptk, ("vtm", kt // 4)], [PT_(po)])
                                            PE(psum[base:base + 64, pd, q0:512], onesb[:, 0:64], P_[:, hh, 0:N],
                                               kt == 0, kt == nkt - 1, [ptk, "onesb"], [PT_(pd)])
                                        if kt == nkt - 1:
                                            R = rec[j % 2]
                                            rk = ("rec", j % 2)
                                            S.add("dve", lambda e: e.reciprocal(out=R[:], in_=PS(pd)), reads=[], writes=[rk, PT_(pd)])
                                            TT("dve", OT[:, j, g * 512:(g + 1) * 512], PS(po), R[:], ALU.mult, [rk], [PT_(po), ("OT", j, g)])
                                    steps.append((stepA, stepB))
                            return steps

                        def pipe(pairs, lag):
                            out = []
                            n = len(pairs)
                            for t_ in range(n + lag):
                                if t_ < n and pairs[t_][0] is not None:
                                    out.append(pairs[t_][0])
                                if t_ - lag >= 0 and pairs[t_ - lag][1] is not None:
                                    out.append(pairs[t_ - lag][1])
                            return out

                        def run(steps):
                            for st_ in steps:
                                st_()

                        def interleave(a, b):
                            na, nb_ = len(a), len(b)
                            ia = ib = 0
                            while ia < na or ib < nb_:
                                if ib >= nb_ or (ia < na and ia * nb_ <= ib * na):
                                    a[ia]()
                                    ia += 1
                                else:
                                    b[ib]()
                                    ib += 1

                        run(pipe(idx_steps(0), 1))
                        run(bis_steps(0))
                        run(sel_steps(0))
                        for g in range(1, NG):
                            run(pipe(idx_steps(g), 1))
                            interleave(bis_steps(g), pipe(attn_steps(g - 1), 2))
                            run(sel_steps(g))
                        run(pipe(attn_steps(NG - 1), 2))
                        if s == 0 and l == layers[0] and "OT" in dbg:
                            DMA("sp", dbg_tensor("OT", [128, 4, L], BF16), OT[:], [("OT", j, g) for j in range(4) for g in range(4)], ["dbg_OT"])
                        S.barrier()
                S.barrier()
                with ExitStack() as eC:
                    ypool = sb(eC, "ypool", [128, 2, L], BF16)
                    yconv = sb(eC, "yconv", [128, 2, L], BF16)
                    outw = sb(eC, "outw", [128, 8, 1024], BF16)
                    DMA("pool", outw[:], W[("outw", l)], [], ["outw"])
                    with ExitStack() as e1:
                        up = sb(e1, "up", [128, 2, 16 + L], F32)
                        sA = sb(e1, "sA", [128, 16 + L], F32)
                        sB = sb(e1, "sB", [128, 16 + L], F32)
                        mixed = sb(e1, "mixed", [128, 2, L], BF16)
                        t16 = sb(e1, "t16", [128, 16], F32)
                        pwb = sb(e1, "pwb", [128, 2, 128], BF16)
                        DMA("pool", pwb[:], W[("pwbd", l)], [], ["pwb"])
                        MS("pool", up[:, :, 0:16], 0.0, [("up", 0, -1), ("up", 1, -1)])
                        MS("pool", sA[:, 0:16], 0.0, ["sA"])
                        MS("pool", sB[:, 0:16], 0.0, ["sB"])
                        stm = Stream(l, CH_POOL)
                        for c in range(2):
                            sl = stm.get(c)
                            for g in range(NG):
                                b = nb()
                                proj_chunk(sl, g, b)
                                ACT(up[:, c, 16 + g * 512:16 + (g + 1) * 512], PS(b), AF.Identity, [pk], [PT_(b), ("up", c, g)],
                                    bias=par[:, PC_BA + c:PC_BA + c + 1])
                        for c in range(2):
                            uk = [("up", c, g) for g in range(-1, 4)]
                            U = up[:, c, :]
                            TT("dve", sA[:, 16:], U[:, 16:], U[:, 15:15 + L], ALU.add, uk, ["sA"])
                            TT("dve", sB[:, 16:], sA[:, 16:], sA[:, 14:14 + L], ALU.add, ["sA"], ["sB"])
                            if c == 1:
                                TT("dve", sA[:, 16:], sB[:, 16:], sB[:, 12:12 + L], ALU.add, ["sB"], ["sA"])
                                TT("dve", sB[:, 16:], sA[:, 16:], sA[:, 8:8 + L], ALU.add, ["sA"], ["sB"])
                            for half, (sbuf_, sk) in enumerate(((sA, "sA"), (sB, "sB"))):
                                widx = 2 * c + half
                                win = float(2 ** (widx + 1))
                                pr = slice(half * 64, half * 64 + 64)
                                STT(mixed[pr, c, 16:], sbuf_[pr, 32:], 1.0 / win, U[pr, 32:], ALU.mult, ALU.subtract, [sk] + uk,
                                    [("mixed", c, half)])
                                TT("dve", t16[pr, :], sbuf_[pr, 16:32], rct[pr, widx, :], ALU.mult, [sk, ("rct", widx)], ["t16"])
                                TT("dve", mixed[pr, c, 0:16], t16[pr, :], U[pr, 16:32], ALU.subtract, ["t16"] + uk, [("mixed", c, half)])
                        for c in range(2):
                            for g in range(NG):
                                b = nb()
                                PE(PS(b), pwb[:, c, :], mixed[:, c, g * 512:(g + 1) * 512], True, True,
                                   ["pwb", ("mixed", c, 0), ("mixed", c, 1)], [PT_(b)])
                                ACT(ypool[:, c, g * 512:(g + 1) * 512], PS(b), AF.Identity, [pk], [PT_(b), ("ypool", c, g)],
                                    scale=par[:, PC_PSC + c:PC_PSC + c + 1])
                        S.barrier()
                    with ExitStack() as e1:
                        glu = sb(e1, "glu", [128, 2, 30 + L], BF16)
                        dg = sb(e1, "dg", [128, 2, 31, 128], BF16)
                        xcv = sb(e1, "xcv", [128, 2, L], F32)
                        xsq = [sb(e1, "xsq%d" % i, [128, 512], F32) for i in range(2)]
                        sgt = [sb(e1, "sgt%d" % i, [128, 512], F32) for i in range(2)]
                        mean_t = sb(e1, "mean_t", [128, 512], F32)
                        var_t = sb(e1, "var_t", [128, 512], F32)
                        dtmp = [sb(e1, "dtmp%d" % i, [128, 512], F32) for i in range(2)]
                        MS("pool", glu[:, :, 0:30], 0.0, [("glu", 0, -1), ("glu", 1, -1)])
                        for c in range(2):
                            for jj in range(31):
                                TS("dve", dg[:, c, jj, :], identb[:], par[:, PC_CDW + c * 31 + jj:PC_CDW + c * 31 + jj + 1], None,
                                   ALU.mult, None, ["identb", pk], [("dg", c)])
                        stm = Stream(l, [CH_CA[0], CH_CG[0], CH_CA[1], CH_CG[1]])
                        cc = 0
                        for c in range(2):
                            sa_ = stm.get(2 * c)
                            sg_ = stm.get(2 * c + 1)
                            for g in range(NG):
                                ba = nb()
                                bg = nb()
                                proj_chunk(sa_, g, ba)
                                proj_chunk(sg_, g, bg)
                                T_ = sgt[cc % 2]
                                tk_ = ("sgt", cc % 2)
                                cc += 1
                                ACT(T_[:], PS(bg), AF.Sigmoid, [pk], [tk_, PT_(bg)], bias=par[:, PC_BA + CH_CG[c]:PC_BA + CH_CG[c] + 1])
                                STT(glu[:, c, 30 + g * 512:30 + (g + 1) * 512], PS(ba), par[:, PC_BA + CH_CA[c]:PC_BA + CH_CA[c] + 1],
                                    T_[:], ALU.add, ALU.mult, [pk, tk_], [PT_(ba), ("glu", c, g)])
                        for g in range(NG):
                            bm = nb()
                            bq = nb()
                            for c in range(2):
                                b = nb()
                                gk = [("glu", c, gg) for gg in range(-1, 4)]
                                for jj in range(31):
                                    PE(PS(b), dg[:, c, jj, :], glu[:, c, g * 512 + jj:g * 512 + jj + 512], jj == 0, jj == 30,
                                       [("dg", c)] + gk, [PT_(b)])
                                X2 = xsq[c]
                                ACT(xcv[:, c, g * 512:(g + 1) * 512], PS(b), AF.Identity, [pk], [PT_(b), ("xcv", c, g)],
                                    bias=par[:, PC_CDB + c:PC_CDB + c + 1])
                                ACT(X2[:], PS(b), AF.Square, [pk], [PT_(b), ("xsq", c)], bias=par[:, PC_CDB + c:PC_CDB + c + 1])
                            for c in range(2):
                                PE(PS(bm), onesf[:], xcv[:, c, g * 512:(g + 1) * 512], c == 0, c == 1, ["onesf", ("xcv", c, g)], [PT_(bm)])
                            for c in range(2):
                                PE(PS(bq), onesf[:], xsq[c][:], c == 0, c == 1, ["onesf", ("xsq", c)], [PT_(bq)])
                            TS("dve", mean_t[:], PS(bm), 1.0 / 256, None, ALU.mult, None, [], ["mean_t", PT_(bm)])
                            TT("dve", var_t[:], mean_t[:], mean_t[:], ALU.mult, ["mean_t"], ["var_t"])
                            STT(var_t[:], PS(bq), 1.0 / 256, var_t[:], ALU.mult, ALU.subtract, ["var_t"], ["var_t", PT_(bq)])
                            ACT(var_t[:], var_t[:], AF.Sqrt, ["var_t"], ["var_t"], bias=EPS)
                            S.add("dve", lambda e: e.reciprocal(out=var_t[:], in_=var_t[:]), reads=["var_t"], writes=["var_t"])
                            for c in range(2):
                                Dm = dtmp[c]
                                dk_ = ("dtmp", c)
                                TT("dve", Dm[:], xcv[:, c, g * 512:(g + 1) * 512], mean_t[:], ALU.subtract, [("xcv", c, g), "mean_t"], [dk_])
                                TT("pool", Dm[:], Dm[:], var_t[:], ALU.mult, [dk_, "var_t"], [dk_])
                                ACT(yconv[:, c, g * 512:(g + 1) * 512], Dm[:], AF.Silu, [dk_, pk], [("yconv", c, g)],
                                    bias=par[:, PC_CLB + c:PC_CLB + c + 1], scale=par[:, PC_CLG + c:PC_CLG + c + 1])
                        S.barrier()
                    if s == 0 and l == layers[0]:
                        if "ypool" in dbg:
                            DMA("sp", dbg_tensor("ypool", [128, 2, L], BF16), ypool[:], [("ypool", c, g) for c in range(2) for g in range(4)], ["dbg_ypool"])
                        if "yconv" in dbg:
                            DMA("sp", dbg_tensor("yconv", [128, 2, L], BF16), yconv[:], [("yconv", c, g) for c in range(2) for g in range(4)], ["dbg_yconv"])
                    with ExitStack() as e4:
                        wo = sb(e4, "wo", [128, 8, 1024], BF16)
                        DMA("pool", wo[:], W[("wo", l)], [], ["wo"])
                        sg3 = [sb(e4, "sg3_%d" % i, [128, 512], F32) for i in range(3)]
                        mm_ = [sb(e4, "mm_%d" % i, [128, 512], F32) for i in range(3)]
                        chunks = []
                        for c in range(8):
                            chunks += [CH_GATE0 + i * 8 + c for i in range(3)]
                        stm = Stream(l, chunks, ahead=1)
                        ysrc = [(ypool, "ypool", 0, 2), (yconv, "yconv", 2, 2), (OT, "OT", 4, 4)]
                        for c in range(8):
                            sl3 = [stm.get(3 * c + i) for i in range(3)]
                            for g in range(NG):
                                gs = slice(g * 512, (g + 1) * 512)
                                for i in range(3):
                                    bg = nb()
                                    proj_chunk(sl3[i], g, bg)
                                    ch = CH_GATE0 + i * 8 + c
                                    ACT(sg3[i][:], PS(bg), AF.Sigmoid, [pk], [("sg3", i), PT_(bg)], bias=par[:, PC_BA + ch:PC_BA + ch + 1])
                                for i, (ysb, yn, k0, nk) in enumerate(ysrc):
                                    by = nb()
                                    for k in range(nk):
                                        PE(PS(by), outw[:, k0 + k, c * 128:(c + 1) * 128], ysb[:, k, gs], k == 0, k == nk - 1,
                                           ["outw", (yn, k, g)], [PT_(by)])
                                    TT("dve", mm_[i][:], PS(by), sg3[i][:], ALU.mult, [("sg3", i)], [("mm_", i), PT_(by)])
                                TT("pool", mm_[0][:], mm_[0][:], mm_[1][:], ALU.add, [("mm_", 0), ("mm_", 1)], [("mm_", 0)])
                                TT("pool", merged[:, c, gs], mm_[0][:], mm_[2][:], ALU.add, [("mm_", 0), ("mm_", 2)], [("merged", c, g)])
                        with ExitStack() as e5:
                            ln_alloc(e5)
                            ln_rows_load(W[("lnr", l)][:, 0:2048])
                            def mm5(i):
                                bp = npair()
                                for n in range(2):
                                    for k in range(8):
                                        PE(PS(bp + n), merged[:, k, i * 128:(i + 1) * 128], wo[:, k, n * 512:(n + 1) * 512], k == 0, k == 7,
                                           ["wo", ("merged", k, i // 4)], [PT_(bp + n)])
                                return bp
                            ln_pipeline(s, list(range(NT)), mm5, False)
                            S.barrier()
                S.barrier()
            if s == 0 and l == layers[0] and "merged" in dbg:
                DMA("sp", dbg_tensor("merged", [128, 8, L], BF16), merged, [("merged", c, g) for c in range(8) for g in range(4)], ["dbg_merged"])
            if s == 0 and l == layers[0] and "hT1" in dbg:
                DMA("sp", dbg_tensor("hT1", [128, 8, L], BF16), hT[:], [("hT", g) for g in range(4)], ["dbg_hT1"])
            with ExitStack() as e6:
                wd = sb(e6, "wd", [128, NJ, 1024], BF16)
                ln_alloc(e6)
                NWU = 3
                wUb = [sb(e6, "wU%d" % i, [128, 8, 256], BF16) for i in range(NWU)]
                xg = [sb(e6, "xg%d" % i, [128, 2 + 1024], F32) for i in range(2)]
                xv = [sb(e6, "xv%d" % i, [128, 2 + 1024], F32) for i in range(2)]
                ug = [sb(e6, "ug%d" % i, [128, 1024], F32) for i in range(2)]
                uv = [sb(e6, "uv%d" % i, [128, 1024], F32) for i in range(2)]
                halo = sb(e6, "halo", [128, 2 * NJ, 2], F32)
                ln_rows_load(W[("lnr", l)][:, 2048:4096])
                wu_ctr = [0]
                wu_slots = {}

                def wu_get(idx):
                    while wu_ctr[0] <= min(idx + 1, 2 * NJ - 1):
                        n_ = wu_ctr[0]
                        sl = n_ % NWU
                        DMA("pool", wUb[sl][:], W[("wU", l)][n_ % NJ], [], [("wU", sl)], slot=("wU", sl))
                        wu_slots[n_] = sl
                        wu_ctr[0] += 1
                    return wu_slots[idx]

                wu_get(0)
                wd_parts = [(0, 6), (6, 11), (11, 17), (17, 22)]
                for kq, (k0_, k1_) in enumerate(wd_parts):
                    DMA("pool", wd[:, k0_:k1_, :], W[("wd", l)][:, k0_:k1_, :], [], [("wd", kq)])
                for hf in range(2):
                    t0 = hf * 1024
                    for j in range(NJ):
                        sl = wu_get(hf * NJ + j)
                        p_ = j % 2
                        XG, XV, UG, UV = xg[p_], xv[p_], ug[p_], uv[p_]
                        for (X, xk, coff, hidx) in ((XG, ("xg", p_), 0, j), (XV, ("xv", p_), 128, NJ + j)):
                            if hf == 0:
                                CP("pool", X[:, 0:2], zero2[:], ["zero2"], [xk])
                            else:
                                CP("pool", X[:, 0:2], halo[:, hidx, :], [("halo", hidx)], [xk])
                            for gg in range(2):
                                g = hf * 2 + gg
                                b = nb()
                                for k in range(8):
                                    PE(PS(b), wUb[sl][:, k, coff:coff + 128], hT[:, k, g * 512:(g + 1) * 512], k == 0, k == 7,
                                       [("wU", sl), ("hT", g)], [PT_(b)])
                                ACT(X[:, 2 + gg * 512:2 + (gg + 1) * 512], PS(b), AF.Identity, [], [PT_(b), xk])
                            if hf == 0:
                                CP("pool", halo[:, hidx, :], X[:, 1024:1026], [xk], [("halo", hidx)])
                        for (X, xk, U, uk, wc, bc) in ((XG, ("xg", p_), UG, ("ug", p_), PC_FWG + 3 * j, PC_FBG + j),
                                                      (XV, ("xv", p_), UV, ("uv", p_), PC_FWV + 3 * j, PC_FBV + j)):
                            TS("dve", U[:], X[:, 2:1026], par[:, wc + 2:wc + 3], par[:, bc:bc + 1], ALU.mult, ALU.add, [xk, pk], [uk])
                            STT(U[:], X[:, 1:1025], par[:, wc + 1:wc + 2], U[:], ALU.mult, ALU.add, [xk, pk, uk], [uk])
                            STT(U[:], X[:, 0:1024], par[:, wc:wc + 1], U[:], ALU.mult, ALU.add, [xk, pk, uk], [uk])
                        ACT(UG[:], UG[:], AF.Silu, [("ug", p_)], [("ug", p_)])
                        TT("dve", actb[:, j, 0:1024], UG[:], UV[:], ALU.mult, [("ug", p_), ("uv", p_)], [("act", j)])
                    def mm6(i, hf=hf):
                        ii = i - hf * 8
                        bp = npair()
                        for n in range(2):
                            for k in range(NJ):
                                PE(PS(bp + n), actb[:, k, ii * 128:(ii + 1) * 128], wd[:, k, n * 512:(n + 1) * 512], k == 0, k == NJ - 1,
                                   [("wd", 0 if k < 6 else (1 if k < 11 else (2 if k < 17 else 3))), ("act", k)], [PT_(bp + n)])
                        return bp
                    ln_pipeline(s, [hf * 8 + ii for ii in range(8)], mm6, last_layer)
                S.barrier()
        if not last:
            pass
    S.emit()
    es_top.close()
    return nc, S, dbg_out


_CACHE = {}


def _in_maps(x, positions, wts, layers, ncores, nseq):
    maps = []
    for c in range(ncores):
        m = {"x": np.ascontiguousarray(x[c * nseq:(c + 1) * nseq]),
             "pos": np.ascontiguousarray(np.broadcast_to(positions[c * nseq:(c + 1) * nseq, None, :], (nseq, 128, L))).astype(np.int32),
             "lnin": wts["lnin"], "cst": wts["cst"]}
        for l in layers:
            for k in WSHAPES:
                m["%s%d" % (k, l)] = wts["%s%d" % (k, l)]
        maps.append(m)
    return maps


def kernel(**inputs):
    x = np.asarray(inputs["x"], np.float32)
    positions = np.asarray(inputs["positions"], np.int32)
    wts = prep_weights(inputs)
    ncores = 8
    nseq = x.shape[0] // ncores
    key = ("fused", nseq)
    if key not in _CACHE:
        _CACHE[key] = build(nseq=nseq, layers=(0, 1))
    nc, S, _ = _CACHE[key]
    maps = _in_maps(x, positions, wts, (0, 1), ncores, nseq)
    res = run_bass_kernel_spmd(nc, maps, core_ids=list(range(ncores)))
    out = np.concatenate([np.asarray(r["y"]) for r in res.results], axis=0)
    return out.astype(np.float32)
```

```python
import numpy as np
from contextlib import ExitStack
import concourse.bass as bass
import concourse.mybir as mybir
from concourse.bass_utils import run_bass_kernel_spmd

F32 = mybir.dt.float32
BF16 = mybir.dt.bfloat16
I32 = mybir.dt.int32
ALU = mybir.AluOpType
AF = mybir.ActivationFunctionType
AX = mybir.AxisListType

L = 2048
D = 1024
NT = 16
NG = 4
DFF = 2816
NJ = 22
DEPTH = 2
ALPHA = float((2 * DEPTH) ** 0.25)
EPS = 1e-5
TOPK = 256
NIT = 16
NEG = -1.0e30
SEM_LIMIT = 30000

CH_POOL = [0, 1]
CH_CA = [2, 3]
CH_CG = [4, 5]
CH_Q = [6, 7, 8, 9]
CH_QS = [10, 11, 12, 13]
CH_K = 14
CH_KS = 15
CH_V = 16
CH_QI = [17, 18, 19, 20]
CH_QIS = [21, 22, 23, 24]
CH_KI = 25
CH_KIS = 26
CH_GATE0 = 27
NCH = 51
PC_BA = 0
PC_PSC = 51
PC_CDB = 53
PC_CLG = 55
PC_CLB = 57
PC_CDW = 59
PC_FWG = 121
PC_FWV = 187
PC_FBG = 253
PC_FBV = 275
NPAR = 297


class _Op:
    __slots__ = ("idx", "eng", "fn", "deps_raw", "deps_other", "is_dma", "slot", "signal", "stream", "val", "vc")

    def __init__(self, idx, eng, fn, is_dma, slot):
        self.idx = idx
        self.eng = eng
        self.fn = fn
        self.is_dma = is_dma
        self.slot = slot
        self.deps_raw = set()
        self.deps_other = set()
        self.signal = False
        self.stream = None
        self.val = 0
        self.vc = None


class Sched:
    def __init__(self, nc):
        self.nc = nc
        self.ops = []
        self.last_writer = {}
        self.readers = {}
        self.last_dma_on_slot = {}
        self.last_on_eng = {}
        self.dma_since_barrier = []
        self.barrier_deps = set()

    def barrier(self):
        self.barrier_deps = set(self.last_on_eng.values()) | set(self.dma_since_barrier)
        self.dma_since_barrier = []
        self.last_writer = {}
        self.readers = {}

    def add(self, eng, fn, reads=(), writes=(), dma=False, slot=None):
        idx = len(self.ops)
        if dma and slot is None:
            slot = ("auto", tuple(writes)[0])
        op = _Op(idx, eng, fn, dma, slot)
        for r in reads:
            w = self.last_writer.get(r)
            if w is not None:
                op.deps_raw.add(w)
        for t in writes:
            w = self.last_writer.get(t)
            if w is not None:
                op.deps_other.add(w)
            for rd in self.readers.get(t, ()):
                op.deps_other.add(rd)
        if dma:
            p = self.last_dma_on_slot.get(slot)
            if p is not None:
                op.deps_raw.add(p)
            self.last_dma_on_slot[slot] = idx
            self.dma_since_barrier.append(idx)
        op.deps_other |= self.barrier_deps
        for r in reads:
            self.readers.setdefault(r, []).append(idx)
        for t in writes:
            self.last_writer[t] = idx
            self.readers[t] = []
        op.deps_other -= op.deps_raw
        self.last_on_eng[eng] = idx
        self.ops.append(op)
        return idx

    def _needed(self, op, d):
        dop = self.ops[d]
        if dop.is_dma or op.is_dma:
            return True
        if dop.eng != op.eng:
            return True
        if op.eng == "pe":
            return False
        return d in op.deps_raw

    def emit(self):
        nc = self.nc
        ops = self.ops
        for op in ops:
            for d in (op.deps_raw | op.deps_other):
                if self._needed(op, d):
                    ops[d].signal = True
        for op in ops:
            if op.is_dma:
                op.signal = True
        sems = {}
        counts = {}
        sem_objs = []

        def new_sem():
            cm = nc.semaphore("s%d" % len(sem_objs))
            h = cm.__enter__()
            sem_objs.append(cm)
            return h

        for op in ops:
            if not op.signal:
                continue
            key = ("slot", op.slot) if op.is_dma else ("eng", op.eng)
            inc = 16 if op.is_dma else 1
            if key not in sems:
                sems[key] = [new_sem()]
                counts[key] = 0
            if counts[key] + inc > SEM_LIMIT:
                sems[key].append(new_sem())
                counts[key] = 0
            counts[key] += inc
            op.stream = (key, len(sems[key]) - 1)
            op.val = counts[key]
        self.n_sems = len(sem_objs)
        eng_clock = {}
        waits = [None] * len(ops)
        for op in ops:
            clk = eng_clock.setdefault(op.eng, {})
            best = {}
            for d in sorted(op.deps_raw | op.deps_other):
                if not self._needed(op, d):
                    continue
                dop = ops[d]
                if clk.get(dop.stream, 0) >= dop.val:
                    continue
                if best.get(dop.stream, 0) < dop.val:
                    best[dop.stream] = dop.val
                for k, v in dop.vc.items():
                    if clk.get(k, 0) < v:
                        clk[k] = v
            waits[op.idx] = best
            if op.signal:
                vc = dict(clk)
                vc[op.stream] = op.val
                op.vc = vc
        per_eng = {}
        for op in ops:
            per_eng.setdefault(op.eng, []).append(op)
        final_waits = {}
        for op in ops:
            if op.is_dma:
                if final_waits.get(op.stream, 0) < op.val:
                    final_waits[op.stream] = op.val

        def semh(stream):
            key, ep = stream
            return sems[key][ep]

        engmap = {"pe": "tensor", "act": "scalar", "dve": "vector", "pool": "gpsimd", "sp": "sync"}
        self.n_waits = 0
        with nc.Block() as block:
            for ename in ("sp", "pe", "act", "dve", "pool"):
                lst = per_eng.get(ename, [])

                def body(e, lst=lst, ename=ename):
                    for op in lst:
                        for s, v in waits[op.idx].items():
                            e.wait_ge(semh(s), v)
                            self.n_waits += 1
                        ins = op.fn(e)
                        if op.signal:
                            ins.then_inc(semh(op.stream), 16 if op.is_dma else 1)
                    if ename == "sp":
                        for s, v in final_waits.items():
                            e.wait_ge(semh(s), v)
                getattr(block, engmap[ename])(body)
        for cm in reversed(sem_objs):
            cm.__exit__(None, None, None)
        return self


def _chunk_cols():
    cols = []
    ar = np.arange

    def hc(base, h):
        return base + 64 * h + ar(64)

    def sw(c):
        return np.concatenate([c[32:], c[:32]])

    cols += [ar(128), 128 + ar(128)]
    cols += [256 + 128 * i + ar(128) for i in range(4)]
    for j in range(4):
        cols.append(np.concatenate([hc(768, j), hc(768, 4 + j)]))
    for j in range(4):
        cols.append(np.concatenate([sw(hc(768, j)), sw(hc(768, 4 + j))]))
    cols.append(np.concatenate([hc(1280, 0), hc(1280, 1)]))
    cols.append(np.concatenate([sw(hc(1280, 0)), sw(hc(1280, 1))]))
    cols.append(1408 + ar(128))
    for j in range(4):
        cols.append(np.concatenate([hc(1536, 2 * j), hc(1536, 2 * j + 1)]))
    for j in range(4):
        cols.append(np.concatenate([sw(hc(1536, 2 * j)), sw(hc(1536, 2 * j + 1))]))
    ki = 2048 + ar(64)
    cols.append(np.concatenate([ki, ki]))
    cols.append(np.concatenate([sw(ki), sw(ki)]))
    for i in range(3):
        for c in range(8):
            cols.append(2120 + 1024 * i + 128 * c + ar(128))
    assert len(cols) == NCH
    return np.stack(cols)


def _kp(w):
    K = w.shape[0] // 128
    return np.ascontiguousarray(w.reshape(K, 128, w.shape[1]).transpose(1, 0, 2))


def prep_weights(inp):
    cols = _chunk_cols()
    out = {}
    f = np.float32
    rep = lambda v: np.ascontiguousarray(np.broadcast_to(np.asarray(v, f)[None, :], (128, v.shape[0])))
    for l in range(DEPTH):
        w_in = np.asarray(inp["w_in"][l], f)
        b_in = np.asarray(inp["b_in"][l], f)
        wg = w_in[:, cols.reshape(-1)].reshape(8, 128, NCH, 128)
        out["wA%d" % l] = np.ascontiguousarray(wg.transpose(2, 1, 0, 3))
        out["wwi%d" % l] = _kp(np.ascontiguousarray(w_in[:, 2112:2120]))
        par = np.zeros((128, NPAR), f)
        par[:, PC_BA:PC_BA + NCH] = b_in[cols].T
        par[:, PC_PSC:PC_PSC + 2] = np.asarray(inp["pool_scale"][l], f).reshape(2, 128).T
        par[:, PC_CDB:PC_CDB + 2] = np.asarray(inp["conv_dw_b"][l], f).reshape(2, 128).T
        par[:, PC_CLG:PC_CLG + 2] = np.asarray(inp["conv_ln_g"][l], f).reshape(2, 128).T
        par[:, PC_CLB:PC_CLB + 2] = np.asarray(inp["conv_ln_b"][l], f).reshape(2, 128).T
        cdw = np.asarray(inp["conv_dw_w"][l], f)
        par[:, PC_CDW:PC_CDW + 62] = cdw.reshape(31, 2, 128).transpose(2, 1, 0).reshape(128, 62)
        fw = np.asarray(inp["ffn_dw_w"][l], f)
        par[:, PC_FWG:PC_FWG + 66] = fw[:, :DFF].reshape(3, NJ, 128).transpose(2, 1, 0).reshape(128, 66)
        par[:, PC_FWV:PC_FWV + 66] = fw[:, DFF:].reshape(3, NJ, 128).transpose(2, 1, 0).reshape(128, 66)
        fb = np.asarray(inp["ffn_dw_b"][l], f)
        par[:, PC_FBG:PC_FBG + NJ] = fb[:DFF].reshape(NJ, 128).T
        par[:, PC_FBV:PC_FBV + NJ] = fb[DFF:].reshape(NJ, 128).T
        out["par%d" % l] = par
        out["bwi%d" % l] = rep(np.tile(b_in[2112:2120], 16))
        pw = np.asarray(inp["pool_w"][l], f)
        bd = np.zeros((128, 2, 128), f)
        for c in range(2):
            bd[0:64, c, 0:64] = pw[2 * c]
            bd[64:128, c, 64:128] = pw[2 * c + 1]
        out["pwbd%d" % l] = bd
        wao = np.asarray(inp["w_attn_out"][l], f)
        rows = np.concatenate([np.concatenate([64 * j + np.arange(64), 64 * (4 + j) + np.arange(64)]) for j in range(4)])
        ow = np.concatenate([_kp(np.asarray(inp["w_pool_out"][l], f)), _kp(np.asarray(inp["w_conv_out"][l], f)),
                             _kp(np.ascontiguousarray(wao[rows]))], axis=1)
        out["outw%d" % l] = np.ascontiguousarray(ow)
        out["wo%d" % l] = _kp(np.asarray(inp["w_o"][l], f))
        wu = np.asarray(inp["w_up"][l], f)
        wug = wu[:, :DFF].reshape(8, 128, NJ, 128)
        wuv = wu[:, DFF:].reshape(8, 128, NJ, 128)
        out["wU%d" % l] = np.ascontiguousarray(np.concatenate([wug, wuv], axis=3).transpose(2, 1, 0, 3))
        out["wd%d" % l] = _kp(np.asarray(inp["w_down"][l], f))
        out["lnr%d" % l] = np.ascontiguousarray(np.concatenate(
            [rep(inp["ln1_g"][l]), rep(inp["ln1_b"][l]), rep(inp["ln2_g"][l]), rep(inp["ln2_b"][l])], axis=1))
    out["lnin"] = np.ascontiguousarray(np.concatenate([rep(inp["ln_in_g"]), rep(inp["ln_in_b"])], axis=1))
    half = 32
    invf = (10000.0 ** (-np.arange(half, dtype=np.float32) / half)).astype(f)
    cst = np.zeros((128, 2), f)
    cst[:, 0] = np.tile(invf, 4)
    cst[:, 1] = np.tile(np.concatenate([-np.ones(32, f), np.ones(32, f)]), 2)
    out["cst"] = cst
    return out


WSHAPES = {
    "wA": [NCH, 128, 8, 128], "wwi": [128, 8, 8], "par": [128, NPAR], "bwi": [128, 128], "pwbd": [128, 2, 128],
    "outw": [128, 8, 1024], "wo": [128, 8, 1024], "wU": [NJ, 128, 8, 256], "wd": [128, NJ, 1024], "lnr": [128, 4096],
}


def build(nseq=2, layers=(0, 1), dbg=None, first=True, last=True):
    dbg = dbg or set()
    nc = bass.Bass("TRN2", target_bir_lowering=False)
    dt = {}
    xin = nc.dram_tensor("x", [nseq, L, D], F32, kind="ExternalInput").ap()
    posin = nc.dram_tensor("pos", [nseq, 128, L], I32, kind="ExternalInput").ap()
    W = {}
    for l in layers:
        for k, shp in WSHAPES.items():
            W[(k, l)] = nc.dram_tensor("%s%d" % (k, l), shp, F32, kind="ExternalInput").ap()
    lnin = nc.dram_tensor("lnin", [128, 2048], F32, kind="ExternalInput").ap()
    cstin = nc.dram_tensor("cst", [128, 2], F32, kind="ExternalInput").ap()
    yout = nc.dram_tensor("y", [nseq, L, D], F32, kind="ExternalOutput").ap()
    hres = nc.dram_tensor("hres", [L, D], F32, kind="Internal").ap()
    ropd = nc.dram_tensor("ropd", [128, 2, L], F32, kind="Internal").ap()
    dbg_out = {}

    def dbg_tensor(name, shape, dtype):
        dbg_out[name] = nc.dram_tensor("dbg_" + name, shape, dtype, kind="ExternalOutput").ap()
        return dbg_out[name]

    S = Sched(nc)
    es_top = ExitStack()

    uid = [0]

    def sb(es, name, shape, dtype):
        uid[0] += 1
        return es.enter_context(nc.sbuf_tensor("sb%d_%s" % (uid[0], name), shape, dtype))

    def PE(out, lhsT, rhs, st, sp, r, w):
        S.add("pe", lambda e: e.matmul(out, lhsT=lhsT, rhs=rhs, start=st, stop=sp), reads=r, writes=w)

    def TR(out, in_, ident, r, w):
        S.add("pe", lambda e: e.transpose(out=out, in_=in_, identity=ident), reads=r, writes=w)

    def ACT(out, in_, func, r, w, bias=0.0, scale=1.0, accum=None):
        if accum is None:
            S.add("act", lambda e: e.activation(out=out, in_=in_, func=func, bias=bias, scale=scale), reads=r, writes=w)
        else:
            S.add("act", lambda e: e.activation(out=out, in_=in_, func=func, bias=bias, scale=scale, accum_out=accum),
                  reads=r, writes=w)

    def TT(eng, out, a, b, op, r, w):
        S.add(eng, lambda e: e.tensor_tensor(out=out, in0=a, in1=b, op=op), reads=r, writes=w)

    def TS(eng, out, a, s1, s2, op0, op1, r, w, accum=None):
        if accum is None:
            if op1 is None:
                S.add(eng, lambda e: e.tensor_scalar(out=out, in0=a, scalar1=s1, scalar2=None, op0=op0), reads=r, writes=w)
            else:
                S.add(eng, lambda e: e.tensor_scalar(out=out, in0=a, scalar1=s1, scalar2=s2, op0=op0, op1=op1),
                      reads=r, writes=w)
        else:
            S.add(eng, lambda e: e.tensor_scalar(out=out, in0=a, scalar1=s1, scalar2=s2, op0=op0, op1=op1,
                                                 accum_out=accum), reads=r, writes=w)

    def STT(out, a, s, b, op0, op1, r, w):
        S.add("dve", lambda e: e.scalar_tensor_tensor(out=out, in0=a, scalar=s, in1=b, op0=op0, op1=op1),
              reads=r, writes=w)

    def CP(eng, out, in_, r, w):
        S.add(eng, lambda e: e.tensor_copy(out=out, in_=in_), reads=r, writes=w)

    def MS(eng, out, v, w):
        S.add(eng, lambda e: e.memset(out, v), writes=w)

    def DMA(q, out, in_, r, w, slot=None):
        if q == "pool":
            S.add(q, lambda e: e.dma_start(out=out, in_=in_, max_dma_last_dim=4096), reads=r, writes=w, dma=True, slot=slot)
        else:
            S.add(q, lambda e: e.dma_start(out=out, in_=in_), reads=r, writes=w, dma=True, slot=slot)

    es = es_top
    psum = es.enter_context(nc.psum_tensor("psum", [128, 8, 512], F32))
    identf = sb(es, "identf", [128, 128], F32)
    identb = sb(es, "identb", [128, 128], BF16)
    onesf = sb(es, "onesf", [128, 128], F32)
    onesb = sb(es, "onesb", [128, 64], BF16)
    cmask = sb(es, "cmask", [128, 128], F32)
    cst = sb(es, "cst", [128, 2], F32)
    rct = sb(es, "rct", [128, 4, 16], F32)
    zero2 = sb(es, "zero2", [128, 2], F32)
    hT = sb(es, "hT", [128, 8, L], BF16)
    arena = sb(es, "arena", [128, NJ * 1024], BF16)
    NWA = 4
    wAb = [sb(es, "wA%d" % i, [128, 8, 128], BF16) for i in range(NWA)]
    parb = {l: sb(es, "par%d" % l, [128, NPAR], F32) for l in layers}
    LN = {}

    def ln_alloc(esx):
        LN["row"] = sb(esx, "lnrow", [128, 2, 1024], F32)
        LN["t"] = [sb(esx, "lnt%d" % i, [128, 1024], F32) for i in range(5)]
        LN["st"] = [sb(esx, "lnst%d" % i, [128, 2, 6], F32) for i in range(5)]
        LN["mv"] = [sb(esx, "lnmv%d" % i, [128, 8], F32) for i in range(5)]

    def PS(b, n0=0, n1=512):
        return psum[:, b, n0:n1]

    def PT_(b):
        return ("ps", b)

    bank_ctr = [0]

    def nb():
        b = bank_ctr[0] % 8
        bank_ctr[0] += 1
        return b

    pair_ctr = [0]

    def npair():
        b = (pair_ctr[0] % 4) * 2
        pair_ctr[0] += 1
        return b

    MS("pool", identf[:], 0.0, ["identf"])
    S.add("pool", lambda e: e.affine_select(out=identf[:], in_=identf[:], pattern=[[-1, 128]], compare_op=ALU.not_equal,
                                            fill=1.0, base=0, channel_multiplier=1), reads=["identf"], writes=["identf"])
    CP("dve", identb[:], identf[:], ["identf"], ["identb"])
    MS("dve", onesf[:], 1.0, ["onesf"])
    MS("dve", onesb[:], 1.0, ["onesb"])
    MS("pool", cmask[:], 0.0, ["cmask"])
    S.add("pool", lambda e: e.affine_select(out=cmask[:], in_=cmask[:], pattern=[[-1, 128]], compare_op=ALU.is_ge,
                                            fill=NEG, base=0, channel_multiplier=1), reads=["cmask"], writes=["cmask"])
    MS("dve", zero2[:], 0.0, ["zero2"])
    DMA("sp", cst[:], cstin, [], ["cst"])
    for l in layers:
        DMA("sp", parb[l][:], W[("par", l)], [], [("par", l)])
    with ExitStack() as es0:
        ti = sb(es0, "ti", [128, 16], I32)
        tf = sb(es0, "tf", [128, 16], F32)
        S.add("pool", lambda e: e.iota(ti[:], pattern=[[1, 16]], base=1, channel_multiplier=0), writes=["ti"])
        CP("dve", tf[:], ti[:], ["ti"], ["tf"])
        for wi_ in range(4):
            TS("dve", rct[:, wi_, :], tf[:], float(2 ** (wi_ + 1)), None, ALU.min, None, ["tf"], [("rct", wi_)])
            S.add("dve", lambda e, wi_=wi_: e.reciprocal(out=rct[:, wi_, :], in_=rct[:, wi_, :]),
                  reads=[("rct", wi_)], writes=[("rct", wi_)])
        S.barrier()

    wa_ctr = [0]

    def load_chunk(l, c):
        slot = wa_ctr[0] % NWA
        wa_ctr[0] += 1
        DMA("pool", wAb[slot][:], W[("wA", l)][c], [], [("wA", slot)], slot=("wA", slot))
        return slot

    class Stream:
        def __init__(self, l, chunks, ahead=2):
            self.l = l
            self.chunks = list(chunks)
            self.slots = {}
            self.next = 0
            self.ahead = ahead

        def get(self, i):
            while self.next < len(self.chunks) and self.next <= i + self.ahead:
                self.slots[self.next] = load_chunk(self.l, self.chunks[self.next])
                self.next += 1
            return self.slots[i]

    def proj_chunk(slot, g, bank, ncols=128):
        for k in range(8):
            PE(PS(bank)[0:ncols, :], wAb[slot][:, k, 0:ncols], hT[:, k, g * 512:(g + 1) * 512], k == 0, k == 7,
               [("wA", slot), ("hT", g)], [PT_(bank)])

    def ln_rows_load(src_ap):
        DMA("sp", LN["row"][:], src_ap.rearrange("p (a n) -> p a n", a=2), [], ["lnrow"])

    def ln_load(s, i, x_src=None):
        t = LN["t"][i % 5]
        tk = ("lnt", i % 5)
        if x_src is not None:
            DMA("sp", t[:], x_src, [], [tk], slot=("lnt_in", i % 5))
        else:
            DMA("sp", t[:], hres[i * 128:(i + 1) * 128, :], [("hres", i)], [tk], slot=("lnt_in", i % 5))

    def ln_a(s, i, mixbank):
        t = LN["t"][i % 5]
        tk = ("lnt", i % 5)
        st = LN["st"][i % 5]
        mv = LN["mv"][i % 5]
        mk = ("lnmv", i % 5)
        if mixbank is not None:
            STT(t[:].rearrange("p (a n) -> p a n", a=2), t[:].rearrange("p (a n) -> p a n", a=2), ALPHA,
                psum[:, mixbank:mixbank + 2, :], ALU.mult, ALU.add, [tk], [tk, PT_(mixbank), PT_(mixbank + 1)])
        for a in range(2):
            S.add("dve", lambda e, a=a: e.bn_stats(out=st[:, a, :], in_=t[:, a * 512:(a + 1) * 512]), reads=[tk],
                  writes=[("lnst", i % 5, a)])
        S.add("dve", lambda e: e.bn_aggr(out=mv[:, 0:2], in_=st[:].rearrange("p a s -> p (a s)")),
              reads=[("lnst", i % 5, 0), ("lnst", i % 5, 1)], writes=[mk])
        ACT(mv[:, 2:3], mv[:, 1:2], AF.Sqrt, [mk], [mk], bias=EPS)
        S.add("dve", lambda e: e.reciprocal(out=mv[:, 3:4], in_=mv[:, 2:3]), reads=[mk], writes=[mk])
        TS("dve", mv[:, 4:5], mv[:, 0:1], mv[:, 3:4], -1.0, ALU.mult, ALU.mult, [mk], [mk])

    def ln_bc(s, i, final):
        lnrow = LN["row"]
        t = LN["t"][i % 5]
        tk = ("lnt", i % 5)
        mv = LN["mv"][i % 5]
        mk = ("lnmv", i % 5)
        ACT(t[:], t[:], AF.Identity, [tk, mk], [tk], bias=mv[:, 4:5], scale=mv[:, 3:4])
        TT("dve", t[:], t[:], lnrow[:, 0, :], ALU.mult, [tk, "lnrow"], [tk])
        TT("pool", t[:], t[:], lnrow[:, 1, :], ALU.add, [tk, "lnrow"], [tk])
        if final:
            DMA("sp", yout[s, i * 128:(i + 1) * 128, :], t[:], [tk], [("yout", i)], slot=("lnt_out", i % 5))
        else:
            DMA("sp", hres[i * 128:(i + 1) * 128, :], t[:], [tk], [("hres", i)], slot=("lnt_out", i % 5))

    def ln_btr(s, i, final):
        t = LN["t"][i % 5]
        tk = ("lnt", i % 5)
        if not final:
            g = i // 4
            for hb in range(2):
                b = nb()
                for kk in range(4):
                    k = hb * 4 + kk
                    TR(PS(b, kk * 128, (kk + 1) * 128), t[:, k * 128:(k + 1) * 128], identf[:], [tk, "identf"], [PT_(b)])
                dst = hT[:, hb * 4:(hb + 1) * 4, i * 128:(i + 1) * 128]
                src = PS(b).rearrange("p (a n) -> p a n", a=4)
                if hb == 0:
                    ACT(dst, src, AF.Identity, [], [PT_(b), ("hT", g)])
                else:
                    CP("dve", dst, src, [], [PT_(b), ("hT", g)])

    def ln_pipeline(s, tiles, mm_fn, final, x_src_fn=None):
        n = len(tiles)
        ln_load(s, tiles[0], None if x_src_fn is None else x_src_fn(tiles[0]))
        for idx in range(n + 3):
            if idx + 1 < n:
                ln_load(s, tiles[idx + 1], None if x_src_fn is None else x_src_fn(tiles[idx + 1]))
            if 0 <= idx - 1 < n:
                ln_bc(s, tiles[idx - 1], final)
            if 0 <= idx - 3 < n:
                ln_btr(s, tiles[idx - 3], final)
            if idx < n:
                bp = mm_fn(tiles[idx]) if mm_fn is not None else None
                ln_a(s, tiles[idx], bp)

    for s in range(nseq):
        with ExitStack() as e0:
            posi = sb(e0, "posi", [128, L], I32)
            ang = sb(e0, "ang", [128, L], F32)
            t1 = sb(e0, "t1", [128, L], F32)
            t2i = sb(e0, "t2i", [128, L], I32)
            t3 = sb(e0, "t3", [128, L], F32)
            rop = sb(e0, "rop", [128, 2, L], F32)
            DMA("sp", posi[:], posin[s], [], ["posi"])
            CP("dve", ang[:], posi[:], ["posi"], ["ang"])
            TS("dve", ang[:], ang[:], cst[:, 0:1], None, ALU.mult, None, ["ang", "cst"], ["ang"])
            C1 = 6.28125
            C2 = float(2.0 * np.pi - 6.28125)
            for which in range(2):
                if which == 0:
                    TS("dve", t1[:], ang[:], float(np.pi / 2), None, ALU.add, None, ["ang"], ["t1"])
                    src = t1
                    sk = "t1"
                else:
                    src = ang
                    sk = "ang"
                TS("dve", t3[:], src[:], float(1.0 / (2 * np.pi)), None, ALU.mult, None, [sk], ["t3"])
                CP("dve", t2i[:], t3[:], ["t3"], ["t2i"])
                CP("dve", t3[:], t2i[:], ["t2i"], ["t3"])
                STT(t1[:], t3[:], -C1, src[:], ALU.mult, ALU.add, ["t3", sk], ["t1"])
                STT(t1[:], t3[:], -C2, t1[:], ALU.mult, ALU.add, ["t3", "t1"], ["t1"])
                TS("dve", t3[:], t1[:], float(np.pi), float(-2 * np.pi), ALU.is_gt, ALU.mult, ["t1"], ["t3"])
                TT("dve", t1[:], t1[:], t3[:], ALU.add, ["t1", "t3"], ["t1"])
                TS("dve", t3[:], t1[:], float(-np.pi), float(2 * np.pi), ALU.is_lt, ALU.mult, ["t1"], ["t3"])
                TT("dve", t1[:], t1[:], t3[:], ALU.add, ["t1", "t3"], ["t1"])
                TS("dve", t1[:], t1[:], 3.1415925, -3.1415925, ALU.min, ALU.max, ["t1"], ["t1"])
                if which == 0:
                    ACT(rop[:, 0, :], t1[:], AF.Sin, ["t1"], [("rop", 0)])
                else:
                    ACT(rop[:, 1, :], t1[:], AF.Sin, ["t1", "cst"], [("rop", 1)], scale=cst[:, 1:2])
            DMA("sp", ropd, rop[:], [("rop", 0), ("rop", 1)], ["ropd"])
            if s == 0 and "rop" in dbg:
                DMA("sp", dbg_tensor("rop", [128, 2, L], F32), rop[:], [("rop", 0), ("rop", 1)], ["dbg_rop"])
            S.barrier()
        e0b = ExitStack()
        ln_alloc(e0b)
        if first:
            ln_rows_load(lnin)
            ln_pipeline(s, list(range(NT)), None, False, x_src_fn=lambda i_: xin[s, i_ * 128:(i_ + 1) * 128, :])
        else:
            for i in range(NT):
                t = LN["t"][i % 2]
                tk = ("lnt", i % 2)
                DMA("sp", t[:], xin[s, i * 128:(i + 1) * 128, :], [], [tk], slot=("lnt_in", i % 2))
                DMA("sp", hres[i * 128:(i + 1) * 128, :], t[:], [tk], [("hres", i)], slot=("lnt_out", i % 2))
                for hb in range(2):
                    b = nb()
                    for kk in range(4):
                        k = hb * 4 + kk
                        TR(PS(b, kk * 128, (kk + 1) * 128), t[:, k * 128:(k + 1) * 128], identf[:], [tk, "identf"], [PT_(b)])
                    CP("dve", hT[:, hb * 4:(hb + 1) * 4, i * 128:(i + 1) * 128], PS(b).rearrange("p (a n) -> p a n", a=4),
                       [], [PT_(b), ("hT", i // 4)])
        if s == 0 and "hT0" in dbg:
            DMA("sp", dbg_tensor("hT0", [128, 8, L], BF16), hT[:], [("hT", g) for g in range(4)], ["dbg_hT0"])
        S.barrier()
        e0b.close()

        for l in layers:
            par = parb[l]
            pk = ("par", l)
            last_layer = (l == layers[-1])
            qT = arena[:, 0:4 * L].rearrange("p (c t) -> p c t", c=4)
            qiT = arena[:, 4 * L:8 * L].rearrange("p (c t) -> p c t", c=4)
            merged = arena[:, 0:8 * L].rearrange("p (c t) -> p c t", c=8)
            actb = arena[:, 0:NJ * 1024].rearrange("p (c t) -> p c t", c=NJ)
            with ExitStack() as eA:
                OT = sb(eA, "OT", [128, 4, L], BF16)
                with ExitStack() as eB:
                    kT = sb(eB, "kT", [128, L], BF16)
                    vtm = sb(eB, "vtm", [128, NT, 128], BF16)
                    kiT = sb(eB, "kiT", [128, L], BF16)
                    witm = sb(eB, "witm", [128, NT * 8], F32)
                    with ExitStack() as e2:
                        rop = sb(e2, "rop2", [128, 2, L], F32)
                        vT = sb(e2, "vT", [128, L], F32)
                        ta = [sb(e2, "ta%d" % i, [128, 512], F32) for i in range(2)]
                        tb = [sb(e2, "tb%d" % i, [128, 512], F32) for i in range(2)]
                        wwib = sb(e2, "wwib", [128, 8, 8], BF16)
                        bwib = sb(e2, "bwib", [128, 128], F32)
                        DMA("sp", rop[:], ropd, ["ropd"], ["rop"])
                        DMA("pool", wwib[:], W[("wwi", l)], [], ["wwib"])
                        DMA("sp", bwib[:], W[("bwi", l)], [], ["bwib"])
                        pairs = []
                        for j in range(4):
                            pairs.append((CH_Q[j], CH_QS[j], ("q", j)))
                        pairs.append((CH_K, CH_KS, ("k", 0)))
                        for j in range(4):
                            pairs.append((CH_QI[j], CH_QIS[j], ("qi", j)))
                        pairs.append((CH_KI, CH_KIS, ("ki", 0)))
                        chunks = []
                        for a_, b_, _ in pairs:
                            chunks += [a_, b_]
                        chunks.append(CH_V)
                        stm = Stream(l, chunks)
                        cnt2 = 0
                        for pi, (ca, cs, (kind, j)) in enumerate(pairs):
                            sa = stm.get(2 * pi)
                            ss = stm.get(2 * pi + 1)
                            for g in range(NG):
                                ba = nb()
                                bs_ = nb()
                                proj_chunk(sa, g, ba)
                                proj_chunk(ss, g, bs_)
                                A_ = ta[cnt2 % 2]
                                B_ = tb[cnt2 % 2]
                                ak = ("ta", cnt2 % 2)
                                bk = ("tb", cnt2 % 2)
                                cnt2 += 1
                                gs = slice(g * 512, (g + 1) * 512)
                                STT(A_[:], PS(ba), par[:, PC_BA + ca:PC_BA + ca + 1], rop[:, 0, gs], ALU.add, ALU.mult,
                                    [pk, "rop"], [ak, PT_(ba)])
                                ACT(B_[:], PS(bs_), AF.Identity, [pk], [bk, PT_(bs_)], bias=par[:, PC_BA + cs:PC_BA + cs + 1])
                                TT("pool", B_[:], B_[:], rop[:, 1, gs], ALU.mult, [bk, "rop"], [bk])
                                if kind == "q":
                                    dst = qT[:, j, gs]
                                elif kind == "k":
                                    dst = kT[:, gs]
                                elif kind == "qi":
                                    dst = qiT[:, j, gs]
                                else:
                                    dst = kiT[:, gs]
                                TT("dve", dst, A_[:], B_[:], ALU.add, [ak, bk], [(kind, j, g)])
                        sv = stm.get(len(chunks) - 1)
                        for g in range(NG):
                            b = nb()
                            proj_chunk(sv, g, b)
                            ACT(vT[:, g * 512:(g + 1) * 512], PS(b), AF.Identity, [pk], [("vT", g), PT_(b)],
                                bias=par[:, PC_BA + CH_V:PC_BA + CH_V + 1])
                        for g in range(NG):
                            b = nb()
                            for kk in range(4):
                                i = g * 4 + kk
                                TR(PS(b, kk * 128, (kk + 1) * 128), vT[:, i * 128:(i + 1) * 128], identf[:],
                                   [("vT", g), "identf"], [PT_(b)])
                            CP("dve", vtm[:, g * 4:(g + 1) * 4, :], PS(b).rearrange("p (a n) -> p a n", a=4), [],
                               [PT_(b), ("vtm", g)])
                        b = nb()
                        for i in range(NT):
                            for k in range(8):
                                PE(PS(b, i * 8, (i + 1) * 8), hT[:, k, i * 128:(i + 1) * 128], wwib[:, k, :], k == 0, k == 7,
                                   ["wwib", ("hT", i // 4)], [PT_(b)])
                        TT("dve", witm[:], PS(b, 0, 128), bwib[:], ALU.add, ["bwib"], [PT_(b), "witm"])
                        if s == 0 and l == layers[0]:
                            if "qT" in dbg:
                                DMA("sp", dbg_tensor("qT", [128, 4, L], BF16), qT, [("q", j, g) for j in range(4) for g in range(4)], ["dbg_qT"])
                            if "kT" in dbg:
                                DMA("sp", dbg_tensor("kT", [128, L], BF16), kT[:], [("k", 0, g) for g in range(4)], ["dbg_kT"])
                            if "kiT" in dbg:
                                DMA("sp", dbg_tensor("kiT", [128, L], BF16), kiT[:], [("ki", 0, g) for g in range(4)], ["dbg_kiT"])
                            if "qiT" in dbg:
                                DMA("sp", dbg_tensor("qiT", [128, 4, L], BF16), qiT, [("qi", j, g) for j in range(4) for g in range(4)], ["dbg_qiT"])
                            if "vtm" in dbg:
                                DMA("sp", dbg_tensor("vtm", [128, NT, 128], BF16), vtm[:], [("vtm", g) for g in range(4)], ["dbg_vtm"])
                            if "witm" in dbg:
                                DMA("sp", dbg_tensor("witm", [128, 128], F32), witm[:], ["witm"], ["dbg_witm"])
                        S.barrier()

                    with ExitStack() as e3:
                        SCW = 58 * 128
                        SC = sb(e3, "SC", [128, SCW], F32)
                        selT = sb(e3, "selT", [128, NT, 512], BF16)
                        selq = sb(e3, "selq", [128, L], F32)
                        junkD = sb(e3, "junkD", [128, L], BF16)
                        junkA = sb(e3, "junkA", [128, L], BF16)
                        rt = [sb(e3, "rt%d" % i, [128, 2, 512], BF16) for i in range(4)]
                        Dg = [sb(e3, "Dg%d" % i, [128, 8, 128], BF16) for i in range(2)]
                        PTb = [sb(e3, "PT%d" % i, [128, 2, 512], BF16) for i in range(4)]
                        rec = [sb(e3, "rec%d" % i, [128, 512], F32) for i in range(2)]
                        amax = sb(e3, "amax", [128, 4], F32)
                        lo = sb(e3, "lo", [128, 4], F32)
                        mid = sb(e3, "mid", [128, 4], F32)
                        nmid = sb(e3, "nmid", [128, 4], F32)
                        cnt = sb(e3, "cnt", [128, 4], F32)
                        thr = sb(e3, "thr", [128, 4], F32)
                        dd = sb(e3, "dd", [128, 4], F32)
                        tt_ = sb(e3, "tt_", [128, 4], F32)
                        Wt = sb(e3, "Wt", [128, NIT, 4], F32)
                        rt_ctr = [0]
                        pt_ctr = [0]
                        dp_ctr = [0]

                        if s == 0 and l == layers[0] and "tau" in dbg:
                            dtau = dbg_tensor("tau", [128, NT], F32)
                            dsel = dbg_tensor("selT", [4, 128, NT, 512], BF16)
                        else:
                            dtau = None

                        def blk_off(g, r):
                            return sum((4 * g + rr + 1) * 128 for rr in range(r))

                        sb_ctr = [0]

                        def dpair():
                            b = (dp_ctr[0] % 3) * 2
                            dp_ctr[0] += 1
                            return b

                        def idx_steps(g):
                            steps = []
                            for r in range(4):
                                i = 4 * g + r
                                off = blk_off(g, r)
                                wdt = (i + 1) * 128

                                def mk_dg(i=i):
                                    D_ = Dg[i % 2]
                                    for h in range(8):
                                        TS("dve", D_[:, h, :], identb[:], witm[:, i * 8 + h:i * 8 + h + 1], None, ALU.mult, None,
                                           ["identb", "witm"], [("Dg", i % 2)])
                                steps.append((mk_dg, None))
                                for kc in range((wdt + 511) // 512):
                                    ncols = min(512, wdt - kc * 512)
                                    sbank = [None]
                                    for jh in range(4):
                                        st = {}

                                        def stepA(i=i, kc=kc, ncols=ncols, jh=jh, st=st):
                                            bp = dpair()
                                            st["R"] = rt[rt_ctr[0] % 4]
                                            st["rk"] = ("rt", rt_ctr[0] % 4)
                                            rt_ctr[0] += 1
                                            for hh in range(2):
                                                base = hh * 64
                                                PE(PS(bp + hh, 0, ncols), qiT[base:base + 64, jh, i * 128:(i + 1) * 128],
                                                   kiT[base:base + 64, kc * 512:kc * 512 + ncols], True, True,
                                                   [("qi", jh, i // 4), ("ki", 0, kc)], [PT_(bp + hh)])
                                            ACT(st["R"][:, :, 0:ncols], psum[:, bp:bp + 2, 0:ncols], AF.Relu, [],
                                                [st["rk"], PT_(bp), PT_(bp + 1)])

                                        def stepB(i=i, off=off, kc=kc, ncols=ncols, jh=jh, r=r, sbank=sbank, st=st):
                                            if jh == 0:
                                                sbank[0] = 6 + (sb_ctr[0] % 2)
                                                sb_ctr[0] += 1
                                            bs_ = sbank[0]
                                            for hh in range(2):
                                                h = 2 * jh + hh
                                                PE(PS(bs_, 0, ncols), Dg[i % 2][:, h, :], st["R"][:, hh, 0:ncols], h == 0, h == 7,
                                                   [st["rk"], ("Dg", i % 2)], [PT_(bs_)])
                                            if jh == 3:
                                                CP("dve", SC[:, off + kc * 512:off + kc * 512 + ncols], PS(bs_, 0, ncols), [],
                                                   [PT_(bs_), ("SC", r, kc)])
                                        steps.append((stepA, stepB))

                                def fin(i=i, off=off, wdt=wdt, r=r):
                                    sks = [("SC", r, kc) for kc in range((wdt + 511) // 512)]
                                    S.add("dve", lambda e: e.tensor_reduce(out=amax[:, r:r + 1], in_=SC[:, off:off + wdt], axis=AX.X,
                                                                          op=ALU.max, apply_absolute_value=True),
                                          reads=sks, writes=[("amax", r)])
                                    dk = ("SC", r, i // 4)
                                    TT("pool", SC[:, off + i * 128:off + (i + 1) * 128], SC[:, off + i * 128:off + (i + 1) * 128],
                                       cmask[:], ALU.add, [dk, "cmask", ("amax", r)], [dk])
                                steps.append((None, fin))
                            return steps

                        def bis_steps(g):
                            steps = []
                            blocks = [r for r in range(4) if 4 * g + r >= 2]

                            def init():
                                aks = [("amax", r) for r in range(4)]
                                MS("dve", lo[:], -1.0e29, ["lo"])
                                MS("dve", thr[:], TOPK - 0.5, ["thr"])
                                MS("dve", cnt[:], 0.0, [("cnt", r) for r in range(4)])
                                for r in blocks:
                                    wdt = (4 * g + r + 1) * 128
                                    TS("dve", lo[:, r:r + 1], amax[:, r:r + 1], -1.0, None, ALU.mult, None, aks, ["lo"])
                                    if r % 2 == 1:
                                        MS("dve", thr[:, r:r + 1], float(2 * TOPK - 1 - wdt), ["thr"])
                                TS("dve", Wt[:, 0, :], amax[:], 1.0000005, 1.0e-30, ALU.mult, ALU.add, aks, ["Wt"])
                                for it in range(1, NIT):
                                    TS("dve", Wt[:, it, :], Wt[:, 0, :], float(2.0 ** (-it)), None, ALU.mult, None, ["Wt"], ["Wt"])
                                for r in range(4):
                                    if r not in blocks:
                                        MS("dve", Wt[:, :, r:r + 1], 0.0, ["Wt"])
                            steps.append(init)
                            if not blocks:
                                return steps
                            for it in range(NIT):
                                def step(it=it):
                                    TT("pool", mid[:], lo[:], Wt[:, it, :], ALU.add, ["lo", "Wt"], ["mid"])
                                    TS("pool", nmid[:], mid[:], -1.0, 0.0, ALU.mult, ALU.add, ["mid"], ["nmid"])
                                    for r in blocks:
                                        i = 4 * g + r
                                        off = blk_off(g, r)
                                        wdt = (i + 1) * 128
                                        sks = [("SC", r, kc) for kc in range((wdt + 511) // 512)]
                                        if r % 2 == 0:
                                            TS("dve", junkD[:, 0:wdt], SC[:, off:off + wdt], mid[:, r:r + 1], None, ALU.is_ge, ALU.add,
                                               sks + ["mid"], ["junkD", ("cnt", r)], accum=cnt[:, r:r + 1])
                                        else:
                                            ACT(junkA[:, 0:wdt], SC[:, off:off + wdt], AF.Sign, sks + ["nmid"], ["junkA", ("cnt", r)],
                                                bias=nmid[:, r:r + 1], accum=cnt[:, r:r + 1])
                                    cks = [("cnt", r) for r in range(4)]
                                    TT("pool", dd[:], cnt[:], thr[:], ALU.subtract, cks + ["thr"], ["dd"])
                                    TS("pool", dd[:], dd[:], 0.0, None, ALU.is_ge, None, ["dd"], ["dd"])
                                    TT("pool", tt_[:], dd[:], Wt[:, it, :], ALU.mult, ["dd", "Wt"], ["tt_"])
                                    TT("pool", lo[:], lo[:], tt_[:], ALU.add, ["lo", "tt_"], ["lo"])
                                steps.append(step)
                            return steps

                        def sel_steps(g):
                            steps = []
                            for r in range(4):
                                def step(r=r):
                                    i = 4 * g + r
                                    off = blk_off(g, r)
                                    wdt = (i + 1) * 128
                                    sks = [("SC", r, kc) for kc in range((wdt + 511) // 512)]
                                    TS("dve", selq[:, 0:wdt], SC[:, off:off + wdt], lo[:, r:r + 1], None, ALU.is_ge, None,
                                       sks + ["lo"], ["selq"])
                                    for m in range((i + 4) // 4):
                                        nk = min(4, i + 1 - 4 * m)
                                        b = nb()
                                        for kk in range(nk):
                                            kt = 4 * m + kk
                                            TR(PS(b, kk * 128, (kk + 1) * 128), selq[:, kt * 128:(kt + 1) * 128], identf[:],
                                               ["selq", "identf"], [PT_(b)])
                                        dst = selT[:, 4 * m:4 * m + nk, r * 128:(r + 1) * 128]
                                        src = PS(b, 0, nk * 128).rearrange("p (a n) -> p a n", a=nk)
                                        wk = [("selT", 4 * m + kk) for kk in range(nk)]
                                        if m % 2 == 0:
                                            ACT(dst, src, AF.Identity, [], [PT_(b)] + wk)
                                        else:
                                            CP("dve", dst, src, [], [PT_(b)] + wk)
                                steps.append(step)
                            if dtau is not None:
                                def dump():
                                    DMA("sp", dtau[:, 4 * g:4 * g + 4], lo[:], ["lo"], [("dtau", g)])
                                    DMA("sp", dsel[g], selT[:], [("selT", kt) for kt in range(NT)], [("dsel", g)])
                                steps.append(dump)
                            return steps

                        def attn_steps(g):
                            steps = []
                            nkt = 4 * g + 4
                            for j in range(4):
                                for kt in range(nkt):
                                    st = {}

                                    def stepA(j=j, kt=kt, st=st):
                                        q0 = max(0, kt - 4 * g) * 128
                                        N = 512 - q0
                                        bp = dpair()
                                        P_ = PTb[pt_ctr[0] % 4]
                                        ptk = ("PT", pt_ctr[0] % 4)
                                        pt_ctr[0] += 1
                                        st["P"] = P_
                                        st["ptk"] = ptk
                                        for hh in range(2):
                                            base = hh * 64
                                            PE(PS(bp + hh, 0, N), kT[base:base + 64, kt * 128:(kt + 1) * 128],
                                               qT[base:base + 64, j, g * 512 + q0:(g + 1) * 512], True, True,
                                               [("k", 0, kt // 4), ("q", j, g)], [PT_(bp + hh)])
                                        ACT(P_[:, :, 0:N], psum[:, bp:bp + 2, 0:N], AF.Exp, [], [ptk, PT_(bp), PT_(bp + 1)], scale=0.125)
                                        for hh in range(2):
                                            TT("dve", P_[:, hh, 0:N], P_[:, hh, 0:N], selT[:, kt, q0:512], ALU.mult,
                                               [ptk, ("selT", kt)], [ptk])

                                    def stepB(j=j, kt=kt, st=st):
                                        q0 = max(0, kt - 4 * g) * 128
                                        N = 512 - q0
                                        po = 6
                                        pd = 7
                                        P_ = st["P"]
                                        ptk = st["ptk"]
                                        for hh in range(2):
                                            base = hh * 64
                                            PE(psum[base:base + 64, po, q0:512], vtm[:, kt, base:base + 64], P_[:, hh, 0:N],
                                               kt == 0, kt == nkt - 1, [ptk, ("vtm", kt // 4)], [PT_(po)])
                                            PE(psum[base:base + 64, pd, q0:512], onesb[:, 0:64], P_[:, hh, 0:N],
                                               kt == 0, kt == nkt - 1, [ptk, "onesb"], [PT_(pd)])
                                        if kt == nkt - 1:
                                            R = rec[j % 2]
                                            rk = ("rec", j % 2)
                                            S.add("dve", lambda e: e.reciprocal(out=R[:], in_=PS(pd)), reads=[], writes=[rk, PT_(pd)])
                                            TT("dve", OT[:, j, g * 512:(g + 1) * 512], PS(po), R[:], ALU.mult, [rk], [PT_(po), ("OT", j, g)])
                                    steps.append((stepA, stepB))
                            return steps

                        def pipe(pairs, lag):
                            out = []
                            n = len(pairs)
                            for t_ in range(n + lag):
                                if t_ < n and pairs[t_][0] is not None:
                                    out.append(pairs[t_][0])
                                if t_ - lag >= 0 and pairs[t_ - lag][1] is not None:
                                    out.append(pairs[t_ - lag][1])
                            return out

                        def run(steps):
                            for st_ in steps:
                                st_()

                        def interleave(a, b):
                            na, nb_ = len(a), len(b)
                            ia = ib = 0
                            while ia < na or ib < nb_:
                                if ib >= nb_ or (ia < na and ia * nb_ <= ib * na):
                                    a[ia]()
                                    ia += 1
                                else:
                                    b[ib]()
                                    ib += 1

                        run(pipe(idx_steps(0), 1))
                        run(bis_steps(0))
                        run(sel_steps(0))
                        for g in range(1, NG):
                            run(pipe(idx_steps(g), 1))
                            interleave(bis_steps(g), pipe(attn_steps(g - 1), 2))
                            run(sel_steps(g))
                        run(pipe(attn_steps(NG - 1), 2))
                        if s == 0 and l == layers[0] and "OT" in dbg:
                            DMA("sp", dbg_tensor("OT", [128, 4, L], BF16), OT[:], [("OT", j, g) for j in range(4) for g in range(4)], ["dbg_OT"])
                        S.barrier()
                S.barrier()
                with ExitStack() as eC:
                    ypool = sb(eC, "ypool", [128, 2, L], BF16)
                    yconv = sb(eC, "yconv", [128, 2, L], BF16)
                    outw = sb(eC, "outw", [128, 8, 1024], BF16)
                    DMA("pool", outw[:], W[("outw", l)], [], ["outw"])
                    with ExitStack() as e1:
                        up = sb(e1, "up", [128, 2, 16 + L], F32)
                        sA = sb(e1, "sA", [128, 16 + L], F32)
                        sB = sb(e1, "sB", [128, 16 + L], F32)
                        mixed = sb(e1, "mixed", [128, 2, L], BF16)
                        t16 = sb(e1, "t16", [128, 16], F32)
                        pwb = sb(e1, "pwb", [128, 2, 128], BF16)
                        DMA("pool", pwb[:], W[("pwbd", l)], [], ["pwb"])
                        MS("pool", up[:, :, 0:16], 0.0, [("up", 0, -1), ("up", 1, -1)])
                        MS("pool", sA[:, 0:16], 0.0, ["sA"])
                        MS("pool", sB[:, 0:16], 0.0, ["sB"])
                        stm = Stream(l, CH_POOL)
                        for c in range(2):
                            sl = stm.get(c)
                            for g in range(NG):
                                b = nb()
                                proj_chunk(sl, g, b)
                                ACT(up[:, c, 16 + g * 512:16 + (g + 1) * 512], PS(b), AF.Identity, [pk], [PT_(b), ("up", c, g)],
                                    bias=par[:, PC_BA + c:PC_BA + c + 1])
                        for c in range(2):
                            uk = [("up", c, g) for g in range(-1, 4)]
                            U = up[:, c, :]
                            TT("dve", sA[:, 16:], U[:, 16:], U[:, 15:15 + L], ALU.add, uk, ["sA"])
                            TT("dve", sB[:, 16:], sA[:, 16:], sA[:, 14:14 + L], ALU.add, ["sA"], ["sB"])
                            if c == 1:
                                TT("dve", sA[:, 16:], sB[:, 16:], sB[:, 12:12 + L], ALU.add, ["sB"], ["sA"])
                                TT("dve", sB[:, 16:], sA[:, 16:], sA[:, 8:8 + L], ALU.add, ["sA"], ["sB"])
                            for half, (sbuf_, sk) in enumerate(((sA, "sA"), (sB, "sB"))):
                                widx = 2 * c + half
                                win = float(2 ** (widx + 1))
                                pr = slice(half * 64, half * 64 + 64)
                                STT(mixed[pr, c, 16:], sbuf_[pr, 32:], 1.0 / win, U[pr, 32:], ALU.mult, ALU.subtract, [sk] + uk,
                                    [("mixed", c, half)])
                                TT("dve", t16[pr, :], sbuf_[pr, 16:32], rct[pr, widx, :], ALU.mult, [sk, ("rct", widx)], ["t16"])
                                TT("dve", mixed[pr, c, 0:16], t16[pr, :], U[pr, 16:32], ALU.subtract, ["t16"] + uk, [("mixed", c, half)])
                        for c in range(2):
                            for g in range(NG):
                                b = nb()
                                PE(PS(b), pwb[:, c, :], mixed[:, c, g * 512:(g + 1) * 512], True, True,
                                   ["pwb", ("mixed", c, 0), ("mixed", c, 1)], [PT_(b)])
                                ACT(ypool[:, c, g * 512:(g + 1) * 512], PS(b), AF.Identity, [pk], [PT_(b), ("ypool", c, g)],
                                    scale=par[:, PC_PSC + c:PC_PSC + c + 1])
                        S.barrier()
                    with ExitStack() as e1:
                        glu = sb(e1, "glu", [128, 2, 30 + L], BF16)
                        dg = sb(e1, "dg", [128, 2, 31, 128], BF16)
                        xcv = sb(e1, "xcv", [128, 2, L], F32)
                        xsq = [sb(e1, "xsq%d" % i, [128, 512], F32) for i in range(2)]
                        sgt = [sb(e1, "sgt%d" % i, [128, 512], F32) for i in range(2)]
                        mean_t = sb(e1, "mean_t", [128, 512], F32)
                        var_t = sb(e1, "var_t", [128, 512], F32)
                        dtmp = [sb(e1, "dtmp%d" % i, [128, 512], F32) for i in range(2)]
                        MS("pool", glu[:, :, 0:30], 0.0, [("glu", 0, -1), ("glu", 1, -1)])
                        for c in range(2):
                            for jj in range(31):
                                TS("dve", dg[:, c, jj, :], identb[:], par[:, PC_CDW + c * 31 + jj:PC_CDW + c * 31 + jj + 1], None,
                                   ALU.mult, None, ["identb", pk], [("dg", c)])
                        stm = Stream(l, [CH_CA[0], CH_CG[0], CH_CA[1], CH_CG[1]])
                        cc = 0
                        for c in range(2):
                            sa_ = stm.get(2 * c)
                            sg_ = stm.get(2 * c + 1)
                            for g in range(NG):
                                ba = nb()
                                bg = nb()
                                proj_chunk(sa_, g, ba)
                                proj_chunk(sg_, g, bg)
                                T_ = sgt[cc % 2]
                                tk_ = ("sgt", cc % 2)
                                cc += 1
                                ACT(T_[:], PS(bg), AF.Sigmoid, [pk], [tk_, PT_(bg)], bias=par[:, PC_BA + CH_CG[c]:PC_BA + CH_CG[c] + 1])
                                STT(glu[:, c, 30 + g * 512:30 + (g + 1) * 512], PS(ba), par[:, PC_BA + CH_CA[c]:PC_BA + CH_CA[c] + 1],
                                    T_[:], ALU.add, ALU.mult, [pk, tk_], [PT_(ba), ("glu", c, g)])
                        for g in range(NG):
                            bm = nb()
                            bq = nb()
                            for c in range(2):
                                b = nb()
                                gk = [("glu", c, gg) for gg in range(-1, 4)]
                                for jj in range(31):
                                    PE(PS(b), dg[:, c, jj, :], glu[:, c, g * 512 + jj:g * 512 + jj + 512], jj == 0, jj == 30,
                                       [("dg", c)] + gk, [PT_(b)])
                                X2 = xsq[c]
                                ACT(xcv[:, c, g * 512:(g + 1) * 512], PS(b), AF.Identity, [pk], [PT_(b), ("xcv", c, g)],
                                    bias=par[:, PC_CDB + c:PC_CDB + c + 1])
                                ACT(X2[:], PS(b), AF.Square, [pk], [PT_(b), ("xsq", c)], bias=par[:, PC_CDB + c:PC_CDB + c + 1])
                            for c in range(2):
                                PE(PS(bm), onesf[:], xcv[:, c, g * 512:(g + 1) * 512], c == 0, c == 1, ["onesf", ("xcv", c, g)], [PT_(bm)])
                            for c in range(2):
                                PE(PS(bq), onesf[:], xsq[c][:], c == 0, c == 1, ["onesf", ("xsq", c)], [PT_(bq)])
                            TS("dve", mean_t[:], PS(bm), 1.0 / 256, None, ALU.mult, None, [], ["mean_t", PT_(bm)])
                            TT("dve", var_t[:], mean_t[:], mean_t[:], ALU.mult, ["mean_t"], ["var_t"])
                            STT(var_t[:], PS(bq), 1.0 / 256, var_t[:], ALU.mult, ALU.subtract, ["var_t"], ["var_t", PT_(bq)])
                            ACT(var_t[:], var_t[:], AF.Sqrt, ["var_t"], ["var_t"], bias=EPS)
                            S.add("dve", lambda e: e.reciprocal(out=var_t[:], in_=var_t[:]), reads=["var_t"], writes=["var_t"])
                            for c in range(2):
                                Dm = dtmp[c]
                                dk_ = ("dtmp", c)
                                TT("dve", Dm[:], xcv[:, c, g * 512:(g + 1) * 512], mean_t[:], ALU.subtract, [("xcv", c, g), "mean_t"], [dk_])
                                TT("pool", Dm[:], Dm[:], var_t[:], ALU.mult, [dk_, "var_t"], [dk_])
                                ACT(yconv[:, c, g * 512:(g + 1) * 512], Dm[:], AF.Silu, [dk_, pk], [("yconv", c, g)],
                                    bias=par[:, PC_CLB + c:PC_CLB + c + 1], scale=par[:, PC_CLG + c:PC_CLG + c + 1])
                        S.barrier()
                    if s == 0 and l == layers[0]:
                        if "ypool" in dbg:
                            DMA("sp", dbg_tensor("ypool", [128, 2, L], BF16), ypool[:], [("ypool", c, g) for c in range(2) for g in range(4)], ["dbg_ypool"])
                        if "yconv" in dbg:
                            DMA("sp", dbg_tensor("yconv", [128, 2, L], BF16), yconv[:], [("yconv", c, g) for c in range(2) for g in range(4)], ["dbg_yconv"])
                    with ExitStack() as e4:
                        wo = sb(e4, "wo", [128, 8, 1024], BF16)
                        DMA("pool", wo[:], W[("wo", l)], [], ["wo"])
                        sg3 = [sb(e4, "sg3_%d" % i, [128, 512], F32) for i in range(3)]
                        mm_ = [sb(e4, "mm_%d" % i, [128, 512], F32) for i in range(3)]
                        chunks = []
                        for c in range(8):
                            chunks += [CH_GATE0 + i * 8 + c for i in range(3)]
                        stm = Stream(l, chunks, ahead=1)
                        ysrc = [(ypool, "ypool", 0, 2), (yconv, "yconv", 2, 2), (OT, "OT", 4, 4)]
                        for c in range(8):
                            sl3 = [stm.get(3 * c + i) for i in range(3)]
                            for g in range(NG):
                                gs = slice(g * 512, (g + 1) * 512)
                                for i in range(3):
                                    bg = nb()
                                    proj_chunk(sl3[i], g, bg)
                                    ch = CH_GATE0 + i * 8 + c
                                    ACT(sg3[i][:], PS(bg), AF.Sigmoid, [pk], [("sg3", i), PT_(bg)], bias=par[:, PC_BA + ch:PC_BA + ch + 1])
                                for i, (ysb, yn, k0, nk) in enumerate(ysrc):
                                    by = nb()
                                    for k in range(nk):
                                        PE(PS(by), outw[:, k0 + k, c * 128:(c + 1) * 128], ysb[:, k, gs], k == 0, k == nk - 1,
                                           ["outw", (yn, k, g)], [PT_(by)])
                                    TT("dve", mm_[i][:], PS(by), sg3[i][:], ALU.mult, [("sg3", i)], [("mm_", i), PT_(by)])
                                TT("pool", mm_[0][:], mm_[0][:], mm_[1][:], ALU.add, [("mm_", 0), ("mm_", 1)], [("mm_", 0)])
                                TT("pool", merged[:, c, gs], mm_[0][:], mm_[2][:], ALU.add, [("mm_", 0), ("mm_", 2)], [("merged", c, g)])
                        with ExitStack() as e5:
                            ln_alloc(e5)
                            ln_rows_load(W[("lnr", l)][:, 0:2048])
                            def mm5(i):
                                bp = npair()
                                for n in range(2):
                                    for k in range(8):
                                        PE(PS(bp + n), merged[:, k, i * 128:(i + 1) * 128], wo[:, k, n * 512:(n + 1) * 512], k == 0, k == 7,
                                           ["wo", ("merged", k, i // 4)], [PT_(bp + n)])
                                return bp
                            ln_pipeline(s, list(range(NT)), mm5, False)
                            S.barrier()
                S.barrier()
            if s == 0 and l == layers[0] and "merged" in dbg:
                DMA("sp", dbg_tensor("merged", [128, 8, L], BF16), merged, [("merged", c, g) for c in range(8) for g in range(4)], ["dbg_merged"])
            if s == 0 and l == layers[0] and "hT1" in dbg:
                DMA("sp", dbg_tensor("hT1", [128, 8, L], BF16), hT[:], [("hT", g) for g in range(4)], ["dbg_hT1"])
            with ExitStack() as e6:
                wd = sb(e6, "wd", [128, NJ, 1024], BF16)
                ln_alloc(e6)
                NWU = 3
                wUb = [sb(e6, "wU%d" % i, [128, 8, 256], BF16) for i in range(NWU)]
                xg = [sb(e6, "xg%d" % i, [128, 2 + 1024], F32) for i in range(2)]
                xv = [sb(e6, "xv%d" % i, [128, 2 + 1024], F32) for i in range(2)]
                ug = [sb(e6, "ug%d" % i, [128, 1024], F32) for i in range(2)]
                uv = [sb(e6, "uv%d" % i, [128, 1024], F32) for i in range(2)]
                halo = sb(e6, "halo", [128, 2 * NJ, 2], F32)
                ln_rows_load(W[("lnr", l)][:, 2048:4096])
                wu_ctr = [0]
                wu_slots = {}

                def wu_get(idx):
                    while wu_ctr[0] <= min(idx + 1, 2 * NJ - 1):
                        n_ = wu_ctr[0]
                        sl = n_ % NWU
                        DMA("pool", wUb[sl][:], W[("wU", l)][n_ % NJ], [], [("wU", sl)], slot=("wU", sl))
                        wu_slots[n_] = sl
                        wu_ctr[0] += 1
                    return wu_slots[idx]

                wu_get(0)
                wd_parts = [(0, 6), (6, 11), (11, 17), (17, 22)]
                for kq, (k0_, k1_) in enumerate(wd_parts):
                    DMA("pool", wd[:, k0_:k1_, :], W[("wd", l)][:, k0_:k1_, :], [], [("wd", kq)])
                for hf in range(2):
                    t0 = hf * 1024
                    for j in range(NJ):
                        sl = wu_get(hf * NJ + j)
                        p_ = j % 2
                        XG, XV, UG, UV = xg[p_], xv[p_], ug[p_], uv[p_]
                        for (X, xk, coff, hidx) in ((XG, ("xg", p_), 0, j), (XV, ("xv", p_), 128, NJ + j)):
                            if hf == 0:
                                CP("pool", X[:, 0:2], zero2[:], ["zero2"], [xk])
                            else:
                                CP("pool", X[:, 0:2], halo[:, hidx, :], [("halo", hidx)], [xk])
                            for gg in range(2):
                                g = hf * 2 + gg
                                b = nb()
                                for k in range(8):
                                    PE(PS(b), wUb[sl][:, k, coff:coff + 128], hT[:, k, g * 512:(g + 1) * 512], k == 0, k == 7,
                                       [("wU", sl), ("hT", g)], [PT_(b)])
                                ACT(X[:, 2 + gg * 512:2 + (gg + 1) * 512], PS(b), AF.Identity, [], [PT_(b), xk])
                            if hf == 0:
                                CP("pool", halo[:, hidx, :], X[:, 1024:1026], [xk], [("halo", hidx)])
                        for (X, xk, U, uk, wc, bc) in ((XG, ("xg", p_), UG, ("ug", p_), PC_FWG + 3 * j, PC_FBG + j),
                                                      (XV, ("xv", p_), UV, ("uv", p_), PC_FWV + 3 * j, PC_FBV + j)):
                            ACT(U[:], X[:, 2:1026], AF.Identity, [xk, pk], [uk], bias=par[:, bc:bc + 1], scale=par[:, wc + 2:wc + 3])
                            STT(U[:], X[:, 1:1025], par[:, wc + 1:wc + 2], U[:], ALU.mult, ALU.add, [xk, pk, uk], [uk])
                            STT(U[:], X[:, 0:1024], par[:, wc:wc + 1], U[:], ALU.mult, ALU.add, [xk, pk, uk], [uk])
                        ACT(UG[:], UG[:], AF.Silu, [("ug", p_)], [("ug", p_)])
                        TT("dve", actb[:, j, 0:1024], UG[:], UV[:], ALU.mult, [("ug", p_), ("uv", p_)], [("act", j)])
                    def mm6(i, hf=hf):
                        ii = i - hf * 8
                        bp = npair()
                        for n in range(2):
                            for k in range(NJ):
                                PE(PS(bp + n), actb[:, k, ii * 128:(ii + 1) * 128], wd[:, k, n * 512:(n + 1) * 512], k == 0, k == NJ - 1,
                                   [("wd", 0 if k < 6 else (1 if k < 11 else (2 if k < 17 else 3))), ("act", k)], [PT_(bp + n)])
                        return bp
                    ln_pipeline(s, [hf * 8 + ii for ii in range(8)], mm6, last_layer)
                S.barrier()
        if not last:
            pass
    S.emit()
    es_top.close()
    return nc, S, dbg_out


_CACHE = {}


def _in_maps(x, positions, wts, layers, ncores, nseq):
    maps = []
    for c in range(ncores):
        m = {"x": np.ascontiguousarray(x[c * nseq:(c + 1) * nseq]),
             "pos": np.ascontiguousarray(np.broadcast_to(positions[c * nseq:(c + 1) * nseq, None, :], (nseq, 128, L))).astype(np.int32),
             "lnin": wts["lnin"], "cst": wts["cst"]}
        for l in layers:
            for k in WSHAPES:
                m["%s%d" % (k, l)] = wts["%s%d" % (k, l)]
        maps.append(m)
    return maps


def kernel(**inputs):
    x = np.asarray(inputs["x"], np.float32)
    positions = np.asarray(inputs["positions"], np.int32)
    wts = prep_weights(inputs)
    ncores = 8
    nseq = x.shape[0] // ncores
    key = ("fused", nseq)
    if key not in _CACHE:
        _CACHE[key] = build(nseq=nseq, layers=(0, 1))
    nc, S, _ = _CACHE[key]
    maps = _in_maps(x, positions, wts, (0, 1), ncores, nseq)
    res = run_bass_kernel_spmd(nc, maps, core_ids=list(range(ncores)))
    out = np.concatenate([np.asarray(r["y"]) for r in res.results], axis=0)
    return out.astype(np.float32)
```
